# Optimizing a Trainium2 kernel written in Bass

```python
import jax
import jax.numpy as jnp
from jax import lax
import numpy as np

D_MODEL = 1024
BATCH = 4
SEQ = 8192
DEPTH = 4

N_MIXERS = 4
HEAD_DIM = 64
D_FF = 4 * D_MODEL
ROPE_THETA = 10000.0
LN_EPS = 1e-5
RMS_EPS = 1e-6
NEG_INF = -1e30
DEEPNORM_ALPHA = (2.0 * DEPTH) ** 0.25
DEEPNORM_BETA = (8.0 * DEPTH) ** -0.25
Q_BLOCK = 128

SB_HEADS = D_MODEL // HEAD_DIM

MOBA_HEADS = D_MODEL // HEAD_DIM
MOBA_BLOCK = 256
MOBA_TOPK = 3
MOBA_Q_CHUNK = 32

MLA_HEADS = 16
MLA_Q_LORA = 256
MLA_KV_LORA = 128
MLA_NOPE = 64
MLA_ROPE = 32
MLA_V = 64
MLA_IN = MLA_Q_LORA + MLA_KV_LORA + MLA_ROPE

NSA_HEADS = 16
NSA_KV_HEADS = 4
NSA_CMP_BLOCK = 32
NSA_CMP_STRIDE = 16
NSA_CMP_HIDDEN = 256
NSA_SLC_BLOCK = 64
NSA_SLC_TOPN = 16
NSA_WINDOW = 512
NSA_Q_CHUNK = 64
NSA_FORCE = 1e9
NSA_IN = NSA_HEADS * HEAD_DIM + 6 * NSA_KV_HEADS * HEAD_DIM + 3 * NSA_HEADS

kernel_name = 'hybrid_sb_moba_mla_nsa_deepnorm'


def layer_norm(x, g, b):
    xf = x.astype(jnp.float32)
    mu = jnp.mean(xf, axis=-1, keepdims=True)
    var = jnp.mean(jnp.square(xf - mu), axis=-1, keepdims=True)
    return ((xf - mu) * lax.rsqrt(var + LN_EPS) * g + b).astype(x.dtype)


def rms_norm(x, g):
    xf = x.astype(jnp.float32)
    return (xf * lax.rsqrt(jnp.mean(jnp.square(xf), axis=-1, keepdims=True) + RMS_EPS) * g).astype(x.dtype)


def rope_tables(seq, dim):
    inv_freq = 1.0 / (ROPE_THETA ** (jnp.arange(0, dim, 2, dtype=jnp.float32) / dim))
    ang = jnp.arange(seq, dtype=jnp.float32)[:, None] * inv_freq[None, :]
    ang = jnp.concatenate([ang, ang], axis=-1)
    return jnp.cos(ang), jnp.sin(ang)


def apply_rope(t, cos, sin):
    half = t.shape[-1] // 2
    rot = jnp.concatenate([-t[..., half:], t[..., :half]], axis=-1)
    return (t * cos + rot * sin).astype(t.dtype)


def split_heads(t, n_heads):
    b, s, _ = t.shape
    return t.reshape(b, s, n_heads, -1).transpose(0, 2, 1, 3)


def merge_blocks(o):
    n, b, h, q, d = o.shape
    return o.transpose(1, 0, 3, 2, 4).reshape(b, n * q, h * d)


def stick_breaking_attention(x, w_qkv, w_o):
    b, s, _ = x.shape
    q, k, v = (split_heads(t, SB_HEADS) for t in jnp.split(x @ w_qkv, 3, axis=-1))
    key_pos = jnp.arange(s)
    scale = HEAD_DIM ** -0.5

    def block(i):
        start = i * Q_BLOCK
        qb = lax.dynamic_slice_in_dim(q, start, Q_BLOCK, axis=2)
        z = jnp.einsum('bhqd,bhkd->bhqk', qb, k).astype(jnp.float32) * scale
        q_pos = start + jnp.arange(Q_BLOCK)
        past = key_pos[None, :] < q_pos[:, None]
        log_keep = jnp.where(past, -jax.nn.softplus(z), 0.0)
        log_keep_between = lax.cumsum(log_keep, axis=3, reverse=True) - log_keep
        w = jnp.where(past, jnp.exp(jax.nn.log_sigmoid(z) + log_keep_between), 0.0)
        return jnp.einsum('bhqk,bhkd->bhqd', w.astype(v.dtype), v)

    o = lax.map(block, jnp.arange(s // Q_BLOCK))
    return merge_blocks(o) @ w_o


def moba_attention(x, w_qkv, w_o, cos, sin):
    b, s, _ = x.shape
    h, d = MOBA_HEADS, HEAD_DIM
    q, k, v = (split_heads(t, h) for t in jnp.split(x @ w_qkv, 3, axis=-1))
    q, k = apply_rope(q, cos, sin), apply_rope(k, cos, sin)
    n_kb = -(-s // MOBA_BLOCK)
    pad = ((0, 0), (0, 0), (0, n_kb * MOBA_BLOCK - s), (0, 0))
    k_pad, v_pad = jnp.pad(k, pad), jnp.pad(v, pad)
    kb = k_pad.reshape(b, h, n_kb, MOBA_BLOCK, d)
    vb = v_pad.reshape(b, h, n_kb, MOBA_BLOCK, d)
    k_mean = jnp.mean(kb.astype(jnp.float32), axis=3)
    top = min(MOBA_TOPK, n_kb)
    scale = d ** -0.5
    bi = jnp.arange(b)[:, None, None, None]
    hi = jnp.arange(h)[None, :, None, None]
    blk_ids = jnp.arange(n_kb)

    def chunk(i):
        start = i * MOBA_Q_CHUNK
        qc = lax.dynamic_slice_in_dim(q, start, MOBA_Q_CHUNK, axis=2)
        q_pos = start + jnp.arange(MOBA_Q_CHUNK)
        own = start // MOBA_BLOCK
        gate = jnp.einsum('bhqd,bhnd->bhqn', qc.astype(jnp.float32), k_mean)
        gate = jnp.where(blk_ids < own, gate, NEG_INF)
        _, sel = lax.top_k(gate, top)
        k_sel, v_sel = kb[bi, hi, sel], vb[bi, hi, sel]
        s_sel = jnp.einsum('bhqd,bhqnkd->bhqnk', qc, k_sel).astype(jnp.float32) * scale
        rank_ok = (jnp.arange(top) < own)[:, None]
        s_sel = jnp.where(rank_ok, s_sel, NEG_INF).reshape(b, h, MOBA_Q_CHUNK, top * MOBA_BLOCK)
        k_own = lax.dynamic_slice_in_dim(k_pad, own * MOBA_BLOCK, MOBA_BLOCK, axis=2)
        v_own = lax.dynamic_slice_in_dim(v_pad, own * MOBA_BLOCK, MOBA_BLOCK, axis=2)
        own_pos = own * MOBA_BLOCK + jnp.arange(MOBA_BLOCK)
        s_own = jnp.einsum('bhqd,bhkd->bhqk', qc, k_own).astype(jnp.float32) * scale
        s_own = jnp.where(own_pos[None, :] <= q_pos[:, None], s_own, NEG_INF)
        p = jax.nn.softmax(jnp.concatenate([s_sel, s_own], axis=-1), axis=-1).astype(v.dtype)
        p_sel = p[..., :top * MOBA_BLOCK].reshape(b, h, MOBA_Q_CHUNK, top, MOBA_BLOCK)
        p_own = p[..., top * MOBA_BLOCK:]
        return (jnp.einsum('bhqnk,bhqnkd->bhqd', p_sel, v_sel)
                + jnp.einsum('bhqk,bhkd->bhqd', p_own, v_own))

    o = lax.map(chunk, jnp.arange(s // MOBA_Q_CHUNK))
    return merge_blocks(o) @ w_o


def multi_head_latent_attention(x, w_in, q_norm_g, kv_norm_g, w_uq, w_ukv, w_o, cos_r, sin_r):
    b, s, _ = x.shape
    h = MLA_HEADS
    hx = x @ w_in
    c_q = hx[..., :MLA_Q_LORA]
    c_kv = hx[..., MLA_Q_LORA:MLA_Q_LORA + MLA_KV_LORA]
    k_rope = apply_rope(hx[..., MLA_Q_LORA + MLA_KV_LORA:], cos_r, sin_r)
    q = split_heads(rms_norm(c_q, q_norm_g) @ w_uq, h)
    q_nope, q_rope = q[..., :MLA_NOPE], apply_rope(q[..., MLA_NOPE:], cos_r, sin_r)
    kv = split_heads(rms_norm(c_kv, kv_norm_g) @ w_ukv, h)
    k_nope, v = kv[..., :MLA_NOPE], kv[..., MLA_NOPE:]
    key_pos = jnp.arange(s)
    scale = (MLA_NOPE + MLA_ROPE) ** -0.5

    def block(i):
        start = i * Q_BLOCK
        qn = lax.dynamic_slice_in_dim(q_nope, start, Q_BLOCK, axis=2)
        qr = lax.dynamic_slice_in_dim(q_rope, start, Q_BLOCK, axis=2)
        sc = (jnp.einsum('bhqd,bhkd->bhqk', qn, k_nope)
              + jnp.einsum('bhqd,bkd->bhqk', qr, k_rope)).astype(jnp.float32) * scale
        q_pos = start + jnp.arange(Q_BLOCK)
        sc = jnp.where(key_pos[None, :] <= q_pos[:, None], sc, NEG_INF)
        p = jax.nn.softmax(sc, axis=-1).astype(v.dtype)
        return jnp.einsum('bhqk,bhkd->bhqd', p, v)

    o = lax.map(block, jnp.arange(s // Q_BLOCK))
    return merge_blocks(o) @ w_o


def native_sparse_attention(x, w_in, cmp_pos_k, cmp_w1_k, cmp_w2_k, cmp_pos_v, cmp_w1_v, cmp_w2_v,
                            w_o, cos, sin):
    b, s, _ = x.shape
    h, g, d = NSA_HEADS, NSA_KV_HEADS, HEAD_DIM
    rep = h // g
    kvw = g * d
    sizes = [h * d, kvw, kvw, kvw, kvw, kvw, kvw, 3 * h]
    offsets = [int(o) for o in np.cumsum(sizes)[:-1]]
    q, k_c, v_c, k_s, v_s, k_w, v_w, gate_in = jnp.split(x @ w_in, offsets, axis=-1)
    q = apply_rope(split_heads(q, h), cos, sin).reshape(b, g, rep, s, d)
    k_c = apply_rope(split_heads(k_c, g), cos, sin)
    k_s = apply_rope(split_heads(k_s, g), cos, sin)
    k_w = apply_rope(split_heads(k_w, g), cos, sin)
    v_c, v_s, v_w = split_heads(v_c, g), split_heads(v_s, g), split_heads(v_w, g)
    gates = jax.nn.sigmoid(gate_in.astype(jnp.float32)).reshape(b, s, h, 3).transpose(0, 2, 1, 3)

    n_cmp = (s - NSA_CMP_BLOCK) // NSA_CMP_STRIDE + 1
    cmp_start = jnp.arange(n_cmp) * NSA_CMP_STRIDE
    cmp_idx = cmp_start[:, None] + jnp.arange(NSA_CMP_BLOCK)[None, :]

    def compress(t, pos, w1, w2):
        blocks = (t[:, :, cmp_idx] + pos).reshape(b, g, n_cmp, NSA_CMP_BLOCK * d)
        return jax.nn.gelu(blocks @ w1) @ w2

    kc = compress(k_c, cmp_pos_k, cmp_w1_k, cmp_w2_k)
    vc = compress(v_c, cmp_pos_v, cmp_w1_v, cmp_w2_v)
    cmp_end = cmp_start + NSA_CMP_BLOCK - 1

    n_slc = s // NSA_SLC_BLOCK
    top_n = min(NSA_SLC_TOPN, n_slc)
    slc_start = jnp.arange(n_slc) * NSA_SLC_BLOCK
    overlap = ((cmp_start[:, None] < slc_start[None, :] + NSA_SLC_BLOCK)
               & (cmp_start[:, None] + NSA_CMP_BLOCK > slc_start[None, :])).astype(jnp.float32)
    ks_blk = k_s.reshape(b, g, n_slc, NSA_SLC_BLOCK, d)
    vs_blk = v_s.reshape(b, g, n_slc, NSA_SLC_BLOCK, d)
    blk_ids = jnp.arange(n_slc)
    bi = jnp.arange(b)[:, None, None, None]
    gi = jnp.arange(g)[None, :, None, None]

    wpad = ((0, 0), (0, 0), (NSA_WINDOW, 0), (0, 0))
    kw_pad, vw_pad = jnp.pad(k_w, wpad), jnp.pad(v_w, wpad)
    scale = d ** -0.5
    qc_len = NSA_Q_CHUNK

    def chunk(i):
        start = i * qc_len
        qc = lax.dynamic_slice_in_dim(q, start, qc_len, axis=3)
        q_pos = start + jnp.arange(qc_len)

        vis_c = cmp_end[None, :] <= q_pos[:, None]
        s_c = jnp.einsum('bgrqd,bgnd->bgrqn', qc, kc).astype(jnp.float32) * scale
        p_c = jnp.where(vis_c, jax.nn.softmax(jnp.where(vis_c, s_c, NEG_INF), axis=-1), 0.0)
        o_c = jnp.einsum('bgrqn,bgnd->bgrqd', p_c.astype(vc.dtype), vc)

        imp = jnp.einsum('bgrqn,ns->bgqs', p_c, overlap)
        cur = q_pos // NSA_SLC_BLOCK
        forced = ((blk_ids[None, :] == 0) | (blk_ids[None, :] == cur[:, None])
                  | (blk_ids[None, :] == cur[:, None] - 1))
        imp = jnp.where(forced, NSA_FORCE, imp)
        imp = jnp.where(blk_ids[None, :] <= cur[:, None], imp, NEG_INF)
        _, sel = lax.top_k(imp, top_n)
        k_sel, v_sel = ks_blk[bi, gi, sel], vs_blk[bi, gi, sel]
        pos_sel = sel[..., None] * NSA_SLC_BLOCK + jnp.arange(NSA_SLC_BLOCK)
        vis_s = (pos_sel <= q_pos[:, None, None])[:, :, None]
        s_s = jnp.einsum('bgrqd,bgqnkd->bgrqnk', qc, k_sel).astype(jnp.float32) * scale
        s_s = jnp.where(vis_s, s_s, NEG_INF).reshape(b, g, rep, qc_len, top_n * NSA_SLC_BLOCK)
        p_s = jax.nn.softmax(s_s, axis=-1).reshape(b, g, rep, qc_len, top_n, NSA_SLC_BLOCK)
        o_s = jnp.einsum('bgrqnk,bgqnkd->bgrqd', p_s.astype(v_sel.dtype), v_sel)

        kwin = lax.dynamic_slice_in_dim(kw_pad, start, NSA_WINDOW + qc_len, axis=2)
        vwin = lax.dynamic_slice_in_dim(vw_pad, start, NSA_WINDOW + qc_len, axis=2)
        win_pos = start - NSA_WINDOW + jnp.arange(NSA_WINDOW + qc_len)
        vis_w = ((win_pos[None, :] <= q_pos[:, None]) & (win_pos[None, :] > q_pos[:, None] - NSA_WINDOW)
                 & (win_pos[None, :] >= 0))
        s_w = jnp.einsum('bgrqd,bgkd->bgrqk', qc, kwin).astype(jnp.float32) * scale
        p_w = jax.nn.softmax(jnp.where(vis_w, s_w, NEG_INF), axis=-1)
        o_w = jnp.einsum('bgrqk,bgkd->bgrqd', p_w.astype(vwin.dtype), vwin)

        gt = lax.dynamic_slice_in_dim(gates, start, qc_len, axis=2).reshape(b, g, rep, qc_len, 3)
        o = gt[..., 0:1] * o_c + gt[..., 1:2] * o_s + gt[..., 2:3] * o_w
        return o.astype(x.dtype).reshape(b, h, qc_len, d)

    o = lax.map(chunk, jnp.arange(s // qc_len))
    return merge_blocks(o) @ w_o


def squared_relu_mlp(x, w1, w2):
    return jnp.square(jax.nn.relu(x @ w1)) @ w2


def _normal(key, shape, scale):
    return jax.random.normal(key, shape, jnp.float32) * scale


def setup_inputs(seed: int = 0) -> dict:
    key = jax.random.key(seed)
    keys = list(jax.random.split(key, 64))

    def w(shape, fan_in, out_proj=False):
        scale = fan_in ** -0.5 * (DEEPNORM_BETA if out_proj else 1.0)
        return _normal(keys.pop(), shape, scale)

    def gain(n):
        return 1.0 + _normal(keys.pop(), (n,), 0.02)

    def bias(n):
        return _normal(keys.pop(), (n,), 0.02)

    p = {}
    p['x'] = _normal(keys.pop(), (BATCH, SEQ, D_MODEL), 1.0)

    def add_channel(prefix):
        p[prefix + '_ln1_g'] = gain(D_MODEL)
        p[prefix + '_ln1_b'] = bias(D_MODEL)
        p[prefix + '_mlp_w1'] = w((D_MODEL, D_FF), D_MODEL)
        p[prefix + '_mlp_w2'] = w((D_FF, D_MODEL), D_FF, out_proj=True)
        p[prefix + '_ln2_g'] = gain(D_MODEL)
        p[prefix + '_ln2_b'] = bias(D_MODEL)

    p['l0_sb_w_qkv'] = w((D_MODEL, 3 * SB_HEADS * HEAD_DIM), D_MODEL)
    p['l0_sb_w_o'] = w((SB_HEADS * HEAD_DIM, D_MODEL), SB_HEADS * HEAD_DIM, out_proj=True)
    add_channel('l0')
    p['l1_moba_w_qkv'] = w((D_MODEL, 3 * MOBA_HEADS * HEAD_DIM), D_MODEL)
    p['l1_moba_w_o'] = w((MOBA_HEADS * HEAD_DIM, D_MODEL), MOBA_HEADS * HEAD_DIM, out_proj=True)
    add_channel('l1')
    p['l2_mla_w_in'] = w((D_MODEL, MLA_IN), D_MODEL)
    p['l2_mla_q_norm'] = gain(MLA_Q_LORA)
    p['l2_mla_kv_norm'] = gain(MLA_KV_LORA)
    p['l2_mla_w_uq'] = w((MLA_Q_LORA, MLA_HEADS * (MLA_NOPE + MLA_ROPE)), MLA_Q_LORA)
    p['l2_mla_w_ukv'] = w((MLA_KV_LORA, MLA_HEADS * (MLA_NOPE + MLA_V)), MLA_KV_LORA)
    p['l2_mla_w_o'] = w((MLA_HEADS * MLA_V, D_MODEL), MLA_HEADS * MLA_V, out_proj=True)
    add_channel('l2')
    p['l3_nsa_w_in'] = w((D_MODEL, NSA_IN), D_MODEL)
    p['l3_nsa_cmp_pos_k'] = _normal(keys.pop(), (NSA_CMP_BLOCK, HEAD_DIM), 0.1)
    p['l3_nsa_cmp_w1_k'] = w((NSA_CMP_BLOCK * HEAD_DIM, NSA_CMP_HIDDEN), NSA_CMP_BLOCK * HEAD_DIM)
    p['l3_nsa_cmp_w2_k'] = w((NSA_CMP_HIDDEN, HEAD_DIM), NSA_CMP_HIDDEN)
    p['l3_nsa_cmp_pos_v'] = _normal(keys.pop(), (NSA_CMP_BLOCK, HEAD_DIM), 0.1)
    p['l3_nsa_cmp_w1_v'] = w((NSA_CMP_BLOCK * HEAD_DIM, NSA_CMP_HIDDEN), NSA_CMP_BLOCK * HEAD_DIM)
    p['l3_nsa_cmp_w2_v'] = w((NSA_CMP_HIDDEN, HEAD_DIM), NSA_CMP_HIDDEN)
    p['l3_nsa_w_o'] = w((NSA_HEADS * HEAD_DIM, D_MODEL), NSA_HEADS * HEAD_DIM, out_proj=True)
    add_channel('l3')
    return p


def reference(x,
              l0_sb_w_qkv, l0_sb_w_o, l0_ln1_g, l0_ln1_b, l0_mlp_w1, l0_mlp_w2, l0_ln2_g, l0_ln2_b,
              l1_moba_w_qkv, l1_moba_w_o, l1_ln1_g, l1_ln1_b, l1_mlp_w1, l1_mlp_w2, l1_ln2_g, l1_ln2_b,
              l2_mla_w_in, l2_mla_q_norm, l2_mla_kv_norm, l2_mla_w_uq, l2_mla_w_ukv, l2_mla_w_o,
              l2_ln1_g, l2_ln1_b, l2_mlp_w1, l2_mlp_w2, l2_ln2_g, l2_ln2_b,
              l3_nsa_w_in, l3_nsa_cmp_pos_k, l3_nsa_cmp_w1_k, l3_nsa_cmp_w2_k,
              l3_nsa_cmp_pos_v, l3_nsa_cmp_w1_v, l3_nsa_cmp_w2_v, l3_nsa_w_o,
              l3_ln1_g, l3_ln1_b, l3_mlp_w1, l3_mlp_w2, l3_ln2_g, l3_ln2_b):
    s = x.shape[1]
    cos_h, sin_h = rope_tables(s, HEAD_DIM)
    cos_r, sin_r = rope_tables(s, MLA_ROPE)
    mixers = (
        lambda t: stick_breaking_attention(t, l0_sb_w_qkv, l0_sb_w_o),
        lambda t: moba_attention(t, l1_moba_w_qkv, l1_moba_w_o, cos_h, sin_h),
        lambda t: multi_head_latent_attention(t, l2_mla_w_in, l2_mla_q_norm, l2_mla_kv_norm,
                                              l2_mla_w_uq, l2_mla_w_ukv, l2_mla_w_o, cos_r, sin_r),
        lambda t: native_sparse_attention(t, l3_nsa_w_in, l3_nsa_cmp_pos_k, l3_nsa_cmp_w1_k,
                                          l3_nsa_cmp_w2_k, l3_nsa_cmp_pos_v, l3_nsa_cmp_w1_v,
                                          l3_nsa_cmp_w2_v, l3_nsa_w_o, cos_h, sin_h),
    )
    channel = (
        (l0_ln1_g, l0_ln1_b, l0_mlp_w1, l0_mlp_w2, l0_ln2_g, l0_ln2_b),
        (l1_ln1_g, l1_ln1_b, l1_mlp_w1, l1_mlp_w2, l1_ln2_g, l1_ln2_b),
        (l2_ln1_g, l2_ln1_b, l2_mlp_w1, l2_mlp_w2, l2_ln2_g, l2_ln2_b),
        (l3_ln1_g, l3_ln1_b, l3_mlp_w1, l3_mlp_w2, l3_ln2_g, l3_ln2_b),
    )
    h = x
    for i in range(DEPTH):
        ln1_g, ln1_b, w1, w2, ln2_g, ln2_b = channel[i]
        h = layer_norm(DEEPNORM_ALPHA * h + mixers[i % N_MIXERS](h), ln1_g, ln1_b)
        h = layer_norm(DEEPNORM_ALPHA * h + squared_relu_mlp(h, w1, w2), ln2_g, ln2_b)
    return h
```

```python
import numpy as np
import ml_dtypes
import concourse.bass as bass
import concourse.mybir as mybir
from concourse.bass_utils import run_bass_kernel_spmd

F32 = mybir.dt.float32
BF16 = mybir.dt.bfloat16
AF = mybir.ActivationFunctionType
ALU = mybir.AluOpType
AX = mybir.AxisListType
NPBF = ml_dtypes.bfloat16

D = 1024
B = 4
S = 8192
DFF = 4096
NCORES = 8
TOK = 4096
ALPHA = float((2.0 * 4) ** 0.25)
LN_EPS = 1e-5
RMS_EPS = 1e-6
BIG = 30000.0

SAME_ENGINE_SYNC = False


class T:
    __slots__ = ("ap", "name", "w", "r", "dsem", "dcnt")

    def __init__(self, ap, name):
        self.ap = ap
        self.name = name
        self.w = None
        self.r = []
        self.dsem = None
        self.dcnt = 0

    def __getitem__(self, idx):
        return self.ap[idx]


class Ctx:
    def __init__(self, nc):
        self.nc = nc
        self.eng = {"pe": nc.tensor, "act": nc.scalar, "dve": nc.vector,
                    "pool": nc.gpsimd, "sp": nc.sync}
        self.sem = {k: nc.alloc_semaphore(name=f"s_{k}") for k in self.eng}
        self.cnt = {k: 0 for k in self.eng}
        self.seen = {k: {} for k in self.eng}
        self.n_inst = 0
        self._stack = []
        self.out_events = []
        self.fused = False
        self.phase = 0
        self.dpool = []
        self.ptiles = []
        self.ccsem = None
        self.cccnt = 0

    def sb(self, name, shape, dt):
        cm = self.nc.sbuf_tensor(f"sb{self.phase}_" + name, list(shape), dt)
        t = cm.__enter__()
        self._stack.append(cm)
        return T(t[:], name)

    def ps(self, name, shape, dt=F32):
        cm = self.nc.psum_tensor(f"ps{self.phase}_" + name, list(shape), dt)
        t = cm.__enter__()
        self._stack.append(cm)
        return T(t[:], name)

    def view(self, ap, name):
        return T(ap, name)

    def _need(self, ek, deps, raw=()):
        best = {}
        for lst, is_raw in ((deps, False), (raw, True)):
            for d in lst:
                if d is None:
                    continue
                sem, val, sk = d
                if sk == ek and ek == "pe":
                    continue
                key = id(sem)
                if key not in best or best[key][1] < val:
                    best[key] = (sem, val)
        seen = self.seen[ek]
        e = self.eng[ek]
        for key, (sem, val) in best.items():
            if seen.get(key, 0) >= val:
                continue
            e.wait_ge(sem, val)
            seen[key] = val

    @staticmethod
    def _compact(r):
        best = {}
        for sem, val, sk in r:
            k = id(sem)
            if k not in best or best[k][1] < val:
                best[k] = (sem, val, sk)
        return list(best.values())

    def op(self, ek, fn, outs=(), ins=(), **kw):
        deps = []
        raw = [t.w for t in ins]
        for t in outs:
            deps.append(t.w)
            deps.extend(t.r)
        self._need(ek, deps, raw)
        inst = fn(**kw)
        self.cnt[ek] += 1
        ev = (self.sem[ek], self.cnt[ek], ek)
        inst.then_inc(self.sem[ek], 1)
        self.n_inst += 1
        for t in ins:
            t.r.append(ev)
            if len(t.r) > 16:
                t.r = self._compact(t.r)
        for t in outs:
            t.w = ev
            t.r = []
        return ev

    def dma(self, ek, out, in_, out_t=None, in_t=None, **kw):
        st = out_t or in_t
        if st.dsem is None:
            if self.dpool:
                st.dsem, st.dcnt = self.dpool.pop()
            else:
                st.dsem = self.nc.alloc_semaphore(name=f"d{self.phase}_{st.name}")
            self.ptiles.append(st)
        deps = []
        if in_t is not None:
            deps.append(in_t.w)
        if out_t is not None:
            deps.append(out_t.w)
            deps.extend(out_t.r)
        self._need(ek, deps)
        inst = self.eng[ek].dma_start(out=out, in_=in_, **kw)
        st.dcnt += 16
        inst.then_inc(st.dsem, 16)
        ev = (st.dsem, st.dcnt, None)
        self.n_inst += 1
        if in_t is not None:
            in_t.r.append(ev)
            if out_t is None:
                self.out_events.append(ev)
        if out_t is not None:
            out_t.w = ev
            out_t.r = []
        return ev

    def barrier(self):
        targets = [(self.sem[k], self.cnt[k], k) for k in self.eng if self.cnt[k] > 0]
        targets += [(t.dsem, t.dcnt, None) for t in self.ptiles if t.dcnt > 0]
        targets += [(sem, cnt, None) for sem, cnt in self.dpool if cnt > 0]
        if self.ccsem is not None and self.cccnt > 0:
            targets.append((self.ccsem, self.cccnt, None))
        for ek in self.eng:
            seen = self.seen[ek]
            for sem, val, sk in targets:
                if sk == ek or seen.get(id(sem), 0) >= val:
                    continue
                self.eng[ek].wait_ge(sem, val)
                seen[id(sem)] = val

    def end_phase(self):
        self.barrier()
        for t in self.ptiles:
            self.dpool.append([t.dsem, t.dcnt])
            t.dsem = None
        self.ptiles = []
        while self._stack:
            self._stack.pop().__exit__(None, None, None)
        self.phase += 1

    def allgather(self, in_ap, out_ap):
        if self.ccsem is None:
            self.ccsem = self.nc.alloc_semaphore(name="ccsem")
        self.nc.gpsimd.collective_compute("AllGather", ALU.bypass,
                                          replica_groups=[[0, 1], [2, 3], [4, 5], [6, 7]],
                                          ins=[in_ap], outs=[out_ap]).then_inc(self.ccsem, 1)
        self.cccnt += 1

    def finish(self, ek="sp"):
        if self.fused:
            return
        best = {}
        for sem, val, _ in self.out_events:
            k = id(sem)
            if k not in best or best[k][1] < val:
                best[k] = (sem, val)
        for sem, val in best.values():
            self.eng[ek].wait_ge(sem, val)

    def close(self):
        if self.fused:
            self.end_phase()
            return
        while self._stack:
            self._stack.pop().__exit__(None, None, None)

    def final_finish(self, ek="sp"):
        best = {}
        for sem, val, _ in self.out_events:
            k = id(sem)
            if k not in best or best[k][1] < val:
                best[k] = (sem, val)
        for sem, val in best.values():
            self.eng[ek].wait_ge(sem, val)


class Fuse:
    active = None

    def __init__(self):
        self.nc = bass.Bass("TRN2", target_bir_lowering=False)
        self.c = None
        self.io = {}
        self.ext = {}
        self.par = None
        self.lay = {}

    def ext_in(self, name, shape, dt):
        if name not in self.ext:
            self.ext[name] = self.nc.dram_tensor(name, list(shape), dt, kind="ExternalInput").ap()
        return self.ext[name]

    def internal(self, name, shape, dt):
        return self.nc.dram_tensor(name, list(shape), dt, kind="Internal").ap()


def new_nc():
    if Fuse.active is not None:
        return Fuse.active.nc
    return bass.Bass("TRN2", target_bir_lowering=False)


def make_ctx(nc):
    F = Fuse.active
    if F is not None:
        if F.c is None:
            F.c = Ctx(nc)
            F.c.fused = True
        return F.c
    return Ctx(nc)


def din(nc, name, shape, dt):
    F = Fuse.active
    if F is not None:
        return F.io[name]
    return nc.dram_tensor(name, list(shape), dt, kind="ExternalInput").ap()


def dout(nc, name, shape, dt):
    F = Fuse.active
    if F is not None:
        return F.io[name]
    return nc.dram_tensor(name, list(shape), dt, kind="ExternalOutput").ap()


def _lay(ap):
    F = Fuse.active
    if F is None:
        return None
    return F.lay.get(ap.tensor.name)


def fm_rows(ap, r0, r1):
    rc = _lay(ap)
    if rc is None:
        return ap[r0:r1, :].rearrange("p (r t) -> p r t", r=2)
    jj = r0 // rc
    assert (r1 - 1) // rc == jj, (r0, r1, rc)
    v = ap[jj * 2 * rc:(jj + 1) * 2 * rc, :].rearrange("(r f) t -> f r t", r=2)
    return v[r0 - jj * rc:r1 - jj * rc, :, :]


def fm_shared(ap):
    if _lay(ap) is None:
        return ap.rearrange("p (r t) -> p r t", r=2)
    return ap.rearrange("(r f) t -> f r t", r=2)


def fm_chunk(ap, r0, r1, i):
    r, t0 = divmod(i * 512, TOK)
    return fm_rows(ap, r0, r1)[:, r, t0:t0 + 512]


def gt_chunk(ap, i):
    if _lay(ap) is None:
        return ap[0:24, i * 512:(i + 1) * 512]
    r, t0 = divmod(i * 512, TOK)
    return ap.rearrange("(r f) t -> f r t", r=2)[:, r, t0:t0 + 512]


def ot_pieces(ap, t0, n):
    rc = _lay(ap)
    if rc is None:
        return [(0, 8, ap.rearrange("(kc p) t -> p kc t", p=128)[:, :, t0:t0 + n])]
    v = ap.rearrange("(j r p) t -> p r j t", r=2, p=128)
    return [(r * 4, 4, v[:, r, :, t0:t0 + n]) for r in range(2)]


def tm_pieces(ap, c0, c1):
    rc = _lay(ap)
    if rc is None:
        return [(0, 64, ap.rearrange("(kt p) n -> p kt n", p=128)[:, :, c0:c1])]
    nj = TOK // rc
    k8 = rc // 128
    v = ap.rearrange("(j r k p) n -> p j r k n", r=2, k=k8, p=128)
    return [(r * (TOK // 128) + j * k8, k8, v[:, j, r, :, c0:c1]) for r in range(2) for j in range(nj)]


class Caster:
    def __init__(self, c, engines=("dve", "pool", "act")):
        self.c = c
        self.engines = engines
        self.i = 0

    def copy(self, out_t, out_ap, in_t, in_ap, eng=None):
        c = self.c
        ek = eng or self.engines[self.i % len(self.engines)]
        self.i += 1
        if ek == "act":
            return c.op("act", c.nc.scalar.copy, outs=[out_t], ins=[in_t], out=out_ap, in_=in_ap)
        e = c.nc.vector if ek == "dve" else c.nc.gpsimd
        return c.op(ek, e.tensor_copy, outs=[out_t], ins=[in_t], out=out_ap, in_=in_ap)


def load_weight_bf16(c, caster, w_ap, K, N, name, stages, queue="sp"):
    nk = K // 128
    big = c.sb(name, [128, nk, N], BF16)
    chunks = [c.view(big[:, kc, :], f"{name}_{kc}") for kc in range(nk)]
    CW = stages[0].ap.shape[-1]
    si = 0
    for kc in range(nk):
        for n0 in range(0, N, CW):
            n1 = min(N, n0 + CW)
            st = stages[si % len(stages)]
            si += 1
            c.dma(queue, st[:, 0:n1 - n0], w_ap[kc * 128:(kc + 1) * 128, n0:n1], out_t=st)
            caster.copy(chunks[kc], big[:, kc, n0:n1], st, st[:, 0:n1 - n0])
    return big, chunks


def build_P_qkv(rope, qscale):
    nc = new_nc()
    hT = din(nc, "hT", [D, TOK], F32)
    w = din(nc, "w", [D, 3 * D], F32)
    if rope:
        cosT = din(nc, "cosT", [128, TOK], F32)
        sinT = din(nc, "sinT", [128, TOK], F32)
    QT = dout(nc, "QT", [D, TOK], BF16)
    KT = dout(nc, "KT", [D, TOK], BF16)
    V = dout(nc, "V", [TOK, D], BF16)
    c = make_ctx(nc)
    cast = Caster(c)
    stages = [c.sb(f"stg{i}", [128, 1024], F32) for i in range(2)]
    wb, wch = load_weight_bf16(c, cast, w, D, 3 * D, "wb", stages)
    if rope:
        wr = c.sb("wrot", [128, 8, 2 * D], BF16)
        wrch = [c.view(wr[:, kc, :], f"wrot_{kc}") for kc in range(8)]
        for kc in range(8):
            src = wb[:, kc, 0:2 * D].rearrange("p (h two d) -> p h two d", two=2, d=32)
            dst = wr[:, kc, :].rearrange("p (h two d) -> p h two d", two=2, d=32)
            c.op("dve", nc.vector.tensor_scalar, outs=[wrch[kc]], ins=[wch[kc]],
                 out=dst[:, :, 0, :], in0=src[:, :, 1, :], scalar1=-1.0, scalar2=None, op0=ALU.mult)
            c.op("pool", nc.gpsimd.tensor_copy, outs=[wrch[kc]], ins=[wch[kc]],
                 out=dst[:, :, 1, :], in_=src[:, :, 0, :])
        cos_sb = c.sb("cos_sb", [128, TOK], F32)
        sin_sb = c.sb("sin_sb", [128, TOK], F32)
        c.dma("sp", cos_sb[:], cosT, out_t=cos_sb)
        c.dma("sp", sin_sb[:], sinT, out_t=sin_sb)
    NT = 512
    hts = [c.sb(f"ht{i}", [128, 8, NT], F32) for i in range(2)]
    hb = c.sb("hb", [128, 8, NT], BF16)
    qk_sb = [c.sb(f"qk{i}", [128, 8, NT], BF16) for i in range(2)]
    v_sb = c.sb("v_sb", [128, 4, D], BF16)
    t1 = [c.sb(f"t1_{i}", [128, NT], F32) for i in range(2)]
    t2 = [c.sb(f"t2_{i}", [128, NT], F32) for i in range(2)]
    banks = [c.ps(f"pb{i}", [128, 512], F32) for i in range(8)]
    bi = 0
    hT_v = hT.rearrange("(kc p) t -> p kc t", p=128)
    for tt in range(TOK // NT):
        ht = hts[tt % 2]
        c.dma("sp", ht[:], hT_v[:, :, tt * NT:(tt + 1) * NT], out_t=ht)
        for kc in range(8):
            cast.copy(hb, hb[:, kc, :], ht, ht[:, kc, :], eng=("dve", "pool")[kc % 2])
        for which in range(2):
            dst = qk_sb[which]
            for fc in range(8):
                col = which * D + fc * 128
                pb = banks[bi % 8]; bi += 1
                for kc in range(8):
                    c.op("pe", nc.tensor.matmul, outs=[pb], ins=[wch[kc], hb],
                         out=pb[:, 0:NT], lhsT=wb[:, kc, col:col + 128], rhs=hb[:, kc, :],
                         start=(kc == 0), stop=(kc == 7))
                sc = qscale if which == 0 else 1.0
                if not rope:
                    c.op("act", nc.scalar.mul, outs=[dst], ins=[pb], out=dst[:, fc, :], in_=pb[:, 0:NT], mul=sc)
                else:
                    pr = banks[bi % 8]; bi += 1
                    for kc in range(8):
                        c.op("pe", nc.tensor.matmul, outs=[pr], ins=[wrch[kc], hb],
                             out=pr[:, 0:NT], lhsT=wr[:, kc, col:col + 128], rhs=hb[:, kc, :],
                             start=(kc == 0), stop=(kc == 7))
                    a = t1[fc % 2]; b_ = t2[fc % 2]
                    c.op("dve", nc.vector.tensor_tensor, outs=[a], ins=[pb, cos_sb],
                         out=a[:], in0=pb[:, 0:NT], in1=cos_sb[:, tt * NT:(tt + 1) * NT], op=ALU.mult)
                    c.op("dve", nc.vector.tensor_tensor, outs=[b_], ins=[pr, sin_sb],
                         out=b_[:], in0=pr[:, 0:NT], in1=sin_sb[:, tt * NT:(tt + 1) * NT], op=ALU.mult)
                    c.op("pool", nc.gpsimd.tensor_tensor, outs=[a], ins=[a, b_], out=a[:], in0=a[:], in1=b_[:],
                         op=ALU.add)
                    c.op("act", nc.scalar.mul, outs=[dst], ins=[a], out=dst[:, fc, :], in_=a[:], mul=sc)
            out_d = (QT, KT)[which].rearrange("(fc p) t -> p fc t", p=128)
            c.dma("pool", out_d[:, :, tt * NT:(tt + 1) * NT], dst[:], in_t=dst)
        for j in range(4):
            for half in range(2):
                pb = banks[bi % 8]; bi += 1
                for kc in range(8):
                    c.op("pe", nc.tensor.matmul, outs=[pb], ins=[wch[kc], hb],
                         out=pb[:], lhsT=hb[:, kc, j * 128:(j + 1) * 128],
                         rhs=wb[:, kc, 2 * D + half * 512:2 * D + (half + 1) * 512],
                         start=(kc == 0), stop=(kc == 7))
                if half == 0:
                    c.op("act", nc.scalar.copy, outs=[v_sb], ins=[pb], out=v_sb[:, j, 0:512], in_=pb[:])
                else:
                    c.op("dve", nc.vector.tensor_copy, outs=[v_sb], ins=[pb], out=v_sb[:, j, 512:1024], in_=pb[:])
        V_v = V.rearrange("(j p) n -> p j n", p=128)
        c.dma("pool", V_v[:, tt * 4:(tt + 1) * 4, :], v_sb[:], in_t=v_sb)
    c.finish("pool")
    c.close()
    return nc


def build_A_sb():
    nc = new_nc()
    QT = din(nc, "QT", [512, S], BF16)
    KT = din(nc, "KT", [512, S], BF16)
    V = din(nc, "V", [S, 512], BF16)
    maskS = din(nc, "maskS", [128, 4, 512], BF16)
    tri = din(nc, "tri", [128, 128], BF16)
    OT = dout(nc, "OT", [512, S], BF16)
    c = make_ctx(nc)
    m_sb = c.sb("maskS", [128, 4, 512], BF16)
    tri_sb = c.sb("tri", [128, 128], BF16)
    ones_sb = c.sb("ones", [128, 128], BF16)
    c.dma("sp", m_sb[:], maskS, out_t=m_sb)
    c.dma("sp", tri_sb[:], tri, out_t=tri_sb)
    c.op("dve", nc.vector.memset, outs=[ones_sb], ap=ones_sb[:], constant=1.0)
    qts = [c.sb(f"qt{i}", [128, S], BF16) for i in range(2)]
    kts = [c.sb(f"kt{i}", [128, S], BF16) for i in range(2)]
    vs = [c.sb(f"v{i}", [128, 64, 128], BF16) for i in range(2)]
    NW = 3
    e_sb = [c.sb(f"e{i}", [128, 512], F32) for i in range(NW)]
    sp_sb = [c.sb(f"sp{i}", [128, 512], BF16) for i in range(NW)]
    w_sb = [c.sb(f"w{i}", [128, 512], BF16) for i in range(NW)]
    run = [c.sb(f"run{i}", [128, 512], BF16) for i in range(2)]
    o_sb = [c.sb(f"o{i}", [64, 512], BF16) for i in range(2)]
    pz = [c.ps(f"pz{i}", [128, 512], F32) for i in range(3)]
    px = [c.ps(f"px{i}", [128, 512], F32) for i in range(3)]
    po = [c.ps(f"po{i}", [128, 512], F32) for i in range(2)]
    V_v = V.rearrange("(kt p) n -> p kt n", p=128)
    it = 0
    oi = 0
    for pair in range(4):
        qt = qts[pair % 2]; kt_ = kts[pair % 2]; v = vs[pair % 2]
        c.dma("sp", qt[:].rearrange("p (r t) -> p r t", r=2), fm_rows(QT, pair * 128, (pair + 1) * 128), out_t=qt)
        c.dma("sp", kt_[:].rearrange("p (r t) -> p r t", r=2), fm_rows(KT, pair * 128, (pair + 1) * 128), out_t=kt_)
        for k0_, nk_, src_ in tm_pieces(V, pair * 128, (pair + 1) * 128):
            c.dma("sp", v[:, k0_:k0_ + nk_, :], src_, out_t=v)
        for hs in range(2):
            r0 = hs * 64
            for i in range(S // 512):
                pout = po[oi % 2]
                osb = o_sb[oi % 2]
                oi += 1
                rn = run[i % 2]
                nkt = 4 * i + 4
                for n, kt in enumerate(range(nkt - 1, -1, -1)):
                    r = kt - 4 * i
                    z = pz[it % 3]; x = px[it % 3]
                    e = e_sb[it % NW]; sp = sp_sb[it % NW]; wt = w_sb[it % NW]
                    it += 1
                    kap = kt_[r0:r0 + 64, kt * 128:(kt + 1) * 128]
                    qap = qt[r0:r0 + 64, i * 512:(i + 1) * 512]
                    c.op("pe", nc.tensor.matmul, outs=[z], ins=[kt_, qt], out=z[:], lhsT=kap, rhs=qap,
                         start=True, stop=True)
                    c.op("act", nc.scalar.activation, outs=[e], ins=[z], out=e[:], in_=z[:], func=AF.Exp,
                         scale=-1.0)
                    c.op("act", nc.scalar.activation, outs=[sp], ins=[e], out=sp[:], in_=e[:], func=AF.Ln,
                         bias=1.0)
                    if r >= 0:
                        c.op("dve", nc.vector.tensor_tensor, outs=[sp], ins=[sp, m_sb], out=sp[:], in0=sp[:],
                             in1=m_sb[:, r, :], op=ALU.mult)
                    c.op("pe", nc.tensor.matmul, outs=[x], ins=[tri_sb, sp], out=x[:], lhsT=tri_sb[:], rhs=sp[:],
                         start=True, stop=False)
                    if n > 0:
                        c.op("pe", nc.tensor.matmul, outs=[x], ins=[ones_sb, rn], out=x[:], lhsT=ones_sb[:],
                             rhs=rn[:], start=False, stop=False)
                    c.op("pe", nc.tensor.matmul, outs=[x], ins=[kt_, qt], out=x[:], lhsT=kap,
                         rhs=qap, start=False, stop=True)
                    c.op("act", nc.scalar.activation, outs=[wt], ins=[x], out=wt[:], in_=x[:], func=AF.Exp,
                         scale=-1.0)
                    if r >= 0:
                        c.op("dve", nc.vector.tensor_tensor, outs=[wt], ins=[wt, m_sb], out=wt[:], in0=wt[:],
                             in1=m_sb[:, r, :], op=ALU.mult)
                    c.op("pe", nc.tensor.matmul, outs=[pout], ins=[v, wt], out=pout[0:64, :],
                         lhsT=v[:, kt, r0:r0 + 64], rhs=wt[:], start=(n == 0), stop=(kt == 0))
                    if kt > 0:
                        if n == 0:
                            c.op("pool", nc.gpsimd.tensor_copy, outs=[rn], ins=[sp], out=rn[:], in_=sp[:])
                        else:
                            c.op("pool", nc.gpsimd.tensor_tensor, outs=[rn], ins=[rn, sp], out=rn[:], in0=rn[:],
                                 in1=sp[:], op=ALU.add)
                c.op("act", nc.scalar.copy, outs=[osb], ins=[pout], out=osb[:], in_=pout[0:64, :])
                row = pair * 128 + hs * 64
                c.dma("pool", OT[row:row + 64, i * 512:(i + 1) * 512], osb[:], in_t=osb)
    c.finish("pool")
    c.close()
    return nc


def ln_fm(c, nc, h_sb, hch, ones32, sq_sb, psS, psQ, g_col, b_col, small, NT, out_bf=None, out_bf_ch=None):
    mean = small["mean"]; rstd = small["rstd"]; msq = small["msq"]
    for fc in range(8):
        c.op("pe", nc.tensor.matmul, outs=[psS], ins=[ones32, hch[fc]], out=psS[:, 0:NT], lhsT=ones32[:],
             rhs=h_sb[:, fc, :], start=(fc == 0), stop=(fc == 7))
    for fc in range(8):
        sq = sq_sb[fc % 2]
        c.op("act", nc.scalar.activation, outs=[sq], ins=[hch[fc]], out=sq[:], in_=h_sb[:, fc, :], func=AF.Square)
        c.op("pe", nc.tensor.matmul, outs=[psQ], ins=[ones32, sq], out=psQ[:, 0:NT], lhsT=ones32[:],
             rhs=sq[:], start=(fc == 0), stop=(fc == 7))
    c.op("dve", nc.vector.tensor_scalar, outs=[mean], ins=[psS], out=mean[:], in0=psS[:, 0:NT],
         scalar1=1.0 / D, scalar2=None, op0=ALU.mult)
    c.op("dve", nc.vector.tensor_tensor, outs=[msq], ins=[mean], out=msq[:], in0=mean[:], in1=mean[:], op=ALU.mult)
    c.op("dve", nc.vector.scalar_tensor_tensor, outs=[rstd], ins=[psQ, msq], out=rstd[:], in0=psQ[:, 0:NT],
         scalar=1.0 / D, in1=msq[:], op0=ALU.mult, op1=ALU.subtract)
    c.op("act", nc.scalar.activation, outs=[rstd], ins=[rstd], out=rstd[:], in_=rstd[:], func=AF.Sqrt, bias=LN_EPS)
    c.op("dve", nc.vector.reciprocal, outs=[rstd], ins=[rstd], out=rstd[:], in_=rstd[:])
    for fc in range(8):
        c.op("dve", nc.vector.tensor_tensor, outs=[hch[fc]], ins=[hch[fc], mean], out=h_sb[:, fc, :],
             in0=h_sb[:, fc, :], in1=mean[:], op=ALU.subtract)
        c.op("pool", nc.gpsimd.tensor_tensor, outs=[hch[fc]], ins=[hch[fc], rstd], out=h_sb[:, fc, :],
             in0=h_sb[:, fc, :], in1=rstd[:], op=ALU.mult)
        c.op("act", nc.scalar.activation, outs=[hch[fc]], ins=[hch[fc], g_col, b_col], out=h_sb[:, fc, :],
             in_=h_sb[:, fc, :], func=AF.Identity, scale=g_col[:, fc:fc + 1], bias=b_col[:, fc:fc + 1])
        if out_bf is not None:
            c.op("dve", nc.vector.tensor_copy, outs=[out_bf_ch[fc]], ins=[hch[fc]], out=out_bf[:, fc, :],
                 in_=h_sb[:, fc, :])


def build_M():
    nc = new_nc()
    OT = din(nc, "OT", [D, TOK], BF16)
    hT = din(nc, "hT", [D, TOK], F32)
    w_o = din(nc, "w_o", [D, D], F32)
    w1 = din(nc, "w1", [D, DFF], F32)
    w2 = din(nc, "w2", [DFF, D], F32)
    lnp = din(nc, "lnp", [128, 4, 8], F32)
    hO = dout(nc, "hO", [D, TOK], F32)
    c = make_ctx(nc)
    cast = Caster(c)
    NT = 256
    stages = [c.sb(f"stg{i}", [128, 1024], F32) for i in range(2)]
    lnp_sb = c.sb("lnp", [128, 4, 8], F32)
    c.dma("sp", lnp_sb[:], lnp, out_t=lnp_sb)
    g1 = c.view(lnp_sb[:, 0, :], "g1"); b1 = c.view(lnp_sb[:, 1, :], "b1")
    g2 = c.view(lnp_sb[:, 2, :], "g2"); b2 = c.view(lnp_sb[:, 3, :], "b2")
    for v_ in (g1, b1, g2, b2):
        v_.w = lnp_sb.w
    ones32 = c.sb("ones32", [128, 128], F32)
    c.op("dve", nc.vector.memset, outs=[ones32], ap=ones32[:], constant=1.0)
    wo_b, wo_ch = load_weight_bf16(c, cast, w_o, D, D, "wo", stages)
    w1_b, w1_ch = load_weight_bf16(c, cast, w1, D, DFF, "w1", stages)
    w2_b, w2_ch = load_weight_bf16(c, cast, w2, DFF, D, "w2", stages)
    hs = [c.sb(f"h{i}", [128, 8, NT], F32) for i in range(2)]
    hchs = [[c.view(h[:, fc, :], f"{h.name}_{fc}") for fc in range(8)] for h in hs]
    ots = [c.sb(f"ot{i}", [128, 8, NT], BF16) for i in range(2)]
    hb = c.sb("hb", [128, 8, NT], BF16)
    hbch = [c.view(hb[:, fc, :], f"hb_{fc}") for fc in range(8)]
    aT = c.sb("aT", [128, 32, NT], BF16)
    aTch = [c.view(aT[:, f, :], f"aT_{f}") for f in range(32)]
    rl = [c.sb(f"rl{i}", [128, 2, NT], F32) for i in range(2)]
    sq_sb = [c.sb(f"sq{i}", [128, NT], F32) for i in range(2)]
    small = {k: c.sb(k, [128, NT], F32) for k in ("mean", "rstd", "msq")}
    banks = [c.ps(f"pb{i}", [128, 512], F32) for i in range(6)]
    psS = c.ps("psS", [128, 512], F32)
    psQ = c.ps("psQ", [128, 512], F32)
    bi = 0
    OT_v = OT.rearrange("(kc p) t -> p kc t", p=128)
    hT_v = hT.rearrange("(kc p) t -> p kc t", p=128)
    hO_v = hO.rearrange("(kc p) t -> p kc t", p=128)
    for tt in range(TOK // NT):
        h = hs[tt % 2]; hch = hchs[tt % 2]; ot = ots[tt % 2]
        sl = slice(tt * NT, (tt + 1) * NT)
        for k0_, nk_, src_ in ot_pieces(OT, tt * NT, NT):
            c.dma("sp", ot[:, k0_:k0_ + nk_, :], src_, out_t=ot)
        deps_ev = c.dma("sp", h[:], hT_v[:, :, sl], out_t=h)
        for v_ in hch:
            v_.w = deps_ev
            v_.r = []
        for fc in range(8):
            pb = banks[bi % 6]; bi += 1
            for kc in range(8):
                c.op("pe", nc.tensor.matmul, outs=[pb], ins=[wo_ch[kc], ot], out=pb[:, 0:NT],
                     lhsT=wo_b[:, kc, fc * 128:(fc + 1) * 128], rhs=ot[:, kc, :], start=(kc == 0), stop=(kc == 7))
            c.op("dve", nc.vector.scalar_tensor_tensor, outs=[hch[fc]], ins=[hch[fc], pb], out=h[:, fc, :],
                 in0=h[:, fc, :], scalar=ALPHA, in1=pb[:, 0:NT], op0=ALU.mult, op1=ALU.add)
        ln_fm(c, nc, h, hch, ones32, sq_sb, psS, psQ, g1, b1, small, NT, out_bf=hb, out_bf_ch=hbch)
        for f2 in range(16):
            pb = banks[bi % 6]; bi += 1
            for sub in range(2):
                f = f2 * 2 + sub
                for kc in range(8):
                    c.op("pe", nc.tensor.matmul, outs=[pb], ins=[w1_ch[kc], hbch[kc]],
                         out=pb[:, sub * NT:(sub + 1) * NT], lhsT=w1_b[:, kc, f * 128:(f + 1) * 128],
                         rhs=hb[:, kc, :], start=(kc == 0), stop=(kc == 7))
            r_ = rl[f2 % 2]
            c.op("act", nc.scalar.activation, outs=[r_], ins=[pb], out=r_[:].rearrange("p a n -> p (a n)"),
                 in_=pb[:, 0:2 * NT], func=AF.Relu)
            eng = ("dve", "pool")[f2 % 2]
            e_ = nc.vector if eng == "dve" else nc.gpsimd
            c.op(eng, e_.tensor_tensor, outs=[aTch[2 * f2], aTch[2 * f2 + 1]], ins=[r_],
                 out=aT[:, 2 * f2:2 * f2 + 2, :], in0=r_[:], in1=r_[:], op=ALU.mult)
        for fc in range(8):
            pb = banks[bi % 6]; bi += 1
            for f in range(32):
                c.op("pe", nc.tensor.matmul, outs=[pb], ins=[w2_ch[f], aTch[f]], out=pb[:, 0:NT],
                     lhsT=w2_b[:, f, fc * 128:(fc + 1) * 128], rhs=aT[:, f, :], start=(f == 0), stop=(f == 31))
            c.op("dve", nc.vector.scalar_tensor_tensor, outs=[hch[fc]], ins=[hch[fc], pb], out=h[:, fc, :],
                 in0=h[:, fc, :], scalar=ALPHA, in1=pb[:, 0:NT], op0=ALU.mult, op1=ALU.add)
        ln_fm(c, nc, h, hch, ones32, sq_sb, psS, psQ, g2, b2, small, NT)
        h.w = None
        h.r = []
        c._need("pool", [v_.w for v_ in hch])
        ev = c.dma("pool", hO_v[:, :, sl], h[:], in_t=h)
        for v_ in hch:
            v_.r.append(ev)
    c.finish("pool")
    c.close()
    return nc


_NC_CACHE = {}


def get_nc(key, fn, *a):
    if key not in _NC_CACHE:
        _NC_CACHE[key] = fn(*a)
    return _NC_CACHE[key]


def launch(nc, in_maps):
    res = run_bass_kernel_spmd(nc, in_maps, core_ids=list(range(NCORES)))
    return res.results


def consts():
    j = np.arange(128)[:, None, None]
    r = np.arange(4)[None, :, None]
    t = np.arange(512)[None, None, :]
    cs = {}
    cs["maskS"] = ((128 * r + j) < t).astype(NPBF)
    cs["maskC"] = ((128 * r + j) <= t).astype(NPBF)
    cs["tri"] = (np.arange(128)[:, None] >= np.arange(128)[None, :]).astype(NPBF)
    return cs


def to_fm(x):
    flat = np.ascontiguousarray(x).reshape(B * S, D)
    return [np.ascontiguousarray(flat[c * TOK:(c + 1) * TOK].T) for c in range(NCORES)]


def lnp_pack(g1, b1, g2, b2):
    return np.ascontiguousarray(np.stack([v.reshape(8, 128).T for v in (g1, b1, g2, b2)], axis=1)).astype(np.float32)


def gather_heads(per_core, key, feature_major):
    outs = []
    for c in range(NCORES):
        b, hh = c // 2, c % 2
        if feature_major:
            full = np.concatenate([per_core[2 * b][key], per_core[2 * b + 1][key]], axis=1)
            n = full.shape[0] // 2
            outs.append(np.ascontiguousarray(full[hh * n:(hh + 1) * n]))
        else:
            full = np.concatenate([per_core[2 * b][key], per_core[2 * b + 1][key]], axis=0)
            n = full.shape[1] // 2
            outs.append(np.ascontiguousarray(full[:, hh * n:(hh + 1) * n]))
    return outs


def scatter_OT(a_res):
    outs = []
    for c in range(NCORES):
        b, half = c // 2, c % 2
        full = np.concatenate([a_res[2 * b]["OT"], a_res[2 * b + 1]["OT"]], axis=0)
        outs.append(np.ascontiguousarray(full[:, half * TOK:(half + 1) * TOK]))
    return outs


def run_M(hT_list, OT_list, w_o, w1, w2, g1, b1, g2, b2):
    nc = get_nc("M", build_M)
    lnp = lnp_pack(g1, b1, g2, b2)
    res = launch(nc, [{"OT": OT_list[c], "hT": hT_list[c], "w_o": w_o, "w1": w1, "w2": w2, "lnp": lnp}
                      for c in range(NCORES)])
    return [r["hO"] for r in res]


def layer0(hT_list, p, cs):
    ncP = get_nc("P_sb", build_P_qkv, False, -0.125)
    pres = launch(ncP, [{"hT": hT_list[c], "w": p["l0_sb_w_qkv"]} for c in range(NCORES)])
    QT = gather_heads(pres, "QT", True)
    KT = gather_heads(pres, "KT", True)
    V = gather_heads(pres, "V", False)
    ncA = get_nc("A_sb", build_A_sb)
    ares = launch(ncA, [{"QT": QT[c], "KT": KT[c], "V": V[c], "maskS": cs["maskS"], "tri": cs["tri"]}
                        for c in range(NCORES)])
    OT = scatter_OT(ares)
    return run_M(hT_list, OT, p["l0_sb_w_o"], p["l0_mlp_w1"], p["l0_mlp_w2"], p["l0_ln1_g"], p["l0_ln1_b"],
                 p["l0_ln2_g"], p["l0_ln2_b"]), dict(pres=pres, ares=ares, OT=OT)


class SoftmaxRes:
    def __init__(self, c, nc, prefix=""):
        self.ps_s = [c.ps(f"{prefix}pss{i}", [128, 512], F32) for i in range(3)]
        self.ps_o = [c.ps(f"{prefix}pso{i}", [128, 512], F32) for i in range(2)]
        self.ps_b = c.ps(f"{prefix}psb", [128, 512], F32)
        self.p_sb = [c.sb(f"{prefix}p{i}", [128, 512], BF16) for i in range(3)]
        self.o32 = [c.sb(f"{prefix}o32_{i}", [64, 512], F32) for i in range(2)]
        self.rs = [c.sb(f"{prefix}rs{i}", [65, 512], F32) for i in range(2)]
        self.o_sb = [c.sb(f"{prefix}ob{i}", [64, 512], BF16) for i in range(2)]
        self.ones32 = c.sb(f"{prefix}ones32", [65, 64], F32)
        c.op("dve", nc.vector.memset, outs=[self.ones32], ap=self.ones32[:], constant=1.0)
        self.it = 0
        self.oi = 0


def softmax_chunk(c, nc, R, k_ts, q_ts, kt_sb, qt_sb, v_sb, KR, i, mask_t, kt_list, mask_of, extra_mm=None):
    po = R.ps_o[R.oi % 2]; o32 = R.o32[R.oi % 2]; rs = R.rs[R.oi % 2]; ob = R.o_sb[R.oi % 2]
    R.oi += 1
    qap = qt_sb[0:KR, i * 512:(i + 1) * 512]
    for n, kt in enumerate(kt_list):
        ps = R.ps_s[R.it % 3]; p = R.p_sb[R.it % 3]
        R.it += 1
        c.op("pe", nc.tensor.matmul, outs=[ps], ins=list(k_ts) + list(q_ts), out=ps[:],
             lhsT=kt_sb[0:KR, kt * 128:(kt + 1) * 128], rhs=qap, start=True, stop=(extra_mm is None))
        if extra_mm is not None:
            extra_mm(ps, kt)
        c.op("act", nc.scalar.activation, outs=[p], ins=[ps], out=p[:], in_=ps[:], func=AF.Exp)
        m = mask_of(kt)
        if m is not None:
            c.op("dve", nc.vector.tensor_tensor, outs=[p], ins=[p, mask_t], out=p[:], in0=p[:], in1=m, op=ALU.mult)
        c.op("pe", nc.tensor.matmul, outs=[po], ins=[v_sb, p], out=po[0:65, :], lhsT=v_sb[:, kt, 0:65], rhs=p[:],
             start=(n == 0), stop=(n == len(kt_list) - 1))
    c.op("act", nc.scalar.copy, outs=[o32], ins=[po], out=o32[:], in_=po[0:64, :])
    c.op("dve", nc.vector.tensor_scalar, outs=[rs], ins=[po], out=rs[64:65, :], in0=po[64:65, :],
         scalar1=1e-30, scalar2=None, op0=ALU.max)
    c.op("dve", nc.vector.reciprocal, outs=[rs], ins=[rs], out=rs[64:65, :], in_=rs[64:65, :])
    c.op("pe", nc.tensor.matmul, outs=[R.ps_b], ins=[R.ones32, rs], out=R.ps_b[0:64, :],
         lhsT=R.ones32[64:65, 0:64], rhs=rs[64:65, :], start=True, stop=True)
    return o32, R.ps_b, ob


def build_A_soft(kind):
    nc = new_nc()
    KR = 96
    if kind == "moba":
        QT = din(nc, "QT", [512, S], BF16)
        KT = din(nc, "KT", [512, S], BF16)
        Eind = din(nc, "Eind", [32, S], BF16)
        ident = din(nc, "ident", [128, 128], F32)
    else:
        QT = din(nc, "QT", [8 * 96, S], BF16)
        KT = din(nc, "KNT", [512, S], BF16)
        KRT = din(nc, "KRT", [32, S], BF16)
    V = din(nc, "V", [S, 512], BF16)
    maskC = din(nc, "maskC", [128, 4, 512], BF16)
    OT = dout(nc, "OT", [512, S], BF16)
    c = make_ctx(nc)
    m_sb = c.sb("maskC", [128, 4, 512], BF16)
    c.dma("sp", m_sb[:], maskC, out_t=m_sb)
    R = SoftmaxRes(c, nc)
    qts = [c.sb(f"qt{i}", [KR, S], BF16) for i in range(2)]
    q_hi = [c.view(q[64:96, :], f"{q.name}_hi") for q in qts]
    kts = [c.sb(f"kt{i}", [KR, S], BF16) for i in range(2)]
    vs = [c.sb(f"v{i}", [128, 64, 65], BF16) for i in range(2)]
    for v in vs:
        c.op("pool", nc.gpsimd.memset, outs=[v], ap=v[:, :, 64:65], constant=1.0)
    for k in kts:
        if kind == "moba":
            c.dma("sp", k[64:96, :], Eind, out_t=k)
        else:
            c.dma("sp", k[64:96, :].rearrange("p (r t) -> p r t", r=2), fm_shared(KRT), out_t=k)
    if kind == "moba":
        id_sb = c.sb("ident", [128, 128], F32)
        c.dma("sp", id_sb[:], ident, out_t=id_sb)
        g_sb = c.sb("g_sb", [128, 32], F32)
        m8 = c.sb("m8", [128, 8], F32)
        nms = [c.sb(f"nm{i}", [128, 96], F32) for i in range(2)]
        for nm in nms:
            c.op("dve", nc.vector.memset, outs=[nm], ap=nm[:], constant=0.0)
        kms32 = c.sb("kms32", [64, 32], F32)
        kms = c.sb("kms", [64, 32], BF16)
        ps_g = c.ps("psg", [128, 512], F32)
        ps_t = c.ps("pst", [128, 512], F32)
    V_v = V.rearrange("(kt p) n -> p kt n", p=128)
    for h in range(8):
        qt = qts[h % 2]; kt_ = kts[h % 2]; v = vs[h % 2]; qhi = q_hi[h % 2]
        c.dma("sp", kt_[0:64, :].rearrange("p (r t) -> p r t", r=2), fm_rows(KT, h * 64, (h + 1) * 64), out_t=kt_)
        if kind == "moba":
            c.dma("sp", qt[0:64, :].rearrange("p (r t) -> p r t", r=2), fm_rows(QT, h * 64, (h + 1) * 64), out_t=qt)
        else:
            c.dma("sp", qt[:].rearrange("p (r t) -> p r t", r=2), fm_rows(QT, h * 96, (h + 1) * 96), out_t=qt)
        for k0_, nk_, src_ in tm_pieces(V, h * 64, (h + 1) * 64):
            c.dma("sp", v[:, k0_:k0_ + nk_, 0:64], src_, out_t=v)
        q_ts = [qt]
        if kind == "moba":
            q_ts = [qt, qhi]
            c.op("dve", nc.vector.memset, outs=[g_sb], ap=g_sb[:], constant=-1e30)
            c.op("dve", nc.vector.tensor_reduce, outs=[kms32], ins=[kt_], out=kms32[:],
                 in_=kt_[0:64, :].rearrange("p (n k) -> p n k", k=256), axis=AX.X, op=ALU.add)
            c.op("dve", nc.vector.tensor_copy, outs=[kms], ins=[kms32], out=kms[:], in_=kms32[:])
            for qi in range(64):
                own = qi // 2
                nm = nms[qi % 2]
                grp = qi % 4
                if own > 3:
                    c.op("pe", nc.tensor.matmul, outs=[ps_g], ins=[qt, kms], out=ps_g[:, 0:32],
                         lhsT=qt[0:64, qi * 128:(qi + 1) * 128], rhs=kms[:], start=True, stop=True)
                    c.op("dve", nc.vector.tensor_copy, outs=[g_sb], ins=[ps_g], out=g_sb[:, 0:own],
                         in_=ps_g[:, 0:own])
                    c.op("dve", nc.vector.max, outs=[m8], ins=[g_sb], out=m8[:], in_=g_sb[:])
                    c.op("dve", nc.vector.tensor_scalar, outs=[nm], ins=[g_sb, m8], out=nm[:, 64:64 + own],
                         in0=g_sb[:, 0:own], scalar1=m8[:, 2:3], scalar2=-BIG, op0=ALU.is_lt, op1=ALU.mult)
                else:
                    if own > 0:
                        c.op("dve", nc.vector.memset, outs=[nm], ap=nm[:, 64:64 + own], constant=0.0)
                c.op("dve", nc.vector.memset, outs=[nm], ap=nm[:, 64 + own:65 + own], constant=0.0)
                if own < 31:
                    c.op("dve", nc.vector.memset, outs=[nm], ap=nm[:, 65 + own:96], constant=-BIG)
                c.op("pe", nc.tensor.transpose, outs=[ps_t], ins=[nm, id_sb],
                     out=ps_t[0:96, grp * 128:(grp + 1) * 128], in_=nm[:], identity=id_sb[:])
                if grp == 3:
                    ch = qi // 4
                    c.op("act", nc.scalar.copy, outs=[qhi], ins=[ps_t], out=qt[64:96, ch * 512:(ch + 1) * 512],
                         in_=ps_t[64:96, :])
        for i in range(S // 512):
            kl = list(range(0, 4 * i + 4))
            o32, pbc, ob = softmax_chunk(c, nc, R, [kt_], q_ts, kt_, qt, v, KR, i, m_sb, kl,
                                         lambda kt, i=i: (m_sb[:, kt - 4 * i, :] if kt >= 4 * i else None))
            c.op("dve", nc.vector.tensor_tensor, outs=[ob], ins=[o32, pbc], out=ob[:], in0=o32[:], in1=pbc[0:64, :],
                 op=ALU.mult)
            c.dma("pool", OT[h * 64:(h + 1) * 64, i * 512:(i + 1) * 512], ob[:], in_t=ob)
        if kind == "moba":
            qt.r.extend(qhi.r)
            qhi.r.extend(qt.r)
    c.finish("pool")
    c.close()
    return nc


def rope_tables_fm(dim, rows):
    inv = 1.0 / (10000.0 ** (np.arange(0, dim, 2, dtype=np.float32) / dim))
    ang = np.arange(S, dtype=np.float32)[:, None] * inv[None, :]
    ang = np.concatenate([ang, ang], axis=-1)
    cos = np.cos(ang).astype(np.float32).T
    sin = np.sin(ang).astype(np.float32).T
    reps = rows // dim
    return np.ascontiguousarray(np.tile(cos, (reps, 1))), np.ascontiguousarray(np.tile(sin, (reps, 1)))


def layer1(hT_list, p, cs):
    ncP = get_nc("P_moba", build_P_qkv, True, 0.125)
    cosF, sinF = rope_tables_fm(64, 128)
    pres = launch(ncP, [{"hT": hT_list[c], "w": p["l1_moba_w_qkv"],
                         "cosT": np.ascontiguousarray(cosF[:, (c % 2) * TOK:(c % 2 + 1) * TOK]),
                         "sinT": np.ascontiguousarray(sinF[:, (c % 2) * TOK:(c % 2 + 1) * TOK])}
                        for c in range(NCORES)])
    QT = gather_heads(pres, "QT", True)
    KT = gather_heads(pres, "KT", True)
    V = gather_heads(pres, "V", False)
    ncA = get_nc("A_moba", build_A_soft, "moba")
    Eind = (np.arange(S)[None, :] // 256 == np.arange(32)[:, None]).astype(NPBF)
    ident = np.eye(128, dtype=np.float32)
    ares = launch(ncA, [{"QT": QT[c], "KT": KT[c], "V": V[c], "maskC": cs["maskC"], "Eind": Eind, "ident": ident}
                        for c in range(NCORES)])
    OT = scatter_OT(ares)
    return run_M(hT_list, OT, p["l1_moba_w_o"], p["l1_mlp_w1"], p["l1_mlp_w2"], p["l1_ln1_g"], p["l1_ln1_b"],
                 p["l1_ln2_g"], p["l1_ln2_b"]), dict(pres=pres, ares=ares, OT=OT)


def build_P_mla():
    nc = new_nc()
    hT = din(nc, "hT", [D, TOK], F32)
    w_in = din(nc, "w_in", [D, 416], F32)
    w_uq = din(nc, "w_uq", [256, 1536], F32)
    w_ukv = din(nc, "w_ukv", [128, 2048], F32)
    gq = din(nc, "gq", [128, 2], F32)
    gkv = din(nc, "gkv", [128, 1], F32)
    cos96 = din(nc, "cos96", [96, TOK], F32)
    sin96 = din(nc, "sin96", [96, TOK], F32)
    QT = dout(nc, "QT", [1536, TOK], BF16)
    KNT = dout(nc, "KNT", [D, TOK], BF16)
    KRT = dout(nc, "KRT", [32, TOK], BF16)
    V = dout(nc, "V", [TOK, D], BF16)
    c = make_ctx(nc)
    cast = Caster(c)
    NT = 512
    QSC = float(96 ** -0.5)
    stages = [c.sb(f"stg{i}", [128, 1024], F32) for i in range(2)]
    win_b, win_ch = load_weight_bf16(c, cast, w_in, D, 416, "win", stages)
    wuq_b, wuq_ch = load_weight_bf16(c, cast, w_uq, 256, 1536, "wuq", stages)
    wukv_b, wukv_ch = load_weight_bf16(c, cast, w_ukv, 128, 2048, "wukv", stages)
    winr = c.sb("winr", [128, 8, 32], BF16)
    for kc in range(8):
        c.op("dve", nc.vector.tensor_scalar, outs=[winr], ins=[win_ch[kc]], out=winr[:, kc, 0:16],
             in0=win_b[:, kc, 400:416], scalar1=-1.0, scalar2=None, op0=ALU.mult)
        c.op("dve", nc.vector.tensor_copy, outs=[winr], ins=[win_ch[kc]], out=winr[:, kc, 16:32],
             in_=win_b[:, kc, 384:400])
    wuqr = c.sb("wuqr", [128, 2, 1536], BF16)
    c.op("pool", nc.gpsimd.memset, outs=[wuqr], ap=wuqr[:], constant=0.0)
    for fc in range(2):
        src = wuq_b[:, fc, :].rearrange("p (h d) -> p h d", d=96)
        dst = wuqr[:, fc, :].rearrange("p (h d) -> p h d", d=96)
        c.op("dve", nc.vector.tensor_scalar, outs=[wuqr], ins=[wuq_ch[fc]], out=dst[:, :, 64:80],
             in0=src[:, :, 80:96], scalar1=-1.0, scalar2=None, op0=ALU.mult)
        c.op("dve", nc.vector.tensor_copy, outs=[wuqr], ins=[wuq_ch[fc]], out=dst[:, :, 80:96],
             in_=src[:, :, 64:80])
    gq_sb = c.sb("gq", [128, 2], F32); gkv_sb = c.sb("gkv", [128, 1], F32)
    c.dma("sp", gq_sb[:], gq, out_t=gq_sb)
    c.dma("sp", gkv_sb[:], gkv, out_t=gkv_sb)
    cos_sb = c.sb("cos96", [96, TOK], F32); sin_sb = c.sb("sin96", [96, TOK], F32)
    c.dma("sp", cos_sb[:], cos96, out_t=cos_sb)
    c.dma("sp", sin_sb[:], sin96, out_t=sin_sb)
    ones32 = c.sb("ones32", [128, 128], F32)
    c.op("dve", nc.vector.memset, outs=[ones32], ap=ones32[:], constant=1.0)
    hts = [c.sb(f"ht{i}", [128, 8, NT], F32) for i in range(2)]
    hb = c.sb("hb", [128, 8, NT], BF16)
    cq32 = c.sb("cq32", [128, 3, NT], F32)
    cqn = c.sb("cqn", [128, 3, NT], BF16)
    sq_sb = [c.sb(f"sq{i}", [128, NT], F32) for i in range(2)]
    rstd = [c.sb(f"rstd{i}", [128, NT], F32) for i in range(2)]
    t1 = [c.sb(f"t1_{i}", [96, NT], F32) for i in range(2)]
    t2 = [c.sb(f"t2_{i}", [96, NT], F32) for i in range(2)]
    q_sb = [c.sb(f"q_sb{i}", [96, NT], BF16) for i in range(2)]
    kn_sb = [c.sb(f"kn_sb{i}", [64, NT], BF16) for i in range(2)]
    kr_sb = c.sb("kr_sb", [32, NT], BF16)
    v_sb = c.sb("v_sb", [128, 4, D], BF16)
    banks = [c.ps(f"pb{i}", [128, 512], F32) for i in range(7)]
    psQ = c.ps("psQ", [128, 512], F32)
    bi = 0
    hT_v = hT.rearrange("(kc p) t -> p kc t", p=128)
    for tt in range(TOK // NT):
        sl = slice(tt * NT, (tt + 1) * NT)
        ht = hts[tt % 2]
        c.dma("sp", ht[:], hT_v[:, :, sl], out_t=ht)
        for kc in range(8):
            cast.copy(hb, hb[:, kc, :], ht, ht[:, kc, :], eng=("dve", "pool")[kc % 2])
        for fc in range(3):
            pb = banks[bi % 7]; bi += 1
            for kc in range(8):
                c.op("pe", nc.tensor.matmul, outs=[pb], ins=[win_ch[kc], hb], out=pb[:, 0:NT],
                     lhsT=win_b[:, kc, fc * 128:(fc + 1) * 128], rhs=hb[:, kc, :], start=(kc == 0), stop=(kc == 7))
            c.op("act", nc.scalar.copy, outs=[cq32], ins=[pb], out=cq32[:, fc, :], in_=pb[:, 0:NT])
        pb = banks[bi % 7]; bi += 1
        pr = banks[bi % 7]; bi += 1
        for kc in range(8):
            c.op("pe", nc.tensor.matmul, outs=[pb], ins=[win_ch[kc], hb], out=pb[0:32, 0:NT],
                 lhsT=win_b[:, kc, 384:416], rhs=hb[:, kc, :], start=(kc == 0), stop=(kc == 7))
        for kc in range(8):
            c.op("pe", nc.tensor.matmul, outs=[pr], ins=[winr, hb], out=pr[0:32, 0:NT],
                 lhsT=winr[:, kc, :], rhs=hb[:, kc, :], start=(kc == 0), stop=(kc == 7))
        a = t1[0]; b_ = t2[0]
        c.op("dve", nc.vector.tensor_tensor, outs=[a], ins=[pb, cos_sb], out=a[0:32, :], in0=pb[0:32, 0:NT],
             in1=cos_sb[64:96, sl], op=ALU.mult)
        c.op("dve", nc.vector.tensor_tensor, outs=[b_], ins=[pr, sin_sb], out=b_[0:32, :], in0=pr[0:32, 0:NT],
             in1=sin_sb[64:96, sl], op=ALU.mult)
        c.op("pool", nc.gpsimd.tensor_tensor, outs=[kr_sb], ins=[a, b_], out=kr_sb[:], in0=a[0:32, :],
             in1=b_[0:32, :], op=ALU.add)
        c.dma("pool", KRT[:, sl], kr_sb[:], in_t=kr_sb)
        for grp, (chs, gsb, n) in enumerate((((0, 1), gq_sb, 256), ((2,), gkv_sb, 128))):
            rs_ = rstd[grp]
            for ci, ch in enumerate(chs):
                sq = sq_sb[ci % 2]
                c.op("act", nc.scalar.activation, outs=[sq], ins=[cq32], out=sq[:], in_=cq32[:, ch, :],
                     func=AF.Square)
                c.op("pe", nc.tensor.matmul, outs=[psQ], ins=[ones32, sq], out=psQ[:, 0:NT], lhsT=ones32[:],
                     rhs=sq[:], start=(ci == 0), stop=(ci == len(chs) - 1))
            c.op("act", nc.scalar.activation, outs=[rs_], ins=[psQ], out=rs_[:], in_=psQ[:, 0:NT], func=AF.Sqrt,
                 scale=1.0 / n, bias=RMS_EPS)
            c.op("dve", nc.vector.reciprocal, outs=[rs_], ins=[rs_], out=rs_[:], in_=rs_[:])
            for ci, ch in enumerate(chs):
                c.op("dve", nc.vector.scalar_tensor_tensor, outs=[cqn], ins=[cq32, gsb, rs_], out=cqn[:, ch, :],
                     in0=cq32[:, ch, :], scalar=gsb[:, ci:ci + 1], in1=rs_[:], op0=ALU.mult, op1=ALU.mult)
        for h in range(16):
            pb = banks[bi % 7]; bi += 1
            pr = banks[bi % 7]; bi += 1
            for fc in range(2):
                c.op("pe", nc.tensor.matmul, outs=[pb], ins=[wuq_ch[fc], cqn], out=pb[0:96, 0:NT],
                     lhsT=wuq_b[:, fc, h * 96:(h + 1) * 96], rhs=cqn[:, fc, :], start=(fc == 0), stop=(fc == 1))
            for fc in range(2):
                c.op("pe", nc.tensor.matmul, outs=[pr], ins=[wuqr, cqn], out=pr[0:96, 0:NT],
                     lhsT=wuqr[:, fc, h * 96:(h + 1) * 96], rhs=cqn[:, fc, :], start=(fc == 0), stop=(fc == 1))
            a = t1[h % 2]; b_ = t2[h % 2]; qs = q_sb[h % 2]
            c.op("dve", nc.vector.tensor_tensor, outs=[a], ins=[pb, cos_sb], out=a[:], in0=pb[0:96, 0:NT],
                 in1=cos_sb[:, sl], op=ALU.mult)
            c.op("dve", nc.vector.tensor_tensor, outs=[b_], ins=[pr, sin_sb], out=b_[:], in0=pr[0:96, 0:NT],
                 in1=sin_sb[:, sl], op=ALU.mult)
            c.op("pool", nc.gpsimd.tensor_tensor, outs=[a], ins=[a, b_], out=a[:], in0=a[:], in1=b_[:], op=ALU.add)
            c.op("act", nc.scalar.mul, outs=[qs], ins=[a], out=qs[:], in_=a[:], mul=QSC)
            c.dma("pool", QT[h * 96:(h + 1) * 96, sl], qs[:], in_t=qs)
        wk_v = wukv_b[:, 0, :].rearrange("p (h two d) -> p h two d", two=2, d=64)
        for h in range(16):
            pb = banks[bi % 7]; bi += 1
            c.op("pe", nc.tensor.matmul, outs=[pb], ins=[wukv_ch[0], cqn], out=pb[0:64, 0:NT],
                 lhsT=wk_v[:, h, 0, :], rhs=cqn[:, 2, :], start=True, stop=True)
            ks = kn_sb[h % 2]
            if h % 2 == 0:
                c.op("act", nc.scalar.copy, outs=[ks], ins=[pb], out=ks[:], in_=pb[0:64, 0:NT])
            else:
                c.op("dve", nc.vector.tensor_copy, outs=[ks], ins=[pb], out=ks[:], in_=pb[0:64, 0:NT])
            c.dma("pool", KNT[h * 64:(h + 1) * 64, sl], ks[:], in_t=ks)
        for j in range(4):
            for half in range(2):
                pb = banks[bi % 7]; bi += 1
                c.op("pe", nc.tensor.matmul, outs=[pb], ins=[wukv_ch[0], cqn], out=pb[:],
                     lhsT=cqn[:, 2, j * 128:(j + 1) * 128], rhs=wk_v[:, half * 8:(half + 1) * 8, 1, :],
                     start=True, stop=True)
                if half == 0:
                    c.op("act", nc.scalar.copy, outs=[v_sb], ins=[pb], out=v_sb[:, j, 0:512], in_=pb[:])
                else:
                    c.op("dve", nc.vector.tensor_copy, outs=[v_sb], ins=[pb], out=v_sb[:, j, 512:1024], in_=pb[:])
        V_v = V.rearrange("(j p) n -> p j n", p=128)
        c.dma("pool", V_v[:, tt * 4:(tt + 1) * 4, :], v_sb[:], in_t=v_sb)
    c.finish("pool")
    c.close()
    return nc


def layer2(hT_list, p, cs):
    ncP = get_nc("P_mla", build_P_mla)
    cos32, sin32 = rope_tables_fm(32, 32)
    cos96 = np.concatenate([np.ones((64, S), np.float32), cos32], axis=0)
    sin96 = np.concatenate([np.zeros((64, S), np.float32), sin32], axis=0)
    gq = np.ascontiguousarray(p["l2_mla_q_norm"].reshape(2, 128).T).astype(np.float32)
    gkv = np.ascontiguousarray(p["l2_mla_kv_norm"].reshape(1, 128).T).astype(np.float32)
    pres = launch(ncP, [{"hT": hT_list[c], "w_in": p["l2_mla_w_in"], "w_uq": p["l2_mla_w_uq"],
                         "w_ukv": p["l2_mla_w_ukv"], "gq": gq, "gkv": gkv,
                         "cos96": np.ascontiguousarray(cos96[:, (c % 2) * TOK:(c % 2 + 1) * TOK]),
                         "sin96": np.ascontiguousarray(sin96[:, (c % 2) * TOK:(c % 2 + 1) * TOK])}
                        for c in range(NCORES)])
    QT = gather_heads(pres, "QT", True)
    KNT = gather_heads(pres, "KNT", True)
    V = gather_heads(pres, "V", False)
    KRT = [np.ascontiguousarray(np.concatenate([pres[2 * (c // 2)]["KRT"], pres[2 * (c // 2) + 1]["KRT"]], axis=1))
           for c in range(NCORES)]
    ncA = get_nc("A_mla", build_A_soft, "mla")
    ares = launch(ncA, [{"QT": QT[c], "KNT": KNT[c], "KRT": KRT[c], "V": V[c], "maskC": cs["maskC"]}
                        for c in range(NCORES)])
    OT = scatter_OT(ares)
    return run_M(hT_list, OT, p["l2_mla_w_o"], p["l2_mlp_w1"], p["l2_mlp_w2"], p["l2_ln1_g"], p["l2_ln1_b"],
                 p["l2_ln2_g"], p["l2_ln2_b"]), dict(pres=pres, ares=ares, OT=OT)


NSA_IN = 2608


def build_P_nsa():
    nc = new_nc()
    hT = din(nc, "hT", [D, TOK], F32)
    w = din(nc, "w", [D, NSA_IN], F32)
    cosT = din(nc, "cosT", [128, TOK], F32)
    sinT = din(nc, "sinT", [128, TOK], F32)
    QT = dout(nc, "QT", [D, TOK], BF16)
    KcT = dout(nc, "KcT", [256, TOK], BF16)
    VcT = dout(nc, "VcT", [256, TOK], BF16)
    KsT = dout(nc, "KsT", [256, TOK], BF16)
    KwT = dout(nc, "KwT", [256, TOK], BF16)
    Vs = dout(nc, "Vs", [TOK, 256], BF16)
    Vw = dout(nc, "Vw", [TOK, 256], BF16)
    GT = dout(nc, "GT", [64, TOK], F32)
    c = make_ctx(nc)
    cast = Caster(c)
    NT = 512
    stages = [c.sb(f"stg{i}", [128, 1024], F32) for i in range(2)]
    wb, wch = load_weight_bf16(c, cast, w, D, NSA_IN, "wb", stages)
    roped = [(0, 8, QT, 0.125), (1024, 2, KcT, 1.0), (1536, 2, KsT, 1.0), (2048, 2, KwT, 1.0)]
    wr = c.sb("wrot", [128, 8, 14 * 128], BF16)
    wrch = [c.view(wr[:, kc, :], f"wrot_{kc}") for kc in range(8)]
    rcol = {}
    o = 0
    for (c0, nch, _, _) in roped:
        rcol[c0] = o
        for kc in range(8):
            src = wb[:, kc, c0:c0 + nch * 128].rearrange("p (h two d) -> p h two d", two=2, d=32)
            dst = wr[:, kc, o:o + nch * 128].rearrange("p (h two d) -> p h two d", two=2, d=32)
            c.op("dve", nc.vector.tensor_scalar, outs=[wrch[kc]], ins=[wch[kc]],
                 out=dst[:, :, 0, :], in0=src[:, :, 1, :], scalar1=-1.0, scalar2=None, op0=ALU.mult)
            c.op("pool", nc.gpsimd.tensor_copy, outs=[wrch[kc]], ins=[wch[kc]],
                 out=dst[:, :, 1, :], in_=src[:, :, 0, :])
        o += nch * 128
    cos_sb = c.sb("cos_sb", [128, TOK], F32)
    sin_sb = c.sb("sin_sb", [128, TOK], F32)
    c.dma("sp", cos_sb[:], cosT, out_t=cos_sb)
    c.dma("sp", sin_sb[:], sinT, out_t=sin_sb)
    hts = [c.sb(f"ht{i}", [128, 8, NT], F32) for i in range(2)]
    hb = c.sb("hb", [128, 8, NT], BF16)
    osb = [c.sb(f"osb{i}", [128, NT], BF16) for i in range(3)]
    t1 = [c.sb(f"t1_{i}", [128, NT], F32) for i in range(2)]
    t2 = [c.sb(f"t2_{i}", [128, NT], F32) for i in range(2)]
    v_sb = c.sb("v_sb", [128, 4, 512], BF16)
    g_sb = c.sb("g_sb", [64, NT], F32)
    banks = [c.ps(f"pb{i}", [128, 512], F32) for i in range(8)]
    bi = 0
    oi = 0
    hT_v = hT.rearrange("(kc p) t -> p kc t", p=128)
    for tt in range(TOK // NT):
        sl = slice(tt * NT, (tt + 1) * NT)
        ht = hts[tt % 2]
        c.dma("sp", ht[:], hT_v[:, :, sl], out_t=ht)
        for kc in range(8):
            cast.copy(hb, hb[:, kc, :], ht, ht[:, kc, :], eng=("dve", "pool")[kc % 2])
        for (c0, nch, out_d, sc) in roped:
            for fc in range(nch):
                col = c0 + fc * 128
                rc = rcol[c0] + fc * 128
                pb = banks[bi % 8]; bi += 1
                pr = banks[bi % 8]; bi += 1
                for kc in range(8):
                    c.op("pe", nc.tensor.matmul, outs=[pb], ins=[wch[kc], hb], out=pb[:, 0:NT],
                         lhsT=wb[:, kc, col:col + 128], rhs=hb[:, kc, :], start=(kc == 0), stop=(kc == 7))
                for kc in range(8):
                    c.op("pe", nc.tensor.matmul, outs=[pr], ins=[wrch[kc], hb], out=pr[:, 0:NT],
                         lhsT=wr[:, kc, rc:rc + 128], rhs=hb[:, kc, :], start=(kc == 0), stop=(kc == 7))
                a = t1[oi % 2]; b_ = t2[oi % 2]; ob = osb[oi % 3]; oi += 1
                c.op("dve", nc.vector.tensor_tensor, outs=[a], ins=[pb, cos_sb], out=a[:], in0=pb[:, 0:NT],
                     in1=cos_sb[:, sl], op=ALU.mult)
                c.op("dve", nc.vector.tensor_tensor, outs=[b_], ins=[pr, sin_sb], out=b_[:], in0=pr[:, 0:NT],
                     in1=sin_sb[:, sl], op=ALU.mult)
                c.op("pool", nc.gpsimd.tensor_tensor, outs=[a], ins=[a, b_], out=a[:], in0=a[:], in1=b_[:],
                     op=ALU.add)
                c.op("act", nc.scalar.mul, outs=[ob], ins=[a], out=ob[:], in_=a[:], mul=sc)
                c.dma("pool", out_d[fc * 128:(fc + 1) * 128, sl], ob[:], in_t=ob)
        for fc in range(2):
            col = 1280 + fc * 128
            pb = banks[bi % 8]; bi += 1
            for kc in range(8):
                c.op("pe", nc.tensor.matmul, outs=[pb], ins=[wch[kc], hb], out=pb[:, 0:NT],
                     lhsT=wb[:, kc, col:col + 128], rhs=hb[:, kc, :], start=(kc == 0), stop=(kc == 7))
            ob = osb[oi % 3]; oi += 1
            c.op("act", nc.scalar.copy, outs=[ob], ins=[pb], out=ob[:], in_=pb[:, 0:NT])
            c.dma("pool", VcT[fc * 128:(fc + 1) * 128, sl], ob[:], in_t=ob)
        import os as _os
        pb = banks[bi % 8]; bi += 1
        for kc in range(8):
            if _os.environ.get("NOGATE"): break
            c.op("pe", nc.tensor.matmul, outs=[pb], ins=[wch[kc], hb], out=pb[0:64, 0:NT],
                 lhsT=wb[:, kc, 2544:2608], rhs=hb[:, kc, :], start=(kc == 0), stop=(kc == 7))
        if not _os.environ.get("NOGATE"):
            c.op("act", nc.scalar.activation, outs=[g_sb], ins=[pb], out=g_sb[:], in_=pb[0:64, 0:NT], func=AF.Sigmoid)
            c.dma("pool", GT[:, sl], g_sb[:], in_t=g_sb)
        for j in range(4):
            pb = banks[bi % 8]; bi += 1
            for hi, col in enumerate((1792, 2304)):
                for kc in range(8):
                    c.op("pe", nc.tensor.matmul, outs=[pb], ins=[wch[kc], hb], out=pb[:, hi * 256:(hi + 1) * 256],
                         lhsT=hb[:, kc, j * 128:(j + 1) * 128], rhs=wb[:, kc, col:col + 256],
                         start=(kc == 0), stop=(kc == 7))
            c.op("act", nc.scalar.copy, outs=[v_sb], ins=[pb], out=v_sb[:, j, :], in_=pb[:])
        Vs_v = Vs.rearrange("(j p) n -> p j n", p=128)
        Vw_v = Vw.rearrange("(j p) n -> p j n", p=128)
        c.dma("pool", Vs_v[:, tt * 4:(tt + 1) * 4, :], v_sb[:, :, 0:256], in_t=v_sb)
        c.dma("pool", Vw_v[:, tt * 4:(tt + 1) * 4, :], v_sb[:, :, 256:512], in_t=v_sb)
    c.finish("pool")
    c.close()
    return nc


GELU_C = 1.5957691216057308


def build_A_nsa():
    nc = new_nc()
    QT = din(nc, "QT", [512, S], BF16)
    KcT = din(nc, "KcT", [128, S], BF16)
    VcT = din(nc, "VcT", [128, S], BF16)
    KsT = din(nc, "KsT", [128, S], BF16)
    KwT = din(nc, "KwT", [128, S], BF16)
    Vs = din(nc, "Vs", [S, 128], BF16)
    Vw = din(nc, "Vw", [S, 128], BF16)
    GT = din(nc, "GT", [32, S], F32)
    posT = din(nc, "posT", [64, 2, 32], F32)
    w1k = din(nc, "w1k", [2048, 256], F32)
    w1v = din(nc, "w1v", [2048, 256], F32)
    w2 = din(nc, "w2", [128, 2, 2, 64], F32)
    maskC = din(nc, "maskC", [128, 4, 512], BF16)
    maskL = din(nc, "maskL", [128, 4, 512], BF16)
    cmask = din(nc, "cmask", [128, 5, 512], BF16)
    ovl = din(nc, "ovl", [128, 4, 128], BF16)
    Eind = din(nc, "Eind", [128, S], BF16)
    JC = din(nc, "JC", [128, 128], F32)
    CB = din(nc, "CB", [128, 128], F32)
    ident = din(nc, "ident", [128, 128], F32)
    SelG = din(nc, "SelG", [32, 24 * 64], F32)
    OT = dout(nc, "OT", [512, S], BF16)
    c = make_ctx(nc)
    cast = Caster(c, engines=("dve", "pool"))

    def const(name, ap, shape, dt):
        t = c.sb(name, shape, dt)
        c.dma("sp", t[:], ap, out_t=t)
        return t
    mC = const("maskC", maskC, [128, 4, 512], BF16)
    mL = const("maskL", maskL, [128, 4, 512], BF16)
    cm = const("cmask", cmask, [128, 5, 512], BF16)
    ovl_sb = const("ovl", ovl, [128, 4, 128], BF16)
    E_sb = const("Eind", Eind, [128, S], BF16)
    JC_sb = const("JC", JC, [128, 128], F32)
    CB_sb = const("CB", CB, [128, 128], F32)
    id_sb = const("ident", ident, [128, 128], F32)
    SelG_sb = const("SelG", SelG, [32, 24 * 64], F32)
    posT32 = const("posT", posT, [64, 2, 32], F32)
    w2_32 = const("w2", w2, [128, 2, 2, 64], F32)
    posTb = c.sb("posTb", [64, 2, 32], BF16)
    c.op("dve", nc.vector.tensor_copy, outs=[posTb], ins=[posT32], out=posTb[:], in_=posT32[:])
    w2b = c.sb("w2b", [128, 2, 2, 64], BF16)
    c.op("dve", nc.vector.tensor_copy, outs=[w2b], ins=[w2_32], out=w2b[:], in_=w2_32[:])
    stg = [c.sb(f"stg{i}", [64, 4, 256], F32) for i in range(2)]
    W1 = []
    si = 0
    for nm_, wd in (("w1k", w1k), ("w1v", w1v)):
        t = c.sb(nm_, [64, 32, 256], BF16)
        wv = wd.rearrange("(l d) n -> d l n", d=64)
        for l0 in range(0, 32, 4):
            st = stg[si % 2]; si += 1
            c.dma("sp", st[:], wv[:, l0:l0 + 4, :], out_t=st)
            cast.copy(t, t[:, l0:l0 + 4, :], st, st[:])
        W1.append(t)
    ones32 = c.sb("ones32", [65, 64], F32)
    c.op("dve", nc.vector.memset, outs=[ones32], ap=ones32[:], constant=1.0)

    kcv_sb = c.sb("kcv", [64, S], BF16)
    ks_sb = c.sb("ksT", [64, S], BF16)
    kw_sb = c.sb("kwT", [64, S], BF16)
    vs_sb = c.sb("vs", [128, 64, 65], BF16)
    vw_sb = c.sb("vw", [128, 64, 65], BF16)
    vc_sb = c.sb("vc", [128, 4, 65], BF16)
    kcT_sb = c.sb("kcT", [64, 512], BF16)
    for v_ in (vs_sb, vw_sb):
        c.op("pool", nc.gpsimd.memset, outs=[v_], ap=v_[:, :, 64:65], constant=1.0)
    c.op("pool", nc.gpsimd.memset, outs=[vc_sb], ap=vc_sb[:, :, 64:65], constant=1.0)
    c.op("pool", nc.gpsimd.memset, outs=[kcT_sb], ap=kcT_sb[:], constant=0.0)
    b1_sb = c.sb("b1", [128, 2], F32)
    x32 = [c.sb(f"x32_{i}", [128, 512], F32) for i in range(2)]
    u32 = [c.sb(f"u32_{i}", [128, 512], F32) for i in range(2)]
    gel = c.sb("gel", [128, 2, 512], BF16)
    c.op("pool", nc.gpsimd.memset, outs=[gel], ap=gel[:], constant=0.0)
    qch = [c.sb(f"qch{i}", [64, 4, 512], BF16) for i in range(2)]
    gch = [c.sb(f"gch{i}", [32, 512], F32) for i in range(2)]
    for gc_ in gch:
        c.op("pool", nc.gpsimd.memset, outs=[gc_], ap=gc_[:], constant=0.0)
    pcs = [c.sb(f"pc{i}", [128, 512], BF16) for i in range(4)]
    p_sb = [c.sb(f"p{i}", [128, 512], BF16) for i in range(3)]
    o32 = [c.sb(f"o32_{i}", [64, 512], F32) for i in range(2)]
    rs = [c.sb(f"rs{i}", [65, 512], F32) for i in range(2)]
    on = [c.sb(f"on{i}", [64, 512], F32) for i in range(2)]
    acc = c.sb("acc", [64, 512], F32)
    stash = [c.sb(f"stash{i}", [64, 512], F32) for i in range(4)]
    ob = [c.sb(f"ob{i}", [64, 512], BF16) for i in range(2)]
    impacc = c.sb("impacc", [128, 4, 128], F32)
    rsT_sb = c.sb("rsT", [128, 4], F32)
    f1 = c.sb("f1", [128, 128], F32)
    pen = c.sb("pen", [128, 128], F32)
    imp3 = c.sb("imp3", [128, 128], F32)
    imp4 = c.sb("imp4", [128, 128], F32)
    m8a = c.sb("m8a", [128, 8], F32)
    m8b = c.sb("m8b", [128, 8], F32)
    nm = [c.sb(f"nm{i}", [128, 128], F32) for i in range(2)]
    nmT = c.sb("nmT", [128, 512], BF16)

    ps_s = [c.ps(f"pss{i}", [128, 512], F32) for i in range(2)]
    ps_o = [c.ps(f"pso{i}", [128, 512], F32) for i in range(2)]
    ps_b = c.ps("psb", [128, 512], F32)
    ps_imp = c.ps("psimp", [128, 512], F32)
    ps_t = c.ps("pst", [128, 512], F32)
    ps_g = c.ps("psgb", [128, 512], F32)
    cnt = {"s": 0, "p": 0, "o": 0}

    Vs_v = Vs.rearrange("(kt p) n -> p kt n", p=128)
    Vw_v = Vw.rearrange("(kt p) n -> p kt n", p=128)

    def attend(k_t, k_ap_of, qap, q_t, v_sb, v_ap_of, kt_list, mask_of, extra_mm=None, keep=None):
        po = ps_o[cnt["o"] % 2]; o3 = o32[cnt["o"] % 2]; rs_ = rs[cnt["o"] % 2]
        cnt["o"] += 1
        for n, kt in enumerate(kt_list):
            ps = ps_s[cnt["s"] % 2]; cnt["s"] += 1
            if keep is not None:
                p = keep[n]
            else:
                p = p_sb[cnt["p"] % 3]; cnt["p"] += 1
            c.op("pe", nc.tensor.matmul, outs=[ps], ins=[k_t, q_t], out=ps[:], lhsT=k_ap_of(kt), rhs=qap,
                 start=True, stop=(extra_mm is None))
            if extra_mm is not None:
                extra_mm(ps, kt)
            c.op("act", nc.scalar.activation, outs=[p], ins=[ps], out=p[:], in_=ps[:], func=AF.Exp)
            m = mask_of(kt)
            if m is not None:
                mt, map_ = m
                c.op("dve", nc.vector.tensor_tensor, outs=[p], ins=[p, mt], out=p[:], in0=p[:], in1=map_, op=ALU.mult)
            c.op("pe", nc.tensor.matmul, outs=[po], ins=[v_sb, p], out=po[0:65, :], lhsT=v_ap_of(kt), rhs=p[:],
                 start=(n == 0), stop=(n == len(kt_list) - 1))
        c.op("act", nc.scalar.copy, outs=[o3], ins=[po], out=o3[:], in_=po[0:64, :])
        c.op("dve", nc.vector.tensor_scalar, outs=[rs_], ins=[po], out=rs_[64:65, :], in0=po[64:65, :],
             scalar1=1e-30, scalar2=None, op0=ALU.max)
        c.op("dve", nc.vector.reciprocal, outs=[rs_], ins=[rs_], out=rs_[64:65, :], in_=rs_[64:65, :])
        c.op("pe", nc.tensor.matmul, outs=[ps_b], ins=[ones32, rs_], out=ps_b[0:64, :],
             lhsT=ones32[64:65, 0:64], rhs=rs_[64:65, :], start=True, stop=True)
        return o3, rs_

    oi = 0
    for g in range(2):
        c.dma("sp", ks_sb[:].rearrange("p (r t) -> p r t", r=2), fm_rows(KsT, g * 64, (g + 1) * 64), out_t=ks_sb)
        c.dma("sp", kw_sb[:].rearrange("p (r t) -> p r t", r=2), fm_rows(KwT, g * 64, (g + 1) * 64), out_t=kw_sb)
        for k0_, nk_, src_ in tm_pieces(Vs, g * 64, (g + 1) * 64):
            c.dma("sp", vs_sb[:, k0_:k0_ + nk_, 0:64], src_, out_t=vs_sb)
        for k0_, nk_, src_ in tm_pieces(Vw, g * 64, (g + 1) * 64):
            c.dma("sp", vw_sb[:, k0_:k0_ + nk_, 0:64], src_, out_t=vw_sb)
        for kv in range(2):
            src = (KcT, VcT)[kv]
            c.dma("sp", kcv_sb[:].rearrange("p (r t) -> p r t", r=2), fm_rows(src, g * 64, (g + 1) * 64), out_t=kcv_sb)
            W = W1[kv]
            for half in range(2):
                pb = ps_s[half]
                for l in range(32):
                    c.op("pe", nc.tensor.matmul, outs=[pb], ins=[W, posTb], out=pb[:, 0:1],
                         lhsT=W[:, l, half * 128:(half + 1) * 128], rhs=posTb[:, kv, l:l + 1],
                         start=(l == 0), stop=(l == 31))
                c.op("act", nc.scalar.copy, outs=[b1_sb], ins=[pb], out=b1_sb[:, half:half + 1], in_=pb[:, 0:1])
            for half in range(2):
                pb = ps_s[half]
                for l in range(32):
                    c.op("pe", nc.tensor.matmul, outs=[pb], ins=[W, kcv_sb], out=pb[:, 0:511],
                         lhsT=W[:, l, half * 128:(half + 1) * 128],
                         rhs=kcv_sb[:, l:l + 16 * 510 + 1:16], start=(l == 0), stop=(l == 31))
                x = x32[half]; u = u32[half]
                c.op("act", nc.scalar.activation, outs=[x], ins=[pb, b1_sb], out=x[:, 0:511], in_=pb[:, 0:511],
                     func=AF.Identity, bias=b1_sb[:, half:half + 1])
                c.op("dve", nc.vector.tensor_tensor, outs=[u], ins=[x], out=u[:, 0:511], in0=x[:, 0:511],
                     in1=x[:, 0:511], op=ALU.mult)
                c.op("dve", nc.vector.tensor_scalar, outs=[u], ins=[u], out=u[:, 0:511], in0=u[:, 0:511],
                     scalar1=0.044715, scalar2=1.0, op0=ALU.mult, op1=ALU.add)
                c.op("dve", nc.vector.tensor_tensor, outs=[u], ins=[u, x], out=u[:, 0:511], in0=u[:, 0:511],
                     in1=x[:, 0:511], op=ALU.mult)
                c.op("act", nc.scalar.activation, outs=[u], ins=[u], out=u[:, 0:511], in_=u[:, 0:511],
                     func=AF.Sigmoid, scale=GELU_C)
                c.op("dve", nc.vector.tensor_tensor, outs=[gel], ins=[u, x], out=gel[:, half, 0:511],
                     in0=u[:, 0:511], in1=x[:, 0:511], op=ALU.mult)
            if kv == 0:
                pb = ps_s[0]
                for half in range(2):
                    c.op("pe", nc.tensor.matmul, outs=[pb], ins=[w2b, gel], out=pb[0:64, 0:511],
                         lhsT=w2b[:, 0, half, :], rhs=gel[:, half, 0:511], start=(half == 0), stop=(half == 1))
                c.op("act", nc.scalar.copy, outs=[kcT_sb], ins=[pb], out=kcT_sb[:, 0:511], in_=pb[0:64, 0:511])
            else:
                for nt in range(4):
                    pb = ps_s[nt % 2]
                    for half in range(2):
                        c.op("pe", nc.tensor.matmul, outs=[pb], ins=[w2b, gel], out=pb[:, 0:64],
                             lhsT=gel[:, half, nt * 128:(nt + 1) * 128], rhs=w2b[:, 1, half, :],
                             start=(half == 0), stop=(half == 1))
                    c.op("act", nc.scalar.copy, outs=[vc_sb], ins=[pb], out=vc_sb[:, nt, 0:64], in_=pb[:, 0:64])
        for i in range(S // 512):
            qc = qch[i % 2]; gc = gch[i % 2]
            for r_ in range(4):
                c.dma("sp", qc[:, r_, :], fm_chunk(QT, g * 256 + r_ * 64, g * 256 + (r_ + 1) * 64, i), out_t=qc)
            c.dma("sp", gc[0:24, :], gt_chunk(GT, i), out_t=gc)
            n_ct = (32 * i + 30) // 128 + 1
            cres = []
            for r in range(4):
                def cmask_of(nt, i=i):
                    dlt = 512 * i - 2048 * nt
                    if dlt >= 2063:
                        return None
                    return (cm, cm[:, dlt // 512, :])
                o3, rs_ = attend(kcT_sb, lambda nt: kcT_sb[:, nt * 128:(nt + 1) * 128], qc[:, r, :], qc,
                                 vc_sb, lambda nt: vc_sb[:, nt, 0:65], list(range(n_ct)), cmask_of,
                                 keep=pcs)
                cres.append(None)
                hrow = (g * 4 + r) * 3
                c.op("dve", nc.vector.tensor_tensor, outs=[on[0]], ins=[o3, ps_b], out=on[0][:], in0=o3[:],
                     in1=ps_b[0:64, :], op=ALU.mult)
                c.op("pe", nc.tensor.matmul, outs=[ps_g], ins=[SelG_sb, gc], out=ps_g[0:64, :],
                     lhsT=SelG_sb[:, hrow * 64:(hrow + 1) * 64], rhs=gc[:], start=True, stop=True)
                st = stash[r]
                c.op("dve", nc.vector.tensor_tensor, outs=[st], ins=[on[0], ps_g], out=st[:], in0=on[0][:],
                     in1=ps_g[0:64, :], op=ALU.mult)
                for j in range(4):
                    for nt in range(n_ct):
                        c.op("pe", nc.tensor.matmul, outs=[ps_imp], ins=[pcs[nt], ovl_sb],
                             out=ps_imp[:, j * 128:(j + 1) * 128], lhsT=pcs[nt][:, j * 128:(j + 1) * 128],
                             rhs=ovl_sb[:, nt, :], start=(nt == 0), stop=(nt == n_ct - 1))
                for j in range(4):
                    c.op("pe", nc.tensor.matmul, outs=[ps_t], ins=[rs_, ones32], out=ps_t[:, j:j + 1],
                         lhsT=rs_[64:65, j * 128:(j + 1) * 128], rhs=ones32[64:65, 0:1], start=True, stop=True)
                c.op("act", nc.scalar.copy, outs=[rsT_sb], ins=[ps_t], out=rsT_sb[:], in_=ps_t[:, 0:4])
                for j in range(4):
                    if r == 0:
                        c.op("dve", nc.vector.tensor_scalar, outs=[impacc], ins=[ps_imp, rsT_sb],
                             out=impacc[:, j, :], in0=ps_imp[:, j * 128:(j + 1) * 128], scalar1=rsT_sb[:, j:j + 1],
                             scalar2=None, op0=ALU.mult)
                    else:
                        c.op("dve", nc.vector.scalar_tensor_tensor, outs=[impacc], ins=[ps_imp, rsT_sb, impacc],
                             out=impacc[:, j, :], in0=ps_imp[:, j * 128:(j + 1) * 128], scalar=rsT_sb[:, j:j + 1],
                             in1=impacc[:, j, :], op0=ALU.mult, op1=ALU.add)
            for j in range(4):
                T2 = 2 * (4 * i + j)
                n_ = nm[j % 2]
                c.op("dve", nc.vector.scalar_tensor_tensor, outs=[f1], ins=[JC_sb, CB_sb], out=f1[:], in0=JC_sb[:],
                     scalar=float(T2 - 1), in1=CB_sb[:], op0=ALU.is_ge, op1=ALU.mult)
                c.op("pool", nc.gpsimd.tensor_scalar, outs=[pen], ins=[JC_sb], out=pen[:], in0=JC_sb[:],
                     scalar1=float(T2), scalar2=-3e30, op0=ALU.is_gt, op1=ALU.mult)
                c.op("dve", nc.vector.tensor_tensor, outs=[imp3], ins=[impacc, f1], out=imp3[:], in0=impacc[:, j, :],
                     in1=f1[:], op=ALU.add)
                c.op("dve", nc.vector.memset, outs=[imp3], ap=imp3[:, 0:1], constant=2e9)
                c.op("dve", nc.vector.tensor_tensor, outs=[imp3], ins=[imp3, pen], out=imp3[:], in0=imp3[:],
                     in1=pen[:], op=ALU.add)
                c.op("dve", nc.vector.max, outs=[m8a], ins=[imp3], out=m8a[:], in_=imp3[:])
                c.op("dve", nc.vector.match_replace, outs=[imp4], ins=[m8a, imp3], out=imp4[:],
                     in_to_replace=m8a[:], in_values=imp3[:], imm_value=-2e30)
                c.op("dve", nc.vector.max, outs=[m8b], ins=[imp4], out=m8b[:], in_=imp4[:])
                c.op("dve", nc.vector.tensor_scalar, outs=[n_], ins=[imp3, m8b], out=n_[:], in0=imp3[:],
                     scalar1=m8b[:, 7:8], scalar2=-BIG, op0=ALU.is_lt, op1=ALU.mult)
                c.op("pe", nc.tensor.transpose, outs=[ps_t], ins=[n_, id_sb], out=ps_t[:, j * 128:(j + 1) * 128],
                     in_=n_[:], identity=id_sb[:])
            c.op("act", nc.scalar.copy, outs=[nmT], ins=[ps_t], out=nmT[:], in_=ps_t[:])
            for r in range(4):
                hrow = (g * 4 + r) * 3

                def sel_extra(ps, kt):
                    c.op("pe", nc.tensor.matmul, outs=[ps], ins=[E_sb, nmT], out=ps[:],
                         lhsT=E_sb[:, kt * 128:(kt + 1) * 128], rhs=nmT[:], start=False, stop=True)
                o3, _ = attend(ks_sb, lambda kt: ks_sb[:, kt * 128:(kt + 1) * 128], qc[:, r, :], qc,
                               vs_sb, lambda kt: vs_sb[:, kt, 0:65], list(range(4 * i + 4)),
                               lambda kt, i=i: ((mC, mC[:, kt - 4 * i, :]) if kt >= 4 * i else None),
                               extra_mm=sel_extra)
                c.op("dve", nc.vector.tensor_tensor, outs=[on[0]], ins=[o3, ps_b], out=on[0][:], in0=o3[:],
                     in1=ps_b[0:64, :], op=ALU.mult)
                c.op("pe", nc.tensor.matmul, outs=[ps_g], ins=[SelG_sb, gc], out=ps_g[0:64, :],
                     lhsT=SelG_sb[:, (hrow + 1) * 64:(hrow + 2) * 64], rhs=gc[:], start=True, stop=True)
                c.op("dve", nc.vector.tensor_tensor, outs=[on[0]], ins=[on[0], ps_g], out=on[0][:], in0=on[0][:],
                     in1=ps_g[0:64, :], op=ALU.mult)
                c.op("pool", nc.gpsimd.tensor_tensor, outs=[acc], ins=[on[0], stash[r]], out=acc[:], in0=on[0][:],
                     in1=stash[r][:], op=ALU.add)
                wl = [kt for kt in range(4 * i - 4, 4 * i + 4) if kt >= 0]
                o3, _ = attend(kw_sb, lambda kt: kw_sb[:, kt * 128:(kt + 1) * 128], qc[:, r, :], qc,
                               vw_sb, lambda kt: vw_sb[:, kt, 0:65], wl,
                               lambda kt, i=i: ((mC, mC[:, kt - 4 * i, :]) if kt >= 4 * i
                                                else (mL, mL[:, kt - 4 * i + 4, :])))
                c.op("dve", nc.vector.tensor_tensor, outs=[on[1]], ins=[o3, ps_b], out=on[1][:], in0=o3[:],
                     in1=ps_b[0:64, :], op=ALU.mult)
                c.op("pe", nc.tensor.matmul, outs=[ps_g], ins=[SelG_sb, gc], out=ps_g[0:64, :],
                     lhsT=SelG_sb[:, (hrow + 2) * 64:(hrow + 3) * 64], rhs=gc[:], start=True, stop=True)
                c.op("dve", nc.vector.tensor_tensor, outs=[on[1]], ins=[on[1], ps_g], out=on[1][:], in0=on[1][:],
                     in1=ps_g[0:64, :], op=ALU.mult)
                o_b = ob[oi % 2]; oi += 1
                c.op("pool", nc.gpsimd.tensor_tensor, outs=[o_b], ins=[acc, on[1]], out=o_b[:], in0=acc[:],
                     in1=on[1][:], op=ALU.add)
                row = (g * 4 + r) * 64
                c.dma("pool", OT[row:row + 64, i * 512:(i + 1) * 512], o_b[:], in_t=o_b)
    c.finish("pool")
    c.close()
    return nc


def nsa_consts():
    cs = {}
    n = np.arange(512)
    j = np.arange(128)
    ov = ((16 * n[:, None] < 64 * j[None, :] + 64) & (16 * n[:, None] + 32 > 64 * j[None, :]) & (n[:, None] < 511))
    cs["ovl"] = np.ascontiguousarray(ov.reshape(4, 128, 128).transpose(1, 0, 2)).astype(NPBF)
    cs["Eind"] = (np.arange(S)[None, :] // 64 == np.arange(128)[:, None]).astype(NPBF)
    p = np.arange(128)
    cs["JC"] = (j[None, :] - (p[:, None] // 64)).astype(np.float32)
    cs["CB"] = np.broadcast_to((1e9 + 1e6 * j)[None, :], (128, 128)).astype(np.float32).copy()
    cs["ident"] = np.eye(128, dtype=np.float32)
    sel = np.zeros((32, 24, 64), np.float32)
    for m in range(24):
        sel[m, m, :] = 1.0
    cs["SelG"] = sel.reshape(32, 24 * 64)
    np_ = np.arange(128)[:, None, None]
    m = np.arange(5)[None, :, None]
    t = np.arange(512)[None, None, :]
    cs["cmask"] = ((16 * np_ + 31 - 512 * m) <= t).astype(NPBF)
    r = np.arange(4)[None, :, None]
    cs["maskL"] = ((128 * r + np_) > t).astype(NPBF)
    return cs


def layer3(hT_list, p, cs):
    ncP = get_nc("P_nsa", build_P_nsa)
    cosF, sinF = rope_tables_fm(64, 128)
    pres = launch(ncP, [{"hT": hT_list[c], "w": p["l3_nsa_w_in"],
                         "cosT": np.ascontiguousarray(cosF[:, (c % 2) * TOK:(c % 2 + 1) * TOK]),
                         "sinT": np.ascontiguousarray(sinF[:, (c % 2) * TOK:(c % 2 + 1) * TOK])}
                        for c in range(NCORES)])
    g = {k: gather_heads(pres, k, True) for k in ("QT", "KcT", "VcT", "KsT", "KwT")}
    g["GT"] = []
    for c in range(NCORES):
        b, hh = c // 2, c % 2
        full = np.concatenate([pres[2 * b]["GT"], pres[2 * b + 1]["GT"]], axis=1)
        pad = np.zeros((32, S), np.float32)
        pad[0:24] = full[16 + hh * 24:16 + hh * 24 + 24]
        g["GT"].append(pad)
    g["Vs"] = gather_heads(pres, "Vs", False)
    g["Vw"] = gather_heads(pres, "Vw", False)
    nsc = nsa_consts()
    posT = np.ascontiguousarray(np.stack([p["l3_nsa_cmp_pos_k"].T, p["l3_nsa_cmp_pos_v"].T], axis=1)).astype(np.float32)
    w2 = np.stack([p["l3_nsa_cmp_w2_k"].reshape(2, 128, 64).transpose(1, 0, 2),
                   p["l3_nsa_cmp_w2_v"].reshape(2, 128, 64).transpose(1, 0, 2)], axis=1)
    w2 = np.ascontiguousarray(w2).astype(np.float32)
    ncA = get_nc("A_nsa", build_A_nsa)
    maps = []
    for c in range(NCORES):
        m = {k: g[k][c] for k in g}
        m.update(posT=posT, w1k=p["l3_nsa_cmp_w1_k"], w1v=p["l3_nsa_cmp_w1_v"], w2=w2, maskC=cs["maskC"])
        m.update(nsc)
        maps.append(m)
    ares = launch(ncA, maps)
    OT = scatter_OT(ares)
    return run_M(hT_list, OT, p["l3_nsa_w_o"], p["l3_mlp_w1"], p["l3_mlp_w2"], p["l3_ln1_g"], p["l3_ln1_b"],
                 p["l3_ln2_g"], p["l3_ln2_b"]), dict(pres=pres, ares=ares, OT=OT)


def kernel_unfused(**inputs):
    p = {k: np.asarray(v) for k, v in inputs.items()}
    cs = consts()
    hT = to_fm(p["x"].astype(np.float32))
    hT, _ = layer0(hT, p, cs)
    hT, _ = layer1(hT, p, cs)
    hT, _ = layer2(hT, p, cs)
    hT, _ = layer3(hT, p, cs)
    out = np.concatenate([h.T for h in hT], axis=0).reshape(B, S, D)
    return np.ascontiguousarray(out).astype(np.float32)


W_SHAPES = {
    "l0_sb_w_qkv": [D, 3 * D], "l0_sb_w_o": [D, D], "l1_moba_w_qkv": [D, 3 * D], "l1_moba_w_o": [D, D],
    "l2_mla_w_in": [D, 416], "l2_mla_w_uq": [256, 1536], "l2_mla_w_ukv": [128, 2048], "l2_mla_w_o": [D, D],
    "l3_nsa_w_in": [D, NSA_IN], "l3_nsa_cmp_w1_k": [2048, 256], "l3_nsa_cmp_w1_v": [2048, 256], "l3_nsa_w_o": [D, D],
}
for _l in range(4):
    W_SHAPES[f"l{_l}_mlp_w1"] = [D, DFF]
    W_SHAPES[f"l{_l}_mlp_w2"] = [DFF, D]


CC_MAX_BYTES = 2 * 1024 * 1024


class Exchange:
    def __init__(self, c, nc, par_sp):
        self.c = c
        self.nc = nc
        self.xsem = nc.alloc_semaphore(name="xsem")
        self.cnt = 0
        self.q = 0
        self.pars = {"sp": par_sp,
                     "pool": nc.gpsimd.snap(nc.gpsimd.partition_id() % 2, min_val=0, max_val=1)}

    @staticmethod
    def alloc(I, name, rows, cols, dt, kind, rc_big=None):
        es = 4 if dt == F32 else 2
        if rows * cols * es <= CC_MAX_BYTES:
            rc = rows
        else:
            rc = rc_big or {"fm": 256, "tm": 1024, "ot": 128}[kind]
        F = Fuse.active
        if kind == "fm":
            mine = I(name + "_m", [rows, cols], dt)
            lay = rc if rc < rows else rows // 2
        elif kind == "fm_all":
            mine = None
            lay = rows
        elif kind == "tm":
            mine = I(name + "_m", [2 * rows, cols // 2], dt)
            lay = rc
        elif kind == "ot":
            mine = I(name + "_m", [2 * rows, cols // 2], dt)
            lay = rc
        elif kind == "gt":
            mine = I(name + "_m", [48, cols], dt)
            lay = 24
        g = I(name + "_g", [2 * rows, cols], dt)
        if F is not None:
            F.lay[(mine if mine is not None else g).tensor.name] = lay
        return I(name + "_s", [rows, cols], dt), g, (mine if mine is not None else g), kind, rc

    def run(self, items):
        c = self.c
        for s_, g_, m_, kind, rc in items:
            rows = s_.shape[0]
            for j in range(rows // rc):
                c.allgather(s_[j * rc:(j + 1) * rc, :], g_[j * 2 * rc:(j + 1) * 2 * rc, :])
        for ek in ("sp", "pool"):
            c.eng[ek].wait_ge(c.ccsem, c.cccnt)
        for s_, g_, m_, kind, rc in items:
            if kind == "fm_all":
                continue
            ek = ("sp", "pool")[self.q % 2]
            self.q += 1
            par = self.pars[ek]
            rows = s_.shape[0]
            nch = rows // rc
            if kind == "fm" and nch == 1:
                src = g_.rearrange("(r h f) t -> r h f t", r=2, h=2)[:, bass.ds(par, 1), :, :] \
                    .rearrange("r 1 f t -> r f t")
                dst = m_.rearrange("(r f) t -> r f t", r=2)
            elif kind == "fm":
                src = g_.rearrange("(h x) t -> h x t", h=2)[bass.ds(par, 1), :, :].rearrange("1 x t -> x t")
                dst = m_
            elif kind == "tm":
                src = g_.rearrange("x (h n) -> x h n", h=2)[:, bass.ds(par, 1), :].rearrange("x 1 n -> x n")
                dst = m_
            elif kind == "ot":
                src = g_.rearrange("x (rr t) -> x rr t", rr=2)[:, bass.ds(par, 1), :].rearrange("x 1 t -> x t")
                dst = m_
            elif kind == "gt":
                src = g_.rearrange("(r f) t -> f r t", r=2)[16:64].rearrange("(h f) r t -> f h r t", h=2)[
                    :, bass.ds(par, 1), :, :].rearrange("f 1 r t -> f r t")
                dst = m_.rearrange("(r f) t -> f r t", r=2)
            c.eng[ek].dma_start(out=dst, in_=src).then_inc(self.xsem, 16)
            self.cnt += 16
        for ek in c.eng:
            c.eng[ek].wait_ge(self.xsem, self.cnt)


def build_fused():
    F = Fuse()
    Fuse.active = F
    try:
        nc = F.nc
        F.par = nc.sync.snap(nc.sync.partition_id() % 2, min_val=0, max_val=1)
        E = F.ext_in
        I = F.internal

        def W(name):
            return E(name, W_SHAPES[name], F32)

        X = Exchange(F.c if F.c is not None else make_ctx(nc), nc, F.par)
        AG = X.run

        def gath(name, rows, cols, dt, kind):
            return X.alloc(I, name, rows, cols, dt, kind)

        def M_phase(l, OTg, h_in, h_out, wo):
            F.io = {"OT": OTg, "hT": h_in, "w_o": W(wo), "w1": W(f"l{l}_mlp_w1"), "w2": W(f"l{l}_mlp_w2"),
                    "lnp": E(f"lnp{l}", [128, 4, 8], F32), "hO": h_out}
            build_M()

        maskS = E("maskS", [128, 4, 512], BF16)
        maskC = E("maskC", [128, 4, 512], BF16)
        tri = E("tri", [128, 128], BF16)
        ident = E("ident", [128, 128], F32)
        cosT = E("cosT", [128, TOK], F32)
        sinT = E("sinT", [128, TOK], F32)
        h0 = E("hT0", [D, TOK], F32)
        h = [h0] + [I(f"h{l}", [D, TOK], F32) for l in (1, 2, 3)]
        out = nc.dram_tensor("out", [D, TOK], F32, kind="ExternalOutput").ap()
        h.append(out)

        q = gath("l0QT", D, TOK, BF16, "fm"); k = gath("l0KT", D, TOK, BF16, "fm"); v = gath("l0V", TOK, D, BF16, "tm")
        F.io = {"hT": h[0], "w": W("l0_sb_w_qkv"), "QT": q[0], "KT": k[0], "V": v[0]}
        build_P_qkv(False, -0.125)
        AG([q, k, v])
        o = gath("l0OT", 512, S, BF16, "ot")
        F.io = {"QT": q[2], "KT": k[2], "V": v[2], "maskS": maskS, "tri": tri, "OT": o[0]}
        build_A_sb()
        AG([o])
        M_phase(0, o[2], h[0], h[1], "l0_sb_w_o")
        q = gath("l1QT", D, TOK, BF16, "fm"); k = gath("l1KT", D, TOK, BF16, "fm"); v = gath("l1V", TOK, D, BF16, "tm")
        F.io = {"hT": h[1], "w": W("l1_moba_w_qkv"), "cosT": cosT, "sinT": sinT, "QT": q[0], "KT": k[0], "V": v[0]}
        build_P_qkv(True, 0.125)
        AG([q, k, v])
        o = gath("l1OT", 512, S, BF16, "ot")
        F.io = {"QT": q[2], "KT": k[2], "V": v[2], "maskC": maskC, "Eind": E("Eind32", [32, S], BF16), "ident": ident,
                "OT": o[0]}
        build_A_soft("moba")
        AG([o])
        M_phase(1, o[2], h[1], h[2], "l1_moba_w_o")
        q = X.alloc(I, "l2QT", 1536, TOK, BF16, "fm", rc_big=192); k = gath("l2KN", D, TOK, BF16, "fm")
        kr = gath("l2KR", 32, TOK, BF16, "fm_all"); v = gath("l2V", TOK, D, BF16, "tm")
        F.io = {"hT": h[2], "w_in": W("l2_mla_w_in"), "w_uq": W("l2_mla_w_uq"), "w_ukv": W("l2_mla_w_ukv"),
                "gq": E("gq", [128, 2], F32), "gkv": E("gkv", [128, 1], F32), "cos96": E("cos96", [96, TOK], F32),
                "sin96": E("sin96", [96, TOK], F32), "QT": q[0], "KNT": k[0], "KRT": kr[0], "V": v[0]}
        build_P_mla()
        AG([q, k, kr, v])
        o = gath("l2OT", 512, S, BF16, "ot")
        F.io = {"QT": q[2], "KNT": k[2], "KRT": kr[2], "V": v[2], "maskC": maskC, "OT": o[0]}
        build_A_soft("mla")
        AG([o])
        M_phase(2, o[2], h[2], h[3], "l2_mla_w_o")
        names = [("QT", D, TOK, BF16, "fm"), ("KcT", 256, TOK, BF16, "fm"), ("VcT", 256, TOK, BF16, "fm"),
                 ("KsT", 256, TOK, BF16, "fm"), ("KwT", 256, TOK, BF16, "fm"), ("Vs", TOK, 256, BF16, "tm"),
                 ("Vw", TOK, 256, BF16, "tm"), ("GT", 64, TOK, F32, "gt")]
        sg = {n: gath("l3" + n, r, cc, dt, kd) for n, r, cc, dt, kd in names}
        F.io = {"hT": h[3], "w": W("l3_nsa_w_in"), "cosT": cosT, "sinT": sinT}
        F.io.update({n: sg[n][0] for n in sg})
        build_P_nsa()
        AG([sg[n] for n in sg])
        o = gath("l3OT", 512, S, BF16, "ot")
        F.io = {n: sg[n][2] for n in sg}
        F.io.update({"posT": E("posT", [64, 2, 32], F32), "w1k": W("l3_nsa_cmp_w1_k"), "w1v": W("l3_nsa_cmp_w1_v"),
                     "w2": E("w2nsa", [128, 2, 2, 64], F32), "maskC": maskC, "maskL": E("maskL", [128, 4, 512], BF16),
                     "cmask": E("cmask", [128, 5, 512], BF16), "ovl": E("ovl", [128, 4, 128], BF16),
                     "Eind": E("Eind128", [128, S], BF16), "JC": E("JC", [128, 128], F32),
                     "CB": E("CB", [128, 128], F32), "ident": ident, "SelG": E("SelG", [32, 24 * 64], F32),
                     "OT": o[0]})
        build_A_nsa()
        AG([o])
        M_phase(3, o[2], h[3], h[4], "l3_nsa_w_o")
        F.c.final_finish("pool")
        F.c.final_finish("sp")
    finally:
        Fuse.active = None
    return nc


def fused_inputs(p):
    cs = consts()
    nsc = nsa_consts()
    hT = to_fm(p["x"].astype(np.float32))
    cosF, sinF = rope_tables_fm(64, 128)
    cos32, sin32 = rope_tables_fm(32, 32)
    cos96 = np.concatenate([np.ones((64, S), np.float32), cos32], axis=0)
    sin96 = np.concatenate([np.zeros((64, S), np.float32), sin32], axis=0)
    common = {k: np.ascontiguousarray(p[k]).astype(np.float32) for k in W_SHAPES}
    for l in range(4):
        common[f"lnp{l}"] = lnp_pack(p[f"l{l}_ln1_g"], p[f"l{l}_ln1_b"], p[f"l{l}_ln2_g"], p[f"l{l}_ln2_b"])
    common.update(maskS=cs["maskS"], maskC=cs["maskC"], tri=cs["tri"], ident=np.eye(128, dtype=np.float32))
    common["Eind32"] = (np.arange(S)[None, :] // 256 == np.arange(32)[:, None]).astype(NPBF)
    common["gq"] = np.ascontiguousarray(p["l2_mla_q_norm"].reshape(2, 128).T).astype(np.float32)
    common["gkv"] = np.ascontiguousarray(p["l2_mla_kv_norm"].reshape(1, 128).T).astype(np.float32)
    common["posT"] = np.ascontiguousarray(
        np.stack([p["l3_nsa_cmp_pos_k"].T, p["l3_nsa_cmp_pos_v"].T], axis=1)).astype(np.float32)
    common["w2nsa"] = np.ascontiguousarray(
        np.stack([p["l3_nsa_cmp_w2_k"].reshape(2, 128, 64).transpose(1, 0, 2),
                  p["l3_nsa_cmp_w2_v"].reshape(2, 128, 64).transpose(1, 0, 2)], axis=1)).astype(np.float32)
    common.update(maskL=nsc["maskL"], cmask=nsc["cmask"], ovl=nsc["ovl"], Eind128=nsc["Eind"], JC=nsc["JC"],
                  CB=nsc["CB"], SelG=nsc["SelG"])
    maps = []
    for c in range(NCORES):
        m = dict(common)
        sl = slice((c % 2) * TOK, (c % 2 + 1) * TOK)
        m["hT0"] = hT[c]
        m["cosT"] = np.ascontiguousarray(cosF[:, sl]); m["sinT"] = np.ascontiguousarray(sinF[:, sl])
        m["cos96"] = np.ascontiguousarray(cos96[:, sl]); m["sin96"] = np.ascontiguousarray(sin96[:, sl])
        maps.append(m)
    return maps


def kernel(**inputs):
    p = {k: np.asarray(v) for k, v in inputs.items()}
    nc = get_nc("fused", build_fused)
    res = launch(nc, fused_inputs(p))
    out = np.concatenate([r["out"].T for r in res], axis=0).reshape(B, S, D)
    return np.ascontiguousarray(out).astype(np.float32)
```

```python
import numpy as np
import ml_dtypes
import concourse.bass as bass
import concourse.mybir as mybir
from concourse.bass_utils import run_bass_kernel_spmd

F32 = mybir.dt.float32
BF16 = mybir.dt.bfloat16
AF = mybir.ActivationFunctionType
ALU = mybir.AluOpType
AX = mybir.AxisListType
NPBF = ml_dtypes.bfloat16

D = 1024
B = 4
S = 8192
DFF = 4096
NCORES = 8
TOK = 4096
ALPHA = float((2.0 * 4) ** 0.25)
LN_EPS = 1e-5
RMS_EPS = 1e-6
BIG = 30000.0

SAME_ENGINE_SYNC = False


class T:
    __slots__ = ("ap", "name", "w", "r", "dsem", "dcnt")

    def __init__(self, ap, name):
        self.ap = ap
        self.name = name
        self.w = None
        self.r = []
        self.dsem = None
        self.dcnt = 0

    def __getitem__(self, idx):
        return self.ap[idx]


class Ctx:
    def __init__(self, nc):
        self.nc = nc
        self.eng = {"pe": nc.tensor, "act": nc.scalar, "dve": nc.vector,
                    "pool": nc.gpsimd, "sp": nc.sync}
        self.sem = {k: nc.alloc_semaphore(name=f"s_{k}") for k in self.eng}
        self.cnt = {k: 0 for k in self.eng}
        self.seen = {k: {} for k in self.eng}
        self.n_inst = 0
        self._stack = []
        self.out_events = []
        self.fused = False
        self.phase = 0
        self.dpool = []
        self.ptiles = []
        self.ccsem = None
        self.cccnt = 0

    def sb(self, name, shape, dt):
        cm = self.nc.sbuf_tensor(f"sb{self.phase}_" + name, list(shape), dt)
        t = cm.__enter__()
        self._stack.append(cm)
        return T(t[:], name)

    def ps(self, name, shape, dt=F32):
        cm = self.nc.psum_tensor(f"ps{self.phase}_" + name, list(shape), dt)
        t = cm.__enter__()
        self._stack.append(cm)
        return T(t[:], name)

    def view(self, ap, name):
        return T(ap, name)

    def _need(self, ek, deps, raw=()):
        best = {}
        for lst, is_raw in ((deps, False), (raw, True)):
            for d in lst:
                if d is None:
                    continue
                sem, val, sk = d
                if sk == ek and ek == "pe":
                    continue
                key = id(sem)
                if key not in best or best[key][1] < val:
                    best[key] = (sem, val)
        seen = self.seen[ek]
        e = self.eng[ek]
        for key, (sem, val) in best.items():
            if seen.get(key, 0) >= val:
                continue
            e.wait_ge(sem, val)
            seen[key] = val

    @staticmethod
    def _compact(r):
        best = {}
        for sem, val, sk in r:
            k = id(sem)
            if k not in best or best[k][1] < val:
                best[k] = (sem, val, sk)
        return list(best.values())

    def op(self, ek, fn, outs=(), ins=(), **kw):
        deps = []
        raw = [t.w for t in ins]
        for t in outs:
            deps.append(t.w)
            deps.extend(t.r)
        self._need(ek, deps, raw)
        inst = fn(**kw)
        self.cnt[ek] += 1
        ev = (self.sem[ek], self.cnt[ek], ek)
        inst.then_inc(self.sem[ek], 1)
        self.n_inst += 1
        for t in ins:
            t.r.append(ev)
            if len(t.r) > 16:
                t.r = self._compact(t.r)
        for t in outs:
            t.w = ev
            t.r = []
        return ev

    def dma(self, ek, out, in_, out_t=None, in_t=None, **kw):
        st = out_t or in_t
        if st.dsem is None:
            if self.dpool:
                st.dsem, st.dcnt = self.dpool.pop()
            else:
                st.dsem = self.nc.alloc_semaphore(name=f"d{self.phase}_{st.name}")
            self.ptiles.append(st)
        deps = []
        if in_t is not None:
            deps.append(in_t.w)
        if out_t is not None:
            deps.append(out_t.w)
            deps.extend(out_t.r)
        self._need(ek, deps)
        inst = self.eng[ek].dma_start(out=out, in_=in_, **kw)
        st.dcnt += 16
        inst.then_inc(st.dsem, 16)
        ev = (st.dsem, st.dcnt, None)
        self.n_inst += 1
        if in_t is not None:
            in_t.r.append(ev)
            if out_t is None:
                self.out_events.append(ev)
        if out_t is not None:
            out_t.w = ev
            out_t.r = []
        return ev

    def barrier(self):
        targets = [(self.sem[k], self.cnt[k], k) for k in self.eng if self.cnt[k] > 0]
        targets += [(t.dsem, t.dcnt, None) for t in self.ptiles if t.dcnt > 0]
        targets += [(sem, cnt, None) for sem, cnt in self.dpool if cnt > 0]
        if self.ccsem is not None and self.cccnt > 0:
            targets.append((self.ccsem, self.cccnt, None))
        for ek in self.eng:
            seen = self.seen[ek]
            for sem, val, sk in targets:
                if sk == ek or seen.get(id(sem), 0) >= val:
                    continue
                self.eng[ek].wait_ge(sem, val)
                seen[id(sem)] = val

    def end_phase(self):
        self.barrier()
        for t in self.ptiles:
            self.dpool.append([t.dsem, t.dcnt])
            t.dsem = None
        self.ptiles = []
        while self._stack:
            self._stack.pop().__exit__(None, None, None)
        self.phase += 1

    def allgather(self, in_ap, out_ap):
        if self.ccsem is None:
            self.ccsem = self.nc.alloc_semaphore(name="ccsem")
        self.nc.gpsimd.collective_compute("AllGather", ALU.bypass,
                                          replica_groups=[[0, 1], [2, 3], [4, 5], [6, 7]],
                                          ins=[in_ap], outs=[out_ap]).then_inc(self.ccsem, 1)
        self.cccnt += 1

    def finish(self, ek="sp"):
        if self.fused:
            return
        best = {}
        for sem, val, _ in self.out_events:
            k = id(sem)
            if k not in best or best[k][1] < val:
                best[k] = (sem, val)
        for sem, val in best.values():
            self.eng[ek].wait_ge(sem, val)

    def close(self):
        if self.fused:
            self.end_phase()
            return
        while self._stack:
            self._stack.pop().__exit__(None, None, None)

    def final_finish(self, ek="sp"):
        best = {}
        for sem, val, _ in self.out_events:
            k = id(sem)
            if k not in best or best[k][1] < val:
                best[k] = (sem, val)
        for sem, val in best.values():
            self.eng[ek].wait_ge(sem, val)


class Fuse:
    active = None

    def __init__(self):
        self.nc = bass.Bass("TRN2", target_bir_lowering=False)
        self.c = None
        self.io = {}
        self.ext = {}
        self.par = None
        self.lay = {}

    def ext_in(self, name, shape, dt):
        if name not in self.ext:
            self.ext[name] = self.nc.dram_tensor(name, list(shape), dt, kind="ExternalInput").ap()
        return self.ext[name]

    def internal(self, name, shape, dt):
        return self.nc.dram_tensor(name, list(shape), dt, kind="Internal").ap()


def new_nc():
    if Fuse.active is not None:
        return Fuse.active.nc
    return bass.Bass("TRN2", target_bir_lowering=False)


def make_ctx(nc):
    F = Fuse.active
    if F is not None:
        if F.c is None:
            F.c = Ctx(nc)
            F.c.fused = True
        return F.c
    return Ctx(nc)


def din(nc, name, shape, dt):
    F = Fuse.active
    if F is not None:
        return F.io[name]
    return nc.dram_tensor(name, list(shape), dt, kind="ExternalInput").ap()


def dout(nc, name, shape, dt):
    F = Fuse.active
    if F is not None:
        return F.io[name]
    return nc.dram_tensor(name, list(shape), dt, kind="ExternalOutput").ap()


def _lay(ap):
    F = Fuse.active
    if F is None:
        return None
    return F.lay.get(ap.tensor.name)


def fm_rows(ap, r0, r1):
    rc = _lay(ap)
    if rc is None:
        return ap[r0:r1, :].rearrange("p (r t) -> p r t", r=2)
    jj = r0 // rc
    assert (r1 - 1) // rc == jj, (r0, r1, rc)
    v = ap[jj * 2 * rc:(jj + 1) * 2 * rc, :].rearrange("(r f) t -> f r t", r=2)
    return v[r0 - jj * rc:r1 - jj * rc, :, :]


def fm_shared(ap):
    if _lay(ap) is None:
        return ap.rearrange("p (r t) -> p r t", r=2)
    return ap.rearrange("(r f) t -> f r t", r=2)


def fm_chunk(ap, r0, r1, i):
    r, t0 = divmod(i * 512, TOK)
    return fm_rows(ap, r0, r1)[:, r, t0:t0 + 512]


def gt_chunk(ap, i):
    if _lay(ap) is None:
        return ap[0:24, i * 512:(i + 1) * 512]
    r, t0 = divmod(i * 512, TOK)
    return ap.rearrange("(r f) t -> f r t", r=2)[:, r, t0:t0 + 512]


def ot_pieces(ap, t0, n):
    rc = _lay(ap)
    if rc is None:
        return [(0, 8, ap.rearrange("(kc p) t -> p kc t", p=128)[:, :, t0:t0 + n])]
    v = ap.rearrange("(j r p) t -> p r j t", r=2, p=128)
    return [(r * 4, 4, v[:, r, :, t0:t0 + n]) for r in range(2)]


def tm_pieces(ap, c0, c1):
    rc = _lay(ap)
    if rc is None:
        return [(0, 64, ap.rearrange("(kt p) n -> p kt n", p=128)[:, :, c0:c1])]
    nj = TOK // rc
    k8 = rc // 128
    v = ap.rearrange("(j r k p) n -> p j r k n", r=2, k=k8, p=128)
    return [(r * (TOK // 128) + j * k8, k8, v[:, j, r, :, c0:c1]) for r in range(2) for j in range(nj)]


class Caster:
    def __init__(self, c, engines=("dve", "pool", "act")):
        self.c = c
        self.engines = engines
        self.i = 0

    def copy(self, out_t, out_ap, in_t, in_ap, eng=None):
        c = self.c
        ek = eng or self.engines[self.i % len(self.engines)]
        self.i += 1
        if ek == "act":
            return c.op("act", c.nc.scalar.copy, outs=[out_t], ins=[in_t], out=out_ap, in_=in_ap)
        e = c.nc.vector if ek == "dve" else c.nc.gpsimd
        return c.op(ek, e.tensor_copy, outs=[out_t], ins=[in_t], out=out_ap, in_=in_ap)


def load_weight_bf16(c, caster, w_ap, K, N, name, stages, queue="sp"):
    nk = K // 128
    big = c.sb(name, [128, nk, N], BF16)
    chunks = [c.view(big[:, kc, :], f"{name}_{kc}") for kc in range(nk)]
    CW = stages[0].ap.shape[-1]
    si = 0
    for kc in range(nk):
        for n0 in range(0, N, CW):
            n1 = min(N, n0 + CW)
            st = stages[si % len(stages)]
            si += 1
            c.dma(queue, st[:, 0:n1 - n0], w_ap[kc * 128:(kc + 1) * 128, n0:n1], out_t=st)
            caster.copy(chunks[kc], big[:, kc, n0:n1], st, st[:, 0:n1 - n0])
    return big, chunks


def build_P_qkv(rope, qscale):
    nc = new_nc()
    hT = din(nc, "hT", [D, TOK], F32)
    w = din(nc, "w", [D, 3 * D], F32)
    if rope:
        cosT = din(nc, "cosT", [128, TOK], F32)
        sinT = din(nc, "sinT", [128, TOK], F32)
    QT = dout(nc, "QT", [D, TOK], BF16)
    KT = dout(nc, "KT", [D, TOK], BF16)
    V = dout(nc, "V", [TOK, D], BF16)
    c = make_ctx(nc)
    cast = Caster(c)
    stages = [c.sb(f"stg{i}", [128, 1024], F32) for i in range(2)]
    wb, wch = load_weight_bf16(c, cast, w, D, 3 * D, "wb", stages)
    if rope:
        wr = c.sb("wrot", [128, 8, 2 * D], BF16)
        wrch = [c.view(wr[:, kc, :], f"wrot_{kc}") for kc in range(8)]
        for kc in range(8):
            src = wb[:, kc, 0:2 * D].rearrange("p (h two d) -> p h two d", two=2, d=32)
            dst = wr[:, kc, :].rearrange("p (h two d) -> p h two d", two=2, d=32)
            c.op("dve", nc.vector.tensor_scalar, outs=[wrch[kc]], ins=[wch[kc]],
                 out=dst[:, :, 0, :], in0=src[:, :, 1, :], scalar1=-1.0, scalar2=None, op0=ALU.mult)
            c.op("pool", nc.gpsimd.tensor_copy, outs=[wrch[kc]], ins=[wch[kc]],
                 out=dst[:, :, 1, :], in_=src[:, :, 0, :])
        cos_sb = c.sb("cos_sb", [128, TOK], F32)
        sin_sb = c.sb("sin_sb", [128, TOK], F32)
        c.dma("sp", cos_sb[:], cosT, out_t=cos_sb)
        c.dma("sp", sin_sb[:], sinT, out_t=sin_sb)
    NT = 512
    hts = [c.sb(f"ht{i}", [128, 8, NT], F32) for i in range(2)]
    hb = c.sb("hb", [128, 8, NT], BF16)
    qk_sb = [c.sb(f"qk{i}", [128, 8, NT], BF16) for i in range(2)]
    v_sb = c.sb("v_sb", [128, 4, D], BF16)
    t1 = [c.sb(f"t1_{i}", [128, NT], F32) for i in range(2)]
    t2 = [c.sb(f"t2_{i}", [128, NT], F32) for i in range(2)]
    banks = [c.ps(f"pb{i}", [128, 512], F32) for i in range(8)]
    bi = 0
    hT_v = hT.rearrange("(kc p) t -> p kc t", p=128)
    for tt in range(TOK // NT):
        ht = hts[tt % 2]
        c.dma("sp", ht[:], hT_v[:, :, tt * NT:(tt + 1) * NT], out_t=ht)
        for kc in range(8):
            cast.copy(hb, hb[:, kc, :], ht, ht[:, kc, :], eng=("dve", "pool")[kc % 2])
        for which in range(2):
            dst = qk_sb[which]
            for fc in range(8):
                col = which * D + fc * 128
                pb = banks[bi % 8]; bi += 1
                for kc in range(8):
                    c.op("pe", nc.tensor.matmul, outs=[pb], ins=[wch[kc], hb],
                         out=pb[:, 0:NT], lhsT=wb[:, kc, col:col + 128], rhs=hb[:, kc, :],
                         start=(kc == 0), stop=(kc == 7))
                sc = qscale if which == 0 else 1.0
                if not rope:
                    c.op("act", nc.scalar.mul, outs=[dst], ins=[pb], out=dst[:, fc, :], in_=pb[:, 0:NT], mul=sc)
                else:
                    pr = banks[bi % 8]; bi += 1
                    for kc in range(8):
                        c.op("pe", nc.tensor.matmul, outs=[pr], ins=[wrch[kc], hb],
                             out=pr[:, 0:NT], lhsT=wr[:, kc, col:col + 128], rhs=hb[:, kc, :],
                             start=(kc == 0), stop=(kc == 7))
                    a = t1[fc % 2]; b_ = t2[fc % 2]
                    c.op("dve", nc.vector.tensor_tensor, outs=[a], ins=[pb, cos_sb],
                         out=a[:], in0=pb[:, 0:NT], in1=cos_sb[:, tt * NT:(tt + 1) * NT], op=ALU.mult)
                    c.op("dve", nc.vector.tensor_tensor, outs=[b_], ins=[pr, sin_sb],
                         out=b_[:], in0=pr[:, 0:NT], in1=sin_sb[:, tt * NT:(tt + 1) * NT], op=ALU.mult)
                    c.op("pool", nc.gpsimd.tensor_tensor, outs=[a], ins=[a, b_], out=a[:], in0=a[:], in1=b_[:],
                         op=ALU.add)
                    c.op("act", nc.scalar.mul, outs=[dst], ins=[a], out=dst[:, fc, :], in_=a[:], mul=sc)
            out_d = (QT, KT)[which].rearrange("(fc p) t -> p fc t", p=128)
            c.dma("pool", out_d[:, :, tt * NT:(tt + 1) * NT], dst[:], in_t=dst)
        for j in range(4):
            for half in range(2):
                pb = banks[bi % 8]; bi += 1
                for kc in range(8):
                    c.op("pe", nc.tensor.matmul, outs=[pb], ins=[wch[kc], hb],
                         out=pb[:], lhsT=hb[:, kc, j * 128:(j + 1) * 128],
                         rhs=wb[:, kc, 2 * D + half * 512:2 * D + (half + 1) * 512],
                         start=(kc == 0), stop=(kc == 7))
                if half == 0:
                    c.op("act", nc.scalar.copy, outs=[v_sb], ins=[pb], out=v_sb[:, j, 0:512], in_=pb[:])
                else:
                    c.op("dve", nc.vector.tensor_copy, outs=[v_sb], ins=[pb], out=v_sb[:, j, 512:1024], in_=pb[:])
        V_v = V.rearrange("(j p) n -> p j n", p=128)
        c.dma("pool", V_v[:, tt * 4:(tt + 1) * 4, :], v_sb[:], in_t=v_sb)
    c.finish("pool")
    c.close()
    return nc


def build_A_sb():
    nc = new_nc()
    QT = din(nc, "QT", [512, S], BF16)
    KT = din(nc, "KT", [512, S], BF16)
    V = din(nc, "V", [S, 512], BF16)
    maskS = din(nc, "maskS", [128, 4, 512], BF16)
    tri = din(nc, "tri", [128, 128], BF16)
    OT = dout(nc, "OT", [512, S], BF16)
    c = make_ctx(nc)
    m_sb = c.sb("maskS", [128, 4, 512], BF16)
    tri_sb = c.sb("tri", [128, 128], BF16)
    ones_sb = c.sb("ones", [128, 128], BF16)
    c.dma("sp", m_sb[:], maskS, out_t=m_sb)
    c.dma("sp", tri_sb[:], tri, out_t=tri_sb)
    c.op("dve", nc.vector.memset, outs=[ones_sb], ap=ones_sb[:], constant=1.0)
    qts = [c.sb(f"qt{i}", [128, S], BF16) for i in range(2)]
    kts = [c.sb(f"kt{i}", [128, S], BF16) for i in range(2)]
    vs = [c.sb(f"v{i}", [128, 64, 128], BF16) for i in range(2)]
    NW = 4
    e_sb = [c.sb(f"e{i}", [128, 512], F32) for i in range(NW)]
    sp_sb = [c.sb(f"sp{i}", [128, 512], BF16) for i in range(NW)]
    w_sb = [c.sb(f"w{i}", [128, 512], BF16) for i in range(NW)]
    run = [c.sb(f"run{i}", [128, 512], BF16) for i in range(2)]
    o_sb = [c.sb(f"o{i}", [64, 512], BF16) for i in range(2)]
    pz = [c.ps(f"pz{i}", [128, 512], F32) for i in range(3)]
    px = [c.ps(f"px{i}", [128, 512], F32) for i in range(3)]
    po = [c.ps(f"po{i}", [128, 512], F32) for i in range(2)]
    V_v = V.rearrange("(kt p) n -> p kt n", p=128)
    it = 0
    ix = 0
    oi = 0
    for pair in range(4):
        qt = qts[pair % 2]; kt_ = kts[pair % 2]; v = vs[pair % 2]
        c.dma("sp", qt[:].rearrange("p (r t) -> p r t", r=2), fm_rows(QT, pair * 128, (pair + 1) * 128), out_t=qt)
        c.dma("sp", kt_[:].rearrange("p (r t) -> p r t", r=2), fm_rows(KT, pair * 128, (pair + 1) * 128), out_t=kt_)
        for k0_, nk_, src_ in tm_pieces(V, pair * 128, (pair + 1) * 128):
            c.dma("sp", v[:, k0_:k0_ + nk_, :], src_, out_t=v)
        for hs in range(2):
            r0 = hs * 64
            for i in range(S // 512):
                pout = po[oi % 2]
                osb = o_sb[oi % 2]
                oi += 1
                rn = run[i % 2]
                nkt = 4 * i + 4
                kts_ = list(range(nkt - 1, -1, -1))
                N = len(kts_)
                qap = qt[r0:r0 + 64, i * 512:(i + 1) * 512]
                stA = []
                stB = []
                for step in range(N + 2):
                    if step < N:
                        n = step; kt = kts_[n]; r = kt - 4 * i
                        z = pz[it % 3]; e = e_sb[it % NW]; sp = sp_sb[it % NW]
                        it += 1
                        kap = kt_[r0:r0 + 64, kt * 128:(kt + 1) * 128]
                        c.op("pe", nc.tensor.matmul, outs=[z], ins=[kt_, qt], out=z[:], lhsT=kap, rhs=qap,
                             start=True, stop=True)
                        c.op("act", nc.scalar.activation, outs=[e], ins=[z], out=e[:], in_=z[:], func=AF.Exp,
                             scale=-1.0)
                        c.op("act", nc.scalar.activation, outs=[sp], ins=[e], out=sp[:], in_=e[:], func=AF.Ln,
                             bias=1.0)
                        if r >= 0:
                            c.op("dve", nc.vector.tensor_tensor, outs=[sp], ins=[sp, m_sb], out=sp[:], in0=sp[:],
                                 in1=m_sb[:, r, :], op=ALU.mult)
                        stA.append((n, kt, sp, kap))
                    if 1 <= step <= N:
                        n, kt, sp, kap = stA.pop(0); r = kt - 4 * i
                        x = px[ix % 3]; wt = w_sb[ix % NW]
                        ix += 1
                        c.op("pe", nc.tensor.matmul, outs=[x], ins=[tri_sb, sp], out=x[:], lhsT=tri_sb[:], rhs=sp[:],
                             start=True, stop=False)
                        if n > 0:
                            c.op("pe", nc.tensor.matmul, outs=[x], ins=[ones_sb, rn], out=x[:], lhsT=ones_sb[:],
                                 rhs=rn[:], start=False, stop=False)
                        c.op("pe", nc.tensor.matmul, outs=[x], ins=[kt_, qt], out=x[:], lhsT=kap,
                             rhs=qap, start=False, stop=True)
                        c.op("act", nc.scalar.activation, outs=[wt], ins=[x], out=wt[:], in_=x[:], func=AF.Exp,
                             scale=-1.0)
                        if r >= 0:
                            c.op("dve", nc.vector.tensor_tensor, outs=[wt], ins=[wt, m_sb], out=wt[:], in0=wt[:],
                                 in1=m_sb[:, r, :], op=ALU.mult)
                        if kt > 0:
                            if n == 0:
                                c.op("pool", nc.gpsimd.tensor_copy, outs=[rn], ins=[sp], out=rn[:], in_=sp[:])
                            else:
                                c.op("pool", nc.gpsimd.tensor_tensor, outs=[rn], ins=[rn, sp], out=rn[:], in0=rn[:],
                                     in1=sp[:], op=ALU.add)
                        stB.append((n, kt, wt))
                    if step >= 2:
                        n, kt, wt = stB.pop(0)
                        c.op("pe", nc.tensor.matmul, outs=[pout], ins=[v, wt], out=pout[0:64, :],
                             lhsT=v[:, kt, r0:r0 + 64], rhs=wt[:], start=(n == 0), stop=(kt == 0))
                c.op("act", nc.scalar.copy, outs=[osb], ins=[pout], out=osb[:], in_=pout[0:64, :])
                row = pair * 128 + hs * 64
                c.dma("pool", OT[row:row + 64, i * 512:(i + 1) * 512], osb[:], in_t=osb)
    c.finish("pool")
    c.close()
    return nc


def ln_fm(c, nc, h_sb, hch, ones32, sq_sb, psS, psQ, g_col, b_col, small, NT, out_bf=None, out_bf_ch=None):
    mean = small["mean"]; rstd = small["rstd"]; msq = small["msq"]
    for fc in range(8):
        c.op("pe", nc.tensor.matmul, outs=[psS], ins=[ones32, hch[fc]], out=psS[:, 0:NT], lhsT=ones32[:],
             rhs=h_sb[:, fc, :], start=(fc == 0), stop=(fc == 7))
    for fc in range(8):
        sq = sq_sb[fc % 2]
        c.op("act", nc.scalar.activation, outs=[sq], ins=[hch[fc]], out=sq[:], in_=h_sb[:, fc, :], func=AF.Square)
        c.op("pe", nc.tensor.matmul, outs=[psQ], ins=[ones32, sq], out=psQ[:, 0:NT], lhsT=ones32[:],
             rhs=sq[:], start=(fc == 0), stop=(fc == 7))
    c.op("dve", nc.vector.tensor_scalar, outs=[mean], ins=[psS], out=mean[:], in0=psS[:, 0:NT],
         scalar1=1.0 / D, scalar2=None, op0=ALU.mult)
    c.op("dve", nc.vector.tensor_tensor, outs=[msq], ins=[mean], out=msq[:], in0=mean[:], in1=mean[:], op=ALU.mult)
    c.op("dve", nc.vector.scalar_tensor_tensor, outs=[rstd], ins=[psQ, msq], out=rstd[:], in0=psQ[:, 0:NT],
         scalar=1.0 / D, in1=msq[:], op0=ALU.mult, op1=ALU.subtract)
    c.op("act", nc.scalar.activation, outs=[rstd], ins=[rstd], out=rstd[:], in_=rstd[:], func=AF.Sqrt, bias=LN_EPS)
    c.op("dve", nc.vector.reciprocal, outs=[rstd], ins=[rstd], out=rstd[:], in_=rstd[:])
    for fc in range(8):
        c.op("dve", nc.vector.tensor_tensor, outs=[hch[fc]], ins=[hch[fc], mean], out=h_sb[:, fc, :],
             in0=h_sb[:, fc, :], in1=mean[:], op=ALU.subtract)
        c.op("pool", nc.gpsimd.tensor_tensor, outs=[hch[fc]], ins=[hch[fc], rstd], out=h_sb[:, fc, :],
             in0=h_sb[:, fc, :], in1=rstd[:], op=ALU.mult)
        c.op("act", nc.scalar.activation, outs=[hch[fc]], ins=[hch[fc], g_col, b_col], out=h_sb[:, fc, :],
             in_=h_sb[:, fc, :], func=AF.Identity, scale=g_col[:, fc:fc + 1], bias=b_col[:, fc:fc + 1])
        if out_bf is not None:
            c.op("dve", nc.vector.tensor_copy, outs=[out_bf_ch[fc]], ins=[hch[fc]], out=out_bf[:, fc, :],
                 in_=h_sb[:, fc, :])


def build_M():
    nc = new_nc()
    OT = din(nc, "OT", [D, TOK], BF16)
    hT = din(nc, "hT", [D, TOK], F32)
    w_o = din(nc, "w_o", [D, D], F32)
    w1 = din(nc, "w1", [D, DFF], F32)
    w2 = din(nc, "w2", [DFF, D], F32)
    lnp = din(nc, "lnp", [128, 4, 8], F32)
    hO = dout(nc, "hO", [D, TOK], F32)
    c = make_ctx(nc)
    cast = Caster(c)
    NT = 256
    stages = [c.sb(f"stg{i}", [128, 1024], F32) for i in range(2)]
    lnp_sb = c.sb("lnp", [128, 4, 8], F32)
    c.dma("sp", lnp_sb[:], lnp, out_t=lnp_sb)
    g1 = c.view(lnp_sb[:, 0, :], "g1"); b1 = c.view(lnp_sb[:, 1, :], "b1")
    g2 = c.view(lnp_sb[:, 2, :], "g2"); b2 = c.view(lnp_sb[:, 3, :], "b2")
    for v_ in (g1, b1, g2, b2):
        v_.w = lnp_sb.w
    ones32 = c.sb("ones32", [128, 128], F32)
    c.op("dve", nc.vector.memset, outs=[ones32], ap=ones32[:], constant=1.0)
    wo_b, wo_ch = load_weight_bf16(c, cast, w_o, D, D, "wo", stages)
    w1_b, w1_ch = load_weight_bf16(c, cast, w1, D, DFF, "w1", stages)
    w2_b, w2_ch = load_weight_bf16(c, cast, w2, DFF, D, "w2", stages)
    hs = [c.sb(f"h{i}", [128, 8, NT], F32) for i in range(2)]
    hchs = [[c.view(h[:, fc, :], f"{h.name}_{fc}") for fc in range(8)] for h in hs]
    ots = [c.sb(f"ot{i}", [128, 8, NT], BF16) for i in range(2)]
    hb = c.sb("hb", [128, 8, NT], BF16)
    hbch = [c.view(hb[:, fc, :], f"hb_{fc}") for fc in range(8)]
    aT = c.sb("aT", [128, 32, NT], BF16)
    aTch = [c.view(aT[:, f, :], f"aT_{f}") for f in range(32)]
    rl = [c.sb(f"rl{i}", [128, 2, NT], F32) for i in range(2)]
    sq_sb = [c.sb(f"sq{i}", [128, NT], F32) for i in range(2)]
    small = {k: c.sb(k, [128, NT], F32) for k in ("mean", "rstd", "msq")}
    banks = [c.ps(f"pb{i}", [128, 512], F32) for i in range(6)]
    psS = c.ps("psS", [128, 512], F32)
    psQ = c.ps("psQ", [128, 512], F32)
    bi = 0
    OT_v = OT.rearrange("(kc p) t -> p kc t", p=128)
    hT_v = hT.rearrange("(kc p) t -> p kc t", p=128)
    hO_v = hO.rearrange("(kc p) t -> p kc t", p=128)
    for tt in range(TOK // NT):
        h = hs[tt % 2]; hch = hchs[tt % 2]; ot = ots[tt % 2]
        sl = slice(tt * NT, (tt + 1) * NT)
        for k0_, nk_, src_ in ot_pieces(OT, tt * NT, NT):
            c.dma("sp", ot[:, k0_:k0_ + nk_, :], src_, out_t=ot)
        deps_ev = c.dma("sp", h[:], hT_v[:, :, sl], out_t=h)
        for v_ in hch:
            v_.w = deps_ev
            v_.r = []
        for fc in range(8):
            pb = banks[bi % 6]; bi += 1
            for kc in range(8):
                c.op("pe", nc.tensor.matmul, outs=[pb], ins=[wo_ch[kc], ot], out=pb[:, 0:NT],
                     lhsT=wo_b[:, kc, fc * 128:(fc + 1) * 128], rhs=ot[:, kc, :], start=(kc == 0), stop=(kc == 7))
            c.op("dve", nc.vector.scalar_tensor_tensor, outs=[hch[fc]], ins=[hch[fc], pb], out=h[:, fc, :],
                 in0=h[:, fc, :], scalar=ALPHA, in1=pb[:, 0:NT], op0=ALU.mult, op1=ALU.add)
        ln_fm(c, nc, h, hch, ones32, sq_sb, psS, psQ, g1, b1, small, NT, out_bf=hb, out_bf_ch=hbch)
        for f2 in range(16):
            pb = banks[bi % 6]; bi += 1
            for sub in range(2):
                f = f2 * 2 + sub
                for kc in range(8):
                    c.op("pe", nc.tensor.matmul, outs=[pb], ins=[w1_ch[kc], hbch[kc]],
                         out=pb[:, sub * NT:(sub + 1) * NT], lhsT=w1_b[:, kc, f * 128:(f + 1) * 128],
                         rhs=hb[:, kc, :], start=(kc == 0), stop=(kc == 7))
            r_ = rl[f2 % 2]
            c.op("act", nc.scalar.activation, outs=[r_], ins=[pb], out=r_[:].rearrange("p a n -> p (a n)"),
                 in_=pb[:, 0:2 * NT], func=AF.Relu)
            eng = ("dve", "pool")[f2 % 2]
            e_ = nc.vector if eng == "dve" else nc.gpsimd
            c.op(eng, e_.tensor_tensor, outs=[aTch[2 * f2], aTch[2 * f2 + 1]], ins=[r_],
                 out=aT[:, 2 * f2:2 * f2 + 2, :], in0=r_[:], in1=r_[:], op=ALU.mult)
        for fc in range(8):
            pb = banks[bi % 6]; bi += 1
            for f in range(32):
                c.op("pe", nc.tensor.matmul, outs=[pb], ins=[w2_ch[f], aTch[f]], out=pb[:, 0:NT],
                     lhsT=w2_b[:, f, fc * 128:(fc + 1) * 128], rhs=aT[:, f, :], start=(f == 0), stop=(f == 31))
            c.op("dve", nc.vector.scalar_tensor_tensor, outs=[hch[fc]], ins=[hch[fc], pb], out=h[:, fc, :],
                 in0=h[:, fc, :], scalar=ALPHA, in1=pb[:, 0:NT], op0=ALU.mult, op1=ALU.add)
        ln_fm(c, nc, h, hch, ones32, sq_sb, psS, psQ, g2, b2, small, NT)
        h.w = None
        h.r = []
        c._need("pool", [v_.w for v_ in hch])
        ev = c.dma("pool", hO_v[:, :, sl], h[:], in_t=h)
        for v_ in hch:
            v_.r.append(ev)
    c.finish("pool")
    c.close()
    return nc


_NC_CACHE = {}


def get_nc(key, fn, *a):
    if key not in _NC_CACHE:
        _NC_CACHE[key] = fn(*a)
    return _NC_CACHE[key]


def launch(nc, in_maps):
    res = run_bass_kernel_spmd(nc, in_maps, core_ids=list(range(NCORES)))
    return res.results


def consts():
    j = np.arange(128)[:, None, None]
    r = np.arange(4)[None, :, None]
    t = np.arange(512)[None, None, :]
    cs = {}
    cs["maskS"] = ((128 * r + j) < t).astype(NPBF)
    cs["maskC"] = ((128 * r + j) <= t).astype(NPBF)
    cs["tri"] = (np.arange(128)[:, None] >= np.arange(128)[None, :]).astype(NPBF)
    return cs


def to_fm(x):
    flat = np.ascontiguousarray(x).reshape(B * S, D)
    return [np.ascontiguousarray(flat[c * TOK:(c + 1) * TOK].T) for c in range(NCORES)]


def lnp_pack(g1, b1, g2, b2):
    return np.ascontiguousarray(np.stack([v.reshape(8, 128).T for v in (g1, b1, g2, b2)], axis=1)).astype(np.float32)


def gather_heads(per_core, key, feature_major):
    outs = []
    for c in range(NCORES):
        b, hh = c // 2, c % 2
        if feature_major:
            full = np.concatenate([per_core[2 * b][key], per_core[2 * b + 1][key]], axis=1)
            n = full.shape[0] // 2
            outs.append(np.ascontiguousarray(full[hh * n:(hh + 1) * n]))
        else:
            full = np.concatenate([per_core[2 * b][key], per_core[2 * b + 1][key]], axis=0)
            n = full.shape[1] // 2
            outs.append(np.ascontiguousarray(full[:, hh * n:(hh + 1) * n]))
    return outs


def scatter_OT(a_res):
    outs = []
    for c in range(NCORES):
        b, half = c // 2, c % 2
        full = np.concatenate([a_res[2 * b]["OT"], a_res[2 * b + 1]["OT"]], axis=0)
        outs.append(np.ascontiguousarray(full[:, half * TOK:(half + 1) * TOK]))
    return outs


def run_M(hT_list, OT_list, w_o, w1, w2, g1, b1, g2, b2):
    nc = get_nc("M", build_M)
    lnp = lnp_pack(g1, b1, g2, b2)
    res = launch(nc, [{"OT": OT_list[c], "hT": hT_list[c], "w_o": w_o, "w1": w1, "w2": w2, "lnp": lnp}
                      for c in range(NCORES)])
    return [r["hO"] for r in res]


def layer0(hT_list, p, cs):
    ncP = get_nc("P_sb", build_P_qkv, False, -0.125)
    pres = launch(ncP, [{"hT": hT_list[c], "w": p["l0_sb_w_qkv"]} for c in range(NCORES)])
    QT = gather_heads(pres, "QT", True)
    KT = gather_heads(pres, "KT", True)
    V = gather_heads(pres, "V", False)
    ncA = get_nc("A_sb", build_A_sb)
    ares = launch(ncA, [{"QT": QT[c], "KT": KT[c], "V": V[c], "maskS": cs["maskS"], "tri": cs["tri"]}
                        for c in range(NCORES)])
    OT = scatter_OT(ares)
    return run_M(hT_list, OT, p["l0_sb_w_o"], p["l0_mlp_w1"], p["l0_mlp_w2"], p["l0_ln1_g"], p["l0_ln1_b"],
                 p["l0_ln2_g"], p["l0_ln2_b"]), dict(pres=pres, ares=ares, OT=OT)


class SoftmaxRes:
    def __init__(self, c, nc, prefix=""):
        self.ps_s = [c.ps(f"{prefix}pss{i}", [128, 512], F32) for i in range(3)]
        self.ps_o = [c.ps(f"{prefix}pso{i}", [128, 512], F32) for i in range(2)]
        self.ps_b = c.ps(f"{prefix}psb", [128, 512], F32)
        self.p_sb = [c.sb(f"{prefix}p{i}", [128, 512], BF16) for i in range(5)]
        self.o32 = [c.sb(f"{prefix}o32_{i}", [64, 512], F32) for i in range(2)]
        self.rs = [c.sb(f"{prefix}rs{i}", [65, 512], F32) for i in range(2)]
        self.o_sb = [c.sb(f"{prefix}ob{i}", [64, 512], BF16) for i in range(2)]
        self.ones32 = c.sb(f"{prefix}ones32", [65, 64], F32)
        c.op("dve", nc.vector.memset, outs=[self.ones32], ap=self.ones32[:], constant=1.0)
        self.it = 0
        self.oi = 0


def softmax_chunk(c, nc, R, k_ts, q_ts, kt_sb, qt_sb, v_sb, KR, i, mask_t, kt_list, mask_of, extra_mm=None):
    po = R.ps_o[R.oi % 2]; o32 = R.o32[R.oi % 2]; rs = R.rs[R.oi % 2]; ob = R.o_sb[R.oi % 2]
    R.oi += 1
    qap = qt_sb[0:KR, i * 512:(i + 1) * 512]
    LAG = 2
    N = len(kt_list)
    pend = []
    for n in range(N + LAG):
        if n < N:
            kt = kt_list[n]
            ps = R.ps_s[R.it % 3]; p = R.p_sb[R.it % len(R.p_sb)]
            R.it += 1
            c.op("pe", nc.tensor.matmul, outs=[ps], ins=list(k_ts) + list(q_ts), out=ps[:],
                 lhsT=kt_sb[0:KR, kt * 128:(kt + 1) * 128], rhs=qap, start=True, stop=(extra_mm is None))
            if extra_mm is not None:
                extra_mm(ps, kt)
            c.op("act", nc.scalar.activation, outs=[p], ins=[ps], out=p[:], in_=ps[:], func=AF.Exp)
            m = mask_of(kt)
            if m is not None:
                c.op("dve", nc.vector.tensor_tensor, outs=[p], ins=[p, mask_t], out=p[:], in0=p[:], in1=m,
                     op=ALU.mult)
            pend.append((n, kt, p))
        if n >= LAG:
            m_, kt, p = pend.pop(0)
            c.op("pe", nc.tensor.matmul, outs=[po], ins=[v_sb, p], out=po[0:65, :], lhsT=v_sb[:, kt, 0:65],
                 rhs=p[:], start=(m_ == 0), stop=(m_ == N - 1))
    c.op("act", nc.scalar.copy, outs=[o32], ins=[po], out=o32[:], in_=po[0:64, :])
    c.op("dve", nc.vector.tensor_scalar, outs=[rs], ins=[po], out=rs[64:65, :], in0=po[64:65, :],
         scalar1=1e-30, scalar2=None, op0=ALU.max)
    c.op("dve", nc.vector.reciprocal, outs=[rs], ins=[rs], out=rs[64:65, :], in_=rs[64:65, :])
    c.op("pe", nc.tensor.matmul, outs=[R.ps_b], ins=[R.ones32, rs], out=R.ps_b[0:64, :],
         lhsT=R.ones32[64:65, 0:64], rhs=rs[64:65, :], start=True, stop=True)
    return o32, R.ps_b, ob


def build_A_soft(kind):
    nc = new_nc()
    KR = 96
    if kind == "moba":
        QT = din(nc, "QT", [512, S], BF16)
        KT = din(nc, "KT", [512, S], BF16)
        Eind = din(nc, "Eind", [32, S], BF16)
        ident = din(nc, "ident", [128, 128], F32)
    else:
        QT = din(nc, "QT", [8 * 96, S], BF16)
        KT = din(nc, "KNT", [512, S], BF16)
        KRT = din(nc, "KRT", [32, S], BF16)
    V = din(nc, "V", [S, 512], BF16)
    maskC = din(nc, "maskC", [128, 4, 512], BF16)
    OT = dout(nc, "OT", [512, S], BF16)
    c = make_ctx(nc)
    m_sb = c.sb("maskC", [128, 4, 512], BF16)
    c.dma("sp", m_sb[:], maskC, out_t=m_sb)
    R = SoftmaxRes(c, nc)
    qts = [c.sb(f"qt{i}", [KR, S], BF16) for i in range(2)]
    q_hi = [c.view(q[64:96, :], f"{q.name}_hi") for q in qts]
    kts = [c.sb(f"kt{i}", [KR, S], BF16) for i in range(2)]
    vs = [c.sb(f"v{i}", [128, 64, 65], BF16) for i in range(2)]
    for v in vs:
        c.op("pool", nc.gpsimd.memset, outs=[v], ap=v[:, :, 64:65], constant=1.0)
    for k in kts:
        if kind == "moba":
            c.dma("sp", k[64:96, :], Eind, out_t=k)
        else:
            c.dma("sp", k[64:96, :].rearrange("p (r t) -> p r t", r=2), fm_shared(KRT), out_t=k)
    if kind == "moba":
        id_sb = c.sb("ident", [128, 128], F32)
        c.dma("sp", id_sb[:], ident, out_t=id_sb)
        g_sb = c.sb("g_sb", [128, 32], F32)
        m8 = c.sb("m8", [128, 8], F32)
        nms = [c.sb(f"nm{i}", [128, 96], F32) for i in range(2)]
        for nm in nms:
            c.op("dve", nc.vector.memset, outs=[nm], ap=nm[:], constant=0.0)
        kms32 = c.sb("kms32", [64, 32], F32)
        kms = c.sb("kms", [64, 32], BF16)
        ps_g = c.ps("psg", [128, 512], F32)
        ps_t = c.ps("pst", [128, 512], F32)
    V_v = V.rearrange("(kt p) n -> p kt n", p=128)
    for h in range(8):
        qt = qts[h % 2]; kt_ = kts[h % 2]; v = vs[h % 2]; qhi = q_hi[h % 2]
        c.dma("sp", kt_[0:64, :].rearrange("p (r t) -> p r t", r=2), fm_rows(KT, h * 64, (h + 1) * 64), out_t=kt_)
        if kind == "moba":
            c.dma("sp", qt[0:64, :].rearrange("p (r t) -> p r t", r=2), fm_rows(QT, h * 64, (h + 1) * 64), out_t=qt)
        else:
            c.dma("sp", qt[:].rearrange("p (r t) -> p r t", r=2), fm_rows(QT, h * 96, (h + 1) * 96), out_t=qt)
        for k0_, nk_, src_ in tm_pieces(V, h * 64, (h + 1) * 64):
            c.dma("sp", v[:, k0_:k0_ + nk_, 0:64], src_, out_t=v)
        q_ts = [qt]
        if kind == "moba":
            q_ts = [qt, qhi]
            c.op("dve", nc.vector.memset, outs=[g_sb], ap=g_sb[:], constant=-1e30)
            c.op("dve", nc.vector.tensor_reduce, outs=[kms32], ins=[kt_], out=kms32[:],
                 in_=kt_[0:64, :].rearrange("p (n k) -> p n k", k=256), axis=AX.X, op=ALU.add)
            c.op("dve", nc.vector.tensor_copy, outs=[kms], ins=[kms32], out=kms[:], in_=kms32[:])
            for qi in range(64):
                own = qi // 2
                nm = nms[qi % 2]
                grp = qi % 4
                if own > 3:
                    c.op("pe", nc.tensor.matmul, outs=[ps_g], ins=[qt, kms], out=ps_g[:, 0:32],
                         lhsT=qt[0:64, qi * 128:(qi + 1) * 128], rhs=kms[:], start=True, stop=True)
                    c.op("dve", nc.vector.tensor_copy, outs=[g_sb], ins=[ps_g], out=g_sb[:, 0:own],
                         in_=ps_g[:, 0:own])
                    c.op("dve", nc.vector.max, outs=[m8], ins=[g_sb], out=m8[:], in_=g_sb[:])
                    c.op("dve", nc.vector.tensor_scalar, outs=[nm], ins=[g_sb, m8], out=nm[:, 64:64 + own],
                         in0=g_sb[:, 0:own], scalar1=m8[:, 2:3], scalar2=-BIG, op0=ALU.is_lt, op1=ALU.mult)
                else:
                    if own > 0:
                        c.op("dve", nc.vector.memset, outs=[nm], ap=nm[:, 64:64 + own], constant=0.0)
                c.op("dve", nc.vector.memset, outs=[nm], ap=nm[:, 64 + own:65 + own], constant=0.0)
                if own < 31:
                    c.op("dve", nc.vector.memset, outs=[nm], ap=nm[:, 65 + own:96], constant=-BIG)
                c.op("pe", nc.tensor.transpose, outs=[ps_t], ins=[nm, id_sb],
                     out=ps_t[0:96, grp * 128:(grp + 1) * 128], in_=nm[:], identity=id_sb[:])
                if grp == 3:
                    ch = qi // 4
                    c.op("act", nc.scalar.copy, outs=[qhi], ins=[ps_t], out=qt[64:96, ch * 512:(ch + 1) * 512],
                         in_=ps_t[64:96, :])
        for i in range(S // 512):
            kl = list(range(0, 4 * i + 4))
            o32, pbc, ob = softmax_chunk(c, nc, R, [kt_], q_ts, kt_, qt, v, KR, i, m_sb, kl,
                                         lambda kt, i=i: (m_sb[:, kt - 4 * i, :] if kt >= 4 * i else None))
            c.op("dve", nc.vector.tensor_tensor, outs=[ob], ins=[o32, pbc], out=ob[:], in0=o32[:], in1=pbc[0:64, :],
                 op=ALU.mult)
            c.dma("pool", OT[h * 64:(h + 1) * 64, i * 512:(i + 1) * 512], ob[:], in_t=ob)
        if kind == "moba":
            qt.r.extend(qhi.r)
            qhi.r.extend(qt.r)
    c.finish("pool")
    c.close()
    return nc


def rope_tables_fm(dim, rows):
    inv = 1.0 / (10000.0 ** (np.arange(0, dim, 2, dtype=np.float32) / dim))
    ang = np.arange(S, dtype=np.float32)[:, None] * inv[None, :]
    ang = np.concatenate([ang, ang], axis=-1)
    cos = np.cos(ang).astype(np.float32).T
    sin = np.sin(ang).astype(np.float32).T
    reps = rows // dim
    return np.ascontiguousarray(np.tile(cos, (reps, 1))), np.ascontiguousarray(np.tile(sin, (reps, 1)))


def layer1(hT_list, p, cs):
    ncP = get_nc("P_moba", build_P_qkv, True, 0.125)
    cosF, sinF = rope_tables_fm(64, 128)
    pres = launch(ncP, [{"hT": hT_list[c], "w": p["l1_moba_w_qkv"],
                         "cosT": np.ascontiguousarray(cosF[:, (c % 2) * TOK:(c % 2 + 1) * TOK]),
                         "sinT": np.ascontiguousarray(sinF[:, (c % 2) * TOK:(c % 2 + 1) * TOK])}
                        for c in range(NCORES)])
    QT = gather_heads(pres, "QT", True)
    KT = gather_heads(pres, "KT", True)
    V = gather_heads(pres, "V", False)
    ncA = get_nc("A_moba", build_A_soft, "moba")
    Eind = (np.arange(S)[None, :] // 256 == np.arange(32)[:, None]).astype(NPBF)
    ident = np.eye(128, dtype=np.float32)
    ares = launch(ncA, [{"QT": QT[c], "KT": KT[c], "V": V[c], "maskC": cs["maskC"], "Eind": Eind, "ident": ident}
                        for c in range(NCORES)])
    OT = scatter_OT(ares)
    return run_M(hT_list, OT, p["l1_moba_w_o"], p["l1_mlp_w1"], p["l1_mlp_w2"], p["l1_ln1_g"], p["l1_ln1_b"],
                 p["l1_ln2_g"], p["l1_ln2_b"]), dict(pres=pres, ares=ares, OT=OT)


def build_P_mla():
    nc = new_nc()
    hT = din(nc, "hT", [D, TOK], F32)
    w_in = din(nc, "w_in", [D, 416], F32)
    w_uq = din(nc, "w_uq", [256, 1536], F32)
    w_ukv = din(nc, "w_ukv", [128, 2048], F32)
    gq = din(nc, "gq", [128, 2], F32)
    gkv = din(nc, "gkv", [128, 1], F32)
    cos96 = din(nc, "cos96", [96, TOK], F32)
    sin96 = din(nc, "sin96", [96, TOK], F32)
    QT = dout(nc, "QT", [1536, TOK], BF16)
    KNT = dout(nc, "KNT", [D, TOK], BF16)
    KRT = dout(nc, "KRT", [32, TOK], BF16)
    V = dout(nc, "V", [TOK, D], BF16)
    c = make_ctx(nc)
    cast = Caster(c)
    NT = 512
    QSC = float(96 ** -0.5)
    stages = [c.sb(f"stg{i}", [128, 1024], F32) for i in range(2)]
    win_b, win_ch = load_weight_bf16(c, cast, w_in, D, 416, "win", stages)
    wuq_b, wuq_ch = load_weight_bf16(c, cast, w_uq, 256, 1536, "wuq", stages)
    wukv_b, wukv_ch = load_weight_bf16(c, cast, w_ukv, 128, 2048, "wukv", stages)
    winr = c.sb("winr", [128, 8, 32], BF16)
    for kc in range(8):
        c.op("dve", nc.vector.tensor_scalar, outs=[winr], ins=[win_ch[kc]], out=winr[:, kc, 0:16],
             in0=win_b[:, kc, 400:416], scalar1=-1.0, scalar2=None, op0=ALU.mult)
        c.op("dve", nc.vector.tensor_copy, outs=[winr], ins=[win_ch[kc]], out=winr[:, kc, 16:32],
             in_=win_b[:, kc, 384:400])
    wuqr = c.sb("wuqr", [128, 2, 1536], BF16)
    c.op("pool", nc.gpsimd.memset, outs=[wuqr], ap=wuqr[:], constant=0.0)
    for fc in range(2):
        src = wuq_b[:, fc, :].rearrange("p (h d) -> p h d", d=96)
        dst = wuqr[:, fc, :].rearrange("p (h d) -> p h d", d=96)
        c.op("dve", nc.vector.tensor_scalar, outs=[wuqr], ins=[wuq_ch[fc]], out=dst[:, :, 64:80],
             in0=src[:, :, 80:96], scalar1=-1.0, scalar2=None, op0=ALU.mult)
        c.op("dve", nc.vector.tensor_copy, outs=[wuqr], ins=[wuq_ch[fc]], out=dst[:, :, 80:96],
             in_=src[:, :, 64:80])
    gq_sb = c.sb("gq", [128, 2], F32); gkv_sb = c.sb("gkv", [128, 1], F32)
    c.dma("sp", gq_sb[:], gq, out_t=gq_sb)
    c.dma("sp", gkv_sb[:], gkv, out_t=gkv_sb)
    cos_sb = c.sb("cos96", [96, TOK], F32); sin_sb = c.sb("sin96", [96, TOK], F32)
    c.dma("sp", cos_sb[:], cos96, out_t=cos_sb)
    c.dma("sp", sin_sb[:], sin96, out_t=sin_sb)
    ones32 = c.sb("ones32", [128, 128], F32)
    c.op("dve", nc.vector.memset, outs=[ones32], ap=ones32[:], constant=1.0)
    hts = [c.sb(f"ht{i}", [128, 8, NT], F32) for i in range(2)]
    hb = c.sb("hb", [128, 8, NT], BF16)
    cq32 = c.sb("cq32", [128, 3, NT], F32)
    cqn = c.sb("cqn", [128, 3, NT], BF16)
    sq_sb = [c.sb(f"sq{i}", [128, NT], F32) for i in range(2)]
    rstd = [c.sb(f"rstd{i}", [128, NT], F32) for i in range(2)]
    t1 = [c.sb(f"t1_{i}", [96, NT], F32) for i in range(2)]
    t2 = [c.sb(f"t2_{i}", [96, NT], F32) for i in range(2)]
    q_sb = [c.sb(f"q_sb{i}", [96, NT], BF16) for i in range(2)]
    kn_sb = [c.sb(f"kn_sb{i}", [64, NT], BF16) for i in range(2)]
    kr_sb = c.sb("kr_sb", [32, NT], BF16)
    v_sb = c.sb("v_sb", [128, 4, D], BF16)
    banks = [c.ps(f"pb{i}", [128, 512], F32) for i in range(7)]
    psQ = c.ps("psQ", [128, 512], F32)
    bi = 0
    hT_v = hT.rearrange("(kc p) t -> p kc t", p=128)
    for tt in range(TOK // NT):
        sl = slice(tt * NT, (tt + 1) * NT)
        ht = hts[tt % 2]
        c.dma("sp", ht[:], hT_v[:, :, sl], out_t=ht)
        for kc in range(8):
            cast.copy(hb, hb[:, kc, :], ht, ht[:, kc, :], eng=("dve", "pool")[kc % 2])
        for fc in range(3):
            pb = banks[bi % 7]; bi += 1
            for kc in range(8):
                c.op("pe", nc.tensor.matmul, outs=[pb], ins=[win_ch[kc], hb], out=pb[:, 0:NT],
                     lhsT=win_b[:, kc, fc * 128:(fc + 1) * 128], rhs=hb[:, kc, :], start=(kc == 0), stop=(kc == 7))
            c.op("act", nc.scalar.copy, outs=[cq32], ins=[pb], out=cq32[:, fc, :], in_=pb[:, 0:NT])
        pb = banks[bi % 7]; bi += 1
        pr = banks[bi % 7]; bi += 1
        for kc in range(8):
            c.op("pe", nc.tensor.matmul, outs=[pb], ins=[win_ch[kc], hb], out=pb[0:32, 0:NT],
                 lhsT=win_b[:, kc, 384:416], rhs=hb[:, kc, :], start=(kc == 0), stop=(kc == 7))
        for kc in range(8):
            c.op("pe", nc.tensor.matmul, outs=[pr], ins=[winr, hb], out=pr[0:32, 0:NT],
                 lhsT=winr[:, kc, :], rhs=hb[:, kc, :], start=(kc == 0), stop=(kc == 7))
        a = t1[0]; b_ = t2[0]
        c.op("dve", nc.vector.tensor_tensor, outs=[a], ins=[pb, cos_sb], out=a[0:32, :], in0=pb[0:32, 0:NT],
             in1=cos_sb[64:96, sl], op=ALU.mult)
        c.op("dve", nc.vector.tensor_tensor, outs=[b_], ins=[pr, sin_sb], out=b_[0:32, :], in0=pr[0:32, 0:NT],
             in1=sin_sb[64:96, sl], op=ALU.mult)
        c.op("pool", nc.gpsimd.tensor_tensor, outs=[kr_sb], ins=[a, b_], out=kr_sb[:], in0=a[0:32, :],
             in1=b_[0:32, :], op=ALU.add)
        c.dma("pool", KRT[:, sl], kr_sb[:], in_t=kr_sb)
        for grp, (chs, gsb, n) in enumerate((((0, 1), gq_sb, 256), ((2,), gkv_sb, 128))):
            rs_ = rstd[grp]
            for ci, ch in enumerate(chs):
                sq = sq_sb[ci % 2]
                c.op("act", nc.scalar.activation, outs=[sq], ins=[cq32], out=sq[:], in_=cq32[:, ch, :],
                     func=AF.Square)
                c.op("pe", nc.tensor.matmul, outs=[psQ], ins=[ones32, sq], out=psQ[:, 0:NT], lhsT=ones32[:],
                     rhs=sq[:], start=(ci == 0), stop=(ci == len(chs) - 1))
            c.op("act", nc.scalar.activation, outs=[rs_], ins=[psQ], out=rs_[:], in_=psQ[:, 0:NT], func=AF.Sqrt,
                 scale=1.0 / n, bias=RMS_EPS)
            c.op("dve", nc.vector.reciprocal, outs=[rs_], ins=[rs_], out=rs_[:], in_=rs_[:])
            for ci, ch in enumerate(chs):
                c.op("dve", nc.vector.scalar_tensor_tensor, outs=[cqn], ins=[cq32, gsb, rs_], out=cqn[:, ch, :],
                     in0=cq32[:, ch, :], scalar=gsb[:, ci:ci + 1], in1=rs_[:], op0=ALU.mult, op1=ALU.mult)
        for h in range(16):
            pb = banks[bi % 7]; bi += 1
            pr = banks[bi % 7]; bi += 1
            for fc in range(2):
                c.op("pe", nc.tensor.matmul, outs=[pb], ins=[wuq_ch[fc], cqn], out=pb[0:96, 0:NT],
                     lhsT=wuq_b[:, fc, h * 96:(h + 1) * 96], rhs=cqn[:, fc, :], start=(fc == 0), stop=(fc == 1))
            for fc in range(2):
                c.op("pe", nc.tensor.matmul, outs=[pr], ins=[wuqr, cqn], out=pr[0:96, 0:NT],
                     lhsT=wuqr[:, fc, h * 96:(h + 1) * 96], rhs=cqn[:, fc, :], start=(fc == 0), stop=(fc == 1))
            a = t1[h % 2]; b_ = t2[h % 2]; qs = q_sb[h % 2]
            c.op("dve", nc.vector.tensor_tensor, outs=[a], ins=[pb, cos_sb], out=a[:], in0=pb[0:96, 0:NT],
                 in1=cos_sb[:, sl], op=ALU.mult)
            c.op("dve", nc.vector.tensor_tensor, outs=[b_], ins=[pr, sin_sb], out=b_[:], in0=pr[0:96, 0:NT],
                 in1=sin_sb[:, sl], op=ALU.mult)
            c.op("pool", nc.gpsimd.tensor_tensor, outs=[a], ins=[a, b_], out=a[:], in0=a[:], in1=b_[:], op=ALU.add)
            c.op("act", nc.scalar.mul, outs=[qs], ins=[a], out=qs[:], in_=a[:], mul=QSC)
            c.dma("pool", QT[h * 96:(h + 1) * 96, sl], qs[:], in_t=qs)
        wk_v = wukv_b[:, 0, :].rearrange("p (h two d) -> p h two d", two=2, d=64)
        for h in range(16):
            pb = banks[bi % 7]; bi += 1
            c.op("pe", nc.tensor.matmul, outs=[pb], ins=[wukv_ch[0], cqn], out=pb[0:64, 0:NT],
                 lhsT=wk_v[:, h, 0, :], rhs=cqn[:, 2, :], start=True, stop=True)
            ks = kn_sb[h % 2]
            if h % 2 == 0:
                c.op("act", nc.scalar.copy, outs=[ks], ins=[pb], out=ks[:], in_=pb[0:64, 0:NT])
            else:
                c.op("dve", nc.vector.tensor_copy, outs=[ks], ins=[pb], out=ks[:], in_=pb[0:64, 0:NT])
            c.dma("pool", KNT[h * 64:(h + 1) * 64, sl], ks[:], in_t=ks)
        for j in range(4):
            for half in range(2):
                pb = banks[bi % 7]; bi += 1
                c.op("pe", nc.tensor.matmul, outs=[pb], ins=[wukv_ch[0], cqn], out=pb[:],
                     lhsT=cqn[:, 2, j * 128:(j + 1) * 128], rhs=wk_v[:, half * 8:(half + 1) * 8, 1, :],
                     start=True, stop=True)
                if half == 0:
                    c.op("act", nc.scalar.copy, outs=[v_sb], ins=[pb], out=v_sb[:, j, 0:512], in_=pb[:])
                else:
                    c.op("dve", nc.vector.tensor_copy, outs=[v_sb], ins=[pb], out=v_sb[:, j, 512:1024], in_=pb[:])
        V_v = V.rearrange("(j p) n -> p j n", p=128)
        c.dma("pool", V_v[:, tt * 4:(tt + 1) * 4, :], v_sb[:], in_t=v_sb)
    c.finish("pool")
    c.close()
    return nc


def layer2(hT_list, p, cs):
    ncP = get_nc("P_mla", build_P_mla)
    cos32, sin32 = rope_tables_fm(32, 32)
    cos96 = np.concatenate([np.ones((64, S), np.float32), cos32], axis=0)
    sin96 = np.concatenate([np.zeros((64, S), np.float32), sin32], axis=0)
    gq = np.ascontiguousarray(p["l2_mla_q_norm"].reshape(2, 128).T).astype(np.float32)
    gkv = np.ascontiguousarray(p["l2_mla_kv_norm"].reshape(1, 128).T).astype(np.float32)
    pres = launch(ncP, [{"hT": hT_list[c], "w_in": p["l2_mla_w_in"], "w_uq": p["l2_mla_w_uq"],
                         "w_ukv": p["l2_mla_w_ukv"], "gq": gq, "gkv": gkv,
                         "cos96": np.ascontiguousarray(cos96[:, (c % 2) * TOK:(c % 2 + 1) * TOK]),
                         "sin96": np.ascontiguousarray(sin96[:, (c % 2) * TOK:(c % 2 + 1) * TOK])}
                        for c in range(NCORES)])
    QT = gather_heads(pres, "QT", True)
    KNT = gather_heads(pres, "KNT", True)
    V = gather_heads(pres, "V", False)
    KRT = [np.ascontiguousarray(np.concatenate([pres[2 * (c // 2)]["KRT"], pres[2 * (c // 2) + 1]["KRT"]], axis=1))
           for c in range(NCORES)]
    ncA = get_nc("A_mla", build_A_soft, "mla")
    ares = launch(ncA, [{"QT": QT[c], "KNT": KNT[c], "KRT": KRT[c], "V": V[c], "maskC": cs["maskC"]}
                        for c in range(NCORES)])
    OT = scatter_OT(ares)
    return run_M(hT_list, OT, p["l2_mla_w_o"], p["l2_mlp_w1"], p["l2_mlp_w2"], p["l2_ln1_g"], p["l2_ln1_b"],
                 p["l2_ln2_g"], p["l2_ln2_b"]), dict(pres=pres, ares=ares, OT=OT)


NSA_IN = 2608


def build_P_nsa():
    nc = new_nc()
    hT = din(nc, "hT", [D, TOK], F32)
    w = din(nc, "w", [D, NSA_IN], F32)
    cosT = din(nc, "cosT", [128, TOK], F32)
    sinT = din(nc, "sinT", [128, TOK], F32)
    QT = dout(nc, "QT", [D, TOK], BF16)
    KcT = dout(nc, "KcT", [256, TOK], BF16)
    VcT = dout(nc, "VcT", [256, TOK], BF16)
    KsT = dout(nc, "KsT", [256, TOK], BF16)
    KwT = dout(nc, "KwT", [256, TOK], BF16)
    Vs = dout(nc, "Vs", [TOK, 256], BF16)
    Vw = dout(nc, "Vw", [TOK, 256], BF16)
    GT = dout(nc, "GT", [64, TOK], F32)
    c = make_ctx(nc)
    cast = Caster(c)
    NT = 512
    stages = [c.sb(f"stg{i}", [128, 1024], F32) for i in range(2)]
    wb, wch = load_weight_bf16(c, cast, w, D, NSA_IN, "wb", stages)
    roped = [(0, 8, QT, 0.125), (1024, 2, KcT, 1.0), (1536, 2, KsT, 1.0), (2048, 2, KwT, 1.0)]
    wr = c.sb("wrot", [128, 8, 14 * 128], BF16)
    wrch = [c.view(wr[:, kc, :], f"wrot_{kc}") for kc in range(8)]
    rcol = {}
    o = 0
    for (c0, nch, _, _) in roped:
        rcol[c0] = o
        for kc in range(8):
            src = wb[:, kc, c0:c0 + nch * 128].rearrange("p (h two d) -> p h two d", two=2, d=32)
            dst = wr[:, kc, o:o + nch * 128].rearrange("p (h two d) -> p h two d", two=2, d=32)
            c.op("dve", nc.vector.tensor_scalar, outs=[wrch[kc]], ins=[wch[kc]],
                 out=dst[:, :, 0, :], in0=src[:, :, 1, :], scalar1=-1.0, scalar2=None, op0=ALU.mult)
            c.op("pool", nc.gpsimd.tensor_copy, outs=[wrch[kc]], ins=[wch[kc]],
                 out=dst[:, :, 1, :], in_=src[:, :, 0, :])
        o += nch * 128
    cos_sb = c.sb("cos_sb", [128, TOK], F32)
    sin_sb = c.sb("sin_sb", [128, TOK], F32)
    c.dma("sp", cos_sb[:], cosT, out_t=cos_sb)
    c.dma("sp", sin_sb[:], sinT, out_t=sin_sb)
    hts = [c.sb(f"ht{i}", [128, 8, NT], F32) for i in range(2)]
    hb = c.sb("hb", [128, 8, NT], BF16)
    osb = [c.sb(f"osb{i}", [128, NT], BF16) for i in range(3)]
    t1 = [c.sb(f"t1_{i}", [128, NT], F32) for i in range(2)]
    t2 = [c.sb(f"t2_{i}", [128, NT], F32) for i in range(2)]
    v_sb = c.sb("v_sb", [128, 4, 512], BF16)
    g_sb = c.sb("g_sb", [64, NT], F32)
    banks = [c.ps(f"pb{i}", [128, 512], F32) for i in range(8)]
    bi = 0
    oi = 0
    hT_v = hT.rearrange("(kc p) t -> p kc t", p=128)
    for tt in range(TOK // NT):
        sl = slice(tt * NT, (tt + 1) * NT)
        ht = hts[tt % 2]
        c.dma("sp", ht[:], hT_v[:, :, sl], out_t=ht)
        for kc in range(8):
            cast.copy(hb, hb[:, kc, :], ht, ht[:, kc, :], eng=("dve", "pool")[kc % 2])
        for (c0, nch, out_d, sc) in roped:
            for fc in range(nch):
                col = c0 + fc * 128
                rc = rcol[c0] + fc * 128
                pb = banks[bi % 8]; bi += 1
                pr = banks[bi % 8]; bi += 1
                for kc in range(8):
                    c.op("pe", nc.tensor.matmul, outs=[pb], ins=[wch[kc], hb], out=pb[:, 0:NT],
                         lhsT=wb[:, kc, col:col + 128], rhs=hb[:, kc, :], start=(kc == 0), stop=(kc == 7))
                for kc in range(8):
                    c.op("pe", nc.tensor.matmul, outs=[pr], ins=[wrch[kc], hb], out=pr[:, 0:NT],
                         lhsT=wr[:, kc, rc:rc + 128], rhs=hb[:, kc, :], start=(kc == 0), stop=(kc == 7))
                a = t1[oi % 2]; b_ = t2[oi % 2]; ob = osb[oi % 3]; oi += 1
                c.op("dve", nc.vector.tensor_tensor, outs=[a], ins=[pb, cos_sb], out=a[:], in0=pb[:, 0:NT],
                     in1=cos_sb[:, sl], op=ALU.mult)
                c.op("dve", nc.vector.tensor_tensor, outs=[b_], ins=[pr, sin_sb], out=b_[:], in0=pr[:, 0:NT],
                     in1=sin_sb[:, sl], op=ALU.mult)
                c.op("pool", nc.gpsimd.tensor_tensor, outs=[a], ins=[a, b_], out=a[:], in0=a[:], in1=b_[:],
                     op=ALU.add)
                c.op("act", nc.scalar.mul, outs=[ob], ins=[a], out=ob[:], in_=a[:], mul=sc)
                c.dma("pool", out_d[fc * 128:(fc + 1) * 128, sl], ob[:], in_t=ob)
        for fc in range(2):
            col = 1280 + fc * 128
            pb = banks[bi % 8]; bi += 1
            for kc in range(8):
                c.op("pe", nc.tensor.matmul, outs=[pb], ins=[wch[kc], hb], out=pb[:, 0:NT],
                     lhsT=wb[:, kc, col:col + 128], rhs=hb[:, kc, :], start=(kc == 0), stop=(kc == 7))
            ob = osb[oi % 3]; oi += 1
            c.op("act", nc.scalar.copy, outs=[ob], ins=[pb], out=ob[:], in_=pb[:, 0:NT])
            c.dma("pool", VcT[fc * 128:(fc + 1) * 128, sl], ob[:], in_t=ob)
        import os as _os
        pb = banks[bi % 8]; bi += 1
        for kc in range(8):
            if _os.environ.get("NOGATE"): break
            c.op("pe", nc.tensor.matmul, outs=[pb], ins=[wch[kc], hb], out=pb[0:64, 0:NT],
                 lhsT=wb[:, kc, 2544:2608], rhs=hb[:, kc, :], start=(kc == 0), stop=(kc == 7))
        if not _os.environ.get("NOGATE"):
            c.op("act", nc.scalar.activation, outs=[g_sb], ins=[pb], out=g_sb[:], in_=pb[0:64, 0:NT], func=AF.Sigmoid)
            c.dma("pool", GT[:, sl], g_sb[:], in_t=g_sb)
        for j in range(4):
            pb = banks[bi % 8]; bi += 1
            for hi, col in enumerate((1792, 2304)):
                for kc in range(8):
                    c.op("pe", nc.tensor.matmul, outs=[pb], ins=[wch[kc], hb], out=pb[:, hi * 256:(hi + 1) * 256],
                         lhsT=hb[:, kc, j * 128:(j + 1) * 128], rhs=wb[:, kc, col:col + 256],
                         start=(kc == 0), stop=(kc == 7))
            c.op("act", nc.scalar.copy, outs=[v_sb], ins=[pb], out=v_sb[:, j, :], in_=pb[:])
        Vs_v = Vs.rearrange("(j p) n -> p j n", p=128)
        Vw_v = Vw.rearrange("(j p) n -> p j n", p=128)
        c.dma("pool", Vs_v[:, tt * 4:(tt + 1) * 4, :], v_sb[:, :, 0:256], in_t=v_sb)
        c.dma("pool", Vw_v[:, tt * 4:(tt + 1) * 4, :], v_sb[:, :, 256:512], in_t=v_sb)
    c.finish("pool")
    c.close()
    return nc


GELU_C = 1.5957691216057308


def build_A_nsa():
    nc = new_nc()
    QT = din(nc, "QT", [512, S], BF16)
    KcT = din(nc, "KcT", [128, S], BF16)
    VcT = din(nc, "VcT", [128, S], BF16)
    KsT = din(nc, "KsT", [128, S], BF16)
    KwT = din(nc, "KwT", [128, S], BF16)
    Vs = din(nc, "Vs", [S, 128], BF16)
    Vw = din(nc, "Vw", [S, 128], BF16)
    GT = din(nc, "GT", [32, S], F32)
    posT = din(nc, "posT", [64, 2, 32], F32)
    w1k = din(nc, "w1k", [2048, 256], F32)
    w1v = din(nc, "w1v", [2048, 256], F32)
    w2 = din(nc, "w2", [128, 2, 2, 64], F32)
    maskC = din(nc, "maskC", [128, 4, 512], BF16)
    maskL = din(nc, "maskL", [128, 4, 512], BF16)
    cmask = din(nc, "cmask", [128, 5, 512], BF16)
    ovl = din(nc, "ovl", [128, 4, 128], BF16)
    Eind = din(nc, "Eind", [128, S], BF16)
    JC = din(nc, "JC", [128, 128], F32)
    CB = din(nc, "CB", [128, 128], F32)
    ident = din(nc, "ident", [128, 128], F32)
    SelG = din(nc, "SelG", [32, 24 * 64], F32)
    OT = dout(nc, "OT", [512, S], BF16)
    c = make_ctx(nc)
    cast = Caster(c, engines=("dve", "pool"))

    def const(name, ap, shape, dt):
        t = c.sb(name, shape, dt)
        c.dma("sp", t[:], ap, out_t=t)
        return t
    mC = const("maskC", maskC, [128, 4, 512], BF16)
    mL = const("maskL", maskL, [128, 4, 512], BF16)
    cm = const("cmask", cmask, [128, 5, 512], BF16)
    ovl_sb = const("ovl", ovl, [128, 4, 128], BF16)
    E_sb = const("Eind", Eind, [128, S], BF16)
    JC_sb = const("JC", JC, [128, 128], F32)
    CB_sb = const("CB", CB, [128, 128], F32)
    id_sb = const("ident", ident, [128, 128], F32)
    SelG_sb = const("SelG", SelG, [32, 24 * 64], F32)
    posT32 = const("posT", posT, [64, 2, 32], F32)
    w2_32 = const("w2", w2, [128, 2, 2, 64], F32)
    posTb = c.sb("posTb", [64, 2, 32], BF16)
    c.op("dve", nc.vector.tensor_copy, outs=[posTb], ins=[posT32], out=posTb[:], in_=posT32[:])
    w2b = c.sb("w2b", [128, 2, 2, 64], BF16)
    c.op("dve", nc.vector.tensor_copy, outs=[w2b], ins=[w2_32], out=w2b[:], in_=w2_32[:])
    stg = [c.sb(f"stg{i}", [64, 4, 256], F32) for i in range(2)]
    W1 = []
    si = 0
    for nm_, wd in (("w1k", w1k), ("w1v", w1v)):
        t = c.sb(nm_, [64, 32, 256], BF16)
        wv = wd.rearrange("(l d) n -> d l n", d=64)
        for l0 in range(0, 32, 4):
            st = stg[si % 2]; si += 1
            c.dma("sp", st[:], wv[:, l0:l0 + 4, :], out_t=st)
            cast.copy(t, t[:, l0:l0 + 4, :], st, st[:])
        W1.append(t)
    ones32 = c.sb("ones32", [65, 64], F32)
    c.op("dve", nc.vector.memset, outs=[ones32], ap=ones32[:], constant=1.0)

    kcv_sb = c.sb("kcv", [64, S], BF16)
    ks_sb = c.sb("ksT", [64, S], BF16)
    kw_sb = c.sb("kwT", [64, S], BF16)
    vs_sb = c.sb("vs", [128, 64, 65], BF16)
    vw_sb = c.sb("vw", [128, 64, 65], BF16)
    vc_sb = c.sb("vc", [128, 4, 65], BF16)
    kcT_sb = c.sb("kcT", [64, 512], BF16)
    for v_ in (vs_sb, vw_sb):
        c.op("pool", nc.gpsimd.memset, outs=[v_], ap=v_[:, :, 64:65], constant=1.0)
    c.op("pool", nc.gpsimd.memset, outs=[vc_sb], ap=vc_sb[:, :, 64:65], constant=1.0)
    c.op("pool", nc.gpsimd.memset, outs=[kcT_sb], ap=kcT_sb[:], constant=0.0)
    b1_sb = c.sb("b1", [128, 2], F32)
    x32 = [c.sb(f"x32_{i}", [128, 512], F32) for i in range(2)]
    u32 = [c.sb(f"u32_{i}", [128, 512], F32) for i in range(2)]
    gel = c.sb("gel", [128, 2, 512], BF16)
    c.op("pool", nc.gpsimd.memset, outs=[gel], ap=gel[:], constant=0.0)
    qch = [c.sb(f"qch{i}", [64, 4, 512], BF16) for i in range(2)]
    gch = [c.sb(f"gch{i}", [32, 512], F32) for i in range(2)]
    for gc_ in gch:
        c.op("pool", nc.gpsimd.memset, outs=[gc_], ap=gc_[:], constant=0.0)
    pcs = [c.sb(f"pc{i}", [128, 512], BF16) for i in range(4)]
    p_sb = [c.sb(f"p{i}", [128, 512], BF16) for i in range(5)]
    o32 = [c.sb(f"o32_{i}", [64, 512], F32) for i in range(2)]
    rs = [c.sb(f"rs{i}", [65, 512], F32) for i in range(2)]
    on = [c.sb(f"on{i}", [64, 512], F32) for i in range(2)]
    acc = c.sb("acc", [64, 512], F32)
    stash = [c.sb(f"stash{i}", [64, 512], F32) for i in range(4)]
    ob = [c.sb(f"ob{i}", [64, 512], BF16) for i in range(2)]
    impacc = c.sb("impacc", [128, 4, 128], F32)
    rsT_sb = c.sb("rsT", [128, 4], F32)
    f1 = c.sb("f1", [128, 128], F32)
    pen = c.sb("pen", [128, 128], F32)
    imp3 = c.sb("imp3", [128, 128], F32)
    imp4 = c.sb("imp4", [128, 128], F32)
    m8a = c.sb("m8a", [128, 8], F32)
    m8b = c.sb("m8b", [128, 8], F32)
    nm = [c.sb(f"nm{i}", [128, 128], F32) for i in range(2)]
    nmT = c.sb("nmT", [128, 512], BF16)

    ps_s = [c.ps(f"pss{i}", [128, 512], F32) for i in range(3)]
    ps_o = [c.ps(f"pso{i}", [128, 512], F32) for i in range(2)]
    ps_b = c.ps("psb", [128, 512], F32)
    ps_imp = c.ps("psimp", [128, 512], F32)
    ps_t = c.ps("pst", [128, 512], F32)
    ps_g = ps_t
    cnt = {"s": 0, "p": 0, "o": 0}

    Vs_v = Vs.rearrange("(kt p) n -> p kt n", p=128)
    Vw_v = Vw.rearrange("(kt p) n -> p kt n", p=128)

    def attend(k_t, k_ap_of, qap, q_t, v_sb, v_ap_of, kt_list, mask_of, extra_mm=None, keep=None):
        po = ps_o[cnt["o"] % 2]; o3 = o32[cnt["o"] % 2]; rs_ = rs[cnt["o"] % 2]
        cnt["o"] += 1
        LAG = 2
        N = len(kt_list)
        pend = []
        for n in range(N + LAG):
            if n < N:
                kt = kt_list[n]
                ps = ps_s[cnt["s"] % len(ps_s)]; cnt["s"] += 1
                if keep is not None:
                    p = keep[n]
                else:
                    p = p_sb[cnt["p"] % len(p_sb)]; cnt["p"] += 1
                c.op("pe", nc.tensor.matmul, outs=[ps], ins=[k_t, q_t], out=ps[:], lhsT=k_ap_of(kt), rhs=qap,
                     start=True, stop=(extra_mm is None))
                if extra_mm is not None:
                    extra_mm(ps, kt)
                c.op("act", nc.scalar.activation, outs=[p], ins=[ps], out=p[:], in_=ps[:], func=AF.Exp)
                m = mask_of(kt)
                if m is not None:
                    mt, map_ = m
                    c.op("dve", nc.vector.tensor_tensor, outs=[p], ins=[p, mt], out=p[:], in0=p[:], in1=map_,
                         op=ALU.mult)
                pend.append((n, kt, p))
            if n >= LAG:
                m_, kt, p = pend.pop(0)
                c.op("pe", nc.tensor.matmul, outs=[po], ins=[v_sb, p], out=po[0:65, :], lhsT=v_ap_of(kt), rhs=p[:],
                     start=(m_ == 0), stop=(m_ == N - 1))
        c.op("act", nc.scalar.copy, outs=[o3], ins=[po], out=o3[:], in_=po[0:64, :])
        c.op("dve", nc.vector.tensor_scalar, outs=[rs_], ins=[po], out=rs_[64:65, :], in0=po[64:65, :],
             scalar1=1e-30, scalar2=None, op0=ALU.max)
        c.op("dve", nc.vector.reciprocal, outs=[rs_], ins=[rs_], out=rs_[64:65, :], in_=rs_[64:65, :])
        c.op("pe", nc.tensor.matmul, outs=[ps_b], ins=[ones32, rs_], out=ps_b[0:64, :],
             lhsT=ones32[64:65, 0:64], rhs=rs_[64:65, :], start=True, stop=True)
        return o3, rs_

    oi = 0
    for g in range(2):
        c.dma("sp", ks_sb[:].rearrange("p (r t) -> p r t", r=2), fm_rows(KsT, g * 64, (g + 1) * 64), out_t=ks_sb)
        c.dma("sp", kw_sb[:].rearrange("p (r t) -> p r t", r=2), fm_rows(KwT, g * 64, (g + 1) * 64), out_t=kw_sb)
        for k0_, nk_, src_ in tm_pieces(Vs, g * 64, (g + 1) * 64):
            c.dma("sp", vs_sb[:, k0_:k0_ + nk_, 0:64], src_, out_t=vs_sb)
        for k0_, nk_, src_ in tm_pieces(Vw, g * 64, (g + 1) * 64):
            c.dma("sp", vw_sb[:, k0_:k0_ + nk_, 0:64], src_, out_t=vw_sb)
        for kv in range(2):
            src = (KcT, VcT)[kv]
            c.dma("sp", kcv_sb[:].rearrange("p (r t) -> p r t", r=2), fm_rows(src, g * 64, (g + 1) * 64), out_t=kcv_sb)
            W = W1[kv]
            for half in range(2):
                pb = ps_s[half]
                for l in range(32):
                    c.op("pe", nc.tensor.matmul, outs=[pb], ins=[W, posTb], out=pb[:, 0:1],
                         lhsT=W[:, l, half * 128:(half + 1) * 128], rhs=posTb[:, kv, l:l + 1],
                         start=(l == 0), stop=(l == 31))
                c.op("act", nc.scalar.copy, outs=[b1_sb], ins=[pb], out=b1_sb[:, half:half + 1], in_=pb[:, 0:1])
            for half in range(2):
                pb = ps_s[half]
                for l in range(32):
                    c.op("pe", nc.tensor.matmul, outs=[pb], ins=[W, kcv_sb], out=pb[:, 0:511],
                         lhsT=W[:, l, half * 128:(half + 1) * 128],
                         rhs=kcv_sb[:, l:l + 16 * 510 + 1:16], start=(l == 0), stop=(l == 31))
                x = x32[half]; u = u32[half]
                c.op("act", nc.scalar.activation, outs=[x], ins=[pb, b1_sb], out=x[:, 0:511], in_=pb[:, 0:511],
                     func=AF.Identity, bias=b1_sb[:, half:half + 1])
                c.op("dve", nc.vector.tensor_tensor, outs=[u], ins=[x], out=u[:, 0:511], in0=x[:, 0:511],
                     in1=x[:, 0:511], op=ALU.mult)
                c.op("dve", nc.vector.tensor_scalar, outs=[u], ins=[u], out=u[:, 0:511], in0=u[:, 0:511],
                     scalar1=0.044715, scalar2=1.0, op0=ALU.mult, op1=ALU.add)
                c.op("dve", nc.vector.tensor_tensor, outs=[u], ins=[u, x], out=u[:, 0:511], in0=u[:, 0:511],
                     in1=x[:, 0:511], op=ALU.mult)
                c.op("act", nc.scalar.activation, outs=[u], ins=[u], out=u[:, 0:511], in_=u[:, 0:511],
                     func=AF.Sigmoid, scale=GELU_C)
                c.op("dve", nc.vector.tensor_tensor, outs=[gel], ins=[u, x], out=gel[:, half, 0:511],
                     in0=u[:, 0:511], in1=x[:, 0:511], op=ALU.mult)
            if kv == 0:
                pb = ps_s[0]
                for half in range(2):
                    c.op("pe", nc.tensor.matmul, outs=[pb], ins=[w2b, gel], out=pb[0:64, 0:511],
                         lhsT=w2b[:, 0, half, :], rhs=gel[:, half, 0:511], start=(half == 0), stop=(half == 1))
                c.op("act", nc.scalar.copy, outs=[kcT_sb], ins=[pb], out=kcT_sb[:, 0:511], in_=pb[0:64, 0:511])
            else:
                for nt in range(4):
                    pb = ps_s[nt % 2]
                    for half in range(2):
                        c.op("pe", nc.tensor.matmul, outs=[pb], ins=[w2b, gel], out=pb[:, 0:64],
                             lhsT=gel[:, half, nt * 128:(nt + 1) * 128], rhs=w2b[:, 1, half, :],
                             start=(half == 0), stop=(half == 1))
                    c.op("act", nc.scalar.copy, outs=[vc_sb], ins=[pb], out=vc_sb[:, nt, 0:64], in_=pb[:, 0:64])
        for i in range(S // 512):
            qc = qch[i % 2]; gc = gch[i % 2]
            for r_ in range(4):
                c.dma("sp", qc[:, r_, :], fm_chunk(QT, g * 256 + r_ * 64, g * 256 + (r_ + 1) * 64, i), out_t=qc)
            c.dma("sp", gc[0:24, :], gt_chunk(GT, i), out_t=gc)
            n_ct = (32 * i + 30) // 128 + 1
            cres = []
            for r in range(4):
                def cmask_of(nt, i=i):
                    dlt = 512 * i - 2048 * nt
                    if dlt >= 2063:
                        return None
                    return (cm, cm[:, dlt // 512, :])
                o3, rs_ = attend(kcT_sb, lambda nt: kcT_sb[:, nt * 128:(nt + 1) * 128], qc[:, r, :], qc,
                                 vc_sb, lambda nt: vc_sb[:, nt, 0:65], list(range(n_ct)), cmask_of,
                                 keep=pcs)
                cres.append(None)
                hrow = (g * 4 + r) * 3
                c.op("dve", nc.vector.tensor_tensor, outs=[on[0]], ins=[o3, ps_b], out=on[0][:], in0=o3[:],
                     in1=ps_b[0:64, :], op=ALU.mult)
                c.op("pe", nc.tensor.matmul, outs=[ps_g], ins=[SelG_sb, gc], out=ps_g[0:64, :],
                     lhsT=SelG_sb[:, hrow * 64:(hrow + 1) * 64], rhs=gc[:], start=True, stop=True)
                st = stash[r]
                c.op("dve", nc.vector.tensor_tensor, outs=[st], ins=[on[0], ps_g], out=st[:], in0=on[0][:],
                     in1=ps_g[0:64, :], op=ALU.mult)
                for j in range(4):
                    for nt in range(n_ct):
                        c.op("pe", nc.tensor.matmul, outs=[ps_imp], ins=[pcs[nt], ovl_sb],
                             out=ps_imp[:, j * 128:(j + 1) * 128], lhsT=pcs[nt][:, j * 128:(j + 1) * 128],
                             rhs=ovl_sb[:, nt, :], start=(nt == 0), stop=(nt == n_ct - 1))
                for j in range(4):
                    c.op("pe", nc.tensor.matmul, outs=[ps_t], ins=[rs_, ones32], out=ps_t[:, j:j + 1],
                         lhsT=rs_[64:65, j * 128:(j + 1) * 128], rhs=ones32[64:65, 0:1], start=True, stop=True)
                c.op("act", nc.scalar.copy, outs=[rsT_sb], ins=[ps_t], out=rsT_sb[:], in_=ps_t[:, 0:4])
                for j in range(4):
                    if r == 0:
                        c.op("dve", nc.vector.tensor_scalar, outs=[impacc], ins=[ps_imp, rsT_sb],
                             out=impacc[:, j, :], in0=ps_imp[:, j * 128:(j + 1) * 128], scalar1=rsT_sb[:, j:j + 1],
                             scalar2=None, op0=ALU.mult)
                    else:
                        c.op("dve", nc.vector.scalar_tensor_tensor, outs=[impacc], ins=[ps_imp, rsT_sb, impacc],
                             out=impacc[:, j, :], in0=ps_imp[:, j * 128:(j + 1) * 128], scalar=rsT_sb[:, j:j + 1],
                             in1=impacc[:, j, :], op0=ALU.mult, op1=ALU.add)
            for j in range(4):
                T2 = 2 * (4 * i + j)
                n_ = nm[j % 2]
                c.op("dve", nc.vector.scalar_tensor_tensor, outs=[f1], ins=[JC_sb, CB_sb], out=f1[:], in0=JC_sb[:],
                     scalar=float(T2 - 1), in1=CB_sb[:], op0=ALU.is_ge, op1=ALU.mult)
                c.op("pool", nc.gpsimd.tensor_scalar, outs=[pen], ins=[JC_sb], out=pen[:], in0=JC_sb[:],
                     scalar1=float(T2), scalar2=-3e30, op0=ALU.is_gt, op1=ALU.mult)
                c.op("dve", nc.vector.tensor_tensor, outs=[imp3], ins=[impacc, f1], out=imp3[:], in0=impacc[:, j, :],
                     in1=f1[:], op=ALU.add)
                c.op("dve", nc.vector.memset, outs=[imp3], ap=imp3[:, 0:1], constant=2e9)
                c.op("dve", nc.vector.tensor_tensor, outs=[imp3], ins=[imp3, pen], out=imp3[:], in0=imp3[:],
                     in1=pen[:], op=ALU.add)
                c.op("dve", nc.vector.max, outs=[m8a], ins=[imp3], out=m8a[:], in_=imp3[:])
                c.op("dve", nc.vector.match_replace, outs=[imp4], ins=[m8a, imp3], out=imp4[:],
                     in_to_replace=m8a[:], in_values=imp3[:], imm_value=-2e30)
                c.op("dve", nc.vector.max, outs=[m8b], ins=[imp4], out=m8b[:], in_=imp4[:])
                c.op("dve", nc.vector.tensor_scalar, outs=[n_], ins=[imp3, m8b], out=n_[:], in0=imp3[:],
                     scalar1=m8b[:, 7:8], scalar2=-BIG, op0=ALU.is_lt, op1=ALU.mult)
                c.op("pe", nc.tensor.transpose, outs=[ps_t], ins=[n_, id_sb], out=ps_t[:, j * 128:(j + 1) * 128],
                     in_=n_[:], identity=id_sb[:])
            c.op("act", nc.scalar.copy, outs=[nmT], ins=[ps_t], out=nmT[:], in_=ps_t[:])
            for r in range(4):
                hrow = (g * 4 + r) * 3

                def sel_extra(ps, kt):
                    c.op("pe", nc.tensor.matmul, outs=[ps], ins=[E_sb, nmT], out=ps[:],
                         lhsT=E_sb[:, kt * 128:(kt + 1) * 128], rhs=nmT[:], start=False, stop=True)
                o3, _ = attend(ks_sb, lambda kt: ks_sb[:, kt * 128:(kt + 1) * 128], qc[:, r, :], qc,
                               vs_sb, lambda kt: vs_sb[:, kt, 0:65], list(range(4 * i + 4)),
                               lambda kt, i=i: ((mC, mC[:, kt - 4 * i, :]) if kt >= 4 * i else None),
                               extra_mm=sel_extra)
                c.op("dve", nc.vector.tensor_tensor, outs=[on[0]], ins=[o3, ps_b], out=on[0][:], in0=o3[:],
                     in1=ps_b[0:64, :], op=ALU.mult)
                c.op("pe", nc.tensor.matmul, outs=[ps_g], ins=[SelG_sb, gc], out=ps_g[0:64, :],
                     lhsT=SelG_sb[:, (hrow + 1) * 64:(hrow + 2) * 64], rhs=gc[:], start=True, stop=True)
                c.op("dve", nc.vector.tensor_tensor, outs=[on[0]], ins=[on[0], ps_g], out=on[0][:], in0=on[0][:],
                     in1=ps_g[0:64, :], op=ALU.mult)
                c.op("pool", nc.gpsimd.tensor_tensor, outs=[acc], ins=[on[0], stash[r]], out=acc[:], in0=on[0][:],
                     in1=stash[r][:], op=ALU.add)
                wl = [kt for kt in range(4 * i - 4, 4 * i + 4) if kt >= 0]
                o3, _ = attend(kw_sb, lambda kt: kw_sb[:, kt * 128:(kt + 1) * 128], qc[:, r, :], qc,
                               vw_sb, lambda kt: vw_sb[:, kt, 0:65], wl,
                               lambda kt, i=i: ((mC, mC[:, kt - 4 * i, :]) if kt >= 4 * i
                                                else (mL, mL[:, kt - 4 * i + 4, :])))
                c.op("dve", nc.vector.tensor_tensor, outs=[on[1]], ins=[o3, ps_b], out=on[1][:], in0=o3[:],
                     in1=ps_b[0:64, :], op=ALU.mult)
                c.op("pe", nc.tensor.matmul, outs=[ps_g], ins=[SelG_sb, gc], out=ps_g[0:64, :],
                     lhsT=SelG_sb[:, (hrow + 2) * 64:(hrow + 3) * 64], rhs=gc[:], start=True, stop=True)
                c.op("dve", nc.vector.tensor_tensor, outs=[on[1]], ins=[on[1], ps_g], out=on[1][:], in0=on[1][:],
                     in1=ps_g[0:64, :], op=ALU.mult)
                o_b = ob[oi % 2]; oi += 1
                c.op("pool", nc.gpsimd.tensor_tensor, outs=[o_b], ins=[acc, on[1]], out=o_b[:], in0=acc[:],
                     in1=on[1][:], op=ALU.add)
                row = (g * 4 + r) * 64
                c.dma("pool", OT[row:row + 64, i * 512:(i + 1) * 512], o_b[:], in_t=o_b)
    c.finish("pool")
    c.close()
    return nc


def nsa_consts():
    cs = {}
    n = np.arange(512)
    j = np.arange(128)
    ov = ((16 * n[:, None] < 64 * j[None, :] + 64) & (16 * n[:, None] + 32 > 64 * j[None, :]) & (n[:, None] < 511))
    cs["ovl"] = np.ascontiguousarray(ov.reshape(4, 128, 128).transpose(1, 0, 2)).astype(NPBF)
    cs["Eind"] = (np.arange(S)[None, :] // 64 == np.arange(128)[:, None]).astype(NPBF)
    p = np.arange(128)
    cs["JC"] = (j[None, :] - (p[:, None] // 64)).astype(np.float32)
    cs["CB"] = np.broadcast_to((1e9 + 1e6 * j)[None, :], (128, 128)).astype(np.float32).copy()
    cs["ident"] = np.eye(128, dtype=np.float32)
    sel = np.zeros((32, 24, 64), np.float32)
    for m in range(24):
        sel[m, m, :] = 1.0
    cs["SelG"] = sel.reshape(32, 24 * 64)
    np_ = np.arange(128)[:, None, None]
    m = np.arange(5)[None, :, None]
    t = np.arange(512)[None, None, :]
    cs["cmask"] = ((16 * np_ + 31 - 512 * m) <= t).astype(NPBF)
    r = np.arange(4)[None, :, None]
    cs["maskL"] = ((128 * r + np_) > t).astype(NPBF)
    return cs


def layer3(hT_list, p, cs):
    ncP = get_nc("P_nsa", build_P_nsa)
    cosF, sinF = rope_tables_fm(64, 128)
    pres = launch(ncP, [{"hT": hT_list[c], "w": p["l3_nsa_w_in"],
                         "cosT": np.ascontiguousarray(cosF[:, (c % 2) * TOK:(c % 2 + 1) * TOK]),
                         "sinT": np.ascontiguousarray(sinF[:, (c % 2) * TOK:(c % 2 + 1) * TOK])}
                        for c in range(NCORES)])
    g = {k: gather_heads(pres, k, True) for k in ("QT", "KcT", "VcT", "KsT", "KwT")}
    g["GT"] = []
    for c in range(NCORES):
        b, hh = c // 2, c % 2
        full = np.concatenate([pres[2 * b]["GT"], pres[2 * b + 1]["GT"]], axis=1)
        pad = np.zeros((32, S), np.float32)
        pad[0:24] = full[16 + hh * 24:16 + hh * 24 + 24]
        g["GT"].append(pad)
    g["Vs"] = gather_heads(pres, "Vs", False)
    g["Vw"] = gather_heads(pres, "Vw", False)
    nsc = nsa_consts()
    posT = np.ascontiguousarray(np.stack([p["l3_nsa_cmp_pos_k"].T, p["l3_nsa_cmp_pos_v"].T], axis=1)).astype(np.float32)
    w2 = np.stack([p["l3_nsa_cmp_w2_k"].reshape(2, 128, 64).transpose(1, 0, 2),
                   p["l3_nsa_cmp_w2_v"].reshape(2, 128, 64).transpose(1, 0, 2)], axis=1)
    w2 = np.ascontiguousarray(w2).astype(np.float32)
    ncA = get_nc("A_nsa", build_A_nsa)
    maps = []
    for c in range(NCORES):
        m = {k: g[k][c] for k in g}
        m.update(posT=posT, w1k=p["l3_nsa_cmp_w1_k"], w1v=p["l3_nsa_cmp_w1_v"], w2=w2, maskC=cs["maskC"])
        m.update(nsc)
        maps.append(m)
    ares = launch(ncA, maps)
    OT = scatter_OT(ares)
    return run_M(hT_list, OT, p["l3_nsa_w_o"], p["l3_mlp_w1"], p["l3_mlp_w2"], p["l3_ln1_g"], p["l3_ln1_b"],
                 p["l3_ln2_g"], p["l3_ln2_b"]), dict(pres=pres, ares=ares, OT=OT)


def kernel_unfused(**inputs):
    p = {k: np.asarray(v) for k, v in inputs.items()}
    cs = consts()
    hT = to_fm(p["x"].astype(np.float32))
    hT, _ = layer0(hT, p, cs)
    hT, _ = layer1(hT, p, cs)
    hT, _ = layer2(hT, p, cs)
    hT, _ = layer3(hT, p, cs)
    out = np.concatenate([h.T for h in hT], axis=0).reshape(B, S, D)
    return np.ascontiguousarray(out).astype(np.float32)


W_SHAPES = {
    "l0_sb_w_qkv": [D, 3 * D], "l0_sb_w_o": [D, D], "l1_moba_w_qkv": [D, 3 * D], "l1_moba_w_o": [D, D],
    "l2_mla_w_in": [D, 416], "l2_mla_w_uq": [256, 1536], "l2_mla_w_ukv": [128, 2048], "l2_mla_w_o": [D, D],
    "l3_nsa_w_in": [D, NSA_IN], "l3_nsa_cmp_w1_k": [2048, 256], "l3_nsa_cmp_w1_v": [2048, 256], "l3_nsa_w_o": [D, D],
}
for _l in range(4):
    W_SHAPES[f"l{_l}_mlp_w1"] = [D, DFF]
    W_SHAPES[f"l{_l}_mlp_w2"] = [DFF, D]


CC_MAX_BYTES = 2 * 1024 * 1024


class Exchange:
    def __init__(self, c, nc, par_sp):
        self.c = c
        self.nc = nc
        self.xsem = nc.alloc_semaphore(name="xsem")
        self.cnt = 0
        self.q = 0
        self.pars = {"sp": par_sp,
                     "pool": nc.gpsimd.snap(nc.gpsimd.partition_id() % 2, min_val=0, max_val=1)}

    @staticmethod
    def alloc(I, name, rows, cols, dt, kind, rc_big=None):
        es = 4 if dt == F32 else 2
        if rows * cols * es <= CC_MAX_BYTES:
            rc = rows
        else:
            rc = rc_big or {"fm": 256, "tm": 1024, "ot": 128}[kind]
        F = Fuse.active
        if kind == "fm":
            mine = I(name + "_m", [rows, cols], dt)
            lay = rc if rc < rows else rows // 2
        elif kind == "fm_all":
            mine = None
            lay = rows
        elif kind == "tm":
            mine = I(name + "_m", [2 * rows, cols // 2], dt)
            lay = rc
        elif kind == "ot":
            mine = I(name + "_m", [2 * rows, cols // 2], dt)
            lay = rc
        elif kind == "gt":
            mine = I(name + "_m", [48, cols], dt)
            lay = 24
        g = I(name + "_g", [2 * rows, cols], dt)
        if F is not None:
            F.lay[(mine if mine is not None else g).tensor.name] = lay
        return I(name + "_s", [rows, cols], dt), g, (mine if mine is not None else g), kind, rc

    def run(self, items):
        c = self.c
        for s_, g_, m_, kind, rc in items:
            rows = s_.shape[0]
            for j in range(rows // rc):
                c.allgather(s_[j * rc:(j + 1) * rc, :], g_[j * 2 * rc:(j + 1) * 2 * rc, :])
        for ek in ("sp", "pool"):
            c.eng[ek].wait_ge(c.ccsem, c.cccnt)
        for s_, g_, m_, kind, rc in items:
            if kind == "fm_all":
                continue
            ek = ("sp", "pool")[self.q % 2]
            self.q += 1
            par = self.pars[ek]
            rows = s_.shape[0]
            nch = rows // rc
            if kind == "fm" and nch == 1:
                src = g_.rearrange("(r h f) t -> r h f t", r=2, h=2)[:, bass.ds(par, 1), :, :] \
                    .rearrange("r 1 f t -> r f t")
                dst = m_.rearrange("(r f) t -> r f t", r=2)
            elif kind == "fm":
                src = g_.rearrange("(h x) t -> h x t", h=2)[bass.ds(par, 1), :, :].rearrange("1 x t -> x t")
                dst = m_
            elif kind == "tm":
                src = g_.rearrange("x (h n) -> x h n", h=2)[:, bass.ds(par, 1), :].rearrange("x 1 n -> x n")
                dst = m_
            elif kind == "ot":
                src = g_.rearrange("x (rr t) -> x rr t", rr=2)[:, bass.ds(par, 1), :].rearrange("x 1 t -> x t")
                dst = m_
            elif kind == "gt":
                src = g_.rearrange("(r f) t -> f r t", r=2)[16:64].rearrange("(h f) r t -> f h r t", h=2)[
                    :, bass.ds(par, 1), :, :].rearrange("f 1 r t -> f r t")
                dst = m_.rearrange("(r f) t -> f r t", r=2)
            c.eng[ek].dma_start(out=dst, in_=src).then_inc(self.xsem, 16)
            self.cnt += 16
        for ek in c.eng:
            c.eng[ek].wait_ge(self.xsem, self.cnt)


def build_fused():
    F = Fuse()
    Fuse.active = F
    try:
        nc = F.nc
        F.par = nc.sync.snap(nc.sync.partition_id() % 2, min_val=0, max_val=1)
        E = F.ext_in
        I = F.internal

        def W(name):
            return E(name, W_SHAPES[name], F32)

        X = Exchange(F.c if F.c is not None else make_ctx(nc), nc, F.par)
        AG = X.run

        def gath(name, rows, cols, dt, kind):
            return X.alloc(I, name, rows, cols, dt, kind)

        def M_phase(l, OTg, h_in, h_out, wo):
            F.io = {"OT": OTg, "hT": h_in, "w_o": W(wo), "w1": W(f"l{l}_mlp_w1"), "w2": W(f"l{l}_mlp_w2"),
                    "lnp": E(f"lnp{l}", [128, 4, 8], F32), "hO": h_out}
            build_M()

        maskS = E("maskS", [128, 4, 512], BF16)
        maskC = E("maskC", [128, 4, 512], BF16)
        tri = E("tri", [128, 128], BF16)
        ident = E("ident", [128, 128], F32)
        cosT = E("cosT", [128, TOK], F32)
        sinT = E("sinT", [128, TOK], F32)
        h0 = E("hT0", [D, TOK], F32)
        h = [h0] + [I(f"h{l}", [D, TOK], F32) for l in (1, 2, 3)]
        out = nc.dram_tensor("out", [D, TOK], F32, kind="ExternalOutput").ap()
        h.append(out)

        q = gath("l0QT", D, TOK, BF16, "fm"); k = gath("l0KT", D, TOK, BF16, "fm"); v = gath("l0V", TOK, D, BF16, "tm")
        F.io = {"hT": h[0], "w": W("l0_sb_w_qkv"), "QT": q[0], "KT": k[0], "V": v[0]}
        build_P_qkv(False, -0.125)
        AG([q, k, v])
        o = gath("l0OT", 512, S, BF16, "ot")
        F.io = {"QT": q[2], "KT": k[2], "V": v[2], "maskS": maskS, "tri": tri, "OT": o[0]}
        build_A_sb()
        AG([o])
        M_phase(0, o[2], h[0], h[1], "l0_sb_w_o")
        q = gath("l1QT", D, TOK, BF16, "fm"); k = gath("l1KT", D, TOK, BF16, "fm"); v = gath("l1V", TOK, D, BF16, "tm")
        F.io = {"hT": h[1], "w": W("l1_moba_w_qkv"), "cosT": cosT, "sinT": sinT, "QT": q[0], "KT": k[0], "V": v[0]}
        build_P_qkv(True, 0.125)
        AG([q, k, v])
        o = gath("l1OT", 512, S, BF16, "ot")
        F.io = {"QT": q[2], "KT": k[2], "V": v[2], "maskC": maskC, "Eind": E("Eind32", [32, S], BF16), "ident": ident,
                "OT": o[0]}
        build_A_soft("moba")
        AG([o])
        M_phase(1, o[2], h[1], h[2], "l1_moba_w_o")
        q = X.alloc(I, "l2QT", 1536, TOK, BF16, "fm", rc_big=192); k = gath("l2KN", D, TOK, BF16, "fm")
        kr = gath("l2KR", 32, TOK, BF16, "fm_all"); v = gath("l2V", TOK, D, BF16, "tm")
        F.io = {"hT": h[2], "w_in": W("l2_mla_w_in"), "w_uq": W("l2_mla_w_uq"), "w_ukv": W("l2_mla_w_ukv"),
                "gq": E("gq", [128, 2], F32), "gkv": E("gkv", [128, 1], F32), "cos96": E("cos96", [96, TOK], F32),
                "sin96": E("sin96", [96, TOK], F32), "QT": q[0], "KNT": k[0], "KRT": kr[0], "V": v[0]}
        build_P_mla()
        AG([q, k, kr, v])
        o = gath("l2OT", 512, S, BF16, "ot")
        F.io = {"QT": q[2], "KNT": k[2], "KRT": kr[2], "V": v[2], "maskC": maskC, "OT": o[0]}
        build_A_soft("mla")
        AG([o])
        M_phase(2, o[2], h[2], h[3], "l2_mla_w_o")
        names = [("QT", D, TOK, BF16, "fm"), ("KcT", 256, TOK, BF16, "fm"), ("VcT", 256, TOK, BF16, "fm"),
                 ("KsT", 256, TOK, BF16, "fm"), ("KwT", 256, TOK, BF16, "fm"), ("Vs", TOK, 256, BF16, "tm"),
                 ("Vw", TOK, 256, BF16, "tm"), ("GT", 64, TOK, F32, "gt")]
        sg = {n: gath("l3" + n, r, cc, dt, kd) for n, r, cc, dt, kd in names}
        F.io = {"hT": h[3], "w": W("l3_nsa_w_in"), "cosT": cosT, "sinT": sinT}
        F.io.update({n: sg[n][0] for n in sg})
        build_P_nsa()
        AG([sg[n] for n in sg])
        o = gath("l3OT", 512, S, BF16, "ot")
        F.io = {n: sg[n][2] for n in sg}
        F.io.update({"posT": E("posT", [64, 2, 32], F32), "w1k": W("l3_nsa_cmp_w1_k"), "w1v": W("l3_nsa_cmp_w1_v"),
                     "w2": E("w2nsa", [128, 2, 2, 64], F32), "maskC": maskC, "maskL": E("maskL", [128, 4, 512], BF16),
                     "cmask": E("cmask", [128, 5, 512], BF16), "ovl": E("ovl", [128, 4, 128], BF16),
                     "Eind": E("Eind128", [128, S], BF16), "JC": E("JC", [128, 128], F32),
                     "CB": E("CB", [128, 128], F32), "ident": ident, "SelG": E("SelG", [32, 24 * 64], F32),
                     "OT": o[0]})
        build_A_nsa()
        AG([o])
        M_phase(3, o[2], h[3], h[4], "l3_nsa_w_o")
        F.c.final_finish("pool")
        F.c.final_finish("sp")
    finally:
        Fuse.active = None
    return nc


def fused_inputs(p):
    cs = consts()
    nsc = nsa_consts()
    hT = to_fm(p["x"].astype(np.float32))
    cosF, sinF = rope_tables_fm(64, 128)
    cos32, sin32 = rope_tables_fm(32, 32)
    cos96 = np.concatenate([np.ones((64, S), np.float32), cos32], axis=0)
    sin96 = np.concatenate([np.zeros((64, S), np.float32), sin32], axis=0)
    common = {k: np.ascontiguousarray(p[k]).astype(np.float32) for k in W_SHAPES}
    for l in range(4):
        common[f"lnp{l}"] = lnp_pack(p[f"l{l}_ln1_g"], p[f"l{l}_ln1_b"], p[f"l{l}_ln2_g"], p[f"l{l}_ln2_b"])
    common.update(maskS=cs["maskS"], maskC=cs["maskC"], tri=cs["tri"], ident=np.eye(128, dtype=np.float32))
    common["Eind32"] = (np.arange(S)[None, :] // 256 == np.arange(32)[:, None]).astype(NPBF)
    common["gq"] = np.ascontiguousarray(p["l2_mla_q_norm"].reshape(2, 128).T).astype(np.float32)
    common["gkv"] = np.ascontiguousarray(p["l2_mla_kv_norm"].reshape(1, 128).T).astype(np.float32)
    common["posT"] = np.ascontiguousarray(
        np.stack([p["l3_nsa_cmp_pos_k"].T, p["l3_nsa_cmp_pos_v"].T], axis=1)).astype(np.float32)
    common["w2nsa"] = np.ascontiguousarray(
        np.stack([p["l3_nsa_cmp_w2_k"].reshape(2, 128, 64).transpose(1, 0, 2),
                  p["l3_nsa_cmp_w2_v"].reshape(2, 128, 64).transpose(1, 0, 2)], axis=1)).astype(np.float32)
    common.update(maskL=nsc["maskL"], cmask=nsc["cmask"], ovl=nsc["ovl"], Eind128=nsc["Eind"], JC=nsc["JC"],
                  CB=nsc["CB"], SelG=nsc["SelG"])
    maps = []
    for c in range(NCORES):
        m = dict(common)
        sl = slice((c % 2) * TOK, (c % 2 + 1) * TOK)
        m["hT0"] = hT[c]
        m["cosT"] = np.ascontiguousarray(cosF[:, sl]); m["sinT"] = np.ascontiguousarray(sinF[:, sl])
        m["cos96"] = np.ascontiguousarray(cos96[:, sl]); m["sin96"] = np.ascontiguousarray(sin96[:, sl])
        maps.append(m)
    return maps


def kernel(**inputs):
    p = {k: np.asarray(v) for k, v in inputs.items()}
    nc = get_nc("fused", build_fused)
    res = launch(nc, fused_inputs(p))
    out = np.concatenate([r["out"].T for r in res], axis=0).reshape(B, S, D)
    return np.ascontiguousarray(out).astype(np.float32)
```

```python
import numpy as np
import ml_dtypes
import concourse.bass as bass
import concourse.mybir as mybir
from concourse.bass_utils import run_bass_kernel_spmd

F32 = mybir.dt.float32
BF16 = mybir.dt.bfloat16
AF = mybir.ActivationFunctionType
ALU = mybir.AluOpType
AX = mybir.AxisListType
NPBF = ml_dtypes.bfloat16

D = 1024
B = 4
S = 8192
DFF = 4096
NCORES = 8
TOK = 4096
ALPHA = float((2.0 * 4) ** 0.25)
LN_EPS = 1e-5
RMS_EPS = 1e-6
BIG = 30000.0

SAME_ENGINE_SYNC = False


class T:
    __slots__ = ("ap", "name", "w", "r", "dsem", "dcnt")

    def __init__(self, ap, name):
        self.ap = ap
        self.name = name
        self.w = None
        self.r = []
        self.dsem = None
        self.dcnt = 0

    def __getitem__(self, idx):
        return self.ap[idx]


class Ctx:
    def __init__(self, nc):
        self.nc = nc
        self.eng = {"pe": nc.tensor, "act": nc.scalar, "dve": nc.vector,
                    "pool": nc.gpsimd, "sp": nc.sync}
        self.sem = {k: nc.alloc_semaphore(name=f"s_{k}") for k in self.eng}
        self.cnt = {k: 0 for k in self.eng}
        self.seen = {k: {} for k in self.eng}
        self.n_inst = 0
        self._stack = []
        self.out_events = []
        self.fused = False
        self.phase = 0
        self.dpool = []
        self.ptiles = []
        self.ccsem = None
        self.cccnt = 0

    def sb(self, name, shape, dt):
        cm = self.nc.sbuf_tensor(f"sb{self.phase}_" + name, list(shape), dt)
        t = cm.__enter__()
        self._stack.append(cm)
        return T(t[:], name)

    def ps(self, name, shape, dt=F32):
        cm = self.nc.psum_tensor(f"ps{self.phase}_" + name, list(shape), dt)
        t = cm.__enter__()
        self._stack.append(cm)
        return T(t[:], name)

    def view(self, ap, name):
        return T(ap, name)

    def _need(self, ek, deps, raw=()):
        best = {}
        for lst, is_raw in ((deps, False), (raw, True)):
            for d in lst:
                if d is None:
                    continue
                sem, val, sk = d
                if sk == ek and ek == "pe":
                    continue
                key = id(sem)
                if key not in best or best[key][1] < val:
                    best[key] = (sem, val)
        seen = self.seen[ek]
        e = self.eng[ek]
        for key, (sem, val) in best.items():
            if seen.get(key, 0) >= val:
                continue
            e.wait_ge(sem, val)
            seen[key] = val

    @staticmethod
    def _compact(r):
        best = {}
        for sem, val, sk in r:
            k = id(sem)
            if k not in best or best[k][1] < val:
                best[k] = (sem, val, sk)
        return list(best.values())

    def op(self, ek, fn, outs=(), ins=(), **kw):
        deps = []
        raw = [t.w for t in ins]
        for t in outs:
            deps.append(t.w)
            deps.extend(t.r)
        self._need(ek, deps, raw)
        inst = fn(**kw)
        self.cnt[ek] += 1
        ev = (self.sem[ek], self.cnt[ek], ek)
        inst.then_inc(self.sem[ek], 1)
        self.n_inst += 1
        for t in ins:
            t.r.append(ev)
            if len(t.r) > 16:
                t.r = self._compact(t.r)
        for t in outs:
            t.w = ev
            t.r = []
        return ev

    def dma(self, ek, out, in_, out_t=None, in_t=None, **kw):
        st = out_t or in_t
        if st.dsem is None:
            if self.dpool:
                st.dsem, st.dcnt = self.dpool.pop()
            else:
                st.dsem = self.nc.alloc_semaphore(name=f"d{self.phase}_{st.name}")
            self.ptiles.append(st)
        deps = []
        if in_t is not None:
            deps.append(in_t.w)
        if out_t is not None:
            deps.append(out_t.w)
            deps.extend(out_t.r)
        self._need(ek, deps)
        inst = self.eng[ek].dma_start(out=out, in_=in_, **kw)
        st.dcnt += 16
        inst.then_inc(st.dsem, 16)
        ev = (st.dsem, st.dcnt, None)
        self.n_inst += 1
        if in_t is not None:
            in_t.r.append(ev)
            if out_t is None:
                self.out_events.append(ev)
        if out_t is not None:
            out_t.w = ev
            out_t.r = []
        return ev

    def barrier(self):
        targets = [(self.sem[k], self.cnt[k], k) for k in self.eng if self.cnt[k] > 0]
        targets += [(t.dsem, t.dcnt, None) for t in self.ptiles if t.dcnt > 0]
        targets += [(sem, cnt, None) for sem, cnt in self.dpool if cnt > 0]
        if self.ccsem is not None and self.cccnt > 0:
            targets.append((self.ccsem, self.cccnt, None))
        for ek in self.eng:
            seen = self.seen[ek]
            for sem, val, sk in targets:
                if sk == ek or seen.get(id(sem), 0) >= val:
                    continue
                self.eng[ek].wait_ge(sem, val)
                seen[id(sem)] = val

    def end_phase(self):
        self.barrier()
        for t in self.ptiles:
            self.dpool.append([t.dsem, t.dcnt])
            t.dsem = None
        self.ptiles = []
        while self._stack:
            self._stack.pop().__exit__(None, None, None)
        self.phase += 1

    def allgather(self, in_ap, out_ap):
        if self.ccsem is None:
            self.ccsem = self.nc.alloc_semaphore(name="ccsem")
        self.nc.gpsimd.collective_compute("AllGather", ALU.bypass,
                                          replica_groups=[[0, 1], [2, 3], [4, 5], [6, 7]],
                                          ins=[in_ap], outs=[out_ap]).then_inc(self.ccsem, 1)
        self.cccnt += 1

    def finish(self, ek="sp"):
        if self.fused:
            return
        best = {}
        for sem, val, _ in self.out_events:
            k = id(sem)
            if k not in best or best[k][1] < val:
                best[k] = (sem, val)
        for sem, val in best.values():
            self.eng[ek].wait_ge(sem, val)

    def close(self):
        if self.fused:
            self.end_phase()
            return
        while self._stack:
            self._stack.pop().__exit__(None, None, None)

    def final_finish(self, ek="sp"):
        best = {}
        for sem, val, _ in self.out_events:
            k = id(sem)
            if k not in best or best[k][1] < val:
                best[k] = (sem, val)
        for sem, val in best.values():
            self.eng[ek].wait_ge(sem, val)


class Fuse:
    active = None

    def __init__(self):
        self.nc = bass.Bass("TRN2", target_bir_lowering=False)
        self.c = None
        self.io = {}
        self.ext = {}
        self.par = None
        self.lay = {}

    def ext_in(self, name, shape, dt):
        if name not in self.ext:
            self.ext[name] = self.nc.dram_tensor(name, list(shape), dt, kind="ExternalInput").ap()
        return self.ext[name]

    def internal(self, name, shape, dt):
        return self.nc.dram_tensor(name, list(shape), dt, kind="Internal").ap()


def new_nc():
    if Fuse.active is not None:
        return Fuse.active.nc
    return bass.Bass("TRN2", target_bir_lowering=False)


def make_ctx(nc):
    F = Fuse.active
    if F is not None:
        if F.c is None:
            F.c = Ctx(nc)
            F.c.fused = True
        return F.c
    return Ctx(nc)


def din(nc, name, shape, dt):
    F = Fuse.active
    if F is not None:
        return F.io[name]
    return nc.dram_tensor(name, list(shape), dt, kind="ExternalInput").ap()


def dout(nc, name, shape, dt):
    F = Fuse.active
    if F is not None:
        return F.io[name]
    return nc.dram_tensor(name, list(shape), dt, kind="ExternalOutput").ap()


def _lay(ap):
    F = Fuse.active
    if F is None:
        return None
    return F.lay.get(ap.tensor.name)


def fm_rows(ap, r0, r1):
    rc = _lay(ap)
    if rc is None:
        return ap[r0:r1, :].rearrange("p (r t) -> p r t", r=2)
    jj = r0 // rc
    assert (r1 - 1) // rc == jj, (r0, r1, rc)
    v = ap[jj * 2 * rc:(jj + 1) * 2 * rc, :].rearrange("(r f) t -> f r t", r=2)
    return v[r0 - jj * rc:r1 - jj * rc, :, :]


def fm_shared(ap):
    if _lay(ap) is None:
        return ap.rearrange("p (r t) -> p r t", r=2)
    return ap.rearrange("(r f) t -> f r t", r=2)


def fm_chunk(ap, r0, r1, i):
    r, t0 = divmod(i * 512, TOK)
    return fm_rows(ap, r0, r1)[:, r, t0:t0 + 512]


def gt_chunk(ap, i):
    if _lay(ap) is None:
        return ap[0:24, i * 512:(i + 1) * 512]
    r, t0 = divmod(i * 512, TOK)
    return ap.rearrange("(r f) t -> f r t", r=2)[:, r, t0:t0 + 512]


def ot_pieces(ap, t0, n):
    rc = _lay(ap)
    if rc is None:
        return [(0, 8, ap.rearrange("(kc p) t -> p kc t", p=128)[:, :, t0:t0 + n])]
    v = ap.rearrange("(j r p) t -> p r j t", r=2, p=128)
    return [(r * 4, 4, v[:, r, :, t0:t0 + n]) for r in range(2)]


def tm_pieces(ap, c0, c1):
    rc = _lay(ap)
    if rc is None:
        return [(0, 64, ap.rearrange("(kt p) n -> p kt n", p=128)[:, :, c0:c1])]
    nj = TOK // rc
    k8 = rc // 128
    v = ap.rearrange("(j r k p) n -> p j r k n", r=2, k=k8, p=128)
    return [(r * (TOK // 128) + j * k8, k8, v[:, j, r, :, c0:c1]) for r in range(2) for j in range(nj)]


class Caster:
    def __init__(self, c, engines=("dve", "pool", "act")):
        self.c = c
        self.engines = engines
        self.i = 0

    def copy(self, out_t, out_ap, in_t, in_ap, eng=None):
        c = self.c
        ek = eng or self.engines[self.i % len(self.engines)]
        self.i += 1
        if ek == "act":
            return c.op("act", c.nc.scalar.copy, outs=[out_t], ins=[in_t], out=out_ap, in_=in_ap)
        e = c.nc.vector if ek == "dve" else c.nc.gpsimd
        return c.op(ek, e.tensor_copy, outs=[out_t], ins=[in_t], out=out_ap, in_=in_ap)


def load_weight_bf16(c, caster, w_ap, K, N, name, stages, queue="sp"):
    nk = K // 128
    big = c.sb(name, [128, nk, N], BF16)
    chunks = [c.view(big[:, kc, :], f"{name}_{kc}") for kc in range(nk)]
    CW = stages[0].ap.shape[-1]
    si = 0
    for kc in range(nk):
        for n0 in range(0, N, CW):
            n1 = min(N, n0 + CW)
            st = stages[si % len(stages)]
            si += 1
            c.dma(queue, st[:, 0:n1 - n0], w_ap[kc * 128:(kc + 1) * 128, n0:n1], out_t=st)
            caster.copy(chunks[kc], big[:, kc, n0:n1], st, st[:, 0:n1 - n0])
    return big, chunks


def build_P_qkv(rope, qscale):
    nc = new_nc()
    hT = din(nc, "hT", [D, TOK], F32)
    w = din(nc, "w", [D, 3 * D], F32)
    if rope:
        cosT = din(nc, "cosT", [128, TOK], F32)
        sinT = din(nc, "sinT", [128, TOK], F32)
    QT = dout(nc, "QT", [D, TOK], BF16)
    KT = dout(nc, "KT", [D, TOK], BF16)
    V = dout(nc, "V", [TOK, D], BF16)
    c = make_ctx(nc)
    cast = Caster(c)
    stages = [c.sb(f"stg{i}", [128, 1024], F32) for i in range(2)]
    wb, wch = load_weight_bf16(c, cast, w, D, 3 * D, "wb", stages)
    if rope:
        wr = c.sb("wrot", [128, 8, 2 * D], BF16)
        wrch = [c.view(wr[:, kc, :], f"wrot_{kc}") for kc in range(8)]
        for kc in range(8):
            src = wb[:, kc, 0:2 * D].rearrange("p (h two d) -> p h two d", two=2, d=32)
            dst = wr[:, kc, :].rearrange("p (h two d) -> p h two d", two=2, d=32)
            c.op("dve", nc.vector.tensor_scalar, outs=[wrch[kc]], ins=[wch[kc]],
                 out=dst[:, :, 0, :], in0=src[:, :, 1, :], scalar1=-1.0, scalar2=None, op0=ALU.mult)
            c.op("pool", nc.gpsimd.tensor_copy, outs=[wrch[kc]], ins=[wch[kc]],
                 out=dst[:, :, 1, :], in_=src[:, :, 0, :])
        cos_sb = c.sb("cos_sb", [128, TOK], F32)
        sin_sb = c.sb("sin_sb", [128, TOK], F32)
        c.dma("sp", cos_sb[:], cosT, out_t=cos_sb)
        c.dma("sp", sin_sb[:], sinT, out_t=sin_sb)
    NT = 512
    hts = [c.sb(f"ht{i}", [128, 8, NT], F32) for i in range(2)]
    hb = c.sb("hb", [128, 8, NT], BF16)
    qk_sb = [c.sb(f"qk{i}", [128, 8, NT], BF16) for i in range(2)]
    v_sb = c.sb("v_sb", [128, 4, D], BF16)
    t1 = [c.sb(f"t1_{i}", [128, NT], F32) for i in range(2)]
    t2 = [c.sb(f"t2_{i}", [128, NT], F32) for i in range(2)]
    banks = [c.ps(f"pb{i}", [128, 512], F32) for i in range(8)]
    bi = 0
    hT_v = hT.rearrange("(kc p) t -> p kc t", p=128)
    for tt in range(TOK // NT):
        ht = hts[tt % 2]
        c.dma("sp", ht[:], hT_v[:, :, tt * NT:(tt + 1) * NT], out_t=ht)
        for kc in range(8):
            cast.copy(hb, hb[:, kc, :], ht, ht[:, kc, :], eng=("dve", "pool")[kc % 2])
        for which in range(2):
            dst = qk_sb[which]
            for fc in range(8):
                col = which * D + fc * 128
                pb = banks[bi % 8]; bi += 1
                for kc in range(8):
                    c.op("pe", nc.tensor.matmul, outs=[pb], ins=[wch[kc], hb],
                         out=pb[:, 0:NT], lhsT=wb[:, kc, col:col + 128], rhs=hb[:, kc, :],
                         start=(kc == 0), stop=(kc == 7))
                sc = qscale if which == 0 else 1.0
                if not rope:
                    c.op("act", nc.scalar.mul, outs=[dst], ins=[pb], out=dst[:, fc, :], in_=pb[:, 0:NT], mul=sc)
                else:
                    pr = banks[bi % 8]; bi += 1
                    for kc in range(8):
                        c.op("pe", nc.tensor.matmul, outs=[pr], ins=[wrch[kc], hb],
                             out=pr[:, 0:NT], lhsT=wr[:, kc, col:col + 128], rhs=hb[:, kc, :],
                             start=(kc == 0), stop=(kc == 7))
                    a = t1[fc % 2]; b_ = t2[fc % 2]
                    c.op("dve", nc.vector.tensor_tensor, outs=[a], ins=[pb, cos_sb],
                         out=a[:], in0=pb[:, 0:NT], in1=cos_sb[:, tt * NT:(tt + 1) * NT], op=ALU.mult)
                    c.op("dve", nc.vector.tensor_tensor, outs=[b_], ins=[pr, sin_sb],
                         out=b_[:], in0=pr[:, 0:NT], in1=sin_sb[:, tt * NT:(tt + 1) * NT], op=ALU.mult)
                    c.op("pool", nc.gpsimd.tensor_tensor, outs=[a], ins=[a, b_], out=a[:], in0=a[:], in1=b_[:],
                         op=ALU.add)
                    c.op("act", nc.scalar.mul, outs=[dst], ins=[a], out=dst[:, fc, :], in_=a[:], mul=sc)
            out_d = (QT, KT)[which].rearrange("(fc p) t -> p fc t", p=128)
            c.dma("pool", out_d[:, :, tt * NT:(tt + 1) * NT], dst[:], in_t=dst)
        for j in range(4):
            for half in range(2):
                pb = banks[bi % 8]; bi += 1
                for kc in range(8):
                    c.op("pe", nc.tensor.matmul, outs=[pb], ins=[wch[kc], hb],
                         out=pb[:], lhsT=hb[:, kc, j * 128:(j + 1) * 128],
                         rhs=wb[:, kc, 2 * D + half * 512:2 * D + (half + 1) * 512],
                         start=(kc == 0), stop=(kc == 7))
                if half == 0:
                    c.op("act", nc.scalar.copy, outs=[v_sb], ins=[pb], out=v_sb[:, j, 0:512], in_=pb[:])
                else:
                    c.op("dve", nc.vector.tensor_copy, outs=[v_sb], ins=[pb], out=v_sb[:, j, 512:1024], in_=pb[:])
        V_v = V.rearrange("(j p) n -> p j n", p=128)
        c.dma("pool", V_v[:, tt * 4:(tt + 1) * 4, :], v_sb[:], in_t=v_sb)
    c.finish("pool")
    c.close()
    return nc


def build_A_sb():
    nc = new_nc()
    QT = din(nc, "QT", [512, S], BF16)
    KT = din(nc, "KT", [512, S], BF16)
    V = din(nc, "V", [S, 512], BF16)
    maskS = din(nc, "maskS", [128, 4, 512], BF16)
    tri = din(nc, "tri", [128, 128], BF16)
    OT = dout(nc, "OT", [512, S], BF16)
    c = make_ctx(nc)
    m_sb = c.sb("maskS", [128, 4, 512], BF16)
    tri_sb = c.sb("tri", [128, 128], BF16)
    ones_sb = c.sb("ones", [128, 128], BF16)
    c.dma("sp", m_sb[:], maskS, out_t=m_sb)
    c.dma("sp", tri_sb[:], tri, out_t=tri_sb)
    c.op("dve", nc.vector.memset, outs=[ones_sb], ap=ones_sb[:], constant=1.0)
    qts = [c.sb(f"qt{i}", [128, S], BF16) for i in range(2)]
    kts = [c.sb(f"kt{i}", [128, S], BF16) for i in range(2)]
    vs = [c.sb(f"v{i}", [128, 64, 128], BF16) for i in range(2)]
    NW = 4
    e_sb = [c.sb(f"e{i}", [128, 512], F32) for i in range(NW)]
    sp_sb = [c.sb(f"sp{i}", [128, 512], BF16) for i in range(NW)]
    w_sb = [c.sb(f"w{i}", [128, 512], BF16) for i in range(NW)]
    run = [c.sb(f"run{i}", [128, 512], BF16) for i in range(2)]
    o_sb = [c.sb(f"o{i}", [64, 512], BF16) for i in range(2)]
    pz = [c.ps(f"pz{i}", [128, 512], F32) for i in range(3)]
    px = [c.ps(f"px{i}", [128, 512], F32) for i in range(3)]
    po = [c.ps(f"po{i}", [128, 512], F32) for i in range(2)]
    V_v = V.rearrange("(kt p) n -> p kt n", p=128)
    it = 0
    ix = 0
    oi = 0
    for pair in range(4):
        qt = qts[pair % 2]; kt_ = kts[pair % 2]; v = vs[pair % 2]
        c.dma("sp", qt[:].rearrange("p (r t) -> p r t", r=2), fm_rows(QT, pair * 128, (pair + 1) * 128), out_t=qt)
        c.dma("sp", kt_[:].rearrange("p (r t) -> p r t", r=2), fm_rows(KT, pair * 128, (pair + 1) * 128), out_t=kt_)
        for k0_, nk_, src_ in tm_pieces(V, pair * 128, (pair + 1) * 128):
            c.dma("sp", v[:, k0_:k0_ + nk_, :], src_, out_t=v)
        for hs in range(2):
            r0 = hs * 64
            for i in range(S // 512):
                pout = po[oi % 2]
                osb = o_sb[oi % 2]
                oi += 1
                rn = run[i % 2]
                nkt = 4 * i + 4
                kts_ = list(range(nkt - 1, -1, -1))
                N = len(kts_)
                qap = qt[r0:r0 + 64, i * 512:(i + 1) * 512]
                stA = []
                stB = []
                for step in range(N + 2):
                    if step < N:
                        n = step; kt = kts_[n]; r = kt - 4 * i
                        z = pz[it % 3]; e = e_sb[it % NW]; sp = sp_sb[it % NW]
                        it += 1
                        kap = kt_[r0:r0 + 64, kt * 128:(kt + 1) * 128]
                        c.op("pe", nc.tensor.matmul, outs=[z], ins=[kt_, qt], out=z[:], lhsT=kap, rhs=qap,
                             start=True, stop=True)
                        c.op("act", nc.scalar.activation, outs=[e], ins=[z], out=e[:], in_=z[:], func=AF.Exp,
                             scale=-1.0)
                        c.op("act", nc.scalar.activation, outs=[sp], ins=[e], out=sp[:], in_=e[:], func=AF.Ln,
                             bias=1.0)
                        if r >= 0:
                            c.op("dve", nc.vector.tensor_tensor, outs=[sp], ins=[sp, m_sb], out=sp[:], in0=sp[:],
                                 in1=m_sb[:, r, :], op=ALU.mult)
                        stA.append((n, kt, sp, kap))
                    if 1 <= step <= N:
                        n, kt, sp, kap = stA.pop(0); r = kt - 4 * i
                        x = px[ix % 3]; wt = w_sb[ix % NW]
                        ix += 1
                        c.op("pe", nc.tensor.matmul, outs=[x], ins=[tri_sb, sp], out=x[:], lhsT=tri_sb[:], rhs=sp[:],
                             start=True, stop=False)
                        if n > 0:
                            c.op("pe", nc.tensor.matmul, outs=[x], ins=[ones_sb, rn], out=x[:], lhsT=ones_sb[:],
                                 rhs=rn[:], start=False, stop=False)
                        c.op("pe", nc.tensor.matmul, outs=[x], ins=[kt_, qt], out=x[:], lhsT=kap,
                             rhs=qap, start=False, stop=True)
                        c.op("act", nc.scalar.activation, outs=[wt], ins=[x], out=wt[:], in_=x[:], func=AF.Exp,
                             scale=-1.0)
                        if r >= 0:
                            c.op("dve", nc.vector.tensor_tensor, outs=[wt], ins=[wt, m_sb], out=wt[:], in0=wt[:],
                                 in1=m_sb[:, r, :], op=ALU.mult)
                        if kt > 0:
                            if n == 0:
                                c.op("pool", nc.gpsimd.tensor_copy, outs=[rn], ins=[sp], out=rn[:], in_=sp[:])
                            else:
                                c.op("pool", nc.gpsimd.tensor_tensor, outs=[rn], ins=[rn, sp], out=rn[:], in0=rn[:],
                                     in1=sp[:], op=ALU.add)
                        stB.append((n, kt, wt))
                    if step >= 2:
                        n, kt, wt = stB.pop(0)
                        c.op("pe", nc.tensor.matmul, outs=[pout], ins=[v, wt], out=pout[0:64, :],
                             lhsT=v[:, kt, r0:r0 + 64], rhs=wt[:], start=(n == 0), stop=(kt == 0))
                c.op("act", nc.scalar.copy, outs=[osb], ins=[pout], out=osb[:], in_=pout[0:64, :])
                row = pair * 128 + hs * 64
                c.dma("pool", OT[row:row + 64, i * 512:(i + 1) * 512], osb[:], in_t=osb)
    c.finish("pool")
    c.close()
    return nc


def ln_fm(c, nc, h_sb, hch, ones32, sq_sb, psS, psQ, g_col, b_col, small, NT, out_bf=None, out_bf_ch=None):
    mean = small["mean"]; rstd = small["rstd"]; msq = small["msq"]
    for fc in range(8):
        c.op("pe", nc.tensor.matmul, outs=[psS], ins=[ones32, hch[fc]], out=psS[:, 0:NT], lhsT=ones32[:],
             rhs=h_sb[:, fc, :], start=(fc == 0), stop=(fc == 7))
    for fc in range(8):
        sq = sq_sb[fc % 2]
        c.op("act", nc.scalar.activation, outs=[sq], ins=[hch[fc]], out=sq[:], in_=h_sb[:, fc, :], func=AF.Square)
        c.op("pe", nc.tensor.matmul, outs=[psQ], ins=[ones32, sq], out=psQ[:, 0:NT], lhsT=ones32[:],
             rhs=sq[:], start=(fc == 0), stop=(fc == 7))
    c.op("dve", nc.vector.tensor_scalar, outs=[mean], ins=[psS], out=mean[:], in0=psS[:, 0:NT],
         scalar1=1.0 / D, scalar2=None, op0=ALU.mult)
    c.op("dve", nc.vector.tensor_tensor, outs=[msq], ins=[mean], out=msq[:], in0=mean[:], in1=mean[:], op=ALU.mult)
    c.op("dve", nc.vector.scalar_tensor_tensor, outs=[rstd], ins=[psQ, msq], out=rstd[:], in0=psQ[:, 0:NT],
         scalar=1.0 / D, in1=msq[:], op0=ALU.mult, op1=ALU.subtract)
    c.op("act", nc.scalar.activation, outs=[rstd], ins=[rstd], out=rstd[:], in_=rstd[:], func=AF.Sqrt, bias=LN_EPS)
    c.op("dve", nc.vector.reciprocal, outs=[rstd], ins=[rstd], out=rstd[:], in_=rstd[:])
    for fc in range(8):
        c.op("dve", nc.vector.tensor_tensor, outs=[hch[fc]], ins=[hch[fc], mean], out=h_sb[:, fc, :],
             in0=h_sb[:, fc, :], in1=mean[:], op=ALU.subtract)
        c.op("pool", nc.gpsimd.tensor_tensor, outs=[hch[fc]], ins=[hch[fc], rstd], out=h_sb[:, fc, :],
             in0=h_sb[:, fc, :], in1=rstd[:], op=ALU.mult)
        c.op("act", nc.scalar.activation, outs=[hch[fc]], ins=[hch[fc], g_col, b_col], out=h_sb[:, fc, :],
             in_=h_sb[:, fc, :], func=AF.Identity, scale=g_col[:, fc:fc + 1], bias=b_col[:, fc:fc + 1])
        if out_bf is not None:
            c.op("dve", nc.vector.tensor_copy, outs=[out_bf_ch[fc]], ins=[hch[fc]], out=out_bf[:, fc, :],
                 in_=h_sb[:, fc, :])


def build_M():
    nc = new_nc()
    OT = din(nc, "OT", [D, TOK], BF16)
    hT = din(nc, "hT", [D, TOK], F32)
    w_o = din(nc, "w_o", [D, D], F32)
    w1 = din(nc, "w1", [D, DFF], F32)
    w2 = din(nc, "w2", [DFF, D], F32)
    lnp = din(nc, "lnp", [128, 4, 8], F32)
    hO = dout(nc, "hO", [D, TOK], F32)
    c = make_ctx(nc)
    cast = Caster(c)
    NT = 256
    stages = [c.sb(f"stg{i}", [128, 1024], F32) for i in range(2)]
    lnp_sb = c.sb("lnp", [128, 4, 8], F32)
    c.dma("sp", lnp_sb[:], lnp, out_t=lnp_sb)
    g1 = c.view(lnp_sb[:, 0, :], "g1"); b1 = c.view(lnp_sb[:, 1, :], "b1")
    g2 = c.view(lnp_sb[:, 2, :], "g2"); b2 = c.view(lnp_sb[:, 3, :], "b2")
    for v_ in (g1, b1, g2, b2):
        v_.w = lnp_sb.w
    ones32 = c.sb("ones32", [128, 128], F32)
    c.op("dve", nc.vector.memset, outs=[ones32], ap=ones32[:], constant=1.0)
    wo_b, wo_ch = load_weight_bf16(c, cast, w_o, D, D, "wo", stages)
    w1_b, w1_ch = load_weight_bf16(c, cast, w1, D, DFF, "w1", stages)
    w2_b, w2_ch = load_weight_bf16(c, cast, w2, DFF, D, "w2", stages)
    hs = [c.sb(f"h{i}", [128, 8, NT], F32) for i in range(2)]
    hchs = [[c.view(h[:, fc, :], f"{h.name}_{fc}") for fc in range(8)] for h in hs]
    ots = [c.sb(f"ot{i}", [128, 8, NT], BF16) for i in range(2)]
    hb = c.sb("hb", [128, 8, NT], BF16)
    hbch = [c.view(hb[:, fc, :], f"hb_{fc}") for fc in range(8)]
    aT = c.sb("aT", [128, 32, NT], BF16)
    aTch = [c.view(aT[:, f, :], f"aT_{f}") for f in range(32)]
    rl = [c.sb(f"rl{i}", [128, 2, NT], F32) for i in range(2)]
    sq_sb = [c.sb(f"sq{i}", [128, NT], F32) for i in range(2)]
    small = {k: c.sb(k, [128, NT], F32) for k in ("mean", "rstd", "msq")}
    banks = [c.ps(f"pb{i}", [128, 512], F32) for i in range(6)]
    psS = c.ps("psS", [128, 512], F32)
    psQ = c.ps("psQ", [128, 512], F32)
    bi = 0
    OT_v = OT.rearrange("(kc p) t -> p kc t", p=128)
    hT_v = hT.rearrange("(kc p) t -> p kc t", p=128)
    hO_v = hO.rearrange("(kc p) t -> p kc t", p=128)
    for tt in range(TOK // NT):
        h = hs[tt % 2]; hch = hchs[tt % 2]; ot = ots[tt % 2]
        sl = slice(tt * NT, (tt + 1) * NT)
        for k0_, nk_, src_ in ot_pieces(OT, tt * NT, NT):
            c.dma("sp", ot[:, k0_:k0_ + nk_, :], src_, out_t=ot)
        deps_ev = c.dma("sp", h[:], hT_v[:, :, sl], out_t=h)
        for v_ in hch:
            v_.w = deps_ev
            v_.r = []
        for fc in range(8):
            pb = banks[bi % 6]; bi += 1
            for kc in range(8):
                c.op("pe", nc.tensor.matmul, outs=[pb], ins=[wo_ch[kc], ot], out=pb[:, 0:NT],
                     lhsT=wo_b[:, kc, fc * 128:(fc + 1) * 128], rhs=ot[:, kc, :], start=(kc == 0), stop=(kc == 7))
            c.op("dve", nc.vector.scalar_tensor_tensor, outs=[hch[fc]], ins=[hch[fc], pb], out=h[:, fc, :],
                 in0=h[:, fc, :], scalar=ALPHA, in1=pb[:, 0:NT], op0=ALU.mult, op1=ALU.add)
        ln_fm(c, nc, h, hch, ones32, sq_sb, psS, psQ, g1, b1, small, NT, out_bf=hb, out_bf_ch=hbch)
        for f2 in range(16):
            pb = banks[bi % 6]; bi += 1
            for sub in range(2):
                f = f2 * 2 + sub
                for kc in range(8):
                    c.op("pe", nc.tensor.matmul, outs=[pb], ins=[w1_ch[kc], hbch[kc]],
                         out=pb[:, sub * NT:(sub + 1) * NT], lhsT=w1_b[:, kc, f * 128:(f + 1) * 128],
                         rhs=hb[:, kc, :], start=(kc == 0), stop=(kc == 7))
            r_ = rl[f2 % 2]
            c.op("act", nc.scalar.activation, outs=[r_], ins=[pb], out=r_[:].rearrange("p a n -> p (a n)"),
                 in_=pb[:, 0:2 * NT], func=AF.Relu)
            eng = ("dve", "pool")[f2 % 2]
            e_ = nc.vector if eng == "dve" else nc.gpsimd
            c.op(eng, e_.tensor_tensor, outs=[aTch[2 * f2], aTch[2 * f2 + 1]], ins=[r_],
                 out=aT[:, 2 * f2:2 * f2 + 2, :], in0=r_[:], in1=r_[:], op=ALU.mult)
        for fc in range(8):
            pb = banks[bi % 6]; bi += 1
            for f in range(32):
                c.op("pe", nc.tensor.matmul, outs=[pb], ins=[w2_ch[f], aTch[f]], out=pb[:, 0:NT],
                     lhsT=w2_b[:, f, fc * 128:(fc + 1) * 128], rhs=aT[:, f, :], start=(f == 0), stop=(f == 31))
            c.op("dve", nc.vector.scalar_tensor_tensor, outs=[hch[fc]], ins=[hch[fc], pb], out=h[:, fc, :],
                 in0=h[:, fc, :], scalar=ALPHA, in1=pb[:, 0:NT], op0=ALU.mult, op1=ALU.add)
        ln_fm(c, nc, h, hch, ones32, sq_sb, psS, psQ, g2, b2, small, NT)
        h.w = None
        h.r = []
        c._need("pool", [v_.w for v_ in hch])
        ev = c.dma("pool", hO_v[:, :, sl], h[:], in_t=h)
        for v_ in hch:
            v_.r.append(ev)
    c.finish("pool")
    c.close()
    return nc


_NC_CACHE = {}


def get_nc(key, fn, *a):
    if key not in _NC_CACHE:
        _NC_CACHE[key] = fn(*a)
    return _NC_CACHE[key]


def launch(nc, in_maps):
    res = run_bass_kernel_spmd(nc, in_maps, core_ids=list(range(NCORES)))
    return res.results


def consts():
    j = np.arange(128)[:, None, None]
    r = np.arange(4)[None, :, None]
    t = np.arange(512)[None, None, :]
    cs = {}
    cs["maskS"] = ((128 * r + j) < t).astype(NPBF)
    cs["maskC"] = ((128 * r + j) <= t).astype(NPBF)
    cs["tri"] = (np.arange(128)[:, None] >= np.arange(128)[None, :]).astype(NPBF)
    return cs


def to_fm(x):
    flat = np.ascontiguousarray(x).reshape(B * S, D)
    return [np.ascontiguousarray(flat[c * TOK:(c + 1) * TOK].T) for c in range(NCORES)]


def lnp_pack(g1, b1, g2, b2):
    return np.ascontiguousarray(np.stack([v.reshape(8, 128).T for v in (g1, b1, g2, b2)], axis=1)).astype(np.float32)


def gather_heads(per_core, key, feature_major):
    outs = []
    for c in range(NCORES):
        b, hh = c // 2, c % 2
        if feature_major:
            full = np.concatenate([per_core[2 * b][key], per_core[2 * b + 1][key]], axis=1)
            n = full.shape[0] // 2
            outs.append(np.ascontiguousarray(full[hh * n:(hh + 1) * n]))
        else:
            full = np.concatenate([per_core[2 * b][key], per_core[2 * b + 1][key]], axis=0)
            n = full.shape[1] // 2
            outs.append(np.ascontiguousarray(full[:, hh * n:(hh + 1) * n]))
    return outs


def scatter_OT(a_res):
    outs = []
    for c in range(NCORES):
        b, half = c // 2, c % 2
        full = np.concatenate([a_res[2 * b]["OT"], a_res[2 * b + 1]["OT"]], axis=0)
        outs.append(np.ascontiguousarray(full[:, half * TOK:(half + 1) * TOK]))
    return outs


def run_M(hT_list, OT_list, w_o, w1, w2, g1, b1, g2, b2):
    nc = get_nc("M", build_M)
    lnp = lnp_pack(g1, b1, g2, b2)
    res = launch(nc, [{"OT": OT_list[c], "hT": hT_list[c], "w_o": w_o, "w1": w1, "w2": w2, "lnp": lnp}
                      for c in range(NCORES)])
    return [r["hO"] for r in res]


def layer0(hT_list, p, cs):
    ncP = get_nc("P_sb", build_P_qkv, False, -0.125)
    pres = launch(ncP, [{"hT": hT_list[c], "w": p["l0_sb_w_qkv"]} for c in range(NCORES)])
    QT = gather_heads(pres, "QT", True)
    KT = gather_heads(pres, "KT", True)
    V = gather_heads(pres, "V", False)
    ncA = get_nc("A_sb", build_A_sb)
    ares = launch(ncA, [{"QT": QT[c], "KT": KT[c], "V": V[c], "maskS": cs["maskS"], "tri": cs["tri"]}
                        for c in range(NCORES)])
    OT = scatter_OT(ares)
    return run_M(hT_list, OT, p["l0_sb_w_o"], p["l0_mlp_w1"], p["l0_mlp_w2"], p["l0_ln1_g"], p["l0_ln1_b"],
                 p["l0_ln2_g"], p["l0_ln2_b"]), dict(pres=pres, ares=ares, OT=OT)


class SoftmaxRes:
    def __init__(self, c, nc, prefix=""):
        self.ps_s = [c.ps(f"{prefix}pss{i}", [128, 512], F32) for i in range(3)]
        self.ps_o = [c.ps(f"{prefix}pso{i}", [128, 512], F32) for i in range(2)]
        self.ps_b = c.ps(f"{prefix}psb", [128, 512], F32)
        self.p_sb = [c.sb(f"{prefix}p{i}", [128, 512], BF16) for i in range(5)]
        self.o32 = [c.sb(f"{prefix}o32_{i}", [64, 512], F32) for i in range(2)]
        self.rs = [c.sb(f"{prefix}rs{i}", [65, 512], F32) for i in range(2)]
        self.o_sb = [c.sb(f"{prefix}ob{i}", [64, 512], BF16) for i in range(2)]
        self.ones32 = c.sb(f"{prefix}ones32", [65, 64], F32)
        c.op("dve", nc.vector.memset, outs=[self.ones32], ap=self.ones32[:], constant=1.0)
        self.it = 0
        self.oi = 0


def softmax_chunk(c, nc, R, k_ts, q_ts, kt_sb, qt_sb, v_sb, KR, i, mask_t, kt_list, mask_of, extra_mm=None,
                  pre_hook=None):
    po = R.ps_o[R.oi % 2]; o32 = R.o32[R.oi % 2]; rs = R.rs[R.oi % 2]; ob = R.o_sb[R.oi % 2]
    R.oi += 1
    qap = qt_sb[0:KR, i * 512:(i + 1) * 512]
    LAG = 2
    N = len(kt_list)
    pend = []
    for n in range(N + LAG):
        if n < N:
            kt = kt_list[n]
            ps = R.ps_s[R.it % 3]; p = R.p_sb[R.it % len(R.p_sb)]
            R.it += 1
            c.op("pe", nc.tensor.matmul, outs=[ps], ins=list(k_ts) + list(q_ts), out=ps[:],
                 lhsT=kt_sb[0:KR, kt * 128:(kt + 1) * 128], rhs=qap, start=True, stop=(extra_mm is None))
            if extra_mm is not None:
                extra_mm(ps, kt)
            c.op("act", nc.scalar.activation, outs=[p], ins=[ps], out=p[:], in_=ps[:], func=AF.Exp)
            m = mask_of(kt)
            if m is not None:
                c.op("dve", nc.vector.tensor_tensor, outs=[p], ins=[p, mask_t], out=p[:], in0=p[:], in1=m,
                     op=ALU.mult)
            pend.append((n, kt, p))
            if pre_hook is not None:
                if n == min(LAG, N) - 1:
                    pre_hook[0]()
                if n == min(LAG + 6, N) - 1:
                    pre_hook[1]()
                    pre_hook = None
        if n >= LAG:
            m_, kt, p = pend.pop(0)
            c.op("pe", nc.tensor.matmul, outs=[po], ins=[v_sb, p], out=po[0:65, :], lhsT=v_sb[:, kt, 0:65],
                 rhs=p[:], start=(m_ == 0), stop=(m_ == N - 1))

    def fin_a():
        c.op("act", nc.scalar.copy, outs=[o32], ins=[po], out=o32[:], in_=po[0:64, :])
        c.op("dve", nc.vector.tensor_scalar, outs=[rs], ins=[po], out=rs[64:65, :], in0=po[64:65, :],
             scalar1=1e-30, scalar2=None, op0=ALU.max)
        c.op("dve", nc.vector.reciprocal, outs=[rs], ins=[rs], out=rs[64:65, :], in_=rs[64:65, :])

    def fin_b():
        c.op("pe", nc.tensor.matmul, outs=[R.ps_b], ins=[R.ones32, rs], out=R.ps_b[0:64, :],
             lhsT=R.ones32[64:65, 0:64], rhs=rs[64:65, :], start=True, stop=True)
    return o32, R.ps_b, ob, (fin_a, fin_b)


def build_A_soft(kind):
    nc = new_nc()
    KR = 96
    if kind == "moba":
        QT = din(nc, "QT", [512, S], BF16)
        KT = din(nc, "KT", [512, S], BF16)
        Eind = din(nc, "Eind", [32, S], BF16)
        ident = din(nc, "ident", [128, 128], F32)
    else:
        QT = din(nc, "QT", [8 * 96, S], BF16)
        KT = din(nc, "KNT", [512, S], BF16)
        KRT = din(nc, "KRT", [32, S], BF16)
    V = din(nc, "V", [S, 512], BF16)
    maskC = din(nc, "maskC", [128, 4, 512], BF16)
    OT = dout(nc, "OT", [512, S], BF16)
    c = make_ctx(nc)
    m_sb = c.sb("maskC", [128, 4, 512], BF16)
    c.dma("sp", m_sb[:], maskC, out_t=m_sb)
    R = SoftmaxRes(c, nc)
    qts = [c.sb(f"qt{i}", [KR, S], BF16) for i in range(2)]
    q_hi = [c.view(q[64:96, :], f"{q.name}_hi") for q in qts]
    kts = [c.sb(f"kt{i}", [KR, S], BF16) for i in range(2)]
    vs = [c.sb(f"v{i}", [128, 64, 65], BF16) for i in range(2)]
    for v in vs:
        c.op("pool", nc.gpsimd.memset, outs=[v], ap=v[:, :, 64:65], constant=1.0)
    for k in kts:
        if kind == "moba":
            c.dma("sp", k[64:96, :], Eind, out_t=k)
        else:
            c.dma("sp", k[64:96, :].rearrange("p (r t) -> p r t", r=2), fm_shared(KRT), out_t=k)
    if kind == "moba":
        id_sb = c.sb("ident", [128, 128], F32)
        c.dma("sp", id_sb[:], ident, out_t=id_sb)
        g_sb = c.sb("g_sb", [128, 32], F32)
        m8 = c.sb("m8", [128, 8], F32)
        nms = [c.sb(f"nm{i}", [128, 96], F32) for i in range(8)]
        for nm in nms:
            c.op("dve", nc.vector.memset, outs=[nm], ap=nm[:], constant=0.0)
        kms32 = c.sb("kms32", [64, 32], F32)
        kms = c.sb("kms", [64, 32], BF16)
        ps_g = c.ps("psg", [128, 512], F32)
        ps_t = c.ps("pst", [128, 512], F32)
    def load_head(h):
        qt = qts[h % 2]; kt_ = kts[h % 2]; v = vs[h % 2]
        c.dma("sp", kt_[0:64, :].rearrange("p (r t) -> p r t", r=2), fm_rows(KT, h * 64, (h + 1) * 64), out_t=kt_)
        if kind == "moba":
            c.dma("sp", qt[0:64, :].rearrange("p (r t) -> p r t", r=2), fm_rows(QT, h * 64, (h + 1) * 64), out_t=qt)
        else:
            c.dma("sp", qt[:].rearrange("p (r t) -> p r t", r=2), fm_rows(QT, h * 96, (h + 1) * 96), out_t=qt)
        for k0_, nk_, src_ in tm_pieces(V, h * 64, (h + 1) * 64):
            c.dma("sp", v[:, k0_:k0_ + nk_, 0:64], src_, out_t=v)

    def prepass(h):
        qt = qts[h % 2]; kt_ = kts[h % 2]; qhi = q_hi[h % 2]
        c.op("dve", nc.vector.memset, outs=[g_sb], ap=g_sb[:], constant=-1e30)
        c.op("dve", nc.vector.tensor_reduce, outs=[kms32], ins=[kt_], out=kms32[:],
             in_=kt_[0:64, :].rearrange("p (n k) -> p n k", k=256), axis=AX.X, op=ALU.add)
        c.op("dve", nc.vector.tensor_copy, outs=[kms], ins=[kms32], out=kms[:], in_=kms32[:])

        def flush(ch):
            for grp in range(4):
                nm = nms[(ch % 2) * 4 + grp]
                c.op("pe", nc.tensor.transpose, outs=[ps_t], ins=[nm, id_sb],
                     out=ps_t[0:96, grp * 128:(grp + 1) * 128], in_=nm[:], identity=id_sb[:])
            c.op("act", nc.scalar.copy, outs=[qhi], ins=[ps_t], out=qt[64:96, ch * 512:(ch + 1) * 512],
                 in_=ps_t[64:96, :])
        for ch in range(16):
            for grp in range(4):
                qi = ch * 4 + grp
                if qi // 2 > 3:
                    c.op("pe", nc.tensor.matmul, outs=[ps_g], ins=[qt, kms], out=ps_g[:, grp * 32:(grp + 1) * 32],
                         lhsT=qt[0:64, qi * 128:(qi + 1) * 128], rhs=kms[:], start=True, stop=True)
            if ch > 0:
                flush(ch - 1)
            for grp in range(4):
                qi = ch * 4 + grp
                own = qi // 2
                nm = nms[(ch % 2) * 4 + grp]
                if own > 3:
                    c.op("dve", nc.vector.tensor_copy, outs=[g_sb], ins=[ps_g], out=g_sb[:, 0:own],
                         in_=ps_g[:, grp * 32:grp * 32 + own])
                    c.op("dve", nc.vector.max, outs=[m8], ins=[g_sb], out=m8[:], in_=g_sb[:])
                    c.op("dve", nc.vector.tensor_scalar, outs=[nm], ins=[g_sb, m8], out=nm[:, 64:64 + own],
                         in0=g_sb[:, 0:own], scalar1=m8[:, 2:3], scalar2=-BIG, op0=ALU.is_lt, op1=ALU.mult)
                else:
                    if own > 0:
                        c.op("dve", nc.vector.memset, outs=[nm], ap=nm[:, 64:64 + own], constant=0.0)
                c.op("dve", nc.vector.memset, outs=[nm], ap=nm[:, 64 + own:65 + own], constant=0.0)
                if own < 31:
                    c.op("dve", nc.vector.memset, outs=[nm], ap=nm[:, 65 + own:96], constant=-BIG)
            yield
        flush(15)
        yield

    load_head(0)
    if kind == "moba":
        for _ in prepass(0):
            pass
    pending = [None]
    for h in range(8):
        qt = qts[h % 2]; kt_ = kts[h % 2]; v = vs[h % 2]; qhi = q_hi[h % 2]
        q_ts = [qt, qhi] if kind == "moba" else [qt]
        gen = None
        if h + 1 < 8:
            if kind == "moba":
                qts[(h + 1) % 2].r.extend(q_hi[(h + 1) % 2].r)
            load_head(h + 1)
            if kind == "moba":
                gen = prepass(h + 1)
        for i in range(S // 512):
            kl = list(range(0, 4 * i + 4))
            o32, pbc, ob, (fin_a, fin_b) = softmax_chunk(
                c, nc, R, [kt_], q_ts, kt_, qt, v, KR, i, m_sb, kl,
                lambda kt, i=i: (m_sb[:, kt - 4 * i, :] if kt >= 4 * i else None), pre_hook=pending[0])

            def fin2(o32=o32, pbc=pbc, ob=ob, fin_b=fin_b, h=h, i=i):
                fin_b()
                c.op("dve", nc.vector.tensor_tensor, outs=[ob], ins=[o32, pbc], out=ob[:], in0=o32[:],
                     in1=pbc[0:64, :], op=ALU.mult)
                c.dma("pool", OT[h * 64:(h + 1) * 64, i * 512:(i + 1) * 512], ob[:], in_t=ob)
            pending[0] = (fin_a, fin2)
            if gen is not None:
                next(gen, None)
        if gen is not None:
            for _ in gen:
                pass
    if pending[0] is not None:
        pending[0][0]()
        pending[0][1]()
    c.finish("pool")
    c.close()
    return nc


def rope_tables_fm(dim, rows):
    inv = 1.0 / (10000.0 ** (np.arange(0, dim, 2, dtype=np.float32) / dim))
    ang = np.arange(S, dtype=np.float32)[:, None] * inv[None, :]
    ang = np.concatenate([ang, ang], axis=-1)
    cos = np.cos(ang).astype(np.float32).T
    sin = np.sin(ang).astype(np.float32).T
    reps = rows // dim
    return np.ascontiguousarray(np.tile(cos, (reps, 1))), np.ascontiguousarray(np.tile(sin, (reps, 1)))


def layer1(hT_list, p, cs):
    ncP = get_nc("P_moba", build_P_qkv, True, 0.125)
    cosF, sinF = rope_tables_fm(64, 128)
    pres = launch(ncP, [{"hT": hT_list[c], "w": p["l1_moba_w_qkv"],
                         "cosT": np.ascontiguousarray(cosF[:, (c % 2) * TOK:(c % 2 + 1) * TOK]),
                         "sinT": np.ascontiguousarray(sinF[:, (c % 2) * TOK:(c % 2 + 1) * TOK])}
                        for c in range(NCORES)])
    QT = gather_heads(pres, "QT", True)
    KT = gather_heads(pres, "KT", True)
    V = gather_heads(pres, "V", False)
    ncA = get_nc("A_moba", build_A_soft, "moba")
    Eind = (np.arange(S)[None, :] // 256 == np.arange(32)[:, None]).astype(NPBF)
    ident = np.eye(128, dtype=np.float32)
    ares = launch(ncA, [{"QT": QT[c], "KT": KT[c], "V": V[c], "maskC": cs["maskC"], "Eind": Eind, "ident": ident}
                        for c in range(NCORES)])
    OT = scatter_OT(ares)
    return run_M(hT_list, OT, p["l1_moba_w_o"], p["l1_mlp_w1"], p["l1_mlp_w2"], p["l1_ln1_g"], p["l1_ln1_b"],
                 p["l1_ln2_g"], p["l1_ln2_b"]), dict(pres=pres, ares=ares, OT=OT)


def build_P_mla():
    nc = new_nc()
    hT = din(nc, "hT", [D, TOK], F32)
    w_in = din(nc, "w_in", [D, 416], F32)
    w_uq = din(nc, "w_uq", [256, 1536], F32)
    w_ukv = din(nc, "w_ukv", [128, 2048], F32)
    gq = din(nc, "gq", [128, 2], F32)
    gkv = din(nc, "gkv", [128, 1], F32)
    cos96 = din(nc, "cos96", [96, TOK], F32)
    sin96 = din(nc, "sin96", [96, TOK], F32)
    QT = dout(nc, "QT", [1536, TOK], BF16)
    KNT = dout(nc, "KNT", [D, TOK], BF16)
    KRT = dout(nc, "KRT", [32, TOK], BF16)
    V = dout(nc, "V", [TOK, D], BF16)
    c = make_ctx(nc)
    cast = Caster(c)
    NT = 512
    QSC = float(96 ** -0.5)
    stages = [c.sb(f"stg{i}", [128, 1024], F32) for i in range(2)]
    win_b, win_ch = load_weight_bf16(c, cast, w_in, D, 416, "win", stages)
    wuq_b, wuq_ch = load_weight_bf16(c, cast, w_uq, 256, 1536, "wuq", stages)
    wukv_b, wukv_ch = load_weight_bf16(c, cast, w_ukv, 128, 2048, "wukv", stages)
    winr = c.sb("winr", [128, 8, 32], BF16)
    for kc in range(8):
        c.op("dve", nc.vector.tensor_scalar, outs=[winr], ins=[win_ch[kc]], out=winr[:, kc, 0:16],
             in0=win_b[:, kc, 400:416], scalar1=-1.0, scalar2=None, op0=ALU.mult)
        c.op("dve", nc.vector.tensor_copy, outs=[winr], ins=[win_ch[kc]], out=winr[:, kc, 16:32],
             in_=win_b[:, kc, 384:400])
    wuqr = c.sb("wuqr", [128, 2, 1536], BF16)
    c.op("pool", nc.gpsimd.memset, outs=[wuqr], ap=wuqr[:], constant=0.0)
    for fc in range(2):
        src = wuq_b[:, fc, :].rearrange("p (h d) -> p h d", d=96)
        dst = wuqr[:, fc, :].rearrange("p (h d) -> p h d", d=96)
        c.op("dve", nc.vector.tensor_scalar, outs=[wuqr], ins=[wuq_ch[fc]], out=dst[:, :, 64:80],
             in0=src[:, :, 80:96], scalar1=-1.0, scalar2=None, op0=ALU.mult)
        c.op("dve", nc.vector.tensor_copy, outs=[wuqr], ins=[wuq_ch[fc]], out=dst[:, :, 80:96],
             in_=src[:, :, 64:80])
    gq_sb = c.sb("gq", [128, 2], F32); gkv_sb = c.sb("gkv", [128, 1], F32)
    c.dma("sp", gq_sb[:], gq, out_t=gq_sb)
    c.dma("sp", gkv_sb[:], gkv, out_t=gkv_sb)
    cos_sb = c.sb("cos96", [96, TOK], F32); sin_sb = c.sb("sin96", [96, TOK], F32)
    c.dma("sp", cos_sb[:], cos96, out_t=cos_sb)
    c.dma("sp", sin_sb[:], sin96, out_t=sin_sb)
    ones32 = c.sb("ones32", [128, 128], F32)
    c.op("dve", nc.vector.memset, outs=[ones32], ap=ones32[:], constant=1.0)
    hts = [c.sb(f"ht{i}", [128, 8, NT], F32) for i in range(2)]
    hb = c.sb("hb", [128, 8, NT], BF16)
    cq32 = c.sb("cq32", [128, 3, NT], F32)
    cqn = c.sb("cqn", [128, 3, NT], BF16)
    sq_sb = [c.sb(f"sq{i}", [128, NT], F32) for i in range(2)]
    rstd = [c.sb(f"rstd{i}", [128, NT], F32) for i in range(2)]
    t1 = [c.sb(f"t1_{i}", [96, NT], F32) for i in range(2)]
    t2 = [c.sb(f"t2_{i}", [96, NT], F32) for i in range(2)]
    q_sb = [c.sb(f"q_sb{i}", [96, NT], BF16) for i in range(2)]
    kn_sb = [c.sb(f"kn_sb{i}", [64, NT], BF16) for i in range(2)]
    kr_sb = c.sb("kr_sb", [32, NT], BF16)
    v_sb = c.sb("v_sb", [128, 4, D], BF16)
    banks = [c.ps(f"pb{i}", [128, 512], F32) for i in range(7)]
    psQ = c.ps("psQ", [128, 512], F32)
    bi = 0
    hT_v = hT.rearrange("(kc p) t -> p kc t", p=128)
    for tt in range(TOK // NT):
        sl = slice(tt * NT, (tt + 1) * NT)
        ht = hts[tt % 2]
        c.dma("sp", ht[:], hT_v[:, :, sl], out_t=ht)
        for kc in range(8):
            cast.copy(hb, hb[:, kc, :], ht, ht[:, kc, :], eng=("dve", "pool")[kc % 2])
        for fc in range(3):
            pb = banks[bi % 7]; bi += 1
            for kc in range(8):
                c.op("pe", nc.tensor.matmul, outs=[pb], ins=[win_ch[kc], hb], out=pb[:, 0:NT],
                     lhsT=win_b[:, kc, fc * 128:(fc + 1) * 128], rhs=hb[:, kc, :], start=(kc == 0), stop=(kc == 7))
            c.op("act", nc.scalar.copy, outs=[cq32], ins=[pb], out=cq32[:, fc, :], in_=pb[:, 0:NT])
        pb = banks[bi % 7]; bi += 1
        pr = banks[bi % 7]; bi += 1
        for kc in range(8):
            c.op("pe", nc.tensor.matmul, outs=[pb], ins=[win_ch[kc], hb], out=pb[0:32, 0:NT],
                 lhsT=win_b[:, kc, 384:416], rhs=hb[:, kc, :], start=(kc == 0), stop=(kc == 7))
        for kc in range(8):
            c.op("pe", nc.tensor.matmul, outs=[pr], ins=[winr, hb], out=pr[0:32, 0:NT],
                 lhsT=winr[:, kc, :], rhs=hb[:, kc, :], start=(kc == 0), stop=(kc == 7))
        a = t1[0]; b_ = t2[0]
        c.op("dve", nc.vector.tensor_tensor, outs=[a], ins=[pb, cos_sb], out=a[0:32, :], in0=pb[0:32, 0:NT],
             in1=cos_sb[64:96, sl], op=ALU.mult)
        c.op("dve", nc.vector.tensor_tensor, outs=[b_], ins=[pr, sin_sb], out=b_[0:32, :], in0=pr[0:32, 0:NT],
             in1=sin_sb[64:96, sl], op=ALU.mult)
        c.op("pool", nc.gpsimd.tensor_tensor, outs=[kr_sb], ins=[a, b_], out=kr_sb[:], in0=a[0:32, :],
             in1=b_[0:32, :], op=ALU.add)
        c.dma("pool", KRT[:, sl], kr_sb[:], in_t=kr_sb)
        for grp, (chs, gsb, n) in enumerate((((0, 1), gq_sb, 256), ((2,), gkv_sb, 128))):
            rs_ = rstd[grp]
            for ci, ch in enumerate(chs):
                sq = sq_sb[ci % 2]
                c.op("act", nc.scalar.activation, outs=[sq], ins=[cq32], out=sq[:], in_=cq32[:, ch, :],
                     func=AF.Square)
                c.op("pe", nc.tensor.matmul, outs=[psQ], ins=[ones32, sq], out=psQ[:, 0:NT], lhsT=ones32[:],
                     rhs=sq[:], start=(ci == 0), stop=(ci == len(chs) - 1))
            c.op("act", nc.scalar.activation, outs=[rs_], ins=[psQ], out=rs_[:], in_=psQ[:, 0:NT], func=AF.Sqrt,
                 scale=1.0 / n, bias=RMS_EPS)
            c.op("dve", nc.vector.reciprocal, outs=[rs_], ins=[rs_], out=rs_[:], in_=rs_[:])
            for ci, ch in enumerate(chs):
                c.op("dve", nc.vector.scalar_tensor_tensor, outs=[cqn], ins=[cq32, gsb, rs_], out=cqn[:, ch, :],
                     in0=cq32[:, ch, :], scalar=gsb[:, ci:ci + 1], in1=rs_[:], op0=ALU.mult, op1=ALU.mult)
        for h in range(16):
            pb = banks[bi % 7]; bi += 1
            pr = banks[bi % 7]; bi += 1
            for fc in range(2):
                c.op("pe", nc.tensor.matmul, outs=[pb], ins=[wuq_ch[fc], cqn], out=pb[0:96, 0:NT],
                     lhsT=wuq_b[:, fc, h * 96:(h + 1) * 96], rhs=cqn[:, fc, :], start=(fc == 0), stop=(fc == 1))
            for fc in range(2):
                c.op("pe", nc.tensor.matmul, outs=[pr], ins=[wuqr, cqn], out=pr[0:96, 0:NT],
                     lhsT=wuqr[:, fc, h * 96:(h + 1) * 96], rhs=cqn[:, fc, :], start=(fc == 0), stop=(fc == 1))
            a = t1[h % 2]; b_ = t2[h % 2]; qs = q_sb[h % 2]
            c.op("dve", nc.vector.tensor_tensor, outs=[a], ins=[pb, cos_sb], out=a[:], in0=pb[0:96, 0:NT],
                 in1=cos_sb[:, sl], op=ALU.mult)
            c.op("dve", nc.vector.tensor_tensor, outs=[b_], ins=[pr, sin_sb], out=b_[:], in0=pr[0:96, 0:NT],
                 in1=sin_sb[:, sl], op=ALU.mult)
            c.op("pool", nc.gpsimd.tensor_tensor, outs=[a], ins=[a, b_], out=a[:], in0=a[:], in1=b_[:], op=ALU.add)
            c.op("act", nc.scalar.mul, outs=[qs], ins=[a], out=qs[:], in_=a[:], mul=QSC)
            c.dma("pool", QT[h * 96:(h + 1) * 96, sl], qs[:], in_t=qs)
        wk_v = wukv_b[:, 0, :].rearrange("p (h two d) -> p h two d", two=2, d=64)
        for h in range(16):
            pb = banks[bi % 7]; bi += 1
            c.op("pe", nc.tensor.matmul, outs=[pb], ins=[wukv_ch[0], cqn], out=pb[0:64, 0:NT],
                 lhsT=wk_v[:, h, 0, :], rhs=cqn[:, 2, :], start=True, stop=True)
            ks = kn_sb[h % 2]
            if h % 2 == 0:
                c.op("act", nc.scalar.copy, outs=[ks], ins=[pb], out=ks[:], in_=pb[0:64, 0:NT])
            else:
                c.op("dve", nc.vector.tensor_copy, outs=[ks], ins=[pb], out=ks[:], in_=pb[0:64, 0:NT])
            c.dma("pool", KNT[h * 64:(h + 1) * 64, sl], ks[:], in_t=ks)
        for j in range(4):
            for half in range(2):
                pb = banks[bi % 7]; bi += 1
                c.op("pe", nc.tensor.matmul, outs=[pb], ins=[wukv_ch[0], cqn], out=pb[:],
                     lhsT=cqn[:, 2, j * 128:(j + 1) * 128], rhs=wk_v[:, half * 8:(half + 1) * 8, 1, :],
                     start=True, stop=True)
                if half == 0:
                    c.op("act", nc.scalar.copy, outs=[v_sb], ins=[pb], out=v_sb[:, j, 0:512], in_=pb[:])
                else:
                    c.op("dve", nc.vector.tensor_copy, outs=[v_sb], ins=[pb], out=v_sb[:, j, 512:1024], in_=pb[:])
        V_v = V.rearrange("(j p) n -> p j n", p=128)
        c.dma("pool", V_v[:, tt * 4:(tt + 1) * 4, :], v_sb[:], in_t=v_sb)
    c.finish("pool")
    c.close()
    return nc


def layer2(hT_list, p, cs):
    ncP = get_nc("P_mla", build_P_mla)
    cos32, sin32 = rope_tables_fm(32, 32)
    cos96 = np.concatenate([np.ones((64, S), np.float32), cos32], axis=0)
    sin96 = np.concatenate([np.zeros((64, S), np.float32), sin32], axis=0)
    gq = np.ascontiguousarray(p["l2_mla_q_norm"].reshape(2, 128).T).astype(np.float32)
    gkv = np.ascontiguousarray(p["l2_mla_kv_norm"].reshape(1, 128).T).astype(np.float32)
    pres = launch(ncP, [{"hT": hT_list[c], "w_in": p["l2_mla_w_in"], "w_uq": p["l2_mla_w_uq"],
                         "w_ukv": p["l2_mla_w_ukv"], "gq": gq, "gkv": gkv,
                         "cos96": np.ascontiguousarray(cos96[:, (c % 2) * TOK:(c % 2 + 1) * TOK]),
                         "sin96": np.ascontiguousarray(sin96[:, (c % 2) * TOK:(c % 2 + 1) * TOK])}
                        for c in range(NCORES)])
    QT = gather_heads(pres, "QT", True)
    KNT = gather_heads(pres, "KNT", True)
    V = gather_heads(pres, "V", False)
    KRT = [np.ascontiguousarray(np.concatenate([pres[2 * (c // 2)]["KRT"], pres[2 * (c // 2) + 1]["KRT"]], axis=1))
           for c in range(NCORES)]
    ncA = get_nc("A_mla", build_A_soft, "mla")
    ares = launch(ncA, [{"QT": QT[c], "KNT": KNT[c], "KRT": KRT[c], "V": V[c], "maskC": cs["maskC"]}
                        for c in range(NCORES)])
    OT = scatter_OT(ares)
    return run_M(hT_list, OT, p["l2_mla_w_o"], p["l2_mlp_w1"], p["l2_mlp_w2"], p["l2_ln1_g"], p["l2_ln1_b"],
                 p["l2_ln2_g"], p["l2_ln2_b"]), dict(pres=pres, ares=ares, OT=OT)


NSA_IN = 2608


def build_P_nsa():
    nc = new_nc()
    hT = din(nc, "hT", [D, TOK], F32)
    w = din(nc, "w", [D, NSA_IN], F32)
    cosT = din(nc, "cosT", [128, TOK], F32)
    sinT = din(nc, "sinT", [128, TOK], F32)
    QT = dout(nc, "QT", [D, TOK], BF16)
    KcT = dout(nc, "KcT", [256, TOK], BF16)
    VcT = dout(nc, "VcT", [256, TOK], BF16)
    KsT = dout(nc, "KsT", [256, TOK], BF16)
    KwT = dout(nc, "KwT", [256, TOK], BF16)
    Vs = dout(nc, "Vs", [TOK, 256], BF16)
    Vw = dout(nc, "Vw", [TOK, 256], BF16)
    GT = dout(nc, "GT", [64, TOK], F32)
    c = make_ctx(nc)
    cast = Caster(c)
    NT = 512
    stages = [c.sb(f"stg{i}", [128, 1024], F32) for i in range(2)]
    wb, wch = load_weight_bf16(c, cast, w, D, NSA_IN, "wb", stages)
    roped = [(0, 8, QT, 0.125), (1024, 2, KcT, 1.0), (1536, 2, KsT, 1.0), (2048, 2, KwT, 1.0)]
    wr = c.sb("wrot", [128, 8, 14 * 128], BF16)
    wrch = [c.view(wr[:, kc, :], f"wrot_{kc}") for kc in range(8)]
    rcol = {}
    o = 0
    for (c0, nch, _, _) in roped:
        rcol[c0] = o
        for kc in range(8):
            src = wb[:, kc, c0:c0 + nch * 128].rearrange("p (h two d) -> p h two d", two=2, d=32)
            dst = wr[:, kc, o:o + nch * 128].rearrange("p (h two d) -> p h two d", two=2, d=32)
            c.op("dve", nc.vector.tensor_scalar, outs=[wrch[kc]], ins=[wch[kc]],
                 out=dst[:, :, 0, :], in0=src[:, :, 1, :], scalar1=-1.0, scalar2=None, op0=ALU.mult)
            c.op("pool", nc.gpsimd.tensor_copy, outs=[wrch[kc]], ins=[wch[kc]],
                 out=dst[:, :, 1, :], in_=src[:, :, 0, :])
        o += nch * 128
    cos_sb = c.sb("cos_sb", [128, TOK], F32)
    sin_sb = c.sb("sin_sb", [128, TOK], F32)
    c.dma("sp", cos_sb[:], cosT, out_t=cos_sb)
    c.dma("sp", sin_sb[:], sinT, out_t=sin_sb)
    hts = [c.sb(f"ht{i}", [128, 8, NT], F32) for i in range(2)]
    hb = c.sb("hb", [128, 8, NT], BF16)
    osb = [c.sb(f"osb{i}", [128, NT], BF16) for i in range(3)]
    t1 = [c.sb(f"t1_{i}", [128, NT], F32) for i in range(2)]
    t2 = [c.sb(f"t2_{i}", [128, NT], F32) for i in range(2)]
    v_sb = c.sb("v_sb", [128, 4, 512], BF16)
    g_sb = c.sb("g_sb", [64, NT], F32)
    banks = [c.ps(f"pb{i}", [128, 512], F32) for i in range(8)]
    bi = 0
    oi = 0
    hT_v = hT.rearrange("(kc p) t -> p kc t", p=128)
    for tt in range(TOK // NT):
        sl = slice(tt * NT, (tt + 1) * NT)
        ht = hts[tt % 2]
        c.dma("sp", ht[:], hT_v[:, :, sl], out_t=ht)
        for kc in range(8):
            cast.copy(hb, hb[:, kc, :], ht, ht[:, kc, :], eng=("dve", "pool")[kc % 2])
        for (c0, nch, out_d, sc) in roped:
            for fc in range(nch):
                col = c0 + fc * 128
                rc = rcol[c0] + fc * 128
                pb = banks[bi % 8]; bi += 1
                pr = banks[bi % 8]; bi += 1
                for kc in range(8):
                    c.op("pe", nc.tensor.matmul, outs=[pb], ins=[wch[kc], hb], out=pb[:, 0:NT],
                         lhsT=wb[:, kc, col:col + 128], rhs=hb[:, kc, :], start=(kc == 0), stop=(kc == 7))
                for kc in range(8):
                    c.op("pe", nc.tensor.matmul, outs=[pr], ins=[wrch[kc], hb], out=pr[:, 0:NT],
                         lhsT=wr[:, kc, rc:rc + 128], rhs=hb[:, kc, :], start=(kc == 0), stop=(kc == 7))
                a = t1[oi % 2]; b_ = t2[oi % 2]; ob = osb[oi % 3]; oi += 1
                c.op("dve", nc.vector.tensor_tensor, outs=[a], ins=[pb, cos_sb], out=a[:], in0=pb[:, 0:NT],
                     in1=cos_sb[:, sl], op=ALU.mult)
                c.op("dve", nc.vector.tensor_tensor, outs=[b_], ins=[pr, sin_sb], out=b_[:], in0=pr[:, 0:NT],
                     in1=sin_sb[:, sl], op=ALU.mult)
                c.op("pool", nc.gpsimd.tensor_tensor, outs=[a], ins=[a, b_], out=a[:], in0=a[:], in1=b_[:],
                     op=ALU.add)
                c.op("act", nc.scalar.mul, outs=[ob], ins=[a], out=ob[:], in_=a[:], mul=sc)
                c.dma("pool", out_d[fc * 128:(fc + 1) * 128, sl], ob[:], in_t=ob)
        for fc in range(2):
            col = 1280 + fc * 128
            pb = banks[bi % 8]; bi += 1
            for kc in range(8):
                c.op("pe", nc.tensor.matmul, outs=[pb], ins=[wch[kc], hb], out=pb[:, 0:NT],
                     lhsT=wb[:, kc, col:col + 128], rhs=hb[:, kc, :], start=(kc == 0), stop=(kc == 7))
            ob = osb[oi % 3]; oi += 1
            c.op("act", nc.scalar.copy, outs=[ob], ins=[pb], out=ob[:], in_=pb[:, 0:NT])
            c.dma("pool", VcT[fc * 128:(fc + 1) * 128, sl], ob[:], in_t=ob)
        import os as _os
        pb = banks[bi % 8]; bi += 1
        for kc in range(8):
            if _os.environ.get("NOGATE"): break
            c.op("pe", nc.tensor.matmul, outs=[pb], ins=[wch[kc], hb], out=pb[0:64, 0:NT],
                 lhsT=wb[:, kc, 2544:2608], rhs=hb[:, kc, :], start=(kc == 0), stop=(kc == 7))
        if not _os.environ.get("NOGATE"):
            c.op("act", nc.scalar.activation, outs=[g_sb], ins=[pb], out=g_sb[:], in_=pb[0:64, 0:NT], func=AF.Sigmoid)
            c.dma("pool", GT[:, sl], g_sb[:], in_t=g_sb)
        for j in range(4):
            pb = banks[bi % 8]; bi += 1
            for hi, col in enumerate((1792, 2304)):
                for kc in range(8):
                    c.op("pe", nc.tensor.matmul, outs=[pb], ins=[wch[kc], hb], out=pb[:, hi * 256:(hi + 1) * 256],
                         lhsT=hb[:, kc, j * 128:(j + 1) * 128], rhs=wb[:, kc, col:col + 256],
                         start=(kc == 0), stop=(kc == 7))
            c.op("act", nc.scalar.copy, outs=[v_sb], ins=[pb], out=v_sb[:, j, :], in_=pb[:])
        Vs_v = Vs.rearrange("(j p) n -> p j n", p=128)
        Vw_v = Vw.rearrange("(j p) n -> p j n", p=128)
        c.dma("pool", Vs_v[:, tt * 4:(tt + 1) * 4, :], v_sb[:, :, 0:256], in_t=v_sb)
        c.dma("pool", Vw_v[:, tt * 4:(tt + 1) * 4, :], v_sb[:, :, 256:512], in_t=v_sb)
    c.finish("pool")
    c.close()
    return nc


GELU_C = 1.5957691216057308


def build_A_nsa():
    nc = new_nc()
    QT = din(nc, "QT", [512, S], BF16)
    KcT = din(nc, "KcT", [128, S], BF16)
    VcT = din(nc, "VcT", [128, S], BF16)
    KsT = din(nc, "KsT", [128, S], BF16)
    KwT = din(nc, "KwT", [128, S], BF16)
    Vs = din(nc, "Vs", [S, 128], BF16)
    Vw = din(nc, "Vw", [S, 128], BF16)
    GT = din(nc, "GT", [32, S], F32)
    posT = din(nc, "posT", [64, 2, 32], F32)
    w1k = din(nc, "w1k", [2048, 256], F32)
    w1v = din(nc, "w1v", [2048, 256], F32)
    w2 = din(nc, "w2", [128, 2, 2, 64], F32)
    maskC = din(nc, "maskC", [128, 4, 512], BF16)
    maskL = din(nc, "maskL", [128, 4, 512], BF16)
    cmask = din(nc, "cmask", [128, 5, 512], BF16)
    ovl = din(nc, "ovl", [128, 4, 128], BF16)
    Eind = din(nc, "Eind", [128, S], BF16)
    JC = din(nc, "JC", [128, 128], F32)
    CB = din(nc, "CB", [128, 128], F32)
    ident = din(nc, "ident", [128, 128], F32)
    SelG = din(nc, "SelG", [32, 24 * 64], F32)
    OT = dout(nc, "OT", [512, S], BF16)
    c = make_ctx(nc)
    cast = Caster(c, engines=("dve", "pool"))

    def const(name, ap, shape, dt):
        t = c.sb(name, shape, dt)
        c.dma("sp", t[:], ap, out_t=t)
        return t
    mC = const("maskC", maskC, [128, 4, 512], BF16)
    mL = const("maskL", maskL, [128, 4, 512], BF16)
    cm = const("cmask", cmask, [128, 5, 512], BF16)
    ovl_sb = const("ovl", ovl, [128, 4, 128], BF16)
    E_sb = const("Eind", Eind, [128, S], BF16)
    JC_sb = const("JC", JC, [128, 128], F32)
    CB_sb = const("CB", CB, [128, 128], F32)
    id_sb = const("ident", ident, [128, 128], F32)
    SelG_sb = const("SelG", SelG, [32, 24 * 64], F32)
    posT32 = const("posT", posT, [64, 2, 32], F32)
    w2_32 = const("w2", w2, [128, 2, 2, 64], F32)
    posTb = c.sb("posTb", [64, 2, 32], BF16)
    c.op("dve", nc.vector.tensor_copy, outs=[posTb], ins=[posT32], out=posTb[:], in_=posT32[:])
    w2b = c.sb("w2b", [128, 2, 2, 64], BF16)
    c.op("dve", nc.vector.tensor_copy, outs=[w2b], ins=[w2_32], out=w2b[:], in_=w2_32[:])
    stg = [c.sb(f"stg{i}", [64, 4, 256], F32) for i in range(2)]
    W1 = []
    si = 0
    for nm_, wd in (("w1k", w1k), ("w1v", w1v)):
        t = c.sb(nm_, [64, 32, 256], BF16)
        wv = wd.rearrange("(l d) n -> d l n", d=64)
        for l0 in range(0, 32, 4):
            st = stg[si % 2]; si += 1
            c.dma("sp", st[:], wv[:, l0:l0 + 4, :], out_t=st)
            cast.copy(t, t[:, l0:l0 + 4, :], st, st[:])
        W1.append(t)
    ones32 = c.sb("ones32", [65, 64], F32)
    c.op("dve", nc.vector.memset, outs=[ones32], ap=ones32[:], constant=1.0)

    kcv_sb = c.sb("kcv", [64, S], BF16)
    ks_sb = c.sb("ksT", [64, S], BF16)
    kw_sb = c.sb("kwT", [64, S], BF16)
    vs_sb = c.sb("vs", [128, 64, 65], BF16)
    vw_sb = c.sb("vw", [128, 64, 65], BF16)
    vc_sb = c.sb("vc", [128, 4, 65], BF16)
    kcT_sb = c.sb("kcT", [64, 512], BF16)
    for v_ in (vs_sb, vw_sb):
        c.op("pool", nc.gpsimd.memset, outs=[v_], ap=v_[:, :, 64:65], constant=1.0)
    c.op("pool", nc.gpsimd.memset, outs=[vc_sb], ap=vc_sb[:, :, 64:65], constant=1.0)
    c.op("pool", nc.gpsimd.memset, outs=[kcT_sb], ap=kcT_sb[:], constant=0.0)
    b1_sb = c.sb("b1", [128, 2], F32)
    x32 = [c.sb(f"x32_{i}", [128, 512], F32) for i in range(2)]
    u32 = [c.sb(f"u32_{i}", [128, 512], F32) for i in range(2)]
    gel = c.sb("gel", [128, 2, 512], BF16)
    c.op("pool", nc.gpsimd.memset, outs=[gel], ap=gel[:], constant=0.0)
    qch = [c.sb(f"qch{i}", [64, 4, 512], BF16) for i in range(2)]
    gch = [c.sb(f"gch{i}", [32, 512], F32) for i in range(2)]
    for gc_ in gch:
        c.op("pool", nc.gpsimd.memset, outs=[gc_], ap=gc_[:], constant=0.0)
    pcs = [c.sb(f"pc{i}", [128, 512], BF16) for i in range(4)]
    p_sb = [c.sb(f"p{i}", [128, 512], BF16) for i in range(5)]
    o32 = [c.sb(f"o32_{i}", [64, 512], F32) for i in range(2)]
    rs = [c.sb(f"rs{i}", [65, 512], F32) for i in range(2)]
    on = [c.sb(f"on{i}", [64, 512], F32) for i in range(2)]
    acc = c.sb("acc", [64, 512], F32)
    stash = [c.sb(f"stash{i}", [64, 512], F32) for i in range(4)]
    ob = [c.sb(f"ob{i}", [64, 512], BF16) for i in range(2)]
    impacc = c.sb("impacc", [128, 4, 128], F32)
    rsT_sb = c.sb("rsT", [128, 4], F32)
    f1 = c.sb("f1", [128, 128], F32)
    pen = c.sb("pen", [128, 128], F32)
    imp3 = c.sb("imp3", [128, 128], F32)
    imp4 = c.sb("imp4", [128, 128], F32)
    m8a = c.sb("m8a", [128, 8], F32)
    m8b = c.sb("m8b", [128, 8], F32)
    nm = [c.sb(f"nm{i}", [128, 128], F32) for i in range(2)]
    nmT = c.sb("nmT", [128, 512], BF16)

    ps_s = [c.ps(f"pss{i}", [128, 512], F32) for i in range(3)]
    ps_o = [c.ps(f"pso{i}", [128, 512], F32) for i in range(2)]
    ps_b = c.ps("psb", [128, 512], F32)
    ps_imp = c.ps("psimp", [128, 512], F32)
    ps_t = c.ps("pst", [128, 512], F32)
    ps_g = ps_t
    cnt = {"s": 0, "p": 0, "o": 0}

    Vs_v = Vs.rearrange("(kt p) n -> p kt n", p=128)
    Vw_v = Vw.rearrange("(kt p) n -> p kt n", p=128)

    def attend(k_t, k_ap_of, qap, q_t, v_sb, v_ap_of, kt_list, mask_of, extra_mm=None, keep=None, pre_hook=None,
               defer=False):
        po = ps_o[cnt["o"] % 2]; o3 = o32[cnt["o"] % 2]; rs_ = rs[cnt["o"] % 2]
        cnt["o"] += 1
        LAG = 2
        N = len(kt_list)
        pend = []
        for n in range(N + LAG):
            if n < N:
                kt = kt_list[n]
                ps = ps_s[cnt["s"] % len(ps_s)]; cnt["s"] += 1
                if keep is not None:
                    p = keep[n]
                else:
                    p = p_sb[cnt["p"] % len(p_sb)]; cnt["p"] += 1
                c.op("pe", nc.tensor.matmul, outs=[ps], ins=[k_t, q_t], out=ps[:], lhsT=k_ap_of(kt), rhs=qap,
                     start=True, stop=(extra_mm is None))
                if extra_mm is not None:
                    extra_mm(ps, kt)
                c.op("act", nc.scalar.activation, outs=[p], ins=[ps], out=p[:], in_=ps[:], func=AF.Exp)
                m = mask_of(kt)
                if m is not None:
                    mt, map_ = m
                    c.op("dve", nc.vector.tensor_tensor, outs=[p], ins=[p, mt], out=p[:], in0=p[:], in1=map_,
                         op=ALU.mult)
                pend.append((n, kt, p))
                if pre_hook is not None:
                    if n == min(LAG, N) - 1:
                        pre_hook[0]()
                    if n == min(LAG + 6, N) - 1:
                        pre_hook[1]()
                        pre_hook = None
            if n >= LAG:
                m_, kt, p = pend.pop(0)
                c.op("pe", nc.tensor.matmul, outs=[po], ins=[v_sb, p], out=po[0:65, :], lhsT=v_ap_of(kt), rhs=p[:],
                     start=(m_ == 0), stop=(m_ == N - 1))

        def tail_a():
            c.op("act", nc.scalar.copy, outs=[o3], ins=[po], out=o3[:], in_=po[0:64, :])
            c.op("dve", nc.vector.tensor_scalar, outs=[rs_], ins=[po], out=rs_[64:65, :], in0=po[64:65, :],
                 scalar1=1e-30, scalar2=None, op0=ALU.max)
            c.op("dve", nc.vector.reciprocal, outs=[rs_], ins=[rs_], out=rs_[64:65, :], in_=rs_[64:65, :])

        def tail_b():
            c.op("pe", nc.tensor.matmul, outs=[ps_b], ins=[ones32, rs_], out=ps_b[0:64, :],
                 lhsT=ones32[64:65, 0:64], rhs=rs_[64:65, :], start=True, stop=True)
        if defer:
            return o3, rs_, (tail_a, tail_b)
        tail_a()
        tail_b()
        return o3, rs_

    oi = 0
    for g in range(2):
        c.dma("sp", ks_sb[:].rearrange("p (r t) -> p r t", r=2), fm_rows(KsT, g * 64, (g + 1) * 64), out_t=ks_sb)
        c.dma("sp", kw_sb[:].rearrange("p (r t) -> p r t", r=2), fm_rows(KwT, g * 64, (g + 1) * 64), out_t=kw_sb)
        for k0_, nk_, src_ in tm_pieces(Vs, g * 64, (g + 1) * 64):
            c.dma("sp", vs_sb[:, k0_:k0_ + nk_, 0:64], src_, out_t=vs_sb)
        for k0_, nk_, src_ in tm_pieces(Vw, g * 64, (g + 1) * 64):
            c.dma("sp", vw_sb[:, k0_:k0_ + nk_, 0:64], src_, out_t=vw_sb)
        for kv in range(2):
            src = (KcT, VcT)[kv]
            c.dma("sp", kcv_sb[:].rearrange("p (r t) -> p r t", r=2), fm_rows(src, g * 64, (g + 1) * 64), out_t=kcv_sb)
            W = W1[kv]
            for half in range(2):
                pb = ps_s[half]
                for l in range(32):
                    c.op("pe", nc.tensor.matmul, outs=[pb], ins=[W, posTb], out=pb[:, 0:1],
                         lhsT=W[:, l, half * 128:(half + 1) * 128], rhs=posTb[:, kv, l:l + 1],
                         start=(l == 0), stop=(l == 31))
                c.op("act", nc.scalar.copy, outs=[b1_sb], ins=[pb], out=b1_sb[:, half:half + 1], in_=pb[:, 0:1])
            for half in range(2):
                pb = ps_s[half]
                for l in range(32):
                    c.op("pe", nc.tensor.matmul, outs=[pb], ins=[W, kcv_sb], out=pb[:, 0:511],
                         lhsT=W[:, l, half * 128:(half + 1) * 128],
                         rhs=kcv_sb[:, l:l + 16 * 510 + 1:16], start=(l == 0), stop=(l == 31))
                x = x32[half]; u = u32[half]
                c.op("act", nc.scalar.activation, outs=[x], ins=[pb, b1_sb], out=x[:, 0:511], in_=pb[:, 0:511],
                     func=AF.Identity, bias=b1_sb[:, half:half + 1])
                c.op("dve", nc.vector.tensor_tensor, outs=[u], ins=[x], out=u[:, 0:511], in0=x[:, 0:511],
                     in1=x[:, 0:511], op=ALU.mult)
                c.op("dve", nc.vector.tensor_scalar, outs=[u], ins=[u], out=u[:, 0:511], in0=u[:, 0:511],
                     scalar1=0.044715, scalar2=1.0, op0=ALU.mult, op1=ALU.add)
                c.op("dve", nc.vector.tensor_tensor, outs=[u], ins=[u, x], out=u[:, 0:511], in0=u[:, 0:511],
                     in1=x[:, 0:511], op=ALU.mult)
                c.op("act", nc.scalar.activation, outs=[u], ins=[u], out=u[:, 0:511], in_=u[:, 0:511],
                     func=AF.Sigmoid, scale=GELU_C)
                c.op("dve", nc.vector.tensor_tensor, outs=[gel], ins=[u, x], out=gel[:, half, 0:511],
                     in0=u[:, 0:511], in1=x[:, 0:511], op=ALU.mult)
            if kv == 0:
                pb = ps_s[0]
                for half in range(2):
                    c.op("pe", nc.tensor.matmul, outs=[pb], ins=[w2b, gel], out=pb[0:64, 0:511],
                         lhsT=w2b[:, 0, half, :], rhs=gel[:, half, 0:511], start=(half == 0), stop=(half == 1))
                c.op("act", nc.scalar.copy, outs=[kcT_sb], ins=[pb], out=kcT_sb[:, 0:511], in_=pb[0:64, 0:511])
            else:
                for nt in range(4):
                    pb = ps_s[nt % 2]
                    for half in range(2):
                        c.op("pe", nc.tensor.matmul, outs=[pb], ins=[w2b, gel], out=pb[:, 0:64],
                             lhsT=gel[:, half, nt * 128:(nt + 1) * 128], rhs=w2b[:, 1, half, :],
                             start=(half == 0), stop=(half == 1))
                    c.op("act", nc.scalar.copy, outs=[vc_sb], ins=[pb], out=vc_sb[:, nt, 0:64], in_=pb[:, 0:64])
        for i in range(S // 512):
            qc = qch[i % 2]; gc = gch[i % 2]
            for r_ in range(4):
                c.dma("sp", qc[:, r_, :], fm_chunk(QT, g * 256 + r_ * 64, g * 256 + (r_ + 1) * 64, i), out_t=qc)
            c.dma("sp", gc[0:24, :], gt_chunk(GT, i), out_t=gc)
            n_ct = (32 * i + 30) // 128 + 1
            cres = []
            for r in range(4):
                def cmask_of(nt, i=i):
                    dlt = 512 * i - 2048 * nt
                    if dlt >= 2063:
                        return None
                    return (cm, cm[:, dlt // 512, :])
                o3, rs_ = attend(kcT_sb, lambda nt: kcT_sb[:, nt * 128:(nt + 1) * 128], qc[:, r, :], qc,
                                 vc_sb, lambda nt: vc_sb[:, nt, 0:65], list(range(n_ct)), cmask_of,
                                 keep=pcs)
                cres.append(None)
                hrow = (g * 4 + r) * 3
                c.op("dve", nc.vector.tensor_tensor, outs=[on[0]], ins=[o3, ps_b], out=on[0][:], in0=o3[:],
                     in1=ps_b[0:64, :], op=ALU.mult)
                c.op("pe", nc.tensor.matmul, outs=[ps_g], ins=[SelG_sb, gc], out=ps_g[0:64, :],
                     lhsT=SelG_sb[:, hrow * 64:(hrow + 1) * 64], rhs=gc[:], start=True, stop=True)
                st = stash[r]
                c.op("dve", nc.vector.tensor_tensor, outs=[st], ins=[on[0], ps_g], out=st[:], in0=on[0][:],
                     in1=ps_g[0:64, :], op=ALU.mult)
                for j in range(4):
                    for nt in range(n_ct):
                        c.op("pe", nc.tensor.matmul, outs=[ps_imp], ins=[pcs[nt], ovl_sb],
                             out=ps_imp[:, j * 128:(j + 1) * 128], lhsT=pcs[nt][:, j * 128:(j + 1) * 128],
                             rhs=ovl_sb[:, nt, :], start=(nt == 0), stop=(nt == n_ct - 1))
                for j in range(4):
                    c.op("pe", nc.tensor.matmul, outs=[ps_t], ins=[rs_, ones32], out=ps_t[:, j:j + 1],
                         lhsT=rs_[64:65, j * 128:(j + 1) * 128], rhs=ones32[64:65, 0:1], start=True, stop=True)
                c.op("act", nc.scalar.copy, outs=[rsT_sb], ins=[ps_t], out=rsT_sb[:], in_=ps_t[:, 0:4])
                for j in range(4):
                    if r == 0:
                        c.op("dve", nc.vector.tensor_scalar, outs=[impacc], ins=[ps_imp, rsT_sb],
                             out=impacc[:, j, :], in0=ps_imp[:, j * 128:(j + 1) * 128], scalar1=rsT_sb[:, j:j + 1],
                             scalar2=None, op0=ALU.mult)
                    else:
                        c.op("dve", nc.vector.scalar_tensor_tensor, outs=[impacc], ins=[ps_imp, rsT_sb, impacc],
                             out=impacc[:, j, :], in0=ps_imp[:, j * 128:(j + 1) * 128], scalar=rsT_sb[:, j:j + 1],
                             in1=impacc[:, j, :], op0=ALU.mult, op1=ALU.add)
            for j in range(4):
                T2 = 2 * (4 * i + j)
                n_ = nm[j % 2]
                c.op("dve", nc.vector.scalar_tensor_tensor, outs=[f1], ins=[JC_sb, CB_sb], out=f1[:], in0=JC_sb[:],
                     scalar=float(T2 - 1), in1=CB_sb[:], op0=ALU.is_ge, op1=ALU.mult)
                c.op("pool", nc.gpsimd.tensor_scalar, outs=[pen], ins=[JC_sb], out=pen[:], in0=JC_sb[:],
                     scalar1=float(T2), scalar2=-3e30, op0=ALU.is_gt, op1=ALU.mult)
                c.op("dve", nc.vector.tensor_tensor, outs=[imp3], ins=[impacc, f1], out=imp3[:], in0=impacc[:, j, :],
                     in1=f1[:], op=ALU.add)
                c.op("dve", nc.vector.memset, outs=[imp3], ap=imp3[:, 0:1], constant=2e9)
                c.op("dve", nc.vector.tensor_tensor, outs=[imp3], ins=[imp3, pen], out=imp3[:], in0=imp3[:],
                     in1=pen[:], op=ALU.add)
                c.op("dve", nc.vector.max, outs=[m8a], ins=[imp3], out=m8a[:], in_=imp3[:])
                c.op("dve", nc.vector.match_replace, outs=[imp4], ins=[m8a, imp3], out=imp4[:],
                     in_to_replace=m8a[:], in_values=imp3[:], imm_value=-2e30)
                c.op("dve", nc.vector.max, outs=[m8b], ins=[imp4], out=m8b[:], in_=imp4[:])
                c.op("dve", nc.vector.tensor_scalar, outs=[n_], ins=[imp3, m8b], out=n_[:], in0=imp3[:],
                     scalar1=m8b[:, 7:8], scalar2=-BIG, op0=ALU.is_lt, op1=ALU.mult)
                c.op("pe", nc.tensor.transpose, outs=[ps_t], ins=[n_, id_sb], out=ps_t[:, j * 128:(j + 1) * 128],
                     in_=n_[:], identity=id_sb[:])
            c.op("act", nc.scalar.copy, outs=[nmT], ins=[ps_t], out=nmT[:], in_=ps_t[:])
            pend_fin = None
            for r in range(4):
                hrow = (g * 4 + r) * 3

                def sel_extra(ps, kt):
                    c.op("pe", nc.tensor.matmul, outs=[ps], ins=[E_sb, nmT], out=ps[:],
                         lhsT=E_sb[:, kt * 128:(kt + 1) * 128], rhs=nmT[:], start=False, stop=True)
                o3s, _, tail_s = attend(ks_sb, lambda kt: ks_sb[:, kt * 128:(kt + 1) * 128], qc[:, r, :], qc,
                                        vs_sb, lambda kt: vs_sb[:, kt, 0:65], list(range(4 * i + 4)),
                                        lambda kt, i=i: ((mC, mC[:, kt - 4 * i, :]) if kt >= 4 * i else None),
                                        extra_mm=sel_extra, pre_hook=pend_fin, defer=True)

                def fin_sel(o3s=o3s, tail_s=tail_s, hrow=hrow, r=r, gc=gc):
                    tail_s[1]()
                    c.op("dve", nc.vector.tensor_tensor, outs=[on[0]], ins=[o3s, ps_b], out=on[0][:], in0=o3s[:],
                         in1=ps_b[0:64, :], op=ALU.mult)
                    c.op("pe", nc.tensor.matmul, outs=[ps_g], ins=[SelG_sb, gc], out=ps_g[0:64, :],
                         lhsT=SelG_sb[:, (hrow + 1) * 64:(hrow + 2) * 64], rhs=gc[:], start=True, stop=True)
                    c.op("dve", nc.vector.tensor_tensor, outs=[on[0]], ins=[on[0], ps_g], out=on[0][:], in0=on[0][:],
                         in1=ps_g[0:64, :], op=ALU.mult)
                    c.op("pool", nc.gpsimd.tensor_tensor, outs=[acc], ins=[on[0], stash[r]], out=acc[:],
                         in0=on[0][:], in1=stash[r][:], op=ALU.add)
                wl = [kt for kt in range(4 * i - 4, 4 * i + 4) if kt >= 0]
                o3w, _, tail_w = attend(kw_sb, lambda kt: kw_sb[:, kt * 128:(kt + 1) * 128], qc[:, r, :], qc,
                                        vw_sb, lambda kt: vw_sb[:, kt, 0:65], wl,
                                        lambda kt, i=i: ((mC, mC[:, kt - 4 * i, :]) if kt >= 4 * i
                                                         else (mL, mL[:, kt - 4 * i + 4, :])),
                                        pre_hook=(tail_s[0], fin_sel), defer=True)
                o_b = ob[oi % 2]; oi += 1
                row = (g * 4 + r) * 64

                def fin_win(o3w=o3w, tail_w=tail_w, hrow=hrow, o_b=o_b, row=row, i=i, gc=gc):
                    tail_w[1]()
                    c.op("dve", nc.vector.tensor_tensor, outs=[on[1]], ins=[o3w, ps_b], out=on[1][:], in0=o3w[:],
                         in1=ps_b[0:64, :], op=ALU.mult)
                    c.op("pe", nc.tensor.matmul, outs=[ps_g], ins=[SelG_sb, gc], out=ps_g[0:64, :],
                         lhsT=SelG_sb[:, (hrow + 2) * 64:(hrow + 3) * 64], rhs=gc[:], start=True, stop=True)
                    c.op("dve", nc.vector.tensor_tensor, outs=[on[1]], ins=[on[1], ps_g], out=on[1][:], in0=on[1][:],
                         in1=ps_g[0:64, :], op=ALU.mult)
                    c.op("pool", nc.gpsimd.tensor_tensor, outs=[o_b], ins=[acc, on[1]], out=o_b[:], in0=acc[:],
                         in1=on[1][:], op=ALU.add)
                    c.dma("pool", OT[row:row + 64, i * 512:(i + 1) * 512], o_b[:], in_t=o_b)
                pend_fin = (tail_w[0], fin_win)
            pend_fin[0]()
            pend_fin[1]()
    c.finish("pool")
    c.close()
    return nc


def nsa_consts():
    cs = {}
    n = np.arange(512)
    j = np.arange(128)
    ov = ((16 * n[:, None] < 64 * j[None, :] + 64) & (16 * n[:, None] + 32 > 64 * j[None, :]) & (n[:, None] < 511))
    cs["ovl"] = np.ascontiguousarray(ov.reshape(4, 128, 128).transpose(1, 0, 2)).astype(NPBF)
    cs["Eind"] = (np.arange(S)[None, :] // 64 == np.arange(128)[:, None]).astype(NPBF)
    p = np.arange(128)
    cs["JC"] = (j[None, :] - (p[:, None] // 64)).astype(np.float32)
    cs["CB"] = np.broadcast_to((1e9 + 1e6 * j)[None, :], (128, 128)).astype(np.float32).copy()
    cs["ident"] = np.eye(128, dtype=np.float32)
    sel = np.zeros((32, 24, 64), np.float32)
    for m in range(24):
        sel[m, m, :] = 1.0
    cs["SelG"] = sel.reshape(32, 24 * 64)
    np_ = np.arange(128)[:, None, None]
    m = np.arange(5)[None, :, None]
    t = np.arange(512)[None, None, :]
    cs["cmask"] = ((16 * np_ + 31 - 512 * m) <= t).astype(NPBF)
    r = np.arange(4)[None, :, None]
    cs["maskL"] = ((128 * r + np_) > t).astype(NPBF)
    return cs


def layer3(hT_list, p, cs):
    ncP = get_nc("P_nsa", build_P_nsa)
    cosF, sinF = rope_tables_fm(64, 128)
    pres = launch(ncP, [{"hT": hT_list[c], "w": p["l3_nsa_w_in"],
                         "cosT": np.ascontiguousarray(cosF[:, (c % 2) * TOK:(c % 2 + 1) * TOK]),
                         "sinT": np.ascontiguousarray(sinF[:, (c % 2) * TOK:(c % 2 + 1) * TOK])}
                        for c in range(NCORES)])
    g = {k: gather_heads(pres, k, True) for k in ("QT", "KcT", "VcT", "KsT", "KwT")}
    g["GT"] = []
    for c in range(NCORES):
        b, hh = c // 2, c % 2
        full = np.concatenate([pres[2 * b]["GT"], pres[2 * b + 1]["GT"]], axis=1)
        pad = np.zeros((32, S), np.float32)
        pad[0:24] = full[16 + hh * 24:16 + hh * 24 + 24]
        g["GT"].append(pad)
    g["Vs"] = gather_heads(pres, "Vs", False)
    g["Vw"] = gather_heads(pres, "Vw", False)
    nsc = nsa_consts()
    posT = np.ascontiguousarray(np.stack([p["l3_nsa_cmp_pos_k"].T, p["l3_nsa_cmp_pos_v"].T], axis=1)).astype(np.float32)
    w2 = np.stack([p["l3_nsa_cmp_w2_k"].reshape(2, 128, 64).transpose(1, 0, 2),
                   p["l3_nsa_cmp_w2_v"].reshape(2, 128, 64).transpose(1, 0, 2)], axis=1)
    w2 = np.ascontiguousarray(w2).astype(np.float32)
    ncA = get_nc("A_nsa", build_A_nsa)
    maps = []
    for c in range(NCORES):
        m = {k: g[k][c] for k in g}
        m.update(posT=posT, w1k=p["l3_nsa_cmp_w1_k"], w1v=p["l3_nsa_cmp_w1_v"], w2=w2, maskC=cs["maskC"])
        m.update(nsc)
        maps.append(m)
    ares = launch(ncA, maps)
    OT = scatter_OT(ares)
    return run_M(hT_list, OT, p["l3_nsa_w_o"], p["l3_mlp_w1"], p["l3_mlp_w2"], p["l3_ln1_g"], p["l3_ln1_b"],
                 p["l3_ln2_g"], p["l3_ln2_b"]), dict(pres=pres, ares=ares, OT=OT)


def kernel_unfused(**inputs):
    p = {k: np.asarray(v) for k, v in inputs.items()}
    cs = consts()
    hT = to_fm(p["x"].astype(np.float32))
    hT, _ = layer0(hT, p, cs)
    hT, _ = layer1(hT, p, cs)
    hT, _ = layer2(hT, p, cs)
    hT, _ = layer3(hT, p, cs)
    out = np.concatenate([h.T for h in hT], axis=0).reshape(B, S, D)
    return np.ascontiguousarray(out).astype(np.float32)


W_SHAPES = {
    "l0_sb_w_qkv": [D, 3 * D], "l0_sb_w_o": [D, D], "l1_moba_w_qkv": [D, 3 * D], "l1_moba_w_o": [D, D],
    "l2_mla_w_in": [D, 416], "l2_mla_w_uq": [256, 1536], "l2_mla_w_ukv": [128, 2048], "l2_mla_w_o": [D, D],
    "l3_nsa_w_in": [D, NSA_IN], "l3_nsa_cmp_w1_k": [2048, 256], "l3_nsa_cmp_w1_v": [2048, 256], "l3_nsa_w_o": [D, D],
}
for _l in range(4):
    W_SHAPES[f"l{_l}_mlp_w1"] = [D, DFF]
    W_SHAPES[f"l{_l}_mlp_w2"] = [DFF, D]


CC_MAX_BYTES = 2 * 1024 * 1024


class Exchange:
    def __init__(self, c, nc, par_sp):
        self.c = c
        self.nc = nc
        self.xsem = nc.alloc_semaphore(name="xsem")
        self.cnt = 0
        self.q = 0
        self.pars = {"sp": par_sp,
                     "pool": nc.gpsimd.snap(nc.gpsimd.partition_id() % 2, min_val=0, max_val=1)}

    @staticmethod
    def alloc(I, name, rows, cols, dt, kind, rc_big=None):
        es = 4 if dt == F32 else 2
        if rows * cols * es <= CC_MAX_BYTES:
            rc = rows
        else:
            rc = rc_big or {"fm": 256, "tm": 1024, "ot": 128}[kind]
        F = Fuse.active
        if kind == "fm":
            mine = I(name + "_m", [rows, cols], dt)
            lay = rc if rc < rows else rows // 2
        elif kind == "fm_all":
            mine = None
            lay = rows
        elif kind == "tm":
            mine = I(name + "_m", [2 * rows, cols // 2], dt)
            lay = rc
        elif kind == "ot":
            mine = I(name + "_m", [2 * rows, cols // 2], dt)
            lay = rc
        elif kind == "gt":
            mine = I(name + "_m", [48, cols], dt)
            lay = 24
        g = I(name + "_g", [2 * rows, cols], dt)
        if F is not None:
            F.lay[(mine if mine is not None else g).tensor.name] = lay
        return I(name + "_s", [rows, cols], dt), g, (mine if mine is not None else g), kind, rc

    def run(self, items):
        c = self.c
        for s_, g_, m_, kind, rc in items:
            rows = s_.shape[0]
            for j in range(rows // rc):
                c.allgather(s_[j * rc:(j + 1) * rc, :], g_[j * 2 * rc:(j + 1) * 2 * rc, :])
        for ek in ("sp", "pool"):
            c.eng[ek].wait_ge(c.ccsem, c.cccnt)
        for s_, g_, m_, kind, rc in items:
            if kind == "fm_all":
                continue
            ek = ("sp", "pool")[self.q % 2]
            self.q += 1
            par = self.pars[ek]
            rows = s_.shape[0]
            nch = rows // rc
            if kind == "fm" and nch == 1:
                src = g_.rearrange("(r h f) t -> r h f t", r=2, h=2)[:, bass.ds(par, 1), :, :] \
                    .rearrange("r 1 f t -> r f t")
                dst = m_.rearrange("(r f) t -> r f t", r=2)
            elif kind == "fm":
                src = g_.rearrange("(h x) t -> h x t", h=2)[bass.ds(par, 1), :, :].rearrange("1 x t -> x t")
                dst = m_
            elif kind == "tm":
                src = g_.rearrange("x (h n) -> x h n", h=2)[:, bass.ds(par, 1), :].rearrange("x 1 n -> x n")
                dst = m_
            elif kind == "ot":
                src = g_.rearrange("x (rr t) -> x rr t", rr=2)[:, bass.ds(par, 1), :].rearrange("x 1 t -> x t")
                dst = m_
            elif kind == "gt":
                src = g_.rearrange("(r f) t -> f r t", r=2)[16:64].rearrange("(h f) r t -> f h r t", h=2)[
                    :, bass.ds(par, 1), :, :].rearrange("f 1 r t -> f r t")
                dst = m_.rearrange("(r f) t -> f r t", r=2)
            c.eng[ek].dma_start(out=dst, in_=src).then_inc(self.xsem, 16)
            self.cnt += 16
        for ek in c.eng:
            c.eng[ek].wait_ge(self.xsem, self.cnt)


def build_fused():
    F = Fuse()
    Fuse.active = F
    try:
        nc = F.nc
        F.par = nc.sync.snap(nc.sync.partition_id() % 2, min_val=0, max_val=1)
        E = F.ext_in
        I = F.internal

        def W(name):
            return E(name, W_SHAPES[name], F32)

        X = Exchange(F.c if F.c is not None else make_ctx(nc), nc, F.par)
        AG = X.run

        def gath(name, rows, cols, dt, kind):
            return X.alloc(I, name, rows, cols, dt, kind)

        def M_phase(l, OTg, h_in, h_out, wo):
            F.io = {"OT": OTg, "hT": h_in, "w_o": W(wo), "w1": W(f"l{l}_mlp_w1"), "w2": W(f"l{l}_mlp_w2"),
                    "lnp": E(f"lnp{l}", [128, 4, 8], F32), "hO": h_out}
            build_M()

        maskS = E("maskS", [128, 4, 512], BF16)
        maskC = E("maskC", [128, 4, 512], BF16)
        tri = E("tri", [128, 128], BF16)
        ident = E("ident", [128, 128], F32)
        cosT = E("cosT", [128, TOK], F32)
        sinT = E("sinT", [128, TOK], F32)
        h0 = E("hT0", [D, TOK], F32)
        h = [h0] + [I(f"h{l}", [D, TOK], F32) for l in (1, 2, 3)]
        out = nc.dram_tensor("out", [D, TOK], F32, kind="ExternalOutput").ap()
        h.append(out)

        q = gath("l0QT", D, TOK, BF16, "fm"); k = gath("l0KT", D, TOK, BF16, "fm"); v = gath("l0V", TOK, D, BF16, "tm")
        F.io = {"hT": h[0], "w": W("l0_sb_w_qkv"), "QT": q[0], "KT": k[0], "V": v[0]}
        build_P_qkv(False, -0.125)
        AG([q, k, v])
        o = gath("l0OT", 512, S, BF16, "ot")
        F.io = {"QT": q[2], "KT": k[2], "V": v[2], "maskS": maskS, "tri": tri, "OT": o[0]}
        build_A_sb()
        AG([o])
        M_phase(0, o[2], h[0], h[1], "l0_sb_w_o")
        q = gath("l1QT", D, TOK, BF16, "fm"); k = gath("l1KT", D, TOK, BF16, "fm"); v = gath("l1V", TOK, D, BF16, "tm")
        F.io = {"hT": h[1], "w": W("l1_moba_w_qkv"), "cosT": cosT, "sinT": sinT, "QT": q[0], "KT": k[0], "V": v[0]}
        build_P_qkv(True, 0.125)
        AG([q, k, v])
        o = gath("l1OT", 512, S, BF16, "ot")
        F.io = {"QT": q[2], "KT": k[2], "V": v[2], "maskC": maskC, "Eind": E("Eind32", [32, S], BF16), "ident": ident,
                "OT": o[0]}
        build_A_soft("moba")
        AG([o])
        M_phase(1, o[2], h[1], h[2], "l1_moba_w_o")
        q = X.alloc(I, "l2QT", 1536, TOK, BF16, "fm", rc_big=192); k = gath("l2KN", D, TOK, BF16, "fm")
        kr = gath("l2KR", 32, TOK, BF16, "fm_all"); v = gath("l2V", TOK, D, BF16, "tm")
        F.io = {"hT": h[2], "w_in": W("l2_mla_w_in"), "w_uq": W("l2_mla_w_uq"), "w_ukv": W("l2_mla_w_ukv"),
                "gq": E("gq", [128, 2], F32), "gkv": E("gkv", [128, 1], F32), "cos96": E("cos96", [96, TOK], F32),
                "sin96": E("sin96", [96, TOK], F32), "QT": q[0], "KNT": k[0], "KRT": kr[0], "V": v[0]}
        build_P_mla()
        AG([q, k, kr, v])
        o = gath("l2OT", 512, S, BF16, "ot")
        F.io = {"QT": q[2], "KNT": k[2], "KRT": kr[2], "V": v[2], "maskC": maskC, "OT": o[0]}
        build_A_soft("mla")
        AG([o])
        M_phase(2, o[2], h[2], h[3], "l2_mla_w_o")
        names = [("QT", D, TOK, BF16, "fm"), ("KcT", 256, TOK, BF16, "fm"), ("VcT", 256, TOK, BF16, "fm"),
                 ("KsT", 256, TOK, BF16, "fm"), ("KwT", 256, TOK, BF16, "fm"), ("Vs", TOK, 256, BF16, "tm"),
                 ("Vw", TOK, 256, BF16, "tm"), ("GT", 64, TOK, F32, "gt")]
        sg = {n: gath("l3" + n, r, cc, dt, kd) for n, r, cc, dt, kd in names}
        F.io = {"hT": h[3], "w": W("l3_nsa_w_in"), "cosT": cosT, "sinT": sinT}
        F.io.update({n: sg[n][0] for n in sg})
        build_P_nsa()
        AG([sg[n] for n in sg])
        o = gath("l3OT", 512, S, BF16, "ot")
        F.io = {n: sg[n][2] for n in sg}
        F.io.update({"posT": E("posT", [64, 2, 32], F32), "w1k": W("l3_nsa_cmp_w1_k"), "w1v": W("l3_nsa_cmp_w1_v"),
                     "w2": E("w2nsa", [128, 2, 2, 64], F32), "maskC": maskC, "maskL": E("maskL", [128, 4, 512], BF16),
                     "cmask": E("cmask", [128, 5, 512], BF16), "ovl": E("ovl", [128, 4, 128], BF16),
                     "Eind": E("Eind128", [128, S], BF16), "JC": E("JC", [128, 128], F32),
                     "CB": E("CB", [128, 128], F32), "ident": ident, "SelG": E("SelG", [32, 24 * 64], F32),
                     "OT": o[0]})
        build_A_nsa()
        AG([o])
        M_phase(3, o[2], h[3], h[4], "l3_nsa_w_o")
        F.c.final_finish("pool")
        F.c.final_finish("sp")
    finally:
        Fuse.active = None
    return nc


def fused_inputs(p):
    cs = consts()
    nsc = nsa_consts()
    hT = to_fm(p["x"].astype(np.float32))
    cosF, sinF = rope_tables_fm(64, 128)
    cos32, sin32 = rope_tables_fm(32, 32)
    cos96 = np.concatenate([np.ones((64, S), np.float32), cos32], axis=0)
    sin96 = np.concatenate([np.zeros((64, S), np.float32), sin32], axis=0)
    common = {k: np.ascontiguousarray(p[k]).astype(np.float32) for k in W_SHAPES}
    for l in range(4):
        common[f"lnp{l}"] = lnp_pack(p[f"l{l}_ln1_g"], p[f"l{l}_ln1_b"], p[f"l{l}_ln2_g"], p[f"l{l}_ln2_b"])
    common.update(maskS=cs["maskS"], maskC=cs["maskC"], tri=cs["tri"], ident=np.eye(128, dtype=np.float32))
    common["Eind32"] = (np.arange(S)[None, :] // 256 == np.arange(32)[:, None]).astype(NPBF)
    common["gq"] = np.ascontiguousarray(p["l2_mla_q_norm"].reshape(2, 128).T).astype(np.float32)
    common["gkv"] = np.ascontiguousarray(p["l2_mla_kv_norm"].reshape(1, 128).T).astype(np.float32)
    common["posT"] = np.ascontiguousarray(
        np.stack([p["l3_nsa_cmp_pos_k"].T, p["l3_nsa_cmp_pos_v"].T], axis=1)).astype(np.float32)
    common["w2nsa"] = np.ascontiguousarray(
        np.stack([p["l3_nsa_cmp_w2_k"].reshape(2, 128, 64).transpose(1, 0, 2),
                  p["l3_nsa_cmp_w2_v"].reshape(2, 128, 64).transpose(1, 0, 2)], axis=1)).astype(np.float32)
    common.update(maskL=nsc["maskL"], cmask=nsc["cmask"], ovl=nsc["ovl"], Eind128=nsc["Eind"], JC=nsc["JC"],
                  CB=nsc["CB"], SelG=nsc["SelG"])
    maps = []
    for c in range(NCORES):
        m = dict(common)
        sl = slice((c % 2) * TOK, (c % 2 + 1) * TOK)
        m["hT0"] = hT[c]
        m["cosT"] = np.ascontiguousarray(cosF[:, sl]); m["sinT"] = np.ascontiguousarray(sinF[:, sl])
        m["cos96"] = np.ascontiguousarray(cos96[:, sl]); m["sin96"] = np.ascontiguousarray(sin96[:, sl])
        maps.append(m)
    return maps


def kernel(**inputs):
    p = {k: np.asarray(v) for k, v in inputs.items()}
    nc = get_nc("fused", build_fused)
    res = launch(nc, fused_inputs(p))
    out = np.concatenate([r["out"].T for r in res], axis=0).reshape(B, S, D)
    return np.ascontiguousarray(out).astype(np.float32)
```

```python
import numpy as np
import ml_dtypes
import concourse.bass as bass
import concourse.mybir as mybir
from concourse.bass_utils import run_bass_kernel_spmd

F32 = mybir.dt.float32
BF16 = mybir.dt.bfloat16
AF = mybir.ActivationFunctionType
ALU = mybir.AluOpType
AX = mybir.AxisListType
NPBF = ml_dtypes.bfloat16

D = 1024
B = 4
S = 8192
DFF = 4096
NCORES = 8
TOK = 4096
ALPHA = float((2.0 * 4) ** 0.25)
LN_EPS = 1e-5
RMS_EPS = 1e-6
BIG = 30000.0

SAME_ENGINE_SYNC = False


class T:
    __slots__ = ("ap", "name", "w", "r", "dsems")

    def __init__(self, ap, name):
        self.ap = ap
        self.name = name
        self.w = None
        self.r = []
        self.dsems = {}

    def __getitem__(self, idx):
        return self.ap[idx]


class Ctx:
    def __init__(self, nc):
        self.nc = nc
        self.eng = {"pe": nc.tensor, "act": nc.scalar, "dve": nc.vector,
                    "pool": nc.gpsimd, "sp": nc.sync}
        self.sem = {k: nc.alloc_semaphore(name=f"s_{k}") for k in self.eng}
        self.cnt = {k: 0 for k in self.eng}
        self.seen = {k: {} for k in self.eng}
        self.n_inst = 0
        self._stack = []
        self.out_events = []
        self.fused = False
        self.phase = 0
        self.dpool = {}
        self.ptiles = []
        self.ccsem = None
        self.cccnt = 0

    def sb(self, name, shape, dt):
        cm = self.nc.sbuf_tensor(f"sb{self.phase}_" + name, list(shape), dt)
        t = cm.__enter__()
        self._stack.append(cm)
        return T(t[:], name)

    def ps(self, name, shape, dt=F32):
        cm = self.nc.psum_tensor(f"ps{self.phase}_" + name, list(shape), dt)
        t = cm.__enter__()
        self._stack.append(cm)
        return T(t[:], name)

    def view(self, ap, name):
        return T(ap, name)

    def _need(self, ek, deps, raw=()):
        best = {}
        for lst, is_raw in ((deps, False), (raw, True)):
            for d in lst:
                if d is None:
                    continue
                sem, val, sk = d
                if sk == ek and ek == "pe":
                    continue
                key = id(sem)
                if key not in best or best[key][1] < val:
                    best[key] = (sem, val)
        seen = self.seen[ek]
        e = self.eng[ek]
        for key, (sem, val) in best.items():
            if seen.get(key, 0) >= val:
                continue
            e.wait_ge(sem, val)
            seen[key] = val

    @staticmethod
    def _compact(r):
        best = {}
        for sem, val, sk in r:
            k = id(sem)
            if k not in best or best[k][1] < val:
                best[k] = (sem, val, sk)
        return list(best.values())

    def op(self, ek, fn, outs=(), ins=(), **kw):
        deps = []
        raw = [t.w for t in ins]
        for t in outs:
            deps.append(t.w)
            deps.extend(t.r)
        self._need(ek, deps, raw)
        inst = fn(**kw)
        self.cnt[ek] += 1
        ev = (self.sem[ek], self.cnt[ek], ek)
        inst.then_inc(self.sem[ek], 1)
        self.n_inst += 1
        for t in ins:
            t.r.append(ev)
            if len(t.r) > 16:
                t.r = self._compact(t.r)
        for t in outs:
            t.w = ev
            t.r = []
        return ev

    def dma(self, ek, out, in_, out_t=None, in_t=None, **kw):
        st = out_t or in_t
        if ek not in st.dsems:
            pool = self.dpool.setdefault(ek, [])
            if pool:
                st.dsems[ek] = pool.pop()
            else:
                st.dsems[ek] = [self.nc.alloc_semaphore(name=f"d{self.phase}_{ek}_{st.name}"), 0]
            if st not in self.ptiles:
                self.ptiles.append(st)
        ds = st.dsems[ek]
        deps = []
        if in_t is not None:
            deps.append(in_t.w)
        if out_t is not None:
            deps.append(out_t.w)
            deps.extend(out_t.r)
        self._need(ek, deps)
        inst = self.eng[ek].dma_start(out=out, in_=in_, **kw)
        ds[1] += 16
        inst.then_inc(ds[0], 16)
        ev = (ds[0], ds[1], None)
        self.n_inst += 1
        if in_t is not None:
            in_t.r.append(ev)
            if out_t is None:
                self.out_events.append(ev)
        if out_t is not None:
            out_t.w = ev
            out_t.r = []
        return ev

    def barrier(self):
        targets = [(self.sem[k], self.cnt[k], k) for k in self.eng if self.cnt[k] > 0]
        targets += [(d[0], d[1], None) for t in self.ptiles for d in t.dsems.values() if d[1] > 0]
        targets += [(d[0], d[1], None) for pool in self.dpool.values() for d in pool if d[1] > 0]
        if self.ccsem is not None and self.cccnt > 0:
            targets.append((self.ccsem, self.cccnt, None))
        for ek in self.eng:
            seen = self.seen[ek]
            for sem, val, sk in targets:
                if sk == ek or seen.get(id(sem), 0) >= val:
                    continue
                self.eng[ek].wait_ge(sem, val)
                seen[id(sem)] = val

    def end_phase(self):
        self.barrier()
        for t in self.ptiles:
            for ek, d in t.dsems.items():
                self.dpool.setdefault(ek, []).append(d)
            t.dsems = {}
        self.ptiles = []
        while self._stack:
            self._stack.pop().__exit__(None, None, None)
        self.phase += 1

    def allgather(self, in_ap, out_ap):
        if self.ccsem is None:
            self.ccsem = self.nc.alloc_semaphore(name="ccsem")
        self.nc.gpsimd.collective_compute("AllGather", ALU.bypass,
                                          replica_groups=[[0, 1], [2, 3], [4, 5], [6, 7]],
                                          ins=[in_ap], outs=[out_ap]).then_inc(self.ccsem, 1)
        self.cccnt += 1

    def finish(self, ek="sp"):
        if self.fused:
            return
        best = {}
        for sem, val, _ in self.out_events:
            k = id(sem)
            if k not in best or best[k][1] < val:
                best[k] = (sem, val)
        for sem, val in best.values():
            self.eng[ek].wait_ge(sem, val)

    def close(self):
        if self.fused:
            self.end_phase()
            return
        while self._stack:
            self._stack.pop().__exit__(None, None, None)

    def final_finish(self, ek="sp"):
        best = {}
        for sem, val, _ in self.out_events:
            k = id(sem)
            if k not in best or best[k][1] < val:
                best[k] = (sem, val)
        for sem, val in best.values():
            self.eng[ek].wait_ge(sem, val)


class Fuse:
    active = None

    def __init__(self):
        self.nc = bass.Bass("TRN2", target_bir_lowering=False)
        self.c = None
        self.io = {}
        self.ext = {}
        self.par = None
        self.lay = {}

    def ext_in(self, name, shape, dt):
        if name not in self.ext:
            self.ext[name] = self.nc.dram_tensor(name, list(shape), dt, kind="ExternalInput").ap()
        return self.ext[name]

    def internal(self, name, shape, dt):
        return self.nc.dram_tensor(name, list(shape), dt, kind="Internal").ap()


def new_nc():
    if Fuse.active is not None:
        return Fuse.active.nc
    return bass.Bass("TRN2", target_bir_lowering=False)


def make_ctx(nc):
    F = Fuse.active
    if F is not None:
        if F.c is None:
            F.c = Ctx(nc)
            F.c.fused = True
        return F.c
    return Ctx(nc)


def din(nc, name, shape, dt):
    F = Fuse.active
    if F is not None:
        return F.io[name]
    return nc.dram_tensor(name, list(shape), dt, kind="ExternalInput").ap()


def dout(nc, name, shape, dt):
    F = Fuse.active
    if F is not None:
        return F.io[name]
    return nc.dram_tensor(name, list(shape), dt, kind="ExternalOutput").ap()


def _lay(ap):
    F = Fuse.active
    if F is None:
        return None
    return F.lay.get(ap.tensor.name)


def fm_rows(ap, r0, r1):
    rc = _lay(ap)
    if rc is None:
        return ap[r0:r1, :].rearrange("p (r t) -> p r t", r=2)
    jj = r0 // rc
    assert (r1 - 1) // rc == jj, (r0, r1, rc)
    v = ap[jj * 2 * rc:(jj + 1) * 2 * rc, :].rearrange("(r f) t -> f r t", r=2)
    return v[r0 - jj * rc:r1 - jj * rc, :, :]


def fm_shared(ap):
    if _lay(ap) is None:
        return ap.rearrange("p (r t) -> p r t", r=2)
    return ap.rearrange("(r f) t -> f r t", r=2)


def fm_chunk(ap, r0, r1, i):
    r, t0 = divmod(i * 512, TOK)
    return fm_rows(ap, r0, r1)[:, r, t0:t0 + 512]


def gt_chunk(ap, i):
    if _lay(ap) is None:
        return ap[0:24, i * 512:(i + 1) * 512]
    r, t0 = divmod(i * 512, TOK)
    return ap.rearrange("(r f) t -> f r t", r=2)[:, r, t0:t0 + 512]


def ot_pieces(ap, t0, n):
    rc = _lay(ap)
    if rc is None:
        return [(0, 8, ap.rearrange("(kc p) t -> p kc t", p=128)[:, :, t0:t0 + n])]
    v = ap.rearrange("(j r p) t -> p r j t", r=2, p=128)
    return [(r * 4, 4, v[:, r, :, t0:t0 + n]) for r in range(2)]


def tm_pieces(ap, c0, c1):
    rc = _lay(ap)
    if rc is None:
        return [(0, 64, ap.rearrange("(kt p) n -> p kt n", p=128)[:, :, c0:c1])]
    nj = TOK // rc
    k8 = rc // 128
    v = ap.rearrange("(j r k p) n -> p j r k n", r=2, k=k8, p=128)
    return [(r * (TOK // 128) + j * k8, k8, v[:, j, r, :, c0:c1]) for r in range(2) for j in range(nj)]


class Caster:
    def __init__(self, c, engines=("dve", "pool", "act")):
        self.c = c
        self.engines = engines
        self.i = 0

    def copy(self, out_t, out_ap, in_t, in_ap, eng=None):
        c = self.c
        ek = eng or self.engines[self.i % len(self.engines)]
        self.i += 1
        if ek == "act":
            return c.op("act", c.nc.scalar.copy, outs=[out_t], ins=[in_t], out=out_ap, in_=in_ap)
        e = c.nc.vector if ek == "dve" else c.nc.gpsimd
        return c.op(ek, e.tensor_copy, outs=[out_t], ins=[in_t], out=out_ap, in_=in_ap)


def load_weight_bf16(c, caster, w_ap, K, N, name, stages, queue="sp"):
    nk = K // 128
    big = c.sb(name, [128, nk, N], BF16)
    chunks = [c.view(big[:, kc, :], f"{name}_{kc}") for kc in range(nk)]
    CW = stages[0].ap.shape[-1]
    si = 0
    for kc in range(nk):
        for n0 in range(0, N, CW):
            n1 = min(N, n0 + CW)
            st = stages[si % len(stages)]
            si += 1
            c.dma(queue, st[:, 0:n1 - n0], w_ap[kc * 128:(kc + 1) * 128, n0:n1], out_t=st)
            caster.copy(chunks[kc], big[:, kc, n0:n1], st, st[:, 0:n1 - n0])
    return big, chunks


def build_P_qkv(rope, qscale):
    nc = new_nc()
    hT = din(nc, "hT", [D, TOK], F32)
    w = din(nc, "w", [D, 3 * D], F32)
    if rope:
        cosT = din(nc, "cosT", [128, TOK], F32)
        sinT = din(nc, "sinT", [128, TOK], F32)
    QT = dout(nc, "QT", [D, TOK], BF16)
    KT = dout(nc, "KT", [D, TOK], BF16)
    V = dout(nc, "V", [TOK, D], BF16)
    c = make_ctx(nc)
    cast = Caster(c)
    stages = [c.sb(f"stg{i}", [128, 1024], F32) for i in range(2)]
    wb, wch = load_weight_bf16(c, cast, w, D, 3 * D, "wb", stages)
    if rope:
        wr = c.sb("wrot", [128, 8, 2 * D], BF16)
        wrch = [c.view(wr[:, kc, :], f"wrot_{kc}") for kc in range(8)]
        for kc in range(8):
            src = wb[:, kc, 0:2 * D].rearrange("p (h two d) -> p h two d", two=2, d=32)
            dst = wr[:, kc, :].rearrange("p (h two d) -> p h two d", two=2, d=32)
            c.op("dve", nc.vector.tensor_scalar, outs=[wrch[kc]], ins=[wch[kc]],
                 out=dst[:, :, 0, :], in0=src[:, :, 1, :], scalar1=-1.0, scalar2=None, op0=ALU.mult)
            c.op("pool", nc.gpsimd.tensor_copy, outs=[wrch[kc]], ins=[wch[kc]],
                 out=dst[:, :, 1, :], in_=src[:, :, 0, :])
        cos_sb = c.sb("cos_sb", [128, TOK], F32)
        sin_sb = c.sb("sin_sb", [128, TOK], F32)
        c.dma("sp", cos_sb[:], cosT, out_t=cos_sb)
        c.dma("sp", sin_sb[:], sinT, out_t=sin_sb)
    NT = 512
    hts = [c.sb(f"ht{i}", [128, 8, NT], F32) for i in range(2)]
    hb = c.sb("hb", [128, 8, NT], BF16)
    qk_sb = [c.sb(f"qk{i}", [128, 8, NT], BF16) for i in range(2)]
    v_sb = c.sb("v_sb", [128, 4, D], BF16)
    t1 = [c.sb(f"t1_{i}", [128, NT], F32) for i in range(2)]
    t2 = [c.sb(f"t2_{i}", [128, NT], F32) for i in range(2)]
    banks = [c.ps(f"pb{i}", [128, 512], F32) for i in range(8)]
    bi = 0
    hT_v = hT.rearrange("(kc p) t -> p kc t", p=128)
    for tt in range(TOK // NT):
        ht = hts[tt % 2]
        c.dma("sp", ht[:], hT_v[:, :, tt * NT:(tt + 1) * NT], out_t=ht)
        for kc in range(8):
            cast.copy(hb, hb[:, kc, :], ht, ht[:, kc, :], eng=("dve", "pool")[kc % 2])
        for which in range(2):
            dst = qk_sb[which]
            for fc in range(8):
                col = which * D + fc * 128
                pb = banks[bi % 8]; bi += 1
                for kc in range(8):
                    c.op("pe", nc.tensor.matmul, outs=[pb], ins=[wch[kc], hb],
                         out=pb[:, 0:NT], lhsT=wb[:, kc, col:col + 128], rhs=hb[:, kc, :],
                         start=(kc == 0), stop=(kc == 7))
                sc = qscale if which == 0 else 1.0
                if not rope:
                    c.op("act", nc.scalar.mul, outs=[dst], ins=[pb], out=dst[:, fc, :], in_=pb[:, 0:NT], mul=sc)
                else:
                    pr = banks[bi % 8]; bi += 1
                    for kc in range(8):
                        c.op("pe", nc.tensor.matmul, outs=[pr], ins=[wrch[kc], hb],
                             out=pr[:, 0:NT], lhsT=wr[:, kc, col:col + 128], rhs=hb[:, kc, :],
                             start=(kc == 0), stop=(kc == 7))
                    a = t1[fc % 2]; b_ = t2[fc % 2]
                    c.op("dve", nc.vector.tensor_tensor, outs=[a], ins=[pb, cos_sb],
                         out=a[:], in0=pb[:, 0:NT], in1=cos_sb[:, tt * NT:(tt + 1) * NT], op=ALU.mult)
                    c.op("dve", nc.vector.tensor_tensor, outs=[b_], ins=[pr, sin_sb],
                         out=b_[:], in0=pr[:, 0:NT], in1=sin_sb[:, tt * NT:(tt + 1) * NT], op=ALU.mult)
                    c.op("pool", nc.gpsimd.tensor_tensor, outs=[a], ins=[a, b_], out=a[:], in0=a[:], in1=b_[:],
                         op=ALU.add)
                    c.op("act", nc.scalar.mul, outs=[dst], ins=[a], out=dst[:, fc, :], in_=a[:], mul=sc)
            out_d = (QT, KT)[which].rearrange("(fc p) t -> p fc t", p=128)
            c.dma("pool", out_d[:, :, tt * NT:(tt + 1) * NT], dst[:], in_t=dst)
        for j in range(4):
            for half in range(2):
                pb = banks[bi % 8]; bi += 1
                for kc in range(8):
                    c.op("pe", nc.tensor.matmul, outs=[pb], ins=[wch[kc], hb],
                         out=pb[:], lhsT=hb[:, kc, j * 128:(j + 1) * 128],
                         rhs=wb[:, kc, 2 * D + half * 512:2 * D + (half + 1) * 512],
                         start=(kc == 0), stop=(kc == 7))
                if half == 0:
                    c.op("act", nc.scalar.copy, outs=[v_sb], ins=[pb], out=v_sb[:, j, 0:512], in_=pb[:])
                else:
                    c.op("dve", nc.vector.tensor_copy, outs=[v_sb], ins=[pb], out=v_sb[:, j, 512:1024], in_=pb[:])
        V_v = V.rearrange("(j p) n -> p j n", p=128)
        c.dma("pool", V_v[:, tt * 4:(tt + 1) * 4, :], v_sb[:], in_t=v_sb)
    c.finish("pool")
    c.close()
    return nc


def build_A_sb():
    nc = new_nc()
    QT = din(nc, "QT", [512, S], BF16)
    KT = din(nc, "KT", [512, S], BF16)
    V = din(nc, "V", [S, 512], BF16)
    maskS = din(nc, "maskS", [128, 4, 512], BF16)
    tri = din(nc, "tri", [128, 128], BF16)
    OT = dout(nc, "OT", [512, S], BF16)
    c = make_ctx(nc)
    m_sb = c.sb("maskS", [128, 4, 512], BF16)
    tri_sb = c.sb("tri", [128, 128], BF16)
    ones_sb = c.sb("ones", [128, 128], BF16)
    c.dma("sp", m_sb[:], maskS, out_t=m_sb)
    c.dma("sp", tri_sb[:], tri, out_t=tri_sb)
    c.op("dve", nc.vector.memset, outs=[ones_sb], ap=ones_sb[:], constant=1.0)
    qts = [c.sb(f"qt{i}", [128, S], BF16) for i in range(2)]
    kts = [c.sb(f"kt{i}", [128, S], BF16) for i in range(2)]
    vs = [c.sb(f"v{i}", [128, 64, 128], BF16) for i in range(2)]
    NW = 4
    e_sb = [c.sb(f"e{i}", [128, 512], F32) for i in range(NW)]
    sp_sb = [c.sb(f"sp{i}", [128, 512], BF16) for i in range(NW)]
    w_sb = [c.sb(f"w{i}", [128, 512], BF16) for i in range(NW)]
    run = [c.sb(f"run{i}", [128, 512], BF16) for i in range(2)]
    o_sb = [c.sb(f"o{i}", [64, 512], BF16) for i in range(2)]
    pz = [c.ps(f"pz{i}", [128, 512], F32) for i in range(3)]
    px = [c.ps(f"px{i}", [128, 512], F32) for i in range(3)]
    po = [c.ps(f"po{i}", [128, 512], F32) for i in range(2)]
    V_v = V.rearrange("(kt p) n -> p kt n", p=128)
    it = 0
    ix = 0
    oi = 0
    for pair in range(4):
        qt = qts[pair % 2]; kt_ = kts[pair % 2]; v = vs[pair % 2]
        c.dma("sp", qt[:].rearrange("p (r t) -> p r t", r=2), fm_rows(QT, pair * 128, (pair + 1) * 128), out_t=qt)
        c.dma("sp", kt_[:].rearrange("p (r t) -> p r t", r=2), fm_rows(KT, pair * 128, (pair + 1) * 128), out_t=kt_)
        for k0_, nk_, src_ in tm_pieces(V, pair * 128, (pair + 1) * 128):
            c.dma("sp", v[:, k0_:k0_ + nk_, :], src_, out_t=v)
        for hs in range(2):
            r0 = hs * 64
            for i in range(S // 512):
                pout = po[oi % 2]
                osb = o_sb[oi % 2]
                oi += 1
                rn = run[i % 2]
                nkt = 4 * i + 4
                kts_ = list(range(nkt - 1, -1, -1))
                N = len(kts_)
                qap = qt[r0:r0 + 64, i * 512:(i + 1) * 512]
                stA = []
                stB = []
                for step in range(N + 2):
                    if step < N:
                        n = step; kt = kts_[n]; r = kt - 4 * i
                        z = pz[it % 3]; e = e_sb[it % NW]; sp = sp_sb[it % NW]
                        it += 1
                        kap = kt_[r0:r0 + 64, kt * 128:(kt + 1) * 128]
                        c.op("pe", nc.tensor.matmul, outs=[z], ins=[kt_, qt], out=z[:], lhsT=kap, rhs=qap,
                             start=True, stop=True)
                        c.op("act", nc.scalar.activation, outs=[e], ins=[z], out=e[:], in_=z[:], func=AF.Exp,
                             scale=-1.0)
                        c.op("act", nc.scalar.activation, outs=[sp], ins=[e], out=sp[:], in_=e[:], func=AF.Ln,
                             bias=1.0)
                        if r >= 0:
                            c.op("dve", nc.vector.tensor_tensor, outs=[sp], ins=[sp, m_sb], out=sp[:], in0=sp[:],
                                 in1=m_sb[:, r, :], op=ALU.mult)
                        stA.append((n, kt, sp, kap))
                    if 1 <= step <= N:
                        n, kt, sp, kap = stA.pop(0); r = kt - 4 * i
                        x = px[ix % 3]; wt = w_sb[ix % NW]
                        ix += 1
                        c.op("pe", nc.tensor.matmul, outs=[x], ins=[tri_sb, sp], out=x[:], lhsT=tri_sb[:], rhs=sp[:],
                             start=True, stop=False)
                        if n > 0:
                            c.op("pe", nc.tensor.matmul, outs=[x], ins=[ones_sb, rn], out=x[:], lhsT=ones_sb[:],
                                 rhs=rn[:], start=False, stop=False)
                        c.op("pe", nc.tensor.matmul, outs=[x], ins=[kt_, qt], out=x[:], lhsT=kap,
                             rhs=qap, start=False, stop=True)
                        c.op("act", nc.scalar.activation, outs=[wt], ins=[x], out=wt[:], in_=x[:], func=AF.Exp,
                             scale=-1.0)
                        if r >= 0:
                            c.op("dve", nc.vector.tensor_tensor, outs=[wt], ins=[wt, m_sb], out=wt[:], in0=wt[:],
                                 in1=m_sb[:, r, :], op=ALU.mult)
                        if kt > 0:
                            if n == 0:
                                c.op("pool", nc.gpsimd.tensor_copy, outs=[rn], ins=[sp], out=rn[:], in_=sp[:])
                            else:
                                c.op("pool", nc.gpsimd.tensor_tensor, outs=[rn], ins=[rn, sp], out=rn[:], in0=rn[:],
                                     in1=sp[:], op=ALU.add)
                        stB.append((n, kt, wt))
                    if step >= 2:
                        n, kt, wt = stB.pop(0)
                        c.op("pe", nc.tensor.matmul, outs=[pout], ins=[v, wt], out=pout[0:64, :],
                             lhsT=v[:, kt, r0:r0 + 64], rhs=wt[:], start=(n == 0), stop=(kt == 0))
                c.op("act", nc.scalar.copy, outs=[osb], ins=[pout], out=osb[:], in_=pout[0:64, :])
                row = pair * 128 + hs * 64
                c.dma("pool", OT[row:row + 64, i * 512:(i + 1) * 512], osb[:], in_t=osb)
    c.finish("pool")
    c.close()
    return nc


def ln_fm(c, nc, h_sb, hch, ones32, sq_sb, psS, psQ, g_col, b_col, small, NT, out_bf=None, out_bf_ch=None):
    mean = small["mean"]; rstd = small["rstd"]; msq = small["msq"]
    for fc in range(8):
        c.op("pe", nc.tensor.matmul, outs=[psS], ins=[ones32, hch[fc]], out=psS[:, 0:NT], lhsT=ones32[:],
             rhs=h_sb[:, fc, :], start=(fc == 0), stop=(fc == 7))
    for fc in range(8):
        sq = sq_sb[fc % 2]
        c.op("act", nc.scalar.activation, outs=[sq], ins=[hch[fc]], out=sq[:], in_=h_sb[:, fc, :], func=AF.Square)
        c.op("pe", nc.tensor.matmul, outs=[psQ], ins=[ones32, sq], out=psQ[:, 0:NT], lhsT=ones32[:],
             rhs=sq[:], start=(fc == 0), stop=(fc == 7))
    c.op("dve", nc.vector.tensor_scalar, outs=[mean], ins=[psS], out=mean[:], in0=psS[:, 0:NT],
         scalar1=1.0 / D, scalar2=None, op0=ALU.mult)
    c.op("dve", nc.vector.tensor_tensor, outs=[msq], ins=[mean], out=msq[:], in0=mean[:], in1=mean[:], op=ALU.mult)
    c.op("dve", nc.vector.scalar_tensor_tensor, outs=[rstd], ins=[psQ, msq], out=rstd[:], in0=psQ[:, 0:NT],
         scalar=1.0 / D, in1=msq[:], op0=ALU.mult, op1=ALU.subtract)
    c.op("act", nc.scalar.activation, outs=[rstd], ins=[rstd], out=rstd[:], in_=rstd[:], func=AF.Sqrt, bias=LN_EPS)
    c.op("dve", nc.vector.reciprocal, outs=[rstd], ins=[rstd], out=rstd[:], in_=rstd[:])
    for fc in range(8):
        c.op("dve", nc.vector.tensor_tensor, outs=[hch[fc]], ins=[hch[fc], mean], out=h_sb[:, fc, :],
             in0=h_sb[:, fc, :], in1=mean[:], op=ALU.subtract)
        c.op("pool", nc.gpsimd.tensor_tensor, outs=[hch[fc]], ins=[hch[fc], rstd], out=h_sb[:, fc, :],
             in0=h_sb[:, fc, :], in1=rstd[:], op=ALU.mult)
        c.op("act", nc.scalar.activation, outs=[hch[fc]], ins=[hch[fc], g_col, b_col], out=h_sb[:, fc, :],
             in_=h_sb[:, fc, :], func=AF.Identity, scale=g_col[:, fc:fc + 1], bias=b_col[:, fc:fc + 1])
        if out_bf is not None:
            c.op("dve", nc.vector.tensor_copy, outs=[out_bf_ch[fc]], ins=[hch[fc]], out=out_bf[:, fc, :],
                 in_=h_sb[:, fc, :])


def build_M():
    nc = new_nc()
    OT = din(nc, "OT", [D, TOK], BF16)
    hT = din(nc, "hT", [D, TOK], F32)
    w_o = din(nc, "w_o", [D, D], F32)
    w1 = din(nc, "w1", [D, DFF], F32)
    w2 = din(nc, "w2", [DFF, D], F32)
    lnp = din(nc, "lnp", [128, 4, 8], F32)
    hO = dout(nc, "hO", [D, TOK], F32)
    c = make_ctx(nc)
    cast = Caster(c)
    NT = 256
    stages = [c.sb(f"stg{i}", [128, 1024], F32) for i in range(2)]
    lnp_sb = c.sb("lnp", [128, 4, 8], F32)
    c.dma("sp", lnp_sb[:], lnp, out_t=lnp_sb)
    g1 = c.view(lnp_sb[:, 0, :], "g1"); b1 = c.view(lnp_sb[:, 1, :], "b1")
    g2 = c.view(lnp_sb[:, 2, :], "g2"); b2 = c.view(lnp_sb[:, 3, :], "b2")
    for v_ in (g1, b1, g2, b2):
        v_.w = lnp_sb.w
    ones32 = c.sb("ones32", [128, 128], F32)
    c.op("dve", nc.vector.memset, outs=[ones32], ap=ones32[:], constant=1.0)
    wo_b, wo_ch = load_weight_bf16(c, cast, w_o, D, D, "wo", stages)
    w1_b, w1_ch = load_weight_bf16(c, cast, w1, D, DFF, "w1", stages)
    w2_b, w2_ch = load_weight_bf16(c, cast, w2, DFF, D, "w2", stages)
    hs = [c.sb(f"h{i}", [128, 8, NT], F32) for i in range(2)]
    hchs = [[c.view(h[:, fc, :], f"{h.name}_{fc}") for fc in range(8)] for h in hs]
    ots = [c.sb(f"ot{i}", [128, 8, NT], BF16) for i in range(2)]
    hb = c.sb("hb", [128, 8, NT], BF16)
    hbch = [c.view(hb[:, fc, :], f"hb_{fc}") for fc in range(8)]
    aT = c.sb("aT", [128, 32, NT], BF16)
    aTch = [c.view(aT[:, f, :], f"aT_{f}") for f in range(32)]
    rl = [c.sb(f"rl{i}", [128, 2, NT], F32) for i in range(2)]
    sq_sb = [c.sb(f"sq{i}", [128, NT], F32) for i in range(2)]
    small = {k: c.sb(k, [128, NT], F32) for k in ("mean", "rstd", "msq")}
    banks = [c.ps(f"pb{i}", [128, 512], F32) for i in range(6)]
    psS = c.ps("psS", [128, 512], F32)
    psQ = c.ps("psQ", [128, 512], F32)
    bi = 0
    OT_v = OT.rearrange("(kc p) t -> p kc t", p=128)
    hT_v = hT.rearrange("(kc p) t -> p kc t", p=128)
    hO_v = hO.rearrange("(kc p) t -> p kc t", p=128)
    for tt in range(TOK // NT):
        h = hs[tt % 2]; hch = hchs[tt % 2]; ot = ots[tt % 2]
        sl = slice(tt * NT, (tt + 1) * NT)
        for k0_, nk_, src_ in ot_pieces(OT, tt * NT, NT):
            c.dma("sp", ot[:, k0_:k0_ + nk_, :], src_, out_t=ot)
        deps_ev = c.dma("sp", h[:], hT_v[:, :, sl], out_t=h)
        for v_ in hch:
            v_.w = deps_ev
            v_.r = []
        for fc in range(8):
            pb = banks[bi % 6]; bi += 1
            for kc in range(8):
                c.op("pe", nc.tensor.matmul, outs=[pb], ins=[wo_ch[kc], ot], out=pb[:, 0:NT],
                     lhsT=wo_b[:, kc, fc * 128:(fc + 1) * 128], rhs=ot[:, kc, :], start=(kc == 0), stop=(kc == 7))
            c.op("dve", nc.vector.scalar_tensor_tensor, outs=[hch[fc]], ins=[hch[fc], pb], out=h[:, fc, :],
                 in0=h[:, fc, :], scalar=ALPHA, in1=pb[:, 0:NT], op0=ALU.mult, op1=ALU.add)
        ln_fm(c, nc, h, hch, ones32, sq_sb, psS, psQ, g1, b1, small, NT, out_bf=hb, out_bf_ch=hbch)
        for f2 in range(16):
            pb = banks[bi % 6]; bi += 1
            for sub in range(2):
                f = f2 * 2 + sub
                for kc in range(8):
                    c.op("pe", nc.tensor.matmul, outs=[pb], ins=[w1_ch[kc], hbch[kc]],
                         out=pb[:, sub * NT:(sub + 1) * NT], lhsT=w1_b[:, kc, f * 128:(f + 1) * 128],
                         rhs=hb[:, kc, :], start=(kc == 0), stop=(kc == 7))
            r_ = rl[f2 % 2]
            c.op("act", nc.scalar.activation, outs=[r_], ins=[pb], out=r_[:].rearrange("p a n -> p (a n)"),
                 in_=pb[:, 0:2 * NT], func=AF.Relu)
            eng = ("dve", "pool")[f2 % 2]
            e_ = nc.vector if eng == "dve" else nc.gpsimd
            c.op(eng, e_.tensor_tensor, outs=[aTch[2 * f2], aTch[2 * f2 + 1]], ins=[r_],
                 out=aT[:, 2 * f2:2 * f2 + 2, :], in0=r_[:], in1=r_[:], op=ALU.mult)
        for fc in range(8):
            pb = banks[bi % 6]; bi += 1
            for f in range(32):
                c.op("pe", nc.tensor.matmul, outs=[pb], ins=[w2_ch[f], aTch[f]], out=pb[:, 0:NT],
                     lhsT=w2_b[:, f, fc * 128:(fc + 1) * 128], rhs=aT[:, f, :], start=(f == 0), stop=(f == 31))
            c.op("dve", nc.vector.scalar_tensor_tensor, outs=[hch[fc]], ins=[hch[fc], pb], out=h[:, fc, :],
                 in0=h[:, fc, :], scalar=ALPHA, in1=pb[:, 0:NT], op0=ALU.mult, op1=ALU.add)
        ln_fm(c, nc, h, hch, ones32, sq_sb, psS, psQ, g2, b2, small, NT)
        h.w = None
        h.r = []
        c._need("pool", [v_.w for v_ in hch])
        ev = c.dma("pool", hO_v[:, :, sl], h[:], in_t=h)
        for v_ in hch:
            v_.r.append(ev)
    c.finish("pool")
    c.close()
    return nc


_NC_CACHE = {}


def get_nc(key, fn, *a):
    if key not in _NC_CACHE:
        _NC_CACHE[key] = fn(*a)
    return _NC_CACHE[key]


def launch(nc, in_maps):
    res = run_bass_kernel_spmd(nc, in_maps, core_ids=list(range(NCORES)))
    return res.results


def consts():
    j = np.arange(128)[:, None, None]
    r = np.arange(4)[None, :, None]
    t = np.arange(512)[None, None, :]
    cs = {}
    cs["maskS"] = ((128 * r + j) < t).astype(NPBF)
    cs["maskC"] = ((128 * r + j) <= t).astype(NPBF)
    cs["tri"] = (np.arange(128)[:, None] >= np.arange(128)[None, :]).astype(NPBF)
    return cs


def to_fm(x):
    flat = np.ascontiguousarray(x).reshape(B * S, D)
    return [np.ascontiguousarray(flat[c * TOK:(c + 1) * TOK].T) for c in range(NCORES)]


def lnp_pack(g1, b1, g2, b2):
    return np.ascontiguousarray(np.stack([v.reshape(8, 128).T for v in (g1, b1, g2, b2)], axis=1)).astype(np.float32)


def gather_heads(per_core, key, feature_major):
    outs = []
    for c in range(NCORES):
        b, hh = c // 2, c % 2
        if feature_major:
            full = np.concatenate([per_core[2 * b][key], per_core[2 * b + 1][key]], axis=1)
            n = full.shape[0] // 2
            outs.append(np.ascontiguousarray(full[hh * n:(hh + 1) * n]))
        else:
            full = np.concatenate([per_core[2 * b][key], per_core[2 * b + 1][key]], axis=0)
            n = full.shape[1] // 2
            outs.append(np.ascontiguousarray(full[:, hh * n:(hh + 1) * n]))
    return outs


def scatter_OT(a_res):
    outs = []
    for c in range(NCORES):
        b, half = c // 2, c % 2
        full = np.concatenate([a_res[2 * b]["OT"], a_res[2 * b + 1]["OT"]], axis=0)
        outs.append(np.ascontiguousarray(full[:, half * TOK:(half + 1) * TOK]))
    return outs


def run_M(hT_list, OT_list, w_o, w1, w2, g1, b1, g2, b2):
    nc = get_nc("M", build_M)
    lnp = lnp_pack(g1, b1, g2, b2)
    res = launch(nc, [{"OT": OT_list[c], "hT": hT_list[c], "w_o": w_o, "w1": w1, "w2": w2, "lnp": lnp}
                      for c in range(NCORES)])
    return [r["hO"] for r in res]


def layer0(hT_list, p, cs):
    ncP = get_nc("P_sb", build_P_qkv, False, -0.125)
    pres = launch(ncP, [{"hT": hT_list[c], "w": p["l0_sb_w_qkv"]} for c in range(NCORES)])
    QT = gather_heads(pres, "QT", True)
    KT = gather_heads(pres, "KT", True)
    V = gather_heads(pres, "V", False)
    ncA = get_nc("A_sb", build_A_sb)
    ares = launch(ncA, [{"QT": QT[c], "KT": KT[c], "V": V[c], "maskS": cs["maskS"], "tri": cs["tri"]}
                        for c in range(NCORES)])
    OT = scatter_OT(ares)
    return run_M(hT_list, OT, p["l0_sb_w_o"], p["l0_mlp_w1"], p["l0_mlp_w2"], p["l0_ln1_g"], p["l0_ln1_b"],
                 p["l0_ln2_g"], p["l0_ln2_b"]), dict(pres=pres, ares=ares, OT=OT)


class SoftmaxRes:
    def __init__(self, c, nc, prefix=""):
        self.ps_s = [c.ps(f"{prefix}pss{i}", [128, 512], F32) for i in range(3)]
        self.ps_o = [c.ps(f"{prefix}pso{i}", [128, 512], F32) for i in range(2)]
        self.ps_b = c.ps(f"{prefix}psb", [128, 512], F32)
        self.p_sb = [c.sb(f"{prefix}p{i}", [128, 512], BF16) for i in range(5)]
        self.o32 = [c.sb(f"{prefix}o32_{i}", [64, 512], F32) for i in range(2)]
        self.rs = [c.sb(f"{prefix}rs{i}", [65, 512], F32) for i in range(2)]
        self.o_sb = [c.sb(f"{prefix}ob{i}", [64, 512], BF16) for i in range(2)]
        self.ones32 = c.sb(f"{prefix}ones32", [65, 64], F32)
        c.op("dve", nc.vector.memset, outs=[self.ones32], ap=self.ones32[:], constant=1.0)
        self.it = 0
        self.oi = 0


def softmax_chunk(c, nc, R, k_ts, q_ts, kt_sb, qt_sb, v_sb, KR, i, mask_t, kt_list, mask_of, extra_mm=None,
                  pre_hook=None):
    po = R.ps_o[R.oi % 2]; o32 = R.o32[R.oi % 2]; rs = R.rs[R.oi % 2]; ob = R.o_sb[R.oi % 2]
    R.oi += 1
    qap = qt_sb[0:KR, i * 512:(i + 1) * 512]
    LAG = 2
    N = len(kt_list)
    pend = []
    for n in range(N + LAG):
        if n < N:
            kt = kt_list[n]
            ps = R.ps_s[R.it % 3]; p = R.p_sb[R.it % len(R.p_sb)]
            R.it += 1
            c.op("pe", nc.tensor.matmul, outs=[ps], ins=list(k_ts) + list(q_ts), out=ps[:],
                 lhsT=kt_sb[0:KR, kt * 128:(kt + 1) * 128], rhs=qap, start=True, stop=(extra_mm is None))
            if extra_mm is not None:
                extra_mm(ps, kt)
            c.op("act", nc.scalar.activation, outs=[p], ins=[ps], out=p[:], in_=ps[:], func=AF.Exp)
            m = mask_of(kt)
            if m is not None:
                c.op("dve", nc.vector.tensor_tensor, outs=[p], ins=[p, mask_t], out=p[:], in0=p[:], in1=m,
                     op=ALU.mult)
            pend.append((n, kt, p))
            if pre_hook is not None:
                if n == min(LAG, N) - 1:
                    pre_hook[0]()
                if n == min(LAG + 6, N) - 1:
                    pre_hook[1]()
                    pre_hook = None
        if n >= LAG:
            m_, kt, p = pend.pop(0)
            c.op("pe", nc.tensor.matmul, outs=[po], ins=[v_sb, p], out=po[0:65, :], lhsT=v_sb[:, kt, 0:65],
                 rhs=p[:], start=(m_ == 0), stop=(m_ == N - 1))

    def fin_a():
        c.op("act", nc.scalar.copy, outs=[o32], ins=[po], out=o32[:], in_=po[0:64, :])
        c.op("dve", nc.vector.tensor_scalar, outs=[rs], ins=[po], out=rs[64:65, :], in0=po[64:65, :],
             scalar1=1e-30, scalar2=None, op0=ALU.max)
        c.op("dve", nc.vector.reciprocal, outs=[rs], ins=[rs], out=rs[64:65, :], in_=rs[64:65, :])

    def fin_b():
        c.op("pe", nc.tensor.matmul, outs=[R.ps_b], ins=[R.ones32, rs], out=R.ps_b[0:64, :],
             lhsT=R.ones32[64:65, 0:64], rhs=rs[64:65, :], start=True, stop=True)
    return o32, R.ps_b, ob, (fin_a, fin_b)


def build_A_soft(kind):
    nc = new_nc()
    KR = 96
    if kind == "moba":
        QT = din(nc, "QT", [512, S], BF16)
        KT = din(nc, "KT", [512, S], BF16)
        Eind = din(nc, "Eind", [32, S], BF16)
        ident = din(nc, "ident", [128, 128], F32)
    else:
        QT = din(nc, "QT", [8 * 96, S], BF16)
        KT = din(nc, "KNT", [512, S], BF16)
        KRT = din(nc, "KRT", [32, S], BF16)
    V = din(nc, "V", [S, 512], BF16)
    maskC = din(nc, "maskC", [128, 4, 512], BF16)
    OT = dout(nc, "OT", [512, S], BF16)
    c = make_ctx(nc)
    m_sb = c.sb("maskC", [128, 4, 512], BF16)
    c.dma("sp", m_sb[:], maskC, out_t=m_sb)
    R = SoftmaxRes(c, nc)
    qts = [c.sb(f"qt{i}", [KR, S], BF16) for i in range(2)]
    q_hi = [c.view(q[64:96, :], f"{q.name}_hi") for q in qts]
    kts = [c.sb(f"kt{i}", [KR, S], BF16) for i in range(2)]
    vs = [c.sb(f"v{i}", [128, 64, 65], BF16) for i in range(2)]
    for v in vs:
        c.op("pool", nc.gpsimd.memset, outs=[v], ap=v[:, :, 64:65], constant=1.0)
    for k in kts:
        if kind == "moba":
            c.dma("sp", k[64:96, :], Eind, out_t=k)
        else:
            c.dma("sp", k[64:96, :].rearrange("p (r t) -> p r t", r=2), fm_shared(KRT), out_t=k)
    if kind == "moba":
        id_sb = c.sb("ident", [128, 128], F32)
        c.dma("sp", id_sb[:], ident, out_t=id_sb)
        g_sb = c.sb("g_sb", [128, 32], F32)
        m8 = c.sb("m8", [128, 8], F32)
        nms = [c.sb(f"nm{i}", [128, 96], F32) for i in range(8)]
        for nm in nms:
            c.op("dve", nc.vector.memset, outs=[nm], ap=nm[:], constant=0.0)
        kms32 = c.sb("kms32", [64, 32], F32)
        kms = c.sb("kms", [64, 32], BF16)
        ps_g = c.ps("psg", [128, 512], F32)
        ps_t = c.ps("pst", [128, 512], F32)
    def load_head(h):
        qt = qts[h % 2]; kt_ = kts[h % 2]; v = vs[h % 2]
        c.dma("sp", kt_[0:64, :].rearrange("p (r t) -> p r t", r=2), fm_rows(KT, h * 64, (h + 1) * 64), out_t=kt_)
        if kind == "moba":
            c.dma("sp", qt[0:64, :].rearrange("p (r t) -> p r t", r=2), fm_rows(QT, h * 64, (h + 1) * 64), out_t=qt)
        else:
            c.dma("sp", qt[:].rearrange("p (r t) -> p r t", r=2), fm_rows(QT, h * 96, (h + 1) * 96), out_t=qt)
        for k0_, nk_, src_ in tm_pieces(V, h * 64, (h + 1) * 64):
            c.dma("sp", v[:, k0_:k0_ + nk_, 0:64], src_, out_t=v)

    def prepass(h):
        qt = qts[h % 2]; kt_ = kts[h % 2]; qhi = q_hi[h % 2]
        c.op("dve", nc.vector.memset, outs=[g_sb], ap=g_sb[:], constant=-1e30)
        c.op("dve", nc.vector.tensor_reduce, outs=[kms32], ins=[kt_], out=kms32[:],
             in_=kt_[0:64, :].rearrange("p (n k) -> p n k", k=256), axis=AX.X, op=ALU.add)
        c.op("dve", nc.vector.tensor_copy, outs=[kms], ins=[kms32], out=kms[:], in_=kms32[:])

        def flush(ch):
            for grp in range(4):
                nm = nms[(ch % 2) * 4 + grp]
                c.op("pe", nc.tensor.transpose, outs=[ps_t], ins=[nm, id_sb],
                     out=ps_t[0:96, grp * 128:(grp + 1) * 128], in_=nm[:], identity=id_sb[:])
            c.op("act", nc.scalar.copy, outs=[qhi], ins=[ps_t], out=qt[64:96, ch * 512:(ch + 1) * 512],
                 in_=ps_t[64:96, :])
        for ch in range(16):
            for grp in range(4):
                qi = ch * 4 + grp
                if qi // 2 > 3:
                    c.op("pe", nc.tensor.matmul, outs=[ps_g], ins=[qt, kms], out=ps_g[:, grp * 32:(grp + 1) * 32],
                         lhsT=qt[0:64, qi * 128:(qi + 1) * 128], rhs=kms[:], start=True, stop=True)
            if ch > 0:
                flush(ch - 1)
            for grp in range(4):
                qi = ch * 4 + grp
                own = qi // 2
                nm = nms[(ch % 2) * 4 + grp]
                if own > 3:
                    c.op("dve", nc.vector.tensor_copy, outs=[g_sb], ins=[ps_g], out=g_sb[:, 0:own],
                         in_=ps_g[:, grp * 32:grp * 32 + own])
                    c.op("dve", nc.vector.max, outs=[m8], ins=[g_sb], out=m8[:], in_=g_sb[:])
                    c.op("dve", nc.vector.tensor_scalar, outs=[nm], ins=[g_sb, m8], out=nm[:, 64:64 + own],
                         in0=g_sb[:, 0:own], scalar1=m8[:, 2:3], scalar2=-BIG, op0=ALU.is_lt, op1=ALU.mult)
                else:
                    if own > 0:
                        c.op("dve", nc.vector.memset, outs=[nm], ap=nm[:, 64:64 + own], constant=0.0)
                c.op("dve", nc.vector.memset, outs=[nm], ap=nm[:, 64 + own:65 + own], constant=0.0)
                if own < 31:
                    c.op("dve", nc.vector.memset, outs=[nm], ap=nm[:, 65 + own:96], constant=-BIG)
            yield
        flush(15)
        yield

    load_head(0)
    if kind == "moba":
        for _ in prepass(0):
            pass
    pending = [None]
    for h in range(8):
        qt = qts[h % 2]; kt_ = kts[h % 2]; v = vs[h % 2]; qhi = q_hi[h % 2]
        q_ts = [qt, qhi] if kind == "moba" else [qt]
        gen = None
        if h + 1 < 8:
            if kind == "moba":
                qts[(h + 1) % 2].r.extend(q_hi[(h + 1) % 2].r)
            load_head(h + 1)
            if kind == "moba":
                gen = prepass(h + 1)
        for i in range(S // 512):
            kl = list(range(0, 4 * i + 4))
            o32, pbc, ob, (fin_a, fin_b) = softmax_chunk(
                c, nc, R, [kt_], q_ts, kt_, qt, v, KR, i, m_sb, kl,
                lambda kt, i=i: (m_sb[:, kt - 4 * i, :] if kt >= 4 * i else None), pre_hook=pending[0])

            def fin2(o32=o32, pbc=pbc, ob=ob, fin_b=fin_b, h=h, i=i):
                fin_b()
                c.op("dve", nc.vector.tensor_tensor, outs=[ob], ins=[o32, pbc], out=ob[:], in0=o32[:],
                     in1=pbc[0:64, :], op=ALU.mult)
                c.dma("pool", OT[h * 64:(h + 1) * 64, i * 512:(i + 1) * 512], ob[:], in_t=ob)
            pending[0] = (fin_a, fin2)
            if gen is not None:
                next(gen, None)
        if gen is not None:
            for _ in gen:
                pass
    if pending[0] is not None:
        pending[0][0]()
        pending[0][1]()
    c.finish("pool")
    c.close()
    return nc


def rope_tables_fm(dim, rows):
    inv = 1.0 / (10000.0 ** (np.arange(0, dim, 2, dtype=np.float32) / dim))
    ang = np.arange(S, dtype=np.float32)[:, None] * inv[None, :]
    ang = np.concatenate([ang, ang], axis=-1)
    cos = np.cos(ang).astype(np.float32).T
    sin = np.sin(ang).astype(np.float32).T
    reps = rows // dim
    return np.ascontiguousarray(np.tile(cos, (reps, 1))), np.ascontiguousarray(np.tile(sin, (reps, 1)))


def layer1(hT_list, p, cs):
    ncP = get_nc("P_moba", build_P_qkv, True, 0.125)
    cosF, sinF = rope_tables_fm(64, 128)
    pres = launch(ncP, [{"hT": hT_list[c], "w": p["l1_moba_w_qkv"],
                         "cosT": np.ascontiguousarray(cosF[:, (c % 2) * TOK:(c % 2 + 1) * TOK]),
                         "sinT": np.ascontiguousarray(sinF[:, (c % 2) * TOK:(c % 2 + 1) * TOK])}
                        for c in range(NCORES)])
    QT = gather_heads(pres, "QT", True)
    KT = gather_heads(pres, "KT", True)
    V = gather_heads(pres, "V", False)
    ncA = get_nc("A_moba", build_A_soft, "moba")
    Eind = (np.arange(S)[None, :] // 256 == np.arange(32)[:, None]).astype(NPBF)
    ident = np.eye(128, dtype=np.float32)
    ares = launch(ncA, [{"QT": QT[c], "KT": KT[c], "V": V[c], "maskC": cs["maskC"], "Eind": Eind, "ident": ident}
                        for c in range(NCORES)])
    OT = scatter_OT(ares)
    return run_M(hT_list, OT, p["l1_moba_w_o"], p["l1_mlp_w1"], p["l1_mlp_w2"], p["l1_ln1_g"], p["l1_ln1_b"],
                 p["l1_ln2_g"], p["l1_ln2_b"]), dict(pres=pres, ares=ares, OT=OT)


def build_P_mla():
    nc = new_nc()
    hT = din(nc, "hT", [D, TOK], F32)
    w_in = din(nc, "w_in", [D, 416], F32)
    w_uq = din(nc, "w_uq", [256, 1536], F32)
    w_ukv = din(nc, "w_ukv", [128, 2048], F32)
    gq = din(nc, "gq", [128, 2], F32)
    gkv = din(nc, "gkv", [128, 1], F32)
    cos96 = din(nc, "cos96", [96, TOK], F32)
    sin96 = din(nc, "sin96", [96, TOK], F32)
    QT = dout(nc, "QT", [1536, TOK], BF16)
    KNT = dout(nc, "KNT", [D, TOK], BF16)
    KRT = dout(nc, "KRT", [32, TOK], BF16)
    V = dout(nc, "V", [TOK, D], BF16)
    c = make_ctx(nc)
    cast = Caster(c)
    NT = 512
    QSC = float(96 ** -0.5)
    stages = [c.sb(f"stg{i}", [128, 1024], F32) for i in range(2)]
    win_b, win_ch = load_weight_bf16(c, cast, w_in, D, 416, "win", stages)
    wuq_b, wuq_ch = load_weight_bf16(c, cast, w_uq, 256, 1536, "wuq", stages)
    wukv_b, wukv_ch = load_weight_bf16(c, cast, w_ukv, 128, 2048, "wukv", stages)
    winr = c.sb("winr", [128, 8, 32], BF16)
    for kc in range(8):
        c.op("dve", nc.vector.tensor_scalar, outs=[winr], ins=[win_ch[kc]], out=winr[:, kc, 0:16],
             in0=win_b[:, kc, 400:416], scalar1=-1.0, scalar2=None, op0=ALU.mult)
        c.op("dve", nc.vector.tensor_copy, outs=[winr], ins=[win_ch[kc]], out=winr[:, kc, 16:32],
             in_=win_b[:, kc, 384:400])
    wuqr = c.sb("wuqr", [128, 2, 1536], BF16)
    c.op("pool", nc.gpsimd.memset, outs=[wuqr], ap=wuqr[:], constant=0.0)
    for fc in range(2):
        src = wuq_b[:, fc, :].rearrange("p (h d) -> p h d", d=96)
        dst = wuqr[:, fc, :].rearrange("p (h d) -> p h d", d=96)
        c.op("dve", nc.vector.tensor_scalar, outs=[wuqr], ins=[wuq_ch[fc]], out=dst[:, :, 64:80],
             in0=src[:, :, 80:96], scalar1=-1.0, scalar2=None, op0=ALU.mult)
        c.op("dve", nc.vector.tensor_copy, outs=[wuqr], ins=[wuq_ch[fc]], out=dst[:, :, 80:96],
             in_=src[:, :, 64:80])
    gq_sb = c.sb("gq", [128, 2], F32); gkv_sb = c.sb("gkv", [128, 1], F32)
    c.dma("sp", gq_sb[:], gq, out_t=gq_sb)
    c.dma("sp", gkv_sb[:], gkv, out_t=gkv_sb)
    cos_sb = c.sb("cos96", [96, TOK], F32); sin_sb = c.sb("sin96", [96, TOK], F32)
    c.dma("sp", cos_sb[:], cos96, out_t=cos_sb)
    c.dma("sp", sin_sb[:], sin96, out_t=sin_sb)
    ones32 = c.sb("ones32", [128, 128], F32)
    c.op("dve", nc.vector.memset, outs=[ones32], ap=ones32[:], constant=1.0)
    hts = [c.sb(f"ht{i}", [128, 8, NT], F32) for i in range(2)]
    hb = c.sb("hb", [128, 8, NT], BF16)
    cq32 = c.sb("cq32", [128, 3, NT], F32)
    cqn = c.sb("cqn", [128, 3, NT], BF16)
    sq_sb = [c.sb(f"sq{i}", [128, NT], F32) for i in range(2)]
    rstd = [c.sb(f"rstd{i}", [128, NT], F32) for i in range(2)]
    t1 = [c.sb(f"t1_{i}", [96, NT], F32) for i in range(2)]
    t2 = [c.sb(f"t2_{i}", [96, NT], F32) for i in range(2)]
    q_sb = [c.sb(f"q_sb{i}", [96, NT], BF16) for i in range(2)]
    kn_sb = [c.sb(f"kn_sb{i}", [64, NT], BF16) for i in range(2)]
    kr_sb = c.sb("kr_sb", [32, NT], BF16)
    v_sb = c.sb("v_sb", [128, 4, D], BF16)
    banks = [c.ps(f"pb{i}", [128, 512], F32) for i in range(7)]
    psQ = c.ps("psQ", [128, 512], F32)
    bi = 0
    hT_v = hT.rearrange("(kc p) t -> p kc t", p=128)
    for tt in range(TOK // NT):
        sl = slice(tt * NT, (tt + 1) * NT)
        ht = hts[tt % 2]
        c.dma("sp", ht[:], hT_v[:, :, sl], out_t=ht)
        for kc in range(8):
            cast.copy(hb, hb[:, kc, :], ht, ht[:, kc, :], eng=("dve", "pool")[kc % 2])
        for fc in range(3):
            pb = banks[bi % 7]; bi += 1
            for kc in range(8):
                c.op("pe", nc.tensor.matmul, outs=[pb], ins=[win_ch[kc], hb], out=pb[:, 0:NT],
                     lhsT=win_b[:, kc, fc * 128:(fc + 1) * 128], rhs=hb[:, kc, :], start=(kc == 0), stop=(kc == 7))
            c.op("act", nc.scalar.copy, outs=[cq32], ins=[pb], out=cq32[:, fc, :], in_=pb[:, 0:NT])
        pb = banks[bi % 7]; bi += 1
        pr = banks[bi % 7]; bi += 1
        for kc in range(8):
            c.op("pe", nc.tensor.matmul, outs=[pb], ins=[win_ch[kc], hb], out=pb[0:32, 0:NT],
                 lhsT=win_b[:, kc, 384:416], rhs=hb[:, kc, :], start=(kc == 0), stop=(kc == 7))
        for kc in range(8):
            c.op("pe", nc.tensor.matmul, outs=[pr], ins=[winr, hb], out=pr[0:32, 0:NT],
                 lhsT=winr[:, kc, :], rhs=hb[:, kc, :], start=(kc == 0), stop=(kc == 7))
        a = t1[0]; b_ = t2[0]
        c.op("dve", nc.vector.tensor_tensor, outs=[a], ins=[pb, cos_sb], out=a[0:32, :], in0=pb[0:32, 0:NT],
             in1=cos_sb[64:96, sl], op=ALU.mult)
        c.op("dve", nc.vector.tensor_tensor, outs=[b_], ins=[pr, sin_sb], out=b_[0:32, :], in0=pr[0:32, 0:NT],
             in1=sin_sb[64:96, sl], op=ALU.mult)
        c.op("pool", nc.gpsimd.tensor_tensor, outs=[kr_sb], ins=[a, b_], out=kr_sb[:], in0=a[0:32, :],
             in1=b_[0:32, :], op=ALU.add)
        c.dma("pool", KRT[:, sl], kr_sb[:], in_t=kr_sb)
        for grp, (chs, gsb, n) in enumerate((((0, 1), gq_sb, 256), ((2,), gkv_sb, 128))):
            rs_ = rstd[grp]
            for ci, ch in enumerate(chs):
                sq = sq_sb[ci % 2]
                c.op("act", nc.scalar.activation, outs=[sq], ins=[cq32], out=sq[:], in_=cq32[:, ch, :],
                     func=AF.Square)
                c.op("pe", nc.tensor.matmul, outs=[psQ], ins=[ones32, sq], out=psQ[:, 0:NT], lhsT=ones32[:],
                     rhs=sq[:], start=(ci == 0), stop=(ci == len(chs) - 1))
            c.op("act", nc.scalar.activation, outs=[rs_], ins=[psQ], out=rs_[:], in_=psQ[:, 0:NT], func=AF.Sqrt,
                 scale=1.0 / n, bias=RMS_EPS)
            c.op("dve", nc.vector.reciprocal, outs=[rs_], ins=[rs_], out=rs_[:], in_=rs_[:])
            for ci, ch in enumerate(chs):
                c.op("dve", nc.vector.scalar_tensor_tensor, outs=[cqn], ins=[cq32, gsb, rs_], out=cqn[:, ch, :],
                     in0=cq32[:, ch, :], scalar=gsb[:, ci:ci + 1], in1=rs_[:], op0=ALU.mult, op1=ALU.mult)
        for h in range(16):
            pb = banks[bi % 7]; bi += 1
            pr = banks[bi % 7]; bi += 1
            for fc in range(2):
                c.op("pe", nc.tensor.matmul, outs=[pb], ins=[wuq_ch[fc], cqn], out=pb[0:96, 0:NT],
                     lhsT=wuq_b[:, fc, h * 96:(h + 1) * 96], rhs=cqn[:, fc, :], start=(fc == 0), stop=(fc == 1))
            for fc in range(2):
                c.op("pe", nc.tensor.matmul, outs=[pr], ins=[wuqr, cqn], out=pr[0:96, 0:NT],
                     lhsT=wuqr[:, fc, h * 96:(h + 1) * 96], rhs=cqn[:, fc, :], start=(fc == 0), stop=(fc == 1))
            a = t1[h % 2]; b_ = t2[h % 2]; qs = q_sb[h % 2]
            c.op("dve", nc.vector.tensor_tensor, outs=[a], ins=[pb, cos_sb], out=a[:], in0=pb[0:96, 0:NT],
                 in1=cos_sb[:, sl], op=ALU.mult)
            c.op("dve", nc.vector.tensor_tensor, outs=[b_], ins=[pr, sin_sb], out=b_[:], in0=pr[0:96, 0:NT],
                 in1=sin_sb[:, sl], op=ALU.mult)
            c.op("pool", nc.gpsimd.tensor_tensor, outs=[a], ins=[a, b_], out=a[:], in0=a[:], in1=b_[:], op=ALU.add)
            c.op("act", nc.scalar.mul, outs=[qs], ins=[a], out=qs[:], in_=a[:], mul=QSC)
            c.dma("pool", QT[h * 96:(h + 1) * 96, sl], qs[:], in_t=qs)
        wk_v = wukv_b[:, 0, :].rearrange("p (h two d) -> p h two d", two=2, d=64)
        for h in range(16):
            pb = banks[bi % 7]; bi += 1
            c.op("pe", nc.tensor.matmul, outs=[pb], ins=[wukv_ch[0], cqn], out=pb[0:64, 0:NT],
                 lhsT=wk_v[:, h, 0, :], rhs=cqn[:, 2, :], start=True, stop=True)
            ks = kn_sb[h % 2]
            if h % 2 == 0:
                c.op("act", nc.scalar.copy, outs=[ks], ins=[pb], out=ks[:], in_=pb[0:64, 0:NT])
            else:
                c.op("dve", nc.vector.tensor_copy, outs=[ks], ins=[pb], out=ks[:], in_=pb[0:64, 0:NT])
            c.dma("pool", KNT[h * 64:(h + 1) * 64, sl], ks[:], in_t=ks)
        for j in range(4):
            for half in range(2):
                pb = banks[bi % 7]; bi += 1
                c.op("pe", nc.tensor.matmul, outs=[pb], ins=[wukv_ch[0], cqn], out=pb[:],
                     lhsT=cqn[:, 2, j * 128:(j + 1) * 128], rhs=wk_v[:, half * 8:(half + 1) * 8, 1, :],
                     start=True, stop=True)
                if half == 0:
                    c.op("act", nc.scalar.copy, outs=[v_sb], ins=[pb], out=v_sb[:, j, 0:512], in_=pb[:])
                else:
                    c.op("dve", nc.vector.tensor_copy, outs=[v_sb], ins=[pb], out=v_sb[:, j, 512:1024], in_=pb[:])
        V_v = V.rearrange("(j p) n -> p j n", p=128)
        c.dma("pool", V_v[:, tt * 4:(tt + 1) * 4, :], v_sb[:], in_t=v_sb)
    c.finish("pool")
    c.close()
    return nc


def layer2(hT_list, p, cs):
    ncP = get_nc("P_mla", build_P_mla)
    cos32, sin32 = rope_tables_fm(32, 32)
    cos96 = np.concatenate([np.ones((64, S), np.float32), cos32], axis=0)
    sin96 = np.concatenate([np.zeros((64, S), np.float32), sin32], axis=0)
    gq = np.ascontiguousarray(p["l2_mla_q_norm"].reshape(2, 128).T).astype(np.float32)
    gkv = np.ascontiguousarray(p["l2_mla_kv_norm"].reshape(1, 128).T).astype(np.float32)
    pres = launch(ncP, [{"hT": hT_list[c], "w_in": p["l2_mla_w_in"], "w_uq": p["l2_mla_w_uq"],
                         "w_ukv": p["l2_mla_w_ukv"], "gq": gq, "gkv": gkv,
                         "cos96": np.ascontiguousarray(cos96[:, (c % 2) * TOK:(c % 2 + 1) * TOK]),
                         "sin96": np.ascontiguousarray(sin96[:, (c % 2) * TOK:(c % 2 + 1) * TOK])}
                        for c in range(NCORES)])
    QT = gather_heads(pres, "QT", True)
    KNT = gather_heads(pres, "KNT", True)
    V = gather_heads(pres, "V", False)
    KRT = [np.ascontiguousarray(np.concatenate([pres[2 * (c // 2)]["KRT"], pres[2 * (c // 2) + 1]["KRT"]], axis=1))
           for c in range(NCORES)]
    ncA = get_nc("A_mla", build_A_soft, "mla")
    ares = launch(ncA, [{"QT": QT[c], "KNT": KNT[c], "KRT": KRT[c], "V": V[c], "maskC": cs["maskC"]}
                        for c in range(NCORES)])
    OT = scatter_OT(ares)
    return run_M(hT_list, OT, p["l2_mla_w_o"], p["l2_mlp_w1"], p["l2_mlp_w2"], p["l2_ln1_g"], p["l2_ln1_b"],
                 p["l2_ln2_g"], p["l2_ln2_b"]), dict(pres=pres, ares=ares, OT=OT)


NSA_IN = 2608


def build_P_nsa():
    nc = new_nc()
    hT = din(nc, "hT", [D, TOK], F32)
    w = din(nc, "w", [D, NSA_IN], F32)
    cosT = din(nc, "cosT", [128, TOK], F32)
    sinT = din(nc, "sinT", [128, TOK], F32)
    QT = dout(nc, "QT", [D, TOK], BF16)
    KcT = dout(nc, "KcT", [256, TOK], BF16)
    VcT = dout(nc, "VcT", [256, TOK], BF16)
    KsT = dout(nc, "KsT", [256, TOK], BF16)
    KwT = dout(nc, "KwT", [256, TOK], BF16)
    Vs = dout(nc, "Vs", [TOK, 256], BF16)
    Vw = dout(nc, "Vw", [TOK, 256], BF16)
    GT = dout(nc, "GT", [64, TOK], F32)
    c = make_ctx(nc)
    cast = Caster(c)
    NT = 512
    stages = [c.sb(f"stg{i}", [128, 1024], F32) for i in range(2)]
    wb, wch = load_weight_bf16(c, cast, w, D, NSA_IN, "wb", stages)
    roped = [(0, 8, QT, 0.125), (1024, 2, KcT, 1.0), (1536, 2, KsT, 1.0), (2048, 2, KwT, 1.0)]
    wr = c.sb("wrot", [128, 8, 14 * 128], BF16)
    wrch = [c.view(wr[:, kc, :], f"wrot_{kc}") for kc in range(8)]
    rcol = {}
    o = 0
    for (c0, nch, _, _) in roped:
        rcol[c0] = o
        for kc in range(8):
            src = wb[:, kc, c0:c0 + nch * 128].rearrange("p (h two d) -> p h two d", two=2, d=32)
            dst = wr[:, kc, o:o + nch * 128].rearrange("p (h two d) -> p h two d", two=2, d=32)
            c.op("dve", nc.vector.tensor_scalar, outs=[wrch[kc]], ins=[wch[kc]],
                 out=dst[:, :, 0, :], in0=src[:, :, 1, :], scalar1=-1.0, scalar2=None, op0=ALU.mult)
            c.op("pool", nc.gpsimd.tensor_copy, outs=[wrch[kc]], ins=[wch[kc]],
                 out=dst[:, :, 1, :], in_=src[:, :, 0, :])
        o += nch * 128
    cos_sb = c.sb("cos_sb", [128, TOK], F32)
    sin_sb = c.sb("sin_sb", [128, TOK], F32)
    c.dma("sp", cos_sb[:], cosT, out_t=cos_sb)
    c.dma("sp", sin_sb[:], sinT, out_t=sin_sb)
    hts = [c.sb(f"ht{i}", [128, 8, NT], F32) for i in range(2)]
    hb = c.sb("hb", [128, 8, NT], BF16)
    osb = [c.sb(f"osb{i}", [128, NT], BF16) for i in range(3)]
    t1 = [c.sb(f"t1_{i}", [128, NT], F32) for i in range(2)]
    t2 = [c.sb(f"t2_{i}", [128, NT], F32) for i in range(2)]
    v_sb = c.sb("v_sb", [128, 4, 512], BF16)
    g_sb = c.sb("g_sb", [64, NT], F32)
    banks = [c.ps(f"pb{i}", [128, 512], F32) for i in range(8)]
    bi = 0
    oi = 0
    hT_v = hT.rearrange("(kc p) t -> p kc t", p=128)
    for tt in range(TOK // NT):
        sl = slice(tt * NT, (tt + 1) * NT)
        ht = hts[tt % 2]
        c.dma("sp", ht[:], hT_v[:, :, sl], out_t=ht)
        for kc in range(8):
            cast.copy(hb, hb[:, kc, :], ht, ht[:, kc, :], eng=("dve", "pool")[kc % 2])
        for (c0, nch, out_d, sc) in roped:
            for fc in range(nch):
                col = c0 + fc * 128
                rc = rcol[c0] + fc * 128
                pb = banks[bi % 8]; bi += 1
                pr = banks[bi % 8]; bi += 1
                for kc in range(8):
                    c.op("pe", nc.tensor.matmul, outs=[pb], ins=[wch[kc], hb], out=pb[:, 0:NT],
                         lhsT=wb[:, kc, col:col + 128], rhs=hb[:, kc, :], start=(kc == 0), stop=(kc == 7))
                for kc in range(8):
                    c.op("pe", nc.tensor.matmul, outs=[pr], ins=[wrch[kc], hb], out=pr[:, 0:NT],
                         lhsT=wr[:, kc, rc:rc + 128], rhs=hb[:, kc, :], start=(kc == 0), stop=(kc == 7))
                a = t1[oi % 2]; b_ = t2[oi % 2]; ob = osb[oi % 3]; oi += 1
                c.op("dve", nc.vector.tensor_tensor, outs=[a], ins=[pb, cos_sb], out=a[:], in0=pb[:, 0:NT],
                     in1=cos_sb[:, sl], op=ALU.mult)
                c.op("dve", nc.vector.tensor_tensor, outs=[b_], ins=[pr, sin_sb], out=b_[:], in0=pr[:, 0:NT],
                     in1=sin_sb[:, sl], op=ALU.mult)
                c.op("pool", nc.gpsimd.tensor_tensor, outs=[a], ins=[a, b_], out=a[:], in0=a[:], in1=b_[:],
                     op=ALU.add)
                c.op("act", nc.scalar.mul, outs=[ob], ins=[a], out=ob[:], in_=a[:], mul=sc)
                c.dma("pool", out_d[fc * 128:(fc + 1) * 128, sl], ob[:], in_t=ob)
        for fc in range(2):
            col = 1280 + fc * 128
            pb = banks[bi % 8]; bi += 1
            for kc in range(8):
                c.op("pe", nc.tensor.matmul, outs=[pb], ins=[wch[kc], hb], out=pb[:, 0:NT],
                     lhsT=wb[:, kc, col:col + 128], rhs=hb[:, kc, :], start=(kc == 0), stop=(kc == 7))
            ob = osb[oi % 3]; oi += 1
            c.op("act", nc.scalar.copy, outs=[ob], ins=[pb], out=ob[:], in_=pb[:, 0:NT])
            c.dma("pool", VcT[fc * 128:(fc + 1) * 128, sl], ob[:], in_t=ob)
        import os as _os
        pb = banks[bi % 8]; bi += 1
        for kc in range(8):
            if _os.environ.get("NOGATE"): break
            c.op("pe", nc.tensor.matmul, outs=[pb], ins=[wch[kc], hb], out=pb[0:64, 0:NT],
                 lhsT=wb[:, kc, 2544:2608], rhs=hb[:, kc, :], start=(kc == 0), stop=(kc == 7))
        if not _os.environ.get("NOGATE"):
            c.op("act", nc.scalar.activation, outs=[g_sb], ins=[pb], out=g_sb[:], in_=pb[0:64, 0:NT], func=AF.Sigmoid)
            c.dma("pool", GT[:, sl], g_sb[:], in_t=g_sb)
        for j in range(4):
            pb = banks[bi % 8]; bi += 1
            for hi, col in enumerate((1792, 2304)):
                for kc in range(8):
                    c.op("pe", nc.tensor.matmul, outs=[pb], ins=[wch[kc], hb], out=pb[:, hi * 256:(hi + 1) * 256],
                         lhsT=hb[:, kc, j * 128:(j + 1) * 128], rhs=wb[:, kc, col:col + 256],
                         start=(kc == 0), stop=(kc == 7))
            c.op("act", nc.scalar.copy, outs=[v_sb], ins=[pb], out=v_sb[:, j, :], in_=pb[:])
        Vs_v = Vs.rearrange("(j p) n -> p j n", p=128)
        Vw_v = Vw.rearrange("(j p) n -> p j n", p=128)
        c.dma("pool", Vs_v[:, tt * 4:(tt + 1) * 4, :], v_sb[:, :, 0:256], in_t=v_sb)
        c.dma("pool", Vw_v[:, tt * 4:(tt + 1) * 4, :], v_sb[:, :, 256:512], in_t=v_sb)
    c.finish("pool")
    c.close()
    return nc


GELU_C = 1.5957691216057308


def build_A_nsa():
    nc = new_nc()
    QT = din(nc, "QT", [512, S], BF16)
    KcT = din(nc, "KcT", [128, S], BF16)
    VcT = din(nc, "VcT", [128, S], BF16)
    KsT = din(nc, "KsT", [128, S], BF16)
    KwT = din(nc, "KwT", [128, S], BF16)
    Vs = din(nc, "Vs", [S, 128], BF16)
    Vw = din(nc, "Vw", [S, 128], BF16)
    GT = din(nc, "GT", [32, S], F32)
    posT = din(nc, "posT", [64, 2, 32], F32)
    w1k = din(nc, "w1k", [2048, 256], F32)
    w1v = din(nc, "w1v", [2048, 256], F32)
    w2 = din(nc, "w2", [128, 2, 2, 64], F32)
    maskC = din(nc, "maskC", [128, 4, 512], BF16)
    maskL = din(nc, "maskL", [128, 4, 512], BF16)
    cmask = din(nc, "cmask", [128, 5, 512], BF16)
    ovl = din(nc, "ovl", [128, 4, 128], BF16)
    Eind = din(nc, "Eind", [128, S], BF16)
    JC = din(nc, "JC", [128, 128], F32)
    CB = din(nc, "CB", [128, 128], F32)
    ident = din(nc, "ident", [128, 128], F32)
    SelG = din(nc, "SelG", [32, 24 * 64], F32)
    OT = dout(nc, "OT", [512, S], BF16)
    c = make_ctx(nc)
    cast = Caster(c, engines=("dve", "pool"))

    def const(name, ap, shape, dt):
        t = c.sb(name, shape, dt)
        c.dma("sp", t[:], ap, out_t=t)
        return t
    mC = const("maskC", maskC, [128, 4, 512], BF16)
    mL = const("maskL", maskL, [128, 4, 512], BF16)
    cm = const("cmask", cmask, [128, 5, 512], BF16)
    ovl_sb = const("ovl", ovl, [128, 4, 128], BF16)
    E_sb = const("Eind", Eind, [128, S], BF16)
    JC_sb = const("JC", JC, [128, 128], F32)
    CB_sb = const("CB", CB, [128, 128], F32)
    id_sb = const("ident", ident, [128, 128], F32)
    SelG_sb = const("SelG", SelG, [32, 24 * 64], F32)
    posT32 = const("posT", posT, [64, 2, 32], F32)
    w2_32 = const("w2", w2, [128, 2, 2, 64], F32)
    posTb = c.sb("posTb", [64, 2, 32], BF16)
    c.op("dve", nc.vector.tensor_copy, outs=[posTb], ins=[posT32], out=posTb[:], in_=posT32[:])
    w2b = c.sb("w2b", [128, 2, 2, 64], BF16)
    c.op("dve", nc.vector.tensor_copy, outs=[w2b], ins=[w2_32], out=w2b[:], in_=w2_32[:])
    stg = [c.sb(f"stg{i}", [64, 4, 256], F32) for i in range(2)]
    W1 = []
    si = 0
    for nm_, wd in (("w1k", w1k), ("w1v", w1v)):
        t = c.sb(nm_, [64, 32, 256], BF16)
        wv = wd.rearrange("(l d) n -> d l n", d=64)
        for l0 in range(0, 32, 4):
            st = stg[si % 2]; si += 1
            c.dma("sp", st[:], wv[:, l0:l0 + 4, :], out_t=st)
            cast.copy(t, t[:, l0:l0 + 4, :], st, st[:])
        W1.append(t)
    ones32 = c.sb("ones32", [65, 64], F32)
    c.op("dve", nc.vector.memset, outs=[ones32], ap=ones32[:], constant=1.0)

    kcv_sb = c.sb("kcv", [64, S], BF16)
    ks_sb = c.sb("ksT", [64, S], BF16)
    kw_sb = c.sb("kwT", [64, S], BF16)
    vs_sb = c.sb("vs", [128, 64, 65], BF16)
    vw_sb = c.sb("vw", [128, 64, 65], BF16)
    vc_sb = c.sb("vc", [128, 4, 65], BF16)
    kcT_sb = c.sb("kcT", [64, 512], BF16)
    for v_ in (vs_sb, vw_sb):
        c.op("pool", nc.gpsimd.memset, outs=[v_], ap=v_[:, :, 64:65], constant=1.0)
    c.op("pool", nc.gpsimd.memset, outs=[vc_sb], ap=vc_sb[:, :, 64:65], constant=1.0)
    c.op("pool", nc.gpsimd.memset, outs=[kcT_sb], ap=kcT_sb[:], constant=0.0)
    b1_sb = c.sb("b1", [128, 2], F32)
    x32 = [c.sb(f"x32_{i}", [128, 512], F32) for i in range(2)]
    u32 = [c.sb(f"u32_{i}", [128, 512], F32) for i in range(2)]
    gel = c.sb("gel", [128, 2, 512], BF16)
    c.op("pool", nc.gpsimd.memset, outs=[gel], ap=gel[:], constant=0.0)
    qch = [c.sb(f"qch{i}", [64, 4, 512], BF16) for i in range(2)]
    gch = [c.sb(f"gch{i}", [32, 512], F32) for i in range(2)]
    for gc_ in gch:
        c.op("pool", nc.gpsimd.memset, outs=[gc_], ap=gc_[:], constant=0.0)
    pcs = [c.sb(f"pc{i}", [128, 512], BF16) for i in range(4)]
    p_sb = [c.sb(f"p{i}", [128, 512], BF16) for i in range(5)]
    o32 = [c.sb(f"o32_{i}", [64, 512], F32) for i in range(2)]
    rs = [c.sb(f"rs{i}", [65, 512], F32) for i in range(2)]
    on = [c.sb(f"on{i}", [64, 512], F32) for i in range(2)]
    acc = c.sb("acc", [64, 512], F32)
    stash = [c.sb(f"stash{i}", [64, 512], F32) for i in range(4)]
    ob = [c.sb(f"ob{i}", [64, 512], BF16) for i in range(2)]
    impacc = c.sb("impacc", [128, 4, 128], F32)
    rsT_sb = c.sb("rsT", [128, 4], F32)
    f1 = c.sb("f1", [128, 128], F32)
    pen = c.sb("pen", [128, 128], F32)
    imp3 = c.sb("imp3", [128, 128], F32)
    imp4 = c.sb("imp4", [128, 128], F32)
    m8a = c.sb("m8a", [128, 8], F32)
    m8b = c.sb("m8b", [128, 8], F32)
    nm = [c.sb(f"nm{i}", [128, 128], F32) for i in range(2)]
    nmT = c.sb("nmT", [128, 512], BF16)

    ps_s = [c.ps(f"pss{i}", [128, 512], F32) for i in range(3)]
    ps_o = [c.ps(f"pso{i}", [128, 512], F32) for i in range(2)]
    ps_b = c.ps("psb", [128, 512], F32)
    ps_imp = c.ps("psimp", [128, 512], F32)
    ps_t = c.ps("pst", [128, 512], F32)
    ps_g = ps_t
    cnt = {"s": 0, "p": 0, "o": 0}

    Vs_v = Vs.rearrange("(kt p) n -> p kt n", p=128)
    Vw_v = Vw.rearrange("(kt p) n -> p kt n", p=128)

    def attend(k_t, k_ap_of, qap, q_t, v_sb, v_ap_of, kt_list, mask_of, extra_mm=None, keep=None, pre_hook=None,
               defer=False):
        po = ps_o[cnt["o"] % 2]; o3 = o32[cnt["o"] % 2]; rs_ = rs[cnt["o"] % 2]
        cnt["o"] += 1
        LAG = 2
        N = len(kt_list)
        pend = []
        for n in range(N + LAG):
            if n < N:
                kt = kt_list[n]
                ps = ps_s[cnt["s"] % len(ps_s)]; cnt["s"] += 1
                if keep is not None:
                    p = keep[n]
                else:
                    p = p_sb[cnt["p"] % len(p_sb)]; cnt["p"] += 1
                c.op("pe", nc.tensor.matmul, outs=[ps], ins=[k_t, q_t], out=ps[:], lhsT=k_ap_of(kt), rhs=qap,
                     start=True, stop=(extra_mm is None))
                if extra_mm is not None:
                    extra_mm(ps, kt)
                c.op("act", nc.scalar.activation, outs=[p], ins=[ps], out=p[:], in_=ps[:], func=AF.Exp)
                m = mask_of(kt)
                if m is not None:
                    mt, map_ = m
                    c.op("dve", nc.vector.tensor_tensor, outs=[p], ins=[p, mt], out=p[:], in0=p[:], in1=map_,
                         op=ALU.mult)
                pend.append((n, kt, p))
                if pre_hook is not None:
                    if n == min(LAG, N) - 1:
                        pre_hook[0]()
                    if n == min(LAG + 6, N) - 1:
                        pre_hook[1]()
                        pre_hook = None
            if n >= LAG:
                m_, kt, p = pend.pop(0)
                c.op("pe", nc.tensor.matmul, outs=[po], ins=[v_sb, p], out=po[0:65, :], lhsT=v_ap_of(kt), rhs=p[:],
                     start=(m_ == 0), stop=(m_ == N - 1))

        def tail_a():
            c.op("act", nc.scalar.copy, outs=[o3], ins=[po], out=o3[:], in_=po[0:64, :])
            c.op("dve", nc.vector.tensor_scalar, outs=[rs_], ins=[po], out=rs_[64:65, :], in0=po[64:65, :],
                 scalar1=1e-30, scalar2=None, op0=ALU.max)
            c.op("dve", nc.vector.reciprocal, outs=[rs_], ins=[rs_], out=rs_[64:65, :], in_=rs_[64:65, :])

        def tail_b():
            c.op("pe", nc.tensor.matmul, outs=[ps_b], ins=[ones32, rs_], out=ps_b[0:64, :],
                 lhsT=ones32[64:65, 0:64], rhs=rs_[64:65, :], start=True, stop=True)
        if defer:
            return o3, rs_, (tail_a, tail_b)
        tail_a()
        tail_b()
        return o3, rs_

    oi = 0
    for g in range(2):
        c.dma("sp", ks_sb[:].rearrange("p (r t) -> p r t", r=2), fm_rows(KsT, g * 64, (g + 1) * 64), out_t=ks_sb)
        c.dma("sp", kw_sb[:].rearrange("p (r t) -> p r t", r=2), fm_rows(KwT, g * 64, (g + 1) * 64), out_t=kw_sb)
        for k0_, nk_, src_ in tm_pieces(Vs, g * 64, (g + 1) * 64):
            c.dma("sp", vs_sb[:, k0_:k0_ + nk_, 0:64], src_, out_t=vs_sb)
        for k0_, nk_, src_ in tm_pieces(Vw, g * 64, (g + 1) * 64):
            c.dma("sp", vw_sb[:, k0_:k0_ + nk_, 0:64], src_, out_t=vw_sb)
        for kv in range(2):
            src = (KcT, VcT)[kv]
            c.dma("sp", kcv_sb[:].rearrange("p (r t) -> p r t", r=2), fm_rows(src, g * 64, (g + 1) * 64), out_t=kcv_sb)
            W = W1[kv]
            for half in range(2):
                pb = ps_s[half]
                for l in range(32):
                    c.op("pe", nc.tensor.matmul, outs=[pb], ins=[W, posTb], out=pb[:, 0:1],
                         lhsT=W[:, l, half * 128:(half + 1) * 128], rhs=posTb[:, kv, l:l + 1],
                         start=(l == 0), stop=(l == 31))
                c.op("act", nc.scalar.copy, outs=[b1_sb], ins=[pb], out=b1_sb[:, half:half + 1], in_=pb[:, 0:1])
            for half in range(2):
                pb = ps_s[half]
                for l in range(32):
                    c.op("pe", nc.tensor.matmul, outs=[pb], ins=[W, kcv_sb], out=pb[:, 0:511],
                         lhsT=W[:, l, half * 128:(half + 1) * 128],
                         rhs=kcv_sb[:, l:l + 16 * 510 + 1:16], start=(l == 0), stop=(l == 31))
                x = x32[half]; u = u32[half]
                c.op("act", nc.scalar.activation, outs=[x], ins=[pb, b1_sb], out=x[:, 0:511], in_=pb[:, 0:511],
                     func=AF.Identity, bias=b1_sb[:, half:half + 1])
                c.op("dve", nc.vector.tensor_tensor, outs=[u], ins=[x], out=u[:, 0:511], in0=x[:, 0:511],
                     in1=x[:, 0:511], op=ALU.mult)
                c.op("dve", nc.vector.tensor_scalar, outs=[u], ins=[u], out=u[:, 0:511], in0=u[:, 0:511],
                     scalar1=0.044715, scalar2=1.0, op0=ALU.mult, op1=ALU.add)
                c.op("dve", nc.vector.tensor_tensor, outs=[u], ins=[u, x], out=u[:, 0:511], in0=u[:, 0:511],
                     in1=x[:, 0:511], op=ALU.mult)
                c.op("act", nc.scalar.activation, outs=[u], ins=[u], out=u[:, 0:511], in_=u[:, 0:511],
                     func=AF.Sigmoid, scale=GELU_C)
                c.op("dve", nc.vector.tensor_tensor, outs=[gel], ins=[u, x], out=gel[:, half, 0:511],
                     in0=u[:, 0:511], in1=x[:, 0:511], op=ALU.mult)
            if kv == 0:
                pb = ps_s[0]
                for half in range(2):
                    c.op("pe", nc.tensor.matmul, outs=[pb], ins=[w2b, gel], out=pb[0:64, 0:511],
                         lhsT=w2b[:, 0, half, :], rhs=gel[:, half, 0:511], start=(half == 0), stop=(half == 1))
                c.op("act", nc.scalar.copy, outs=[kcT_sb], ins=[pb], out=kcT_sb[:, 0:511], in_=pb[0:64, 0:511])
            else:
                for nt in range(4):
                    pb = ps_s[nt % 2]
                    for half in range(2):
                        c.op("pe", nc.tensor.matmul, outs=[pb], ins=[w2b, gel], out=pb[:, 0:64],
                             lhsT=gel[:, half, nt * 128:(nt + 1) * 128], rhs=w2b[:, 1, half, :],
                             start=(half == 0), stop=(half == 1))
                    c.op("act", nc.scalar.copy, outs=[vc_sb], ins=[pb], out=vc_sb[:, nt, 0:64], in_=pb[:, 0:64])
        for i in range(S // 512):
            qc = qch[i % 2]; gc = gch[i % 2]
            for r_ in range(4):
                c.dma("sp", qc[:, r_, :], fm_chunk(QT, g * 256 + r_ * 64, g * 256 + (r_ + 1) * 64, i), out_t=qc)
            c.dma("sp", gc[0:24, :], gt_chunk(GT, i), out_t=gc)
            n_ct = (32 * i + 30) // 128 + 1
            cres = []
            for r in range(4):
                def cmask_of(nt, i=i):
                    dlt = 512 * i - 2048 * nt
                    if dlt >= 2063:
                        return None
                    return (cm, cm[:, dlt // 512, :])
                o3, rs_ = attend(kcT_sb, lambda nt: kcT_sb[:, nt * 128:(nt + 1) * 128], qc[:, r, :], qc,
                                 vc_sb, lambda nt: vc_sb[:, nt, 0:65], list(range(n_ct)), cmask_of,
                                 keep=pcs)
                cres.append(None)
                hrow = (g * 4 + r) * 3
                c.op("dve", nc.vector.tensor_tensor, outs=[on[0]], ins=[o3, ps_b], out=on[0][:], in0=o3[:],
                     in1=ps_b[0:64, :], op=ALU.mult)
                c.op("pe", nc.tensor.matmul, outs=[ps_g], ins=[SelG_sb, gc], out=ps_g[0:64, :],
                     lhsT=SelG_sb[:, hrow * 64:(hrow + 1) * 64], rhs=gc[:], start=True, stop=True)
                st = stash[r]
                c.op("dve", nc.vector.tensor_tensor, outs=[st], ins=[on[0], ps_g], out=st[:], in0=on[0][:],
                     in1=ps_g[0:64, :], op=ALU.mult)
                for j in range(4):
                    for nt in range(n_ct):
                        c.op("pe", nc.tensor.matmul, outs=[ps_imp], ins=[pcs[nt], ovl_sb],
                             out=ps_imp[:, j * 128:(j + 1) * 128], lhsT=pcs[nt][:, j * 128:(j + 1) * 128],
                             rhs=ovl_sb[:, nt, :], start=(nt == 0), stop=(nt == n_ct - 1))
                for j in range(4):
                    c.op("pe", nc.tensor.matmul, outs=[ps_t], ins=[rs_, ones32], out=ps_t[:, j:j + 1],
                         lhsT=rs_[64:65, j * 128:(j + 1) * 128], rhs=ones32[64:65, 0:1], start=True, stop=True)
                c.op("act", nc.scalar.copy, outs=[rsT_sb], ins=[ps_t], out=rsT_sb[:], in_=ps_t[:, 0:4])
                for j in range(4):
                    if r == 0:
                        c.op("dve", nc.vector.tensor_scalar, outs=[impacc], ins=[ps_imp, rsT_sb],
                             out=impacc[:, j, :], in0=ps_imp[:, j * 128:(j + 1) * 128], scalar1=rsT_sb[:, j:j + 1],
                             scalar2=None, op0=ALU.mult)
                    else:
                        c.op("dve", nc.vector.scalar_tensor_tensor, outs=[impacc], ins=[ps_imp, rsT_sb, impacc],
                             out=impacc[:, j, :], in0=ps_imp[:, j * 128:(j + 1) * 128], scalar=rsT_sb[:, j:j + 1],
                             in1=impacc[:, j, :], op0=ALU.mult, op1=ALU.add)
            for j in range(4):
                T2 = 2 * (4 * i + j)
                n_ = nm[j % 2]
                c.op("dve", nc.vector.scalar_tensor_tensor, outs=[f1], ins=[JC_sb, CB_sb], out=f1[:], in0=JC_sb[:],
                     scalar=float(T2 - 1), in1=CB_sb[:], op0=ALU.is_ge, op1=ALU.mult)
                c.op("pool", nc.gpsimd.tensor_scalar, outs=[pen], ins=[JC_sb], out=pen[:], in0=JC_sb[:],
                     scalar1=float(T2), scalar2=-3e30, op0=ALU.is_gt, op1=ALU.mult)
                c.op("dve", nc.vector.tensor_tensor, outs=[imp3], ins=[impacc, f1], out=imp3[:], in0=impacc[:, j, :],
                     in1=f1[:], op=ALU.add)
                c.op("dve", nc.vector.memset, outs=[imp3], ap=imp3[:, 0:1], constant=2e9)
                c.op("dve", nc.vector.tensor_tensor, outs=[imp3], ins=[imp3, pen], out=imp3[:], in0=imp3[:],
                     in1=pen[:], op=ALU.add)
                c.op("dve", nc.vector.max, outs=[m8a], ins=[imp3], out=m8a[:], in_=imp3[:])
                c.op("dve", nc.vector.match_replace, outs=[imp4], ins=[m8a, imp3], out=imp4[:],
                     in_to_replace=m8a[:], in_values=imp3[:], imm_value=-2e30)
                c.op("dve", nc.vector.max, outs=[m8b], ins=[imp4], out=m8b[:], in_=imp4[:])
                c.op("dve", nc.vector.tensor_scalar, outs=[n_], ins=[imp3, m8b], out=n_[:], in0=imp3[:],
                     scalar1=m8b[:, 7:8], scalar2=-BIG, op0=ALU.is_lt, op1=ALU.mult)
                c.op("pe", nc.tensor.transpose, outs=[ps_t], ins=[n_, id_sb], out=ps_t[:, j * 128:(j + 1) * 128],
                     in_=n_[:], identity=id_sb[:])
            c.op("act", nc.scalar.copy, outs=[nmT], ins=[ps_t], out=nmT[:], in_=ps_t[:])
            pend_fin = None
            for r in range(4):
                hrow = (g * 4 + r) * 3

                def sel_extra(ps, kt):
                    c.op("pe", nc.tensor.matmul, outs=[ps], ins=[E_sb, nmT], out=ps[:],
                         lhsT=E_sb[:, kt * 128:(kt + 1) * 128], rhs=nmT[:], start=False, stop=True)
                o3s, _, tail_s = attend(ks_sb, lambda kt: ks_sb[:, kt * 128:(kt + 1) * 128], qc[:, r, :], qc,
                                        vs_sb, lambda kt: vs_sb[:, kt, 0:65], list(range(4 * i + 4)),
                                        lambda kt, i=i: ((mC, mC[:, kt - 4 * i, :]) if kt >= 4 * i else None),
                                        extra_mm=sel_extra, pre_hook=pend_fin, defer=True)

                def fin_sel(o3s=o3s, tail_s=tail_s, hrow=hrow, r=r, gc=gc):
                    tail_s[1]()
                    c.op("dve", nc.vector.tensor_tensor, outs=[on[0]], ins=[o3s, ps_b], out=on[0][:], in0=o3s[:],
                         in1=ps_b[0:64, :], op=ALU.mult)
                    c.op("pe", nc.tensor.matmul, outs=[ps_g], ins=[SelG_sb, gc], out=ps_g[0:64, :],
                         lhsT=SelG_sb[:, (hrow + 1) * 64:(hrow + 2) * 64], rhs=gc[:], start=True, stop=True)
                    c.op("dve", nc.vector.tensor_tensor, outs=[on[0]], ins=[on[0], ps_g], out=on[0][:], in0=on[0][:],
                         in1=ps_g[0:64, :], op=ALU.mult)
                    c.op("pool", nc.gpsimd.tensor_tensor, outs=[acc], ins=[on[0], stash[r]], out=acc[:],
                         in0=on[0][:], in1=stash[r][:], op=ALU.add)
                wl = [kt for kt in range(4 * i - 4, 4 * i + 4) if kt >= 0]
                o3w, _, tail_w = attend(kw_sb, lambda kt: kw_sb[:, kt * 128:(kt + 1) * 128], qc[:, r, :], qc,
                                        vw_sb, lambda kt: vw_sb[:, kt, 0:65], wl,
                                        lambda kt, i=i: ((mC, mC[:, kt - 4 * i, :]) if kt >= 4 * i
                                                         else (mL, mL[:, kt - 4 * i + 4, :])),
                                        pre_hook=(tail_s[0], fin_sel), defer=True)
                o_b = ob[oi % 2]; oi += 1
                row = (g * 4 + r) * 64

                def fin_win(o3w=o3w, tail_w=tail_w, hrow=hrow, o_b=o_b, row=row, i=i, gc=gc):
                    tail_w[1]()
                    c.op("dve", nc.vector.tensor_tensor, outs=[on[1]], ins=[o3w, ps_b], out=on[1][:], in0=o3w[:],
                         in1=ps_b[0:64, :], op=ALU.mult)
                    c.op("pe", nc.tensor.matmul, outs=[ps_g], ins=[SelG_sb, gc], out=ps_g[0:64, :],
                         lhsT=SelG_sb[:, (hrow + 2) * 64:(hrow + 3) * 64], rhs=gc[:], start=True, stop=True)
                    c.op("dve", nc.vector.tensor_tensor, outs=[on[1]], ins=[on[1], ps_g], out=on[1][:], in0=on[1][:],
                         in1=ps_g[0:64, :], op=ALU.mult)
                    c.op("pool", nc.gpsimd.tensor_tensor, outs=[o_b], ins=[acc, on[1]], out=o_b[:], in0=acc[:],
                         in1=on[1][:], op=ALU.add)
                    c.dma("pool", OT[row:row + 64, i * 512:(i + 1) * 512], o_b[:], in_t=o_b)
                pend_fin = (tail_w[0], fin_win)
            pend_fin[0]()
            pend_fin[1]()
    c.finish("pool")
    c.close()
    return nc


def nsa_consts():
    cs = {}
    n = np.arange(512)
    j = np.arange(128)
    ov = ((16 * n[:, None] < 64 * j[None, :] + 64) & (16 * n[:, None] + 32 > 64 * j[None, :]) & (n[:, None] < 511))
    cs["ovl"] = np.ascontiguousarray(ov.reshape(4, 128, 128).transpose(1, 0, 2)).astype(NPBF)
    cs["Eind"] = (np.arange(S)[None, :] // 64 == np.arange(128)[:, None]).astype(NPBF)
    p = np.arange(128)
    cs["JC"] = (j[None, :] - (p[:, None] // 64)).astype(np.float32)
    cs["CB"] = np.broadcast_to((1e9 + 1e6 * j)[None, :], (128, 128)).astype(np.float32).copy()
    cs["ident"] = np.eye(128, dtype=np.float32)
    sel = np.zeros((32, 24, 64), np.float32)
    for m in range(24):
        sel[m, m, :] = 1.0
    cs["SelG"] = sel.reshape(32, 24 * 64)
    np_ = np.arange(128)[:, None, None]
    m = np.arange(5)[None, :, None]
    t = np.arange(512)[None, None, :]
    cs["cmask"] = ((16 * np_ + 31 - 512 * m) <= t).astype(NPBF)
    r = np.arange(4)[None, :, None]
    cs["maskL"] = ((128 * r + np_) > t).astype(NPBF)
    return cs


def layer3(hT_list, p, cs):
    ncP = get_nc("P_nsa", build_P_nsa)
    cosF, sinF = rope_tables_fm(64, 128)
    pres = launch(ncP, [{"hT": hT_list[c], "w": p["l3_nsa_w_in"],
                         "cosT": np.ascontiguousarray(cosF[:, (c % 2) * TOK:(c % 2 + 1) * TOK]),
                         "sinT": np.ascontiguousarray(sinF[:, (c % 2) * TOK:(c % 2 + 1) * TOK])}
                        for c in range(NCORES)])
    g = {k: gather_heads(pres, k, True) for k in ("QT", "KcT", "VcT", "KsT", "KwT")}
    g["GT"] = []
    for c in range(NCORES):
        b, hh = c // 2, c % 2
        full = np.concatenate([pres[2 * b]["GT"], pres[2 * b + 1]["GT"]], axis=1)
        pad = np.zeros((32, S), np.float32)
        pad[0:24] = full[16 + hh * 24:16 + hh * 24 + 24]
        g["GT"].append(pad)
    g["Vs"] = gather_heads(pres, "Vs", False)
    g["Vw"] = gather_heads(pres, "Vw", False)
    nsc = nsa_consts()
    posT = np.ascontiguousarray(np.stack([p["l3_nsa_cmp_pos_k"].T, p["l3_nsa_cmp_pos_v"].T], axis=1)).astype(np.float32)
    w2 = np.stack([p["l3_nsa_cmp_w2_k"].reshape(2, 128, 64).transpose(1, 0, 2),
                   p["l3_nsa_cmp_w2_v"].reshape(2, 128, 64).transpose(1, 0, 2)], axis=1)
    w2 = np.ascontiguousarray(w2).astype(np.float32)
    ncA = get_nc("A_nsa", build_A_nsa)
    maps = []
    for c in range(NCORES):
        m = {k: g[k][c] for k in g}
        m.update(posT=posT, w1k=p["l3_nsa_cmp_w1_k"], w1v=p["l3_nsa_cmp_w1_v"], w2=w2, maskC=cs["maskC"])
        m.update(nsc)
        maps.append(m)
    ares = launch(ncA, maps)
    OT = scatter_OT(ares)
    return run_M(hT_list, OT, p["l3_nsa_w_o"], p["l3_mlp_w1"], p["l3_mlp_w2"], p["l3_ln1_g"], p["l3_ln1_b"],
                 p["l3_ln2_g"], p["l3_ln2_b"]), dict(pres=pres, ares=ares, OT=OT)


def kernel_unfused(**inputs):
    p = {k: np.asarray(v) for k, v in inputs.items()}
    cs = consts()
    hT = to_fm(p["x"].astype(np.float32))
    hT, _ = layer0(hT, p, cs)
    hT, _ = layer1(hT, p, cs)
    hT, _ = layer2(hT, p, cs)
    hT, _ = layer3(hT, p, cs)
    out = np.concatenate([h.T for h in hT], axis=0).reshape(B, S, D)
    return np.ascontiguousarray(out).astype(np.float32)


W_SHAPES = {
    "l0_sb_w_qkv": [D, 3 * D], "l0_sb_w_o": [D, D], "l1_moba_w_qkv": [D, 3 * D], "l1_moba_w_o": [D, D],
    "l2_mla_w_in": [D, 416], "l2_mla_w_uq": [256, 1536], "l2_mla_w_ukv": [128, 2048], "l2_mla_w_o": [D, D],
    "l3_nsa_w_in": [D, NSA_IN], "l3_nsa_cmp_w1_k": [2048, 256], "l3_nsa_cmp_w1_v": [2048, 256], "l3_nsa_w_o": [D, D],
}
for _l in range(4):
    W_SHAPES[f"l{_l}_mlp_w1"] = [D, DFF]
    W_SHAPES[f"l{_l}_mlp_w2"] = [DFF, D]


CC_MAX_BYTES = 2 * 1024 * 1024


class Exchange:
    def __init__(self, c, nc, par_sp):
        self.c = c
        self.nc = nc
        self.xsem = {"sp": nc.alloc_semaphore(name="xsem_sp"), "pool": nc.alloc_semaphore(name="xsem_pool")}
        self.cnt = {"sp": 0, "pool": 0}
        self.q = 0
        self.pars = {"sp": par_sp,
                     "pool": nc.gpsimd.snap(nc.gpsimd.partition_id() % 2, min_val=0, max_val=1)}

    @staticmethod
    def alloc(I, name, rows, cols, dt, kind, rc_big=None):
        es = 4 if dt == F32 else 2
        if rows * cols * es <= CC_MAX_BYTES:
            rc = rows
        else:
            rc = rc_big or {"fm": 256, "tm": 1024, "ot": 128}[kind]
        F = Fuse.active
        if kind == "fm":
            mine = I(name + "_m", [rows, cols], dt)
            lay = rc if rc < rows else rows // 2
        elif kind == "fm_all":
            mine = None
            lay = rows
        elif kind == "tm":
            mine = I(name + "_m", [2 * rows, cols // 2], dt)
            lay = rc
        elif kind == "ot":
            mine = I(name + "_m", [2 * rows, cols // 2], dt)
            lay = rc
        elif kind == "gt":
            mine = I(name + "_m", [48, cols], dt)
            lay = 24
        g = I(name + "_g", [2 * rows, cols], dt)
        if F is not None:
            F.lay[(mine if mine is not None else g).tensor.name] = lay
        return I(name + "_s", [rows, cols], dt), g, (mine if mine is not None else g), kind, rc

    def run(self, items):
        c = self.c
        for s_, g_, m_, kind, rc in items:
            rows = s_.shape[0]
            for j in range(rows // rc):
                c.allgather(s_[j * rc:(j + 1) * rc, :], g_[j * 2 * rc:(j + 1) * 2 * rc, :])
        for ek in ("sp", "pool"):
            c.eng[ek].wait_ge(c.ccsem, c.cccnt)
        for s_, g_, m_, kind, rc in items:
            if kind == "fm_all":
                continue
            ek = ("sp", "pool")[self.q % 2]
            self.q += 1
            par = self.pars[ek]
            rows = s_.shape[0]
            nch = rows // rc
            if kind == "fm" and nch == 1:
                src = g_.rearrange("(r h f) t -> r h f t", r=2, h=2)[:, bass.ds(par, 1), :, :] \
                    .rearrange("r 1 f t -> r f t")
                dst = m_.rearrange("(r f) t -> r f t", r=2)
            elif kind == "fm":
                src = g_.rearrange("(h x) t -> h x t", h=2)[bass.ds(par, 1), :, :].rearrange("1 x t -> x t")
                dst = m_
            elif kind == "tm":
                src = g_.rearrange("x (h n) -> x h n", h=2)[:, bass.ds(par, 1), :].rearrange("x 1 n -> x n")
                dst = m_
            elif kind == "ot":
                src = g_.rearrange("x (rr t) -> x rr t", rr=2)[:, bass.ds(par, 1), :].rearrange("x 1 t -> x t")
                dst = m_
            elif kind == "gt":
                src = g_.rearrange("(r f) t -> f r t", r=2)[16:64].rearrange("(h f) r t -> f h r t", h=2)[
                    :, bass.ds(par, 1), :, :].rearrange("f 1 r t -> f r t")
                dst = m_.rearrange("(r f) t -> f r t", r=2)
            c.eng[ek].dma_start(out=dst, in_=src).then_inc(self.xsem[ek], 16)
            self.cnt[ek] += 16
        for ek in c.eng:
            for q in ("sp", "pool"):
                if self.cnt[q] > 0:
                    c.eng[ek].wait_ge(self.xsem[q], self.cnt[q])


def build_fused():
    F = Fuse()
    Fuse.active = F
    try:
        nc = F.nc
        F.par = nc.sync.snap(nc.sync.partition_id() % 2, min_val=0, max_val=1)
        E = F.ext_in
        I = F.internal

        def W(name):
            return E(name, W_SHAPES[name], F32)

        X = Exchange(F.c if F.c is not None else make_ctx(nc), nc, F.par)
        AG = X.run

        def gath(name, rows, cols, dt, kind):
            return X.alloc(I, name, rows, cols, dt, kind)

        def M_phase(l, OTg, h_in, h_out, wo):
            F.io = {"OT": OTg, "hT": h_in, "w_o": W(wo), "w1": W(f"l{l}_mlp_w1"), "w2": W(f"l{l}_mlp_w2"),
                    "lnp": E(f"lnp{l}", [128, 4, 8], F32), "hO": h_out}
            build_M()

        maskS = E("maskS", [128, 4, 512], BF16)
        maskC = E("maskC", [128, 4, 512], BF16)
        tri = E("tri", [128, 128], BF16)
        ident = E("ident", [128, 128], F32)
        cosT = E("cosT", [128, TOK], F32)
        sinT = E("sinT", [128, TOK], F32)
        h0 = E("hT0", [D, TOK], F32)
        h = [h0] + [I(f"h{l}", [D, TOK], F32) for l in (1, 2, 3)]
        out = nc.dram_tensor("out", [D, TOK], F32, kind="ExternalOutput").ap()
        h.append(out)

        q = gath("l0QT", D, TOK, BF16, "fm"); k = gath("l0KT", D, TOK, BF16, "fm"); v = gath("l0V", TOK, D, BF16, "tm")
        F.io = {"hT": h[0], "w": W("l0_sb_w_qkv"), "QT": q[0], "KT": k[0], "V": v[0]}
        build_P_qkv(False, -0.125)
        AG([q, k, v])
        o = gath("l0OT", 512, S, BF16, "ot")
        F.io = {"QT": q[2], "KT": k[2], "V": v[2], "maskS": maskS, "tri": tri, "OT": o[0]}
        build_A_sb()
        AG([o])
        M_phase(0, o[2], h[0], h[1], "l0_sb_w_o")
        q = gath("l1QT", D, TOK, BF16, "fm"); k = gath("l1KT", D, TOK, BF16, "fm"); v = gath("l1V", TOK, D, BF16, "tm")
        F.io = {"hT": h[1], "w": W("l1_moba_w_qkv"), "cosT": cosT, "sinT": sinT, "QT": q[0], "KT": k[0], "V": v[0]}
        build_P_qkv(True, 0.125)
        AG([q, k, v])
        o = gath("l1OT", 512, S, BF16, "ot")
        F.io = {"QT": q[2], "KT": k[2], "V": v[2], "maskC": maskC, "Eind": E("Eind32", [32, S], BF16), "ident": ident,
                "OT": o[0]}
        build_A_soft("moba")
        AG([o])
        M_phase(1, o[2], h[1], h[2], "l1_moba_w_o")
        q = X.alloc(I, "l2QT", 1536, TOK, BF16, "fm", rc_big=192); k = gath("l2KN", D, TOK, BF16, "fm")
        kr = gath("l2KR", 32, TOK, BF16, "fm_all"); v = gath("l2V", TOK, D, BF16, "tm")
        F.io = {"hT": h[2], "w_in": W("l2_mla_w_in"), "w_uq": W("l2_mla_w_uq"), "w_ukv": W("l2_mla_w_ukv"),
                "gq": E("gq", [128, 2], F32), "gkv": E("gkv", [128, 1], F32), "cos96": E("cos96", [96, TOK], F32),
                "sin96": E("sin96", [96, TOK], F32), "QT": q[0], "KNT": k[0], "KRT": kr[0], "V": v[0]}
        build_P_mla()
        AG([q, k, kr, v])
        o = gath("l2OT", 512, S, BF16, "ot")
        F.io = {"QT": q[2], "KNT": k[2], "KRT": kr[2], "V": v[2], "maskC": maskC, "OT": o[0]}
        build_A_soft("mla")
        AG([o])
        M_phase(2, o[2], h[2], h[3], "l2_mla_w_o")
        names = [("QT", D, TOK, BF16, "fm"), ("KcT", 256, TOK, BF16, "fm"), ("VcT", 256, TOK, BF16, "fm"),
                 ("KsT", 256, TOK, BF16, "fm"), ("KwT", 256, TOK, BF16, "fm"), ("Vs", TOK, 256, BF16, "tm"),
                 ("Vw", TOK, 256, BF16, "tm"), ("GT", 64, TOK, F32, "gt")]
        sg = {n: gath("l3" + n, r, cc, dt, kd) for n, r, cc, dt, kd in names}
        F.io = {"hT": h[3], "w": W("l3_nsa_w_in"), "cosT": cosT, "sinT": sinT}
        F.io.update({n: sg[n][0] for n in sg})
        build_P_nsa()
        AG([sg[n] for n in sg])
        o = gath("l3OT", 512, S, BF16, "ot")
        F.io = {n: sg[n][2] for n in sg}
        F.io.update({"posT": E("posT", [64, 2, 32], F32), "w1k": W("l3_nsa_cmp_w1_k"), "w1v": W("l3_nsa_cmp_w1_v"),
                     "w2": E("w2nsa", [128, 2, 2, 64], F32), "maskC": maskC, "maskL": E("maskL", [128, 4, 512], BF16),
                     "cmask": E("cmask", [128, 5, 512], BF16), "ovl": E("ovl", [128, 4, 128], BF16),
                     "Eind": E("Eind128", [128, S], BF16), "JC": E("JC", [128, 128], F32),
                     "CB": E("CB", [128, 128], F32), "ident": ident, "SelG": E("SelG", [32, 24 * 64], F32),
                     "OT": o[0]})
        build_A_nsa()
        AG([o])
        M_phase(3, o[2], h[3], h[4], "l3_nsa_w_o")
        F.c.final_finish("pool")
        F.c.final_finish("sp")
    finally:
        Fuse.active = None
    return nc


def fused_inputs(p):
    cs = consts()
    nsc = nsa_consts()
    hT = to_fm(p["x"].astype(np.float32))
    cosF, sinF = rope_tables_fm(64, 128)
    cos32, sin32 = rope_tables_fm(32, 32)
    cos96 = np.concatenate([np.ones((64, S), np.float32), cos32], axis=0)
    sin96 = np.concatenate([np.zeros((64, S), np.float32), sin32], axis=0)
    common = {k: np.ascontiguousarray(p[k]).astype(np.float32) for k in W_SHAPES}
    for l in range(4):
        common[f"lnp{l}"] = lnp_pack(p[f"l{l}_ln1_g"], p[f"l{l}_ln1_b"], p[f"l{l}_ln2_g"], p[f"l{l}_ln2_b"])
    common.update(maskS=cs["maskS"], maskC=cs["maskC"], tri=cs["tri"], ident=np.eye(128, dtype=np.float32))
    common["Eind32"] = (np.arange(S)[None, :] // 256 == np.arange(32)[:, None]).astype(NPBF)
    common["gq"] = np.ascontiguousarray(p["l2_mla_q_norm"].reshape(2, 128).T).astype(np.float32)
    common["gkv"] = np.ascontiguousarray(p["l2_mla_kv_norm"].reshape(1, 128).T).astype(np.float32)
    common["posT"] = np.ascontiguousarray(
        np.stack([p["l3_nsa_cmp_pos_k"].T, p["l3_nsa_cmp_pos_v"].T], axis=1)).astype(np.float32)
    common["w2nsa"] = np.ascontiguousarray(
        np.stack([p["l3_nsa_cmp_w2_k"].reshape(2, 128, 64).transpose(1, 0, 2),
                  p["l3_nsa_cmp_w2_v"].reshape(2, 128, 64).transpose(1, 0, 2)], axis=1)).astype(np.float32)
    common.update(maskL=nsc["maskL"], cmask=nsc["cmask"], ovl=nsc["ovl"], Eind128=nsc["Eind"], JC=nsc["JC"],
                  CB=nsc["CB"], SelG=nsc["SelG"])
    maps = []
    for c in range(NCORES):
        m = dict(common)
        sl = slice((c % 2) * TOK, (c % 2 + 1) * TOK)
        m["hT0"] = hT[c]
        m["cosT"] = np.ascontiguousarray(cosF[:, sl]); m["sinT"] = np.ascontiguousarray(sinF[:, sl])
        m["cos96"] = np.ascontiguousarray(cos96[:, sl]); m["sin96"] = np.ascontiguousarray(sin96[:, sl])
        maps.append(m)
    return maps


def kernel(**inputs):
    p = {k: np.asarray(v) for k, v in inputs.items()}
    nc = get_nc("fused", build_fused)
    res = launch(nc, fused_inputs(p))
    out = np.concatenate([r["out"].T for r in res], axis=0).reshape(B, S, D)
    return np.ascontiguousarray(out).astype(np.float32)
```
